# Optimizing a Trainium2 kernel written in Bass

```python
import math
import jax, jax.numpy as jnp
from jax import lax
import numpy as np

D_MODEL = 1024
BATCH = 2
SEQ = 16384
DEPTH = 1
DEC_BATCH = 32
DEC_SEQ = 16
PAST_LEN = 1024

CHUNK = 64
N_META = 16
N_HEADS = 8
HEAD_DIM = 64
D_ATT = N_HEADS * HEAD_DIM
SSM_GROUP = 16
D_SSM = 512
N_SSM_GROUPS = D_SSM // SSM_GROUP
SSM_STATE = 64
D_MIX = D_ATT + D_SSM
Q_BLOCK = 128
EPS = 1e-6
NEG_INF = -1e30
D_IN = 3 * D_ATT + N_HEADS + D_ATT + 2 * D_SSM
SPLITS = (D_ATT, 2 * D_ATT, 3 * D_ATT, 3 * D_ATT + N_HEADS, 4 * D_ATT + N_HEADS,
          4 * D_ATT + N_HEADS + D_SSM)

kernel_name = 'hymba_fox_s5_streaming_step'


def rms_norm(x, g):
    xf = x.astype(jnp.float32)
    y = xf * lax.rsqrt(jnp.mean(xf * xf, axis=-1, keepdims=True) + EPS)
    return (y * g.astype(jnp.float32)).astype(x.dtype)


def mixer_inputs(x, norm_g, w_in, b_f, q_norm_g, k_norm_g):
    bsz, L = x.shape[0], x.shape[1]
    z = rms_norm(x, norm_g) @ w_in
    q, k, v, f_logit, gate_a, u, gate_s = jnp.split(z, SPLITS, axis=-1)
    q = rms_norm(q.reshape(bsz, L, N_HEADS, HEAD_DIM), q_norm_g)
    k = rms_norm(k.reshape(bsz, L, N_HEADS, HEAD_DIM), k_norm_g)
    v = v.reshape(bsz, L, N_HEADS, HEAD_DIM)
    logf = jax.nn.log_sigmoid(f_logit.astype(jnp.float32) + b_f.astype(jnp.float32))
    u = u.reshape(bsz, L, N_SSM_GROUPS, SSM_GROUP)
    return q, k, v, logf, gate_a, u, gate_s


def fox_attend(q, k, v, cum_q, cum_k, q_pos):
    s = jnp.einsum('bthd,bshd->bhts', q, k, preferred_element_type=jnp.float32) * (HEAD_DIM ** -0.5)
    bias = jnp.swapaxes(cum_q, 1, 2)[:, :, :, None] - jnp.swapaxes(cum_k, 1, 2)[:, :, None, :]
    mask = jnp.arange(k.shape[1])[None, :] <= q_pos[:, None]
    p = jax.nn.softmax(jnp.where(mask, s + bias, NEG_INF), axis=-1)
    return jnp.einsum('bhts,bshd->bthd', p.astype(v.dtype), v)


def fox_prompt(q, k, v, logf):
    bsz, L = q.shape[0], q.shape[1]
    n_blk = -(-L // Q_BLOCK)
    pad = n_blk * Q_BLOCK - L
    cum = jnp.cumsum(logf, axis=1)
    q_p = jnp.pad(q, ((0, 0), (0, pad), (0, 0), (0, 0)))
    cum_p = jnp.pad(cum, ((0, 0), (0, pad), (0, 0)))

    def one_block(i):
        start = i * Q_BLOCK
        qb = lax.dynamic_slice_in_dim(q_p, start, Q_BLOCK, axis=1)
        cb = lax.dynamic_slice_in_dim(cum_p, start, Q_BLOCK, axis=1)
        return fox_attend(qb, k, v, cb, cum, start + jnp.arange(Q_BLOCK))

    out = lax.map(one_block, jnp.arange(n_blk))
    out = jnp.moveaxis(out, 0, 1).reshape(bsz, n_blk * Q_BLOCK, N_HEADS, HEAD_DIM)
    return out[:, :L]


def fox_sample(q, k, v, logf, cache_k, cache_v, cache_logf):
    past = cache_k.shape[1]
    T = q.shape[1]
    k_all = jnp.concatenate([cache_k.astype(k.dtype), k], axis=1)
    v_all = jnp.concatenate([cache_v.astype(v.dtype), v], axis=1)
    cum = jnp.cumsum(jnp.concatenate([cache_logf.astype(jnp.float32), logf], axis=1), axis=1)
    return fox_attend(q, k_all, v_all, cum[:, past:], cum, past + jnp.arange(T))


def s5_scan(u, x0, a_re, a_im, log_dt, b_re, b_im, c_re, c_im, d):
    f32 = jnp.float32
    A = lax.complex(a_re.astype(f32), a_im.astype(f32))
    dt = jnp.exp(log_dt.astype(f32))[:, None]
    a_bar = jnp.exp(A * dt)
    b_bar = ((a_bar - 1.0) / A)[..., None] * lax.complex(b_re.astype(f32), b_im.astype(f32))
    c = lax.complex(c_re.astype(f32), c_im.astype(f32))
    uf = u.astype(f32)
    bu = jnp.einsum('gph,blgh->blgp', b_bar, uf)
    a_seq = jnp.broadcast_to(a_bar, bu.shape)

    def combine(left, right):
        a1, b1 = left
        a2, b2 = right
        return a2 * a1, a2 * b1 + b2

    a_cum, xs = lax.associative_scan(combine, (a_seq, bu), axis=1)
    xs = xs + a_cum * x0[:, None]
    y = jnp.einsum('ghp,blgp->blgh', c, xs).real + d.astype(f32) * uf
    return y, xs[:, -1]


def mixer_output(att, gate_a, ys, gate_s, w_glu, b_glu, w_out):
    bsz, L = att.shape[0], att.shape[1]
    att = att.reshape(bsz, L, D_ATT) * jax.nn.silu(gate_a)
    z = jax.nn.gelu(ys.reshape(bsz, L, D_SSM)).astype(gate_s.dtype)
    s5 = z * jax.nn.sigmoid(z @ w_glu + b_glu) * jax.nn.silu(gate_s)
    return jnp.concatenate([att, s5], axis=-1) @ w_out


def setup_inputs(seed: int = 0) -> dict:
    key = jax.random.key(seed)
    ks = jax.random.split(key, 24)
    f32 = jnp.float32

    def nrm(k, shape, s):
        return s * jax.random.normal(k, shape, f32)

    G, P = N_SSM_GROUPS, SSM_STATE
    return {
        'x_prompt': nrm(ks[0], (BATCH, SEQ, D_MODEL), 1.0),
        'x_sample': nrm(ks[1], (DEC_BATCH, DEC_SEQ, D_MODEL), 1.0),
        'cache_k': nrm(ks[2], (DEPTH, DEC_BATCH, PAST_LEN, N_HEADS, HEAD_DIM), 1.0),
        'cache_v': nrm(ks[3], (DEPTH, DEC_BATCH, PAST_LEN, N_HEADS, HEAD_DIM), 1.0),
        'cache_logf': jax.nn.log_sigmoid(2.5 + jax.random.normal(ks[4], (DEPTH, DEC_BATCH, PAST_LEN, N_HEADS), f32)),
        'state_s5_re': nrm(ks[5], (DEPTH, DEC_BATCH, G, P), 0.1),
        'state_s5_im': nrm(ks[6], (DEPTH, DEC_BATCH, G, P), 0.1),
        'meta_tokens': nrm(ks[7], (N_META, D_MODEL), 1.0),
        'norm_g': 1.0 + nrm(ks[8], (DEPTH, D_MODEL), 0.02),
        'w_in': nrm(ks[9], (DEPTH, D_MODEL, D_IN), D_MODEL ** -0.5),
        'b_f': jax.random.uniform(ks[10], (DEPTH, N_HEADS), f32, 1.0, 4.0),
        'q_norm_g': 1.0 + nrm(ks[11], (DEPTH, HEAD_DIM), 0.02),
        'k_norm_g': 1.0 + nrm(ks[12], (DEPTH, HEAD_DIM), 0.02),
        's5_a_re': -0.5 + nrm(ks[13], (DEPTH, G, P), 0.01),
        's5_a_im': math.pi * jnp.arange(P, dtype=f32) + nrm(ks[14], (DEPTH, G, P), 0.01),
        's5_log_dt': jax.random.uniform(ks[15], (DEPTH, G), f32, math.log(1e-3), math.log(1e-1)),
        's5_b_re': nrm(ks[16], (DEPTH, G, P, SSM_GROUP), (2 * SSM_GROUP) ** -0.5),
        's5_b_im': nrm(ks[17], (DEPTH, G, P, SSM_GROUP), (2 * SSM_GROUP) ** -0.5),
        's5_c_re': nrm(ks[18], (DEPTH, G, SSM_GROUP, P), (2 * P) ** -0.5),
        's5_c_im': nrm(ks[19], (DEPTH, G, SSM_GROUP, P), (2 * P) ** -0.5),
        's5_d': nrm(ks[20], (DEPTH, G, SSM_GROUP), 1.0),
        'w_glu': nrm(ks[21], (DEPTH, D_SSM, D_SSM), D_SSM ** -0.5),
        'b_glu': nrm(ks[22], (DEPTH, D_SSM), 0.01),
        'w_out': nrm(ks[23], (DEPTH, D_MIX, D_MODEL), D_MIX ** -0.5),
    }


def reference(x_prompt, x_sample, cache_k, cache_v, cache_logf, state_s5_re, state_s5_im,
              meta_tokens, norm_g, w_in, b_f, q_norm_g, k_norm_g, s5_a_re, s5_a_im, s5_log_dt,
              s5_b_re, s5_b_im, s5_c_re, s5_c_im, s5_d, w_glu, b_glu, w_out):
    bp = x_prompt.shape[0]
    meta = jnp.broadcast_to(meta_tokens.astype(x_prompt.dtype)[None], (bp, N_META, D_MODEL))
    hp = jnp.concatenate([meta, x_prompt], axis=1)
    hs = x_sample
    kp, vp, fp, srp, sip = [], [], [], [], []
    kss, vss, fss, srs, sis = [], [], [], [], []
    for l in range(DEPTH):
        proj = (norm_g[l], w_in[l], b_f[l], q_norm_g[l], k_norm_g[l])
        ssm = (s5_a_re[l], s5_a_im[l], s5_log_dt[l], s5_b_re[l], s5_b_im[l],
               s5_c_re[l], s5_c_im[l], s5_d[l])
        outp = (w_glu[l], b_glu[l], w_out[l])

        q, k, v, logf, ga, u, gs = mixer_inputs(hp, *proj)
        att = fox_prompt(q, k, v, logf)
        x0 = jnp.zeros((bp, N_SSM_GROUPS, SSM_STATE), jnp.complex64)
        ys, xl = s5_scan(u, x0, *ssm)
        hp = hp + mixer_output(att, ga, ys, gs, *outp)
        kp.append(k); vp.append(v); fp.append(logf)
        srp.append(xl.real); sip.append(xl.imag)

        q, k, v, logf, ga, u, gs = mixer_inputs(hs, *proj)
        att = fox_sample(q, k, v, logf, cache_k[l], cache_v[l], cache_logf[l])
        x0 = lax.complex(state_s5_re[l].astype(jnp.float32), state_s5_im[l].astype(jnp.float32))
        ys, xl = s5_scan(u, x0, *ssm)
        hs = hs + mixer_output(att, ga, ys, gs, *outp)
        kss.append(k); vss.append(v); fss.append(logf)
        srs.append(xl.real); sis.append(xl.imag)

    y_prompt = hp[:, N_META:]
    y_sample = hs
    new_k_prompt = jnp.stack(kp)
    new_v_prompt = jnp.stack(vp)
    new_logf_prompt = jnp.stack(fp)
    new_s5_re_prompt = jnp.stack(srp)
    new_s5_im_prompt = jnp.stack(sip)
    new_k_sample = jnp.stack(kss)
    new_v_sample = jnp.stack(vss)
    new_logf_sample = jnp.stack(fss)
    new_s5_re_sample = jnp.stack(srs)
    new_s5_im_sample = jnp.stack(sis)
    return (y_prompt, y_sample, new_k_prompt, new_v_prompt, new_logf_prompt,
            new_s5_re_prompt, new_s5_im_prompt, new_k_sample, new_v_sample,
            new_logf_sample, new_s5_re_sample, new_s5_im_sample)
```

```python
import math
import numpy as np
import ml_dtypes
from contextlib import ExitStack
import concourse.bass as bass
import concourse.mybir as mybir
from concourse.bass_utils import run_bass_kernel_spmd

F32 = mybir.dt.float32
BF16 = mybir.dt.bfloat16
AF = mybir.ActivationFunctionType
ALU = mybir.AluOpType

D_MODEL = 1024
SEQ = 16384
N_META = 16
L_REAL = SEQ + N_META
TW = 512
import os
NTILE = int(os.environ.get("MK_NTILE", "33"))
TP = NTILE * TW
NBLK = TP // 128
TX = TP + 256
NCOL = 770
EPS = 1e-6
NEG = -30000.0


class Buf:
    __slots__ = ("name", "w", "rs", "const", "excl")

    def __init__(self, name, const=False, excl=False):
        self.name = name
        self.w = None
        self.rs = []
        self.const = const
        self.excl = excl


class Node:
    __slots__ = ("eng", "fn", "deps", "kind", "sem", "val", "used", "idx", "fuse", "cost", "seg", "fin", "prio")

    def __init__(self, eng, fn, kind):
        self.eng = eng
        self.fn = fn
        self.kind = kind
        self.deps = []
        self.sem = None
        self.val = 0
        self.used = False
        self.fuse = True
        self.cost = None
        self.seg = 0
        self.fin = 0.0
        self.prio = 0


ENGS = ("pe", "act", "dve", "pool", "sp")
N_DMA_SEMS = 48
SEM_ROLL = 30000


class Prog:
    def __init__(self, nc):
        self.nc = nc
        self.q = {e: [] for e in ENGS}
        self.nodes = []
        self.dma_i = 0
        self.dma_j = 0
        self.seg = 0
        self.cur_prio = 0
        self.dma_last = [None] * N_DMA_SEMS

    def _mk(self, eng, fn, kind, reads, writes, extra):
        n = Node(eng, fn, kind)
        seen = set()
        ex = [b for b in reads if b.excl]
        if ex:
            reads = [b for b in reads if not b.excl]
            writes = list(writes) + [b for b in ex if b not in writes]

        def add(d, k):
            if d is None or (id(d), k) in seen:
                return
            seen.add((id(d), k))
            n.deps.append((d, k))
        for b in reads:
            add(b.w, "raw")
        for b in writes:
            add(b.w, "waw")
            for r in b.rs:
                add(r, "war")
        for d in extra:
            add(d, "raw")
        for b in reads:
            if not b.const:
                b.rs.append(n)
        for b in writes:
            b.w = n
            b.rs = []
        n.idx = len(self.nodes)
        n.seg = self.seg
        n.prio = self.cur_prio
        self.nodes.append(n)
        self.q[eng].append(n)
        return n

    def op(self, eng, fn, reads=(), writes=(), deps=(), fuse=True, cost=None):
        n = self._mk(eng, fn, "c", reads, writes, deps)
        n.fuse = fuse
        n.cost = cost
        return n

    def dma(self, eng, out, in_, reads=(), writes=(), deps=()):
        half = N_DMA_SEMS // 2
        if eng == "pool":
            k = half + self.dma_j % half
            self.dma_j += 1
        else:
            k = self.dma_i % half
            self.dma_i += 1
        extra = list(deps)
        if self.dma_last[k] is not None:
            extra.append(self.dma_last[k])
        n = self._mk(eng, lambda e: e.dma_start(out=out, in_=in_), "d", reads, writes, extra)
        n.sem = k
        self.dma_last[k] = n
        return n

    def coll(self, fn, reads=(), writes=()):
        return self._mk("pool", fn, "x", reads, writes, ())

    def wait(self, eng, deps):
        return self._mk(eng, None, "w", (), (), deps)

    def barrier(self):
        deps = []
        for e in ENGS:
            for n in reversed(self.q[e]):
                if n.kind == "c":
                    deps.append(n)
                    break
        deps += [n for n in self.dma_last if n is not None]
        deps += [n for n in self.nodes if n.kind == "x"]
        self.seg += 1
        for e in ENGS:
            self.wait(e, deps)
        self.seg += 1

    DEF_COST = {"pe": 0.25, "act": 0.75, "dve": 0.75, "pool": 0.8, "sp": 0.1}

    def reschedule(self, window=48, hop=1.2):
        segs = {}
        for n in self.nodes:
            segs.setdefault(n.seg, []).append(n)
        clock = {e: 0.0 for e in ENGS}
        newq = {e: [] for e in ENGS}
        done = set()
        for sg in sorted(segs):
            if all(n.kind == "w" for n in segs[sg]):
                lastc = []
                for e in ENGS:
                    for m in reversed(newq[e]):
                        if m.kind == "c":
                            lastc.append(m)
                            break
                for n in segs[sg]:
                    keep = [(d, k) for d, k in n.deps if d.kind in ("d", "x")]
                    n.deps = keep + [(m, "raw") for m in lastc]
            pend = {e: [n for n in segs[sg] if n.eng == e] for e in ENGS}
            left = sum(len(v) for v in pend.values())
            while left:
                best = None
                for e in ENGS:
                    cand = pend[e]
                    lim = min(window, len(cand))
                    for ci in range(lim):
                        n = cand[ci]
                        ok = True
                        rdy = 0.0
                        for d, k in n.deps:
                            if id(d) not in done:
                                ok = False
                                break
                            t = d.fin + (0.05 if (d.eng == e and d.kind == "c") else hop)
                            if t > rdy:
                                rdy = t
                        if not ok:
                            continue
                        st = max(clock[e], rdy)
                        key = (st + (0.8 if n.prio else 0.0), n.idx)
                        if best is None or key < best[0]:
                            best = (key, e, ci, n, st)
                        if rdy <= clock[e] and not n.prio:
                            break
                assert best is not None, "scheduler stuck"
                _, e, ci, n, st = best
                pend[e].pop(ci)
                left -= 1
                c = n.cost
                if c is None:
                    c = 0.0 if n.kind == "w" else (2.5 if n.kind in ("d", "x") else self.DEF_COST[e])
                if n.kind in ("d", "x"):
                    clock[e] = st + 0.3
                    n.fin = st + c
                else:
                    clock[e] = st + c
                    n.fin = st + c
                done.add(id(n))
                newq[e].append(n)
        self.q = newq
        return max(clock.values())

    def emit(self, stack):
        nc = self.nc
        for n in self.nodes:
            for d, k in n.deps:
                if d.kind in ("d", "x"):
                    d.used = True
                elif d.eng != n.eng:
                    d.used = True
                elif n.eng != "pe":
                    d.used = True
                elif n.kind == "d":
                    d.used = True
        esems = {e: [stack.enter_context(nc.semaphore(f"s_{e}0"))] for e in ENGS}
        ecnt = {e: 0 for e in ENGS}
        dsems = [stack.enter_context(nc.semaphore(f"s_dma{i}")) for i in range(N_DMA_SEMS)]
        dcnt = [0] * N_DMA_SEMS
        for n in self.nodes:
            if n.kind == "x":
                n.sem = stack.enter_context(nc.semaphore(f"s_cc{n.idx}"))
                n.val = 1
            elif n.kind == "d":
                dcnt[n.sem] += 16
                n.val = dcnt[n.sem]
                n.sem = dsems[n.sem]
        for e in ENGS:
            for n in self.q[e]:
                if n.kind == "c" and n.used:
                    if ecnt[e] >= SEM_ROLL:
                        esems[e].append(stack.enter_context(nc.semaphore(f"s_{e}{len(esems[e])}")))
                        ecnt[e] = 0
                    ecnt[e] += 1
                    n.val = ecnt[e]
                    n.sem = esems[e][-1]
        block = stack.enter_context(nc.Block())
        handles = {"pe": block.tensor, "act": block.scalar, "dve": block.vector,
                   "pool": block.gpsimd, "sp": block.sync}
        stats = {}
        for e in ENGS:
            queue = self.q[e]

            def body(eng, queue=queue, e=e):
                waited = {}
                nw = 0
                for n in queue:
                    pend = []
                    for d, k in n.deps:
                        if d.kind not in ("d", "x"):
                            if d.eng == e and e == "pe" and n.kind != "d":
                                continue
                        if d.sem is None:
                            continue
                        key = id(d.sem)
                        if waited.get(key, 0) >= d.val:
                            continue
                        waited[key] = d.val
                        pend.append((d.sem, d.val))
                        nw += 1
                    best = {}
                    for s_, v_ in pend:
                        if id(s_) not in best or best[id(s_)][1] < v_:
                            best[id(s_)] = (s_, v_)
                    pend = list(best.values())
                    fuse = None
                    if pend and n.kind == "c" and n.fuse and e in ("act", "dve", "pool"):
                        fuse = pend.pop()
                    for s_, v_ in pend:
                        eng.wait_ge(s_, v_)
                    if n.kind == "w":
                        continue
                    ins = n.fn(eng)
                    if fuse is not None:
                        ins._wait_ge(fuse[0], fuse[1])
                    if n.kind == "x":
                        ins.then_inc(n.sem, 1)
                    elif n.kind == "d":
                        ins.then_inc(n.sem, 16)
                    elif n.used:
                        ins.then_inc(n.sem, 1)
                stats[e] = (len(queue), nw)
            handles[e](body)
        return stats


QB = 1024
ARENA_BYTES = 161 * 1024
STAGES = os.environ.get("MK_STAGES", "A,ATT,SAMP,X")


def build_program(debug=False):
    nc = bass.Bass("TRN2", target_bir_lowering=False)
    P = Prog(nc)
    stack = ExitStack()
    stages = set(STAGES.split(","))

    def din(name, shape, dt=F32):
        return nc.dram_tensor(name, list(shape), dt, kind="ExternalInput").ap()

    def dout(name, shape, dt=F32):
        return nc.dram_tensor(name, list(shape), dt, kind="ExternalOutput").ap()

    def dscr(name, shape, dt):
        return nc.dram_tensor(name, list(shape), dt).ap()

    def sb(name, shape, dt):
        return stack.enter_context(nc.sbuf_tensor("sb_" + name, list(shape), dt))[:]

    def ps(name, shape, dt):
        return stack.enter_context(nc.psum_tensor("ps_" + name, list(shape), dt))[:]

    arena = sb("arena", [128, ARENA_BYTES // 4], F32)
    ar_off = [0]

    def aa(shape, dt):
        nfree = int(np.prod(shape[1:]))
        esz = 2 if dt == BF16 else 4
        nbytes = (nfree * esz + 31) // 32 * 32
        w0 = ar_off[0] // 4
        ar_off[0] += nbytes
        assert ar_off[0] <= ARENA_BYTES, ("arena overflow", ar_off[0])
        v = arena[0:shape[0], w0:w0 + nbytes // 4]
        if dt != F32:
            v = v.bitcast(dt)
        v = v[:, 0:nfree]
        if len(shape) == 3:
            v = v.rearrange("p (a b) -> p a b", a=shape[1])
        elif len(shape) == 4:
            v = v.rearrange("p (a b c) -> p a b c", a=shape[1], b=shape[2])
        return v

    x_d = din("x", [TP, D_MODEL])
    w_d = din("w_in_c", [D_MODEL, NCOL])
    ng_d = din("norm_g", [128, 8])
    bf_d = din("b_f", [128, 2])
    qg_d = din("qg", [128, 1])
    kg_d = din("kg", [128, 1])

    S5D = {}
    for nm in ("s5_are", "s5_aim", "s5_ldt"):
        S5D[nm] = din(nm, [128, 8])
    S5D["s5_d"] = din("s5_d", [128, 1])
    for nm in ("s5_x1", "s5_x2", "s5_cx1", "s5_cx2"):
        S5D[nm] = din(nm, [128, 8, 16])
    s5fin_o = dout("s5fin_out", [128, 8])
    PZ, PM_ = 4096, 2048
    npz = (TX + PZ - 1) // PZ
    npm = (TX + PM_ - 1) // PM_
    zin_p = [dscr(f"zin{i}", [128, min(PZ, TX - i * PZ)], BF16) for i in range(npz)]
    zall_p = [dscr(f"zall{i}", [512, min(PZ, TX - i * PZ)], BF16) for i in range(npz)]
    mixin_p = [dscr(f"mixin{i}", [256, min(PM_, TX - i * PM_)], BF16) for i in range(npm)]
    mixall_p = [dscr(f"mixall{i}", [1024, min(PM_, TX - i * PM_)], BF16) for i in range(npm)]

    def zin_ap(c0, n):
        return zin_p[c0 // PZ][:, c0 % PZ:c0 % PZ + n]

    def zall_ap(c0, n):
        return zall_p[c0 // PZ][:, c0 % PZ:c0 % PZ + n]

    def mixin_ap(r0, r1, c0, n):
        return mixin_p[c0 // PM_][r0:r1, c0 % PM_:c0 % PM_ + n]

    def mixall_ap(c0, n):
        return mixall_p[c0 // PM_][:, c0 % PM_:c0 % PM_ + n]
    sgs_d = dscr("sgs_scr", [128, TX], BF16)

    xs_d = din("xsmp", [256, D_MODEL])
    clf_d = din("clf", [32, 1024])
    kc_d = din("kcT", [16, 2, 64, 1024])
    vc_d = din("vc", [16, 2, 1024, 64])
    wglu_d = din("wglu_c", [512, 128])
    bglu_d = din("bglu_c", [128, 1])
    wout_d = din("wout_c", [1024, 256])
    xc_d = din("x_c", [TX, 256])
    s5x0_d = din("s5x0", [128, 8, 16])
    y_o = dout("y_out", [TX, 256])
    ks_o = dout("ks_out", [256, 128])
    vs_o = dout("vs_out", [256, 128])
    lfs_o = dout("lfs_out", [256, 2])
    s5fins_o = dout("s5fins_out", [128, 8, 16])

    k_o = dout("k_out", [TP, 128])
    v_o = dout("v_out", [TP, 128])
    lf_o = dout("logf_out", [TP, 2])

    qt_d = dscr("qt_scr", [2, 65, TX], BF16)
    kt_d = dscr("kt_scr", [2, 64, TX], BF16)
    sga_d = dscr("sga_scr", [2, 64, TX], BF16)

    ng = sb("ng", [128, 8], F32)
    bfp = sb("bfp", [128, 2], F32)
    nbf = sb("nbf", [128, 2], F32)
    qg = sb("qg", [128, 1], F32)
    kg = sb("kg", [128, 1], F32)
    qg8 = sb("qg8", [128, 1], F32)
    onesf = sb("onesf", [128, 128], F32)
    identf = sb("identf", [128, 128], F32)
    identb = sb("identb", [128, 128], BF16)
    BO = sb("BO", [128, 128], BF16)
    MN = sb("MN", [128, 128], BF16)
    ones2 = sb("ones2", [2, TW], F32)
    onesP = sb("onesP", [128, 64], F32)
    VP = sb("VP", [128, NBLK + 2, 2, 65], BF16)
    NCK = sb("NCK", [128, NBLK + 2, 2], F32)
    cumrow = [sb(f"cumrow{i}", [2, TW], F32) for i in range(2)]
    S5FIN = sb("s5fin", [128, 8], F32)
    S5FINS = sb("s5fins", [128, 8, 16], F32)
    S5X0 = sb("s5x0s", [128, 8, 16], F32)
    segm = sb("segm", [2, TW], F32)

    Z = [ps(f"Z{i}", [128, 1024], F32) for i in range(4)]

    def zb(i, half):
        return Z[i][:, half * 512:(half + 1) * 512]

    pT = [zb(0, h).bitcast(BF16).rearrange("p (k t) -> p k t", k=8) for h in range(2)]
    pP = [zb(1, 0), zb(1, 1)]
    pM = zb(2, 0)
    pV = zb(2, 1).rearrange("p (s f) -> p s f", s=4)
    pK = zb(3, 0).rearrange("p (s f) -> p s f", s=4)
    pS = zb(3, 1)
    PALIAS = {"pT0": "pZ00", "pT1": "pZ01", "pP0": "pZ10", "pP1": "pZ11", "pM": "pZ20", "pV": "pZ21",
              "pK": "pZ30", "pS": "pZ31"}

    B = {}

    def buf(name, const=False):
        name = PALIAS.get(name, name)
        name = {"lnb": "lnb0", "rr": "rr0", "sq": "sq0"}.get(name, name)
        if name not in B:
            B[name] = Buf(name, const, excl=name.startswith("pZ"))
        return B[name]

    P.dma("sp", ng, ng_d[:, :], writes=[buf("ng")])
    P.dma("sp", bfp, bf_d[:, :], writes=[buf("bfp")])
    P.dma("sp", qg, qg_d[:, :], writes=[buf("qg")])
    P.dma("sp", kg, kg_d[:, :], writes=[buf("kg")])
    P.op("dve", lambda e: e.tensor_scalar(out=nbf, in0=bfp, scalar1=-1.0, scalar2=None, op0=ALU.mult),
         reads=[buf("bfp")], writes=[buf("nbf")])
    P.op("dve", lambda e: e.tensor_scalar(out=qg8, in0=qg, scalar1=0.125, scalar2=None, op0=ALU.mult),
         reads=[buf("qg")], writes=[buf("qg8")])
    P.op("pool", lambda e: e.memset(onesf, 1.0), writes=[buf("onesf")])
    P.op("pool", lambda e: e.memset(ones2, 1.0), writes=[buf("ones2")])
    P.op("pool", lambda e: e.memset(onesP, 1.0), writes=[buf("onesP")])
    P.op("pool", lambda e: e.memset(segm, 1.0), writes=[buf("segm")])
    P.op("pool", lambda e: e.memset(segm.rearrange("p (q t) -> p q t", t=16)[:, :, 0:1], 0.0), writes=[buf("segm")])
    P.op("pool", lambda e: e.affine_select(out=identf, in_=onesf, pattern=[[-1, 128]],
                                           compare_op=ALU.is_equal, fill=0.0, base=0, channel_multiplier=1),
         reads=[buf("onesf")], writes=[buf("identf")])
    P.op("pool", lambda e: e.tensor_copy(out=identb, in_=identf), reads=[buf("identf")], writes=[buf("identb")])
    P.op("pool", lambda e: e.memset(onesf, NEG), reads=[buf("identf")], writes=[buf("onesf")])
    P.op("pool", lambda e: e.affine_select(out=MN, in_=onesf, pattern=[[-1, 128]],
                                           compare_op=ALU.is_gt, fill=0.0, base=0, channel_multiplier=1),
         reads=[buf("onesf")], writes=[buf("MN")])
    P.op("pool", lambda e: e.memset(BO, 0.0), writes=[buf("BO")])
    P.op("pool", lambda e: e.memset(BO[0:64, 0:64], 1.0 / 64), writes=[buf("BO")])
    P.op("pool", lambda e: e.memset(BO[64:128, 64:128], 1.0 / 64), writes=[buf("BO")])
    P.op("pool", lambda e: e.memset(VP[:, :, :, 64:65], 1.0), writes=[buf("VP")])

    out_dmas = []

    ar_off[0] = 0
    Wb = aa([128, 8, NCOL], BF16)
    s5_mark = [0]
    def wb_setup():
      wst = [aa([128, NCOL], F32) for i in range(2)]
      for kc in range(8):
        s = kc % 2
        P.dma("sp", wst[s], w_d[kc * 128:(kc + 1) * 128, :], writes=[buf(f"wst{s}")])
        P.op("dve", lambda e, kc=kc, s=s: e.tensor_scalar(
            out=Wb[:, kc, :], in0=wst[s], scalar1=ng[:, kc:kc + 1], scalar2=None, op0=ALU.mult),
            reads=[buf(f"wst{s}"), buf("ng")], writes=[buf("Wb")])

    def silu2_from_psum(items, N, tmps):
        for (i, o, ob), (l, lb, r, rb) in zip(items, tmps):
            P.op("act", lambda e, i=i, l=l: e.activation(out=l[:, 0:N], in_=pP[i][:, 0:N], func=AF.Exp, scale=-1.0),
                 reads=[buf(f"pP{i}")], writes=[buf(lb)])
        for (i, o, ob), (l, lb, r, rb) in zip(items, tmps):
            P.op("act", lambda e, l=l: e.activation(out=l[:, 0:N], in_=l[:, 0:N], func=AF.Ln, bias=1.0, scale=1.0),
                 reads=[buf(lb)], writes=[buf(lb)])
        for (i, o, ob), (l, lb, r, rb) in zip(items, tmps):
            P.op("act", lambda e, l=l, r=r: e.activation(out=r[:, 0:N], in_=l[:, 0:N], func=AF.Exp, scale=-1.0),
                 reads=[buf(lb)], writes=[buf(rb)])
        for (i, o, ob), (l, lb, r, rb) in zip(items, tmps):
            P.op("dve", lambda e, i=i, o=o, r=r: e.tensor_tensor(out=o[:, 0:N], in0=pP[i][:, 0:N], in1=r[:, 0:N], op=ALU.mult),
                 reads=[buf(f"pP{i}"), buf(rb)], writes=[ob])

    def silu_from_psum(i, out_ap, out_buf, N=TW):
        P.op("act", lambda e: e.activation(out=lnb[:, 0:N], in_=pP[i][:, 0:N], func=AF.Exp, scale=-1.0),
             reads=[buf(f"pP{i}")], writes=[buf("lnb")])
        P.op("act", lambda e: e.activation(out=lnb[:, 0:N], in_=lnb[:, 0:N], func=AF.Ln, bias=1.0, scale=1.0),
             reads=[buf("lnb")], writes=[buf("lnb")])
        P.op("act", lambda e: e.activation(out=rr[:, 0:N], in_=lnb[:, 0:N], func=AF.Exp, scale=-1.0),
             reads=[buf("lnb")], writes=[buf("rr")])
        P.op("dve", lambda e: e.tensor_tensor(out=out_ap[:, 0:N], in0=pP[i][:, 0:N], in1=rr[:, 0:N], op=ALU.mult),
             reads=[buf(f"pP{i}"), buf("rr")], writes=[out_buf])

    def s5_alloc_weights():
        S = {}
        S["are"] = aa([128, 8], F32)
        S["aim"] = aa([128, 8], F32)
        S["ldt"] = aa([128, 8], F32)
        S["dvec"] = aa([128, 1], F32)
        S["rho8"] = aa([128, 8], F32)
        S["ar8"] = aa([128, 8], F32)
        S["ai8"] = aa([128, 8], F32)
        S["PiT"] = aa([128, 128], F32)
        S["Wfir"] = aa([128, 8, 128], BF16)
        S["Wst"] = aa([128, 8, 8, 128], BF16)
        S["Wint"] = aa([128, 8, 8, 128], BF16)
        S["COS"] = aa([128, 8, 256], F32)
        S["SIN"] = aa([128, 8, 256], F32)
        S["X0"] = [aa([128, 8], F32) for i in range(2)]
        return S

    def s5_setup(S):
        wb_setup()
        P.cur_prio = 1
        for nm, dn in (("are", "s5_are"), ("aim", "s5_aim"), ("ldt", "s5_ldt"), ("dvec", "s5_d")):
            P.dma("sp", S[nm], S5D[dn][:, :], writes=[buf("s5" + nm)])
        X1 = aa([128, 8, 16], F32)
        X2 = aa([128, 8, 16], F32)
        CX1 = aa([128, 8, 16], F32)
        CX2 = aa([128, 8, 16], F32)
        P.dma("sp", X1, S5D["s5_x1"][:, :, :], writes=[buf("s5X1")])
        P.dma("sp", X2, S5D["s5_x2"][:, :, :], writes=[buf("s5X2")])
        P.dma("sp", CX1, S5D["s5_cx1"][:, :, :], writes=[buf("s5CX1")])
        P.dma("sp", CX2, S5D["s5_cx2"][:, :, :], writes=[buf("s5CX2")])
        sg1 = aa([128, 1], F32)
        sg2 = aa([128, 1], F32)
        dt = aa([128, 8], F32)
        lr = aa([128, 8], F32)
        th = aa([128, 8], F32)
        mlr = aa([128, 8, 9], F32)
        mth = aa([128, 8, 9], F32)
        mcol = aa([128, 9], F32)
        mag = aa([128, 8, 9], F32)
        sn = aa([128, 8, 9], F32)
        cs = aa([128, 8, 9], F32)
        AR = aa([128, 8, 9], F32)
        AI = aa([128, 8, 9], F32)
        t8a = aa([128, 8], F32)
        t8b = aa([128, 8], F32)
        t8c = aa([128, 8], F32)
        cr = aa([128, 8], F32)
        ci = aa([128, 8], F32)
        Bst = aa([128, 8, 16], F32)
        Bsw = aa([128, 8, 16], F32)
        tB = aa([128, 8, 16], F32)
        CA = aa([128, 9, 8, 16], F32)
        Bpad = aa([128, 8, 128], F32)
        ABp = aa([128, 128], F32)
        iot = aa([128, 256], F32)
        ang = aa([128, 256], F32)
        wk = [aa([128, 256], F32) for _ in range(4)]
        wki = aa([128, 256], mybir.dt.int32)
        bT = "s5tmp"

        def dv(fn, extra_r=(), extra_w=()):
            P.op("dve", fn, reads=[buf(bT)] + list(extra_r), writes=[buf(bT)] + list(extra_w))

        def sincos(ang_ap, sin_out, cos_out, n):
            y, kf, f, g_ = wk[0][:, 0:n], wk[1][:, 0:n], wk[2][:, 0:n], wk[3][:, 0:n]
            ki = wki[:, 0:n]
            dv(lambda e: e.tensor_scalar(out=y, in0=ang_ap, scalar1=1.0 / (2 * math.pi), scalar2=None, op0=ALU.mult))
            dv(lambda e: e.tensor_copy(out=ki, in_=y))
            dv(lambda e: e.tensor_copy(out=kf, in_=ki))
            dv(lambda e: e.tensor_tensor(out=f, in0=y, in1=kf, op=ALU.subtract))
            for shift, dst in ((0.0, sin_out), (0.25, cos_out)):
                if shift:
                    dv(lambda e: e.tensor_scalar(out=f, in0=f, scalar1=shift, scalar2=None, op0=ALU.add))
                dv(lambda e: e.tensor_scalar(out=g_, in0=f, scalar1=0.5, scalar2=None, op0=ALU.is_gt))
                dv(lambda e: e.tensor_tensor(out=f, in0=f, in1=g_, op=ALU.subtract))
                dv(lambda e: e.tensor_scalar(out=g_, in0=f, scalar1=-0.5, scalar2=None, op0=ALU.is_lt))
                dv(lambda e: e.tensor_tensor(out=f, in0=f, in1=g_, op=ALU.add))
                P.op("act", lambda e, dst=dst: e.activation(out=dst, in_=f, func=AF.Sin, scale=2 * math.pi),
                     reads=[buf(bT)], writes=[buf(bT)])

        rd_in = [buf("s5are"), buf("s5aim"), buf("s5ldt"), buf("s5dvec"), buf("s5X1"), buf("s5X2"),
                 buf("s5CX1"), buf("s5CX2"), buf("identf")]
        P.op("pool", lambda e: e.memset(sg1[0:64, :], 1.0), reads=rd_in, writes=[buf(bT)])
        P.op("pool", lambda e: e.memset(sg1[64:128, :], -1.0), writes=[buf(bT)])
        P.op("pool", lambda e: e.memset(sg2[0:64, :], -1.0), writes=[buf(bT)])
        P.op("pool", lambda e: e.memset(sg2[64:128, :], 1.0), writes=[buf(bT)])
        P.op("pool", lambda e: e.memset(S["PiT"], 0.0), writes=[buf(bT), buf("s5PiT")])
        P.op("pool", lambda e: e.memset(Bpad, 0.0), writes=[buf(bT)])
        P.op("pool", lambda e: e.memset(S["Wint"], 0.0), writes=[buf(bT), buf("s5Wint")])
        P.op("pool", lambda e: e.iota(mcol, pattern=[[1, 9]], base=0, channel_multiplier=0,
                                      allow_small_or_imprecise_dtypes=True), writes=[buf(bT)])
        P.op("pool", lambda e: e.iota(iot, pattern=[[1, 256]], base=1, channel_multiplier=0,
                                      allow_small_or_imprecise_dtypes=True), writes=[buf(bT)])
        dv(lambda e: e.tensor_copy(out=S["PiT"][0:64, 64:128], in_=identf[0:64, 0:64]), extra_w=[buf("s5PiT")])
        dv(lambda e: e.tensor_scalar(out=S["PiT"][64:128, 0:64], in0=identf[64:128, 64:128], scalar1=-1.0,
                                     scalar2=None, op0=ALU.mult), extra_w=[buf("s5PiT")])
        P.op("act", lambda e: e.activation(out=dt, in_=S["ldt"], func=AF.Exp), reads=[buf(bT)], writes=[buf(bT)])
        dv(lambda e: e.tensor_tensor(out=lr, in0=S["are"], in1=dt, op=ALU.mult))
        dv(lambda e: e.tensor_tensor(out=th, in0=S["aim"], in1=dt, op=ALU.mult))
        for g in range(8):
            dv(lambda e, g=g: e.tensor_scalar(out=mlr[:, g, :], in0=mcol, scalar1=lr[:, g:g + 1], scalar2=None,
                                              op0=ALU.mult))
            dv(lambda e, g=g: e.tensor_scalar(out=mth[:, g, :], in0=mcol, scalar1=th[:, g:g + 1], scalar2=None,
                                              op0=ALU.mult))
        fl = "p g m -> p (g m)"
        P.op("act", lambda e: e.activation(out=mag.rearrange(fl), in_=mlr.rearrange(fl), func=AF.Exp),
             reads=[buf(bT)], writes=[buf(bT)])
        sincos(mth.rearrange(fl), sn.rearrange(fl), cs.rearrange(fl), 72)
        dv(lambda e: e.tensor_tensor(out=AR.rearrange(fl), in0=mag.rearrange(fl), in1=cs.rearrange(fl), op=ALU.mult))
        dv(lambda e: e.tensor_tensor(out=AI.rearrange(fl), in0=mag.rearrange(fl), in1=sn.rearrange(fl), op=ALU.mult))
        dv(lambda e: e.tensor_copy(out=S["rho8"], in_=mag[:, :, 8]), extra_w=[buf("s5rho8")])
        dv(lambda e: e.tensor_copy(out=S["ar8"], in_=AR[:, :, 8]), extra_w=[buf("s5ar8")])
        dv(lambda e: e.tensor_copy(out=S["ai8"], in_=AI[:, :, 8]), extra_w=[buf("s5ai8")])
        dv(lambda e: e.tensor_scalar(out=t8a, in0=AR[:, :, 1], scalar1=-1.0, scalar2=None, op0=ALU.add))
        dv(lambda e: e.tensor_tensor(out=t8b, in0=S["are"], in1=S["are"], op=ALU.mult))
        dv(lambda e: e.tensor_tensor(out=t8c, in0=S["aim"], in1=S["aim"], op=ALU.mult))
        dv(lambda e: e.tensor_tensor(out=t8b, in0=t8b, in1=t8c, op=ALU.add))
        dv(lambda e: e.reciprocal(out=t8b, in_=t8b))
        dv(lambda e: e.tensor_tensor(out=cr, in0=t8a, in1=S["are"], op=ALU.mult))
        dv(lambda e: e.tensor_tensor(out=t8c, in0=AI[:, :, 1], in1=S["aim"], op=ALU.mult))
        dv(lambda e: e.tensor_tensor(out=cr, in0=cr, in1=t8c, op=ALU.add))
        dv(lambda e: e.tensor_tensor(out=cr, in0=cr, in1=t8b, op=ALU.mult))
        dv(lambda e: e.tensor_tensor(out=ci, in0=AI[:, :, 1], in1=S["are"], op=ALU.mult))
        dv(lambda e: e.tensor_tensor(out=t8c, in0=t8a, in1=S["aim"], op=ALU.mult))
        dv(lambda e: e.tensor_tensor(out=ci, in0=ci, in1=t8c, op=ALU.subtract))
        dv(lambda e: e.tensor_tensor(out=ci, in0=ci, in1=t8b, op=ALU.mult))
        dv(lambda e: e.tensor_scalar(out=X2.rearrange("p g h -> p (g h)"), in0=X2.rearrange("p g h -> p (g h)"),
                                     scalar1=sg2[:, 0:1], scalar2=None, op0=ALU.mult))
        dv(lambda e: e.tensor_scalar(out=CX1.rearrange("p g h -> p (g h)"), in0=CX1.rearrange("p g h -> p (g h)"),
                                     scalar1=sg1[:, 0:1], scalar2=None, op0=ALU.mult))
        for g in range(8):
            dv(lambda e, g=g: e.tensor_scalar(out=tB[:, g, :], in0=X2[:, g, :], scalar1=ci[:, g:g + 1], scalar2=None,
                                              op0=ALU.mult))
            dv(lambda e, g=g: e.scalar_tensor_tensor(out=Bst[:, g, :], in0=X1[:, g, :], scalar=cr[:, g:g + 1],
                                                     in1=tB[:, g, :], op0=ALU.mult, op1=ALU.add))
            dv(lambda e, g=g: e.tensor_scalar(out=tB[:, g, :], in0=X1[:, g, :], scalar1=ci[:, g:g + 1], scalar2=None,
                                              op0=ALU.mult))
            dv(lambda e, g=g: e.scalar_tensor_tensor(out=Bsw[:, g, :], in0=X2[:, g, :], scalar=cr[:, g:g + 1],
                                                     in1=tB[:, g, :], op0=ALU.mult, op1=ALU.subtract))
            dv(lambda e, g=g: e.tensor_copy(out=Bpad[:, g, 16 * g:16 * g + 16], in_=Bst[:, g, :]))
            for m in range(9):
                dv(lambda e, g=g, m=m: e.tensor_scalar(out=tB[:, g, :], in0=CX2[:, g, :], scalar1=AI[:, g, m:m + 1],
                                                       scalar2=None, op0=ALU.mult))
                dv(lambda e, g=g, m=m: e.scalar_tensor_tensor(out=CA[:, m, g, :], in0=CX1[:, g, :],
                                                              scalar=AR[:, g, m:m + 1], in1=tB[:, g, :],
                                                              op0=ALU.mult, op1=ALU.subtract))
        for j in range(8):
            for g in range(8):
                dv(lambda e, j=j, g=g: e.tensor_copy(out=S["Wint"][:, j, g, 16 * g:16 * g + 16], in_=CA[:, j + 1, g, :]),
                   extra_w=[buf("s5Wint")])
        for tau in range(8):
            for g in range(8):
                P.op("pe", lambda e, tau=tau, g=g: e.matmul(pM[:, 16 * g:16 * g + 16], lhsT=Bpad[:, g, :],
                                                            rhs=CA[:, tau, g, :], start=True, stop=True),
                     reads=[buf(bT)], writes=[buf("pM")])
            if tau == 0:
                P.op("dve", lambda e: e.scalar_tensor_tensor(out=S["Wfir"][:, 0, :], in0=identf, scalar=S["dvec"][:, 0:1],
                                                             in1=pM[:, 0:128], op0=ALU.mult, op1=ALU.add),
                     reads=[buf("pM"), buf(bT)], writes=[buf("s5Wfir")])
            else:
                P.op("dve", lambda e, tau=tau: e.tensor_copy(out=S["Wfir"][:, tau, :], in_=pM[:, 0:128]),
                     reads=[buf("pM")], writes=[buf("s5Wfir")])
        for s in range(8):
            m = 7 - s
            for g in range(8):
                dv(lambda e, g=g, m=m: e.tensor_scalar(out=tB[:, g, :], in0=Bsw[:, g, :], scalar1=AI[:, g, m:m + 1],
                                                       scalar2=None, op0=ALU.mult))
                dv(lambda e, g=g: e.memset(ABp, 0.0))
                dv(lambda e, g=g, m=m: e.scalar_tensor_tensor(out=ABp[:, 16 * g:16 * g + 16], in0=Bst[:, g, :],
                                                              scalar=AR[:, g, m:m + 1], in1=tB[:, g, :],
                                                              op0=ALU.mult, op1=ALU.add))
                P.op("pe", lambda e: e.transpose(out=pM[:, 0:128], in_=ABp, identity=identf),
                     reads=[buf(bT), buf("identf")], writes=[buf("pM")])
                P.op("dve", lambda e, s=s, g=g: e.tensor_copy(out=S["Wst"][:, s, g, :], in_=pM[:, 0:128]),
                     reads=[buf("pM")], writes=[buf("s5Wst"), buf(bT)])
        for g in range(8):
            dv(lambda e, g=g: e.tensor_scalar(out=ang, in0=iot, scalar1=mth[:, g, 8:9], scalar2=None, op0=ALU.mult))
            sincos(ang, S["SIN"][:, g, :], S["COS"][:, g, :], 256)
        P.op("dve", lambda e: e.tensor_copy(out=wk[0], in_=wk[0]), reads=[buf(bT)],
             writes=[buf(bT), buf("s5SIN"), buf("s5COS")])
        P.op("dve", lambda e: e.memset(S["X0"][0], 0.0), writes=[buf("s5X00")])
        P.cur_prio = 0
        return S

    def s5_alloc_work():
        Wk = {}
        Wk["ub"] = aa([128, 8, 256], BF16)
        Wk["Ssb"] = aa([128, 256], F32)
        Wk["t1"] = aa([128, 256], F32)
        Wk["t2"] = aa([128, 256], F32)
        Wk["Wsc"] = aa([128, 256], F32)
        Wk["Xn"] = aa([128, 256], F32)
        Wk["Xin"] = aa([128, 8, 256], BF16)
        for nm in ("S4", "T1", "T2"):
            Wk[nm] = aa([128, 4, 256], F32)
        Wk["W4"] = Wk["S4"]
        Wk["X4"] = Wk["T2"]
        Wk["ysb"] = aa([128, 2, 256], F32)
        Wk["g1"] = aa([128, 2, 256], F32)
        Wk["g2"] = aa([128, 2, 256], F32)
        Wk["zfull"] = aa([128, 2048], BF16)
        return Wk

    GC = math.sqrt(2.0 / math.pi)
    YBANK = [zb(2, 0), zb(2, 1), zb(3, 0), zb(3, 1)]
    YBUF = ["pZ20", "pZ21", "pZ30", "pZ31"]

    def s5_fir_inter(S, Wk, nc_, dst_ap, dst_buf):
        ub, Xin, zfull = Wk["ub"], Wk["Xin"], Wk["zfull"]
        zv = zfull.rearrange("p (c j) -> p j c", j=8)
        for j in range(8):
            bk = j // 2
            Yj = YBANK[bk][:, (j % 2) * 256:(j % 2) * 256 + nc_]
            for tau in range(j + 1):
                P.op("pe", lambda e, Yj=Yj, tau=tau, j=j: e.matmul(Yj, lhsT=S["Wfir"][:, tau, :], rhs=ub[:, j - tau, 0:nc_],
                                                                 start=(tau == 0), stop=False),
                     reads=[buf("s5ub"), buf("s5Wfir")], writes=[buf(YBUF[bk])])
            for g in range(8):
                P.op("pe", lambda e, Yj=Yj, j=j, g=g: e.matmul(Yj, lhsT=S["Wint"][:, j, g, :], rhs=Xin[:, g, 0:nc_],
                                                             start=False, stop=(g == 7)),
                     reads=[buf("s5Xin"), buf("s5Wint")], writes=[buf(YBUF[bk])])
            if j % 2 == 1:
                Yb = YBANK[bk].rearrange("p (j c) -> p j c", j=2)[:, :, 0:nc_]
                ysb, g1, g2 = Wk["ysb"][:, :, 0:nc_], Wk["g1"][:, :, 0:nc_], Wk["g2"][:, :, 0:nc_]
                P.op("act", lambda e, Yb=Yb: e.activation(out=ysb, in_=Yb, func=AF.Copy),
                     reads=[buf(YBUF[bk])], writes=[buf("s5ysb")])
                P.op("dve", lambda e: e.tensor_tensor(out=g1, in0=ysb, in1=ysb, op=ALU.mult),
                     reads=[buf("s5ysb")], writes=[buf("s5g1")])
                P.op("dve", lambda e: e.tensor_scalar(out=g1, in0=g1, scalar1=0.044715, scalar2=1.0, op0=ALU.mult,
                                                      op1=ALU.add), reads=[buf("s5g1")], writes=[buf("s5g1")])
                P.op("dve", lambda e: e.tensor_tensor(out=g1, in0=g1, in1=ysb, op=ALU.mult),
                     reads=[buf("s5g1"), buf("s5ysb")], writes=[buf("s5g1")])
                P.op("dve", lambda e: e.tensor_scalar(out=g1, in0=g1, scalar1=-18.0, scalar2=None, op0=ALU.max),
                     reads=[buf("s5g1")], writes=[buf("s5g1")])
                P.op("act", lambda e: e.activation(out=g2, in_=g1, func=AF.Exp, scale=-2.0 * GC),
                     reads=[buf("s5g1")], writes=[buf("s5g2")])
                P.op("act", lambda e: e.activation(out=g2, in_=g2, func=AF.Ln, bias=1.0, scale=1.0),
                     reads=[buf("s5g2")], writes=[buf("s5g2")])
                P.op("act", lambda e: e.activation(out=g2, in_=g2, func=AF.Exp, scale=-1.0),
                     reads=[buf("s5g2")], writes=[buf("s5g2")])
                P.op("dve", lambda e, j=j: e.tensor_tensor(out=zv[:, j - 1:j + 1, 0:nc_], in0=ysb, in1=g2, op=ALU.mult),
                     reads=[buf("s5ysb"), buf("s5g2")], writes=[buf("s5zfull")])
        P.dma("pool", dst_ap, zfull[:, 0:nc_ * 8], reads=[buf("s5zfull")], writes=[dst_buf])

    def s5_states(S, Wk, nc_, g, dstps, dstbuf):
        for s in range(8):
            P.op("pe", lambda e, s=s: e.matmul(dstps[:, 0:nc_], lhsT=S["Wst"][:, s, g, :], rhs=Wk["ub"][:, s, 0:nc_],
                                                 start=(s == 0), stop=(s == 7)),
                 reads=[buf("s5ub"), buf("s5Wst")], writes=[buf(dstbuf)])

    def s5_supertile(S, Wk, sti, ntok, tok0, final_k=None):
        nc_ = ntok // 8
        X0 = S["X0"][sti % 2]
        X0n = S["X0"][(sti + 1) % 2]
        bX0, bX0n = buf(f"s5X0{sti % 2}"), buf(f"s5X0{(sti + 1) % 2}")
        Xin = Wk["Xin"]
        S4, T1, T2, W4, X4 = (Wk[k] for k in ("S4", "T1", "T2", "W4", "X4"))
        ZA = Z[0].rearrange("p (g c) -> p g c", g=4)
        ZB = Z[1].rearrange("p (g c) -> p g c", g=4)
        bZA = [buf("pZ00"), buf("pZ01")]
        bZB = [buf("pZ10"), buf("pZ11")]
        for hf in range(2):
            gs = slice(4 * hf, 4 * hf + 4)
            for gl in range(4):
                g = 4 * hf + gl
                for s in range(8):
                    P.op("pe", lambda e, s=s, g=g, gl=gl: e.matmul(ZA[:, gl, 0:nc_], lhsT=S["Wst"][:, s, g, :],
                                                                   rhs=Wk["ub"][:, s, 0:nc_], start=(s == 0), stop=(s == 7)),
                         reads=[buf("s5ub"), buf("s5Wst")], writes=[bZA[gl // 2]])
            P.op("dve", lambda e: e.tensor_copy(out=S4[:, :, 0:nc_], in_=ZA[:, :, 0:nc_]), reads=bZA, writes=[buf("s5S4")])
            for gl in range(4):
                P.op("pe", lambda e, gl=gl: e.matmul(ZB[:, gl, 0:nc_], lhsT=S["PiT"], rhs=S4[:, gl, 0:nc_], start=True, stop=True),
                     reads=[buf("s5S4"), buf("s5PiT")], writes=[bZB[gl // 2]])
            P.op("dve", lambda e, gs=gs: e.tensor_tensor(out=T1[:, :, 0:nc_], in0=S["SIN"][:, gs, 0:nc_], in1=ZB[:, :, 0:nc_],
                                                         op=ALU.mult), reads=bZB + [buf("s5SIN")], writes=[buf("s5T1")])
            P.op("dve", lambda e, gs=gs: e.tensor_tensor(out=T2[:, :, 0:nc_], in0=S["COS"][:, gs, 0:nc_], in1=S4[:, :, 0:nc_],
                                                         op=ALU.mult), reads=[buf("s5S4"), buf("s5COS")], writes=[buf("s5T2")])
            P.op("dve", lambda e: e.tensor_tensor(out=T2[:, :, 0:nc_], in0=T2[:, :, 0:nc_], in1=T1[:, :, 0:nc_], op=ALU.subtract),
                 reads=[buf("s5T1"), buf("s5T2")], writes=[buf("s5T2")])
            for gl in range(4):
                g = 4 * hf + gl
                P.op("dve", lambda e, g=g, gl=gl: e.tensor_tensor_scan(
                    out=W4[:, gl, 0:nc_], data0=S["rho8"][:, g:g + 1].to_broadcast([128, nc_]), data1=T2[:, gl, 0:nc_],
                    initial=X0[:, g:g + 1], op0=ALU.mult, op1=ALU.add),
                    reads=[buf("s5T2"), buf("s5rho8"), bX0], writes=[buf("s5S4")])
            for gl in range(4):
                P.op("pe", lambda e, gl=gl: e.matmul(ZB[:, gl, 0:nc_], lhsT=S["PiT"], rhs=W4[:, gl, 0:nc_], start=True, stop=True),
                     reads=[buf("s5S4"), buf("s5PiT")], writes=[bZB[gl // 2]])
            P.op("dve", lambda e, gs=gs: e.tensor_tensor(out=T1[:, :, 0:nc_], in0=S["SIN"][:, gs, 0:nc_], in1=ZB[:, :, 0:nc_],
                                                         op=ALU.mult), reads=bZB + [buf("s5SIN")], writes=[buf("s5T1")])
            P.op("dve", lambda e, gs=gs: e.tensor_tensor(out=X4[:, :, 0:nc_], in0=S["COS"][:, gs, 0:nc_], in1=W4[:, :, 0:nc_],
                                                         op=ALU.mult), reads=[buf("s5S4"), buf("s5COS")], writes=[buf("s5T2")])
            P.op("dve", lambda e: e.tensor_tensor(out=X4[:, :, 0:nc_], in0=X4[:, :, 0:nc_], in1=T1[:, :, 0:nc_], op=ALU.add),
                 reads=[buf("s5T1"), buf("s5T2")], writes=[buf("s5T2")])
            P.op("dve", lambda e, gs=gs: e.tensor_copy(out=Xin[:, gs, 0:1], in_=X0[:, gs].rearrange("p (g o) -> p g o", o=1)),
                 reads=[bX0], writes=[buf("s5Xin")])
            if nc_ > 1:
                P.op("dve", lambda e, gs=gs: e.tensor_copy(out=Xin[:, gs, 1:nc_], in_=X4[:, :, 0:nc_ - 1]),
                     reads=[buf("s5T2")], writes=[buf("s5Xin")])
            P.op("dve", lambda e, gs=gs: e.tensor_copy(out=X0n[:, gs].rearrange("p (g o) -> p g o", o=1), in_=X4[:, :, nc_ - 1:nc_]),
                 reads=[buf("s5T2")], writes=[bX0n])
            if final_k is not None:
                P.op("dve", lambda e, gs=gs: e.tensor_copy(out=S5FIN[:, gs].rearrange("p (g o) -> p g o", o=1),
                                                           in_=X4[:, :, final_k - 1:final_k]),
                     reads=[buf("s5T2")], writes=[buf("s5fin")])
        s5_fir_inter(S, Wk, nc_, zin_ap(tok0, ntok), buf(f"zinP{tok0 // PZ}"))

    def s5_sample(S, Wk):
        nc_ = 32
        Ssb, t1, t2, Xn, Xin = (Wk[k] for k in ("Ssb", "t1", "t2", "Xn", "Xin"))
        pA, pB, bA, bB = zb(1, 0), zb(1, 1), "pZ10", "pZ11"
        P.dma("sp", S5X0, s5x0_d[:, :, :], writes=[buf("s5x0s")])
        Sv = Ssb[:, 0:32].rearrange("p (q c) -> p q c", c=2)
        Xv = Xin[:, :, 0:32].rearrange("p g (q c) -> p g q c", c=2)
        for g in range(8):
            s5_states(S, Wk, nc_, g, pA, bA)
            P.op("dve", lambda e: e.tensor_copy(out=Ssb[:, 0:nc_], in_=pA[:, 0:nc_]), reads=[buf(bA)], writes=[buf("s5Ssb")])
            Z0 = S5X0[:, g, :]
            X1 = Xn[:, 0:16]
            Fn = Xn[:, 16:32]
            P.op("pe", lambda e, Z0=Z0: e.matmul(pB[:, 0:16], lhsT=S["PiT"], rhs=Z0, start=True, stop=True),
                 reads=[buf("s5x0s"), buf("s5PiT")], writes=[buf(bB)])
            P.op("dve", lambda e, g=g: e.tensor_scalar(out=t1[:, 0:16], in0=pB[:, 0:16], scalar1=S["ai8"][:, g:g + 1], scalar2=None,
                                                       op0=ALU.mult), reads=[buf(bB), buf("s5ai8")], writes=[buf("s5t1")])
            P.op("dve", lambda e, g=g, Z0=Z0: e.scalar_tensor_tensor(out=X1, in0=Z0, scalar=S["ar8"][:, g:g + 1], in1=t1[:, 0:16],
                                                                      op0=ALU.mult, op1=ALU.add),
                 reads=[buf("s5x0s"), buf("s5t1"), buf("s5ar8")], writes=[buf("s5Xn")])
            P.op("dve", lambda e: e.tensor_tensor(out=X1, in0=X1, in1=Sv[:, :, 0], op=ALU.add),
                 reads=[buf("s5Xn"), buf("s5Ssb")], writes=[buf("s5Xn")])
            P.op("pe", lambda e: e.matmul(pB[:, 0:16], lhsT=S["PiT"], rhs=X1, start=True, stop=True),
                 reads=[buf("s5Xn"), buf("s5PiT")], writes=[buf(bB)])
            P.op("dve", lambda e, g=g: e.tensor_scalar(out=t1[:, 0:16], in0=pB[:, 0:16], scalar1=S["ai8"][:, g:g + 1], scalar2=None,
                                                       op0=ALU.mult), reads=[buf(bB), buf("s5ai8")], writes=[buf("s5t1")])
            P.op("dve", lambda e, g=g: e.scalar_tensor_tensor(out=Fn, in0=X1, scalar=S["ar8"][:, g:g + 1], in1=t1[:, 0:16],
                                                              op0=ALU.mult, op1=ALU.add),
                 reads=[buf("s5Xn"), buf("s5t1"), buf("s5ar8")], writes=[buf("s5Xn")])
            P.op("dve", lambda e, g=g: e.tensor_tensor(out=S5FINS[:, g, :], in0=Fn, in1=Sv[:, :, 1], op=ALU.add),
                 reads=[buf("s5Xn"), buf("s5Ssb")], writes=[buf("s5fins")])
            P.op("dve", lambda e, g=g, Z0=Z0: e.tensor_copy(out=Xv[:, g, :, 0], in_=Z0), reads=[buf("s5x0s")], writes=[buf("s5Xin")])
            P.op("dve", lambda e, g=g: e.tensor_copy(out=Xv[:, g, :, 1], in_=X1), reads=[buf("s5Xn")], writes=[buf("s5Xin")])
        s5_fir_inter(S, Wk, nc_, zin_ap(TP, 256), buf(f"zinP{TP // PZ}"))

    GROUPS = [[0, 1, 2, 3], [4, 5, 6, 7]]

    def ag1(pi):
        P.coll(lambda e: e.collective_compute("AllGather", ALU.bypass, replica_groups=GROUPS,
                                              ins=[zin_p[pi]], outs=[zall_p[pi]]),
               reads=[buf(f"zinP{pi}")], writes=[buf(f"zallP{pi}")])

    def phase_a_tile(ti, sample=False):
        NS = 2 if sample else 4
        N = NS * 128
        t0 = TP if sample else ti * TW
        j0 = NBLK if sample else ti * 4
        par = ti % 2
        xsrc = xs_d if sample else x_d
        xrow0 = 0 if sample else t0
        ko, vo, lo = (ks_o, vs_o, lfs_o) if sample else (k_o, v_o, lf_o)
        orow0 = 0 if sample else t0
        XT = xT[par]
        bXT = buf(f"xT{par}")
        for s in range(NS):
            slot = (ti * 4 + s) % XR
            bx = buf(f"xr{slot}")
            P.dma("sp", xr[slot], xsrc[xrow0 + s * 128:xrow0 + (s + 1) * 128, :], writes=[bx])
            P.op("act", lambda e, slot=slot, s=s: e.activation(
                out=junk, in_=xr[slot], func=AF.Square, accum_out=ss[:, s:s + 1]),
                reads=[bx], writes=[buf("junk"), buf("ss")], fuse=False, cost=1.2)
        P.op("act", lambda e: e.activation(out=lnt[:, 0:NS], in_=ss[:, 0:NS], func=AF.Ln, bias=EPS, scale=1.0 / D_MODEL),
             reads=[buf("ss")], writes=[buf("lnt")])
        RS = rstd[par]
        bRS = buf(f"rstd{par}")
        P.op("act", lambda e: e.activation(out=RS[:, 0:NS], in_=lnt[:, 0:NS], func=AF.Exp, scale=-0.5),
             reads=[buf("lnt")], writes=[bRS])
        for s in range(NS):
            slot = (ti * 4 + s) % XR
            bx = buf(f"xr{slot}")
            xp = s % 2
            P.op("dve", lambda e, slot=slot, s=s, xp=xp: e.tensor_scalar(
                out=xs[xp], in0=xr[slot], scalar1=RS[:, s:s + 1], scalar2=None, op0=ALU.mult),
                reads=[bx, bRS], writes=[buf(f"xs{xp}")])
            for kc in range(8):
                P.op("pe", lambda e, xp=xp, kc=kc: e.transpose(
                    out=pT[xp][:, kc, :], in_=xs[xp][:, kc * 128:(kc + 1) * 128], identity=identb),
                    reads=[buf(f"xs{xp}"), buf("identb")], writes=[buf(f"pT{xp}")])
            if s % 2 == 0:
                P.op("dve", lambda e, xp=xp, s=s: e.tensor_copy(out=XT[:, :, s * 128:(s + 1) * 128], in_=pT[xp]),
                     reads=[buf(f"pT{xp}")], writes=[bXT])
            else:
                P.op("act", lambda e, xp=xp, s=s: e.activation(
                    out=XT[:, :, s * 128:(s + 1) * 128], in_=pT[xp], func=AF.Copy),
                    reads=[buf(f"pT{xp}")], writes=[bXT])
            yield "front"
        yield "FRONT_DONE"

        pp_i = [0]

        def proj(col0):
            i = pp_i[0] % 2
            pp_i[0] += 1
            for kc in range(8):
                P.op("pe", lambda e, kc=kc, i=i: e.matmul(
                    pP[i][:, 0:N], lhsT=Wb[:, kc, col0:col0 + 128], rhs=XT[:, kc, 0:N], start=(kc == 0), stop=(kc == 7)),
                    reads=[bXT, buf("Wb")], writes=[buf(f"pP{i}")])
            return i

        def headnorm2(items):
            for k_, (i, gain, gain_buf, out_ap, out_buf) in enumerate(items):
                P.op("act", lambda e, i=i, k_=k_: e.activation(out=sq2[k_][:, 0:N], in_=pP[i][:, 0:N], func=AF.Square),
                     reads=[buf(f"pP{i}")], writes=[buf(SQN[k_])])
            for k_, (i, gain, gain_buf, out_ap, out_buf) in enumerate(items):
                P.op("pe", lambda e, k_=k_: e.matmul(pMs[k_][:, 0:N], lhsT=BO, rhs=sq2[k_][:, 0:N], start=True, stop=True),
                     reads=[buf(SQN[k_]), buf("BO")], writes=[buf(pMn[k_])])
            for k_, (i, gain, gain_buf, out_ap, out_buf) in enumerate(items):
                P.op("act", lambda e, k_=k_: e.activation(out=ln2[k_][:, 0:N], in_=pMs[k_][:, 0:N], func=AF.Ln, bias=EPS, scale=1.0),
                     reads=[buf(pMn[k_])], writes=[buf(LNN[k_])])
            for k_, (i, gain, gain_buf, out_ap, out_buf) in enumerate(items):
                P.op("act", lambda e, k_=k_: e.activation(out=rr2[k_][:, 0:N], in_=ln2[k_][:, 0:N], func=AF.Exp, scale=-0.5),
                     reads=[buf(LNN[k_])], writes=[buf(RRN[k_])])
            for k_, (i, gain, gain_buf, out_ap, out_buf) in enumerate(items):
                P.op("dve", lambda e, i=i, k_=k_, gain=gain, out_ap=out_ap: e.scalar_tensor_tensor(
                    out=out_ap, in0=pP[i][:, 0:N], scalar=gain, in1=rr2[k_][:, 0:N], op0=ALU.mult, op1=ALU.mult),
                    reads=[buf(f"pP{i}"), buf(RRN[k_]), gain_buf], writes=[out_buf])

        sq2, ln2, rr2 = [sq, sqB], [lnb, lnbB], [rr, rrB]
        SQN, LNN, RRN = ["sq0", "sgb0"], ["lnb0", "kst"], ["rr0", "vst"]
        pMs, pMn = [pM, pK.rearrange("p s f -> p (s f)")], ["pM", "pK"]
        iq = proj(0)
        ik = proj(128)
        headnorm2([(iq, qg8[:, 0:1], buf("qg8"), qnb[:, 0:N], buf("qnb")),
                   (ik, kg[:, 0:1], buf("kg"), knf[:, 0:N], buf("knf"))])
        for h in range(2):
            P.dma("pool", qt_d[h, 0:64, t0:t0 + N], qnb[h * 64:(h + 1) * 64, 0:N],
                  reads=[buf("qnb")], writes=[buf(f"qt_d{ti}")])
        yield "back"
        P.op("act", lambda e: e.activation(out=knb[:, 0:N], in_=knf[:, 0:N], func=AF.Copy),
             reads=[buf("knf")], writes=[buf("knb")])
        for h in range(2):
            P.dma("pool", kt_d[h, :, t0:t0 + N], knb[h * 64:(h + 1) * 64, 0:N],
                  reads=[buf("knb")], writes=[buf(f"kt_d{ti}")])
        for s in range(NS):
            P.op("pe", lambda e, s=s: e.transpose(out=pK[:, s, :], in_=knf[:, s * 128:(s + 1) * 128], identity=identf),
                 reads=[buf("knf"), buf("identf")], writes=[buf("pK")])
        P.op("dve", lambda e: e.tensor_copy(out=kst[:, 0:NS, :], in_=pK[:, 0:NS, :]), reads=[buf("pK")], writes=[buf("kst")])
        out_dmas.append(P.dma("pool", ko[orow0:orow0 + N, :].rearrange("(s p) f -> p s f", p=128), kst[:, 0:NS, :],
                              reads=[buf("kst")]))
        yield "back"
        for s in range(NS):
            for kc in range(8):
                P.op("pe", lambda e, s=s, kc=kc: e.matmul(
                    pV[:, s, :], lhsT=XT[:, kc, s * 128:(s + 1) * 128], rhs=Wb[:, kc, 256:384],
                    start=(kc == 0), stop=(kc == 7)),
                    reads=[bXT, buf("Wb")], writes=[buf("pV")])
        for s in range(NS):
            for kc in range(8):
                P.op("pe", lambda e, s=s, kc=kc: e.matmul(
                    pS[:, 2 * s:2 * s + 2], lhsT=XT[:, kc, s * 128:(s + 1) * 128], rhs=Wb[:, kc, 768:770],
                    start=(kc == 0), stop=(kc == 7)),
                    reads=[bXT, buf("Wb")], writes=[buf("pS")])
        P.op("act", lambda e: e.activation(out=vst[:, 0:NS, :], in_=pV[:, 0:NS, :], func=AF.Copy),
             reads=[buf("pV")], writes=[buf("vst")])
        out_dmas.append(P.dma("pool", vo[orow0:orow0 + N, :].rearrange("(s p) f -> p s f", p=128), vst[:, 0:NS, :],
                              reads=[buf("vst")]))
        P.op("dve", lambda e: e.tensor_copy(
            out=VP[:, j0:j0 + NS, :, 0:64], in_=pV[:, 0:NS, :].rearrange("p s (h d) -> p s h d", h=2)),
            reads=[buf("pV")], writes=[buf("VP")])
        yield "back"
        pSv = pS[:, 0:2 * NS].rearrange("p (s h) -> p s h", h=2)
        for h in range(2):
            P.op("act", lambda e, h=h: e.activation(
                out=ef[:, 0:NS, h], in_=pSv[:, :, h], func=AF.Exp, bias=nbf[:, h:h + 1], scale=-1.0),
                reads=[buf("pS"), buf("nbf")], writes=[buf("ef")])
        P.op("act", lambda e: e.activation(out=lf[:, 0:NS, :], in_=ef[:, 0:NS, :], func=AF.Ln, bias=1.0, scale=1.0),
             reads=[buf("ef")], writes=[buf("lf")])
        P.op("dve", lambda e: e.tensor_scalar(out=lf[:, 0:NS, :], in0=lf[:, 0:NS, :], scalar1=-1.0, scalar2=None, op0=ALU.mult),
             reads=[buf("lf")], writes=[buf("lf")])
        out_dmas.append(P.dma("pool", lo[orow0:orow0 + N, :].rearrange("(s p) h -> p s h", p=128), lf[:, 0:NS, :],
                              reads=[buf("lf")]))
        yield "back"
        for s in range(NS):
            P.op("pe", lambda e, s=s: e.transpose(out=pM[0:2, s * 128:(s + 1) * 128], in_=lf[:, s, :],
                                                   identity=identf),
                 reads=[buf("lf"), buf("identf")], writes=[buf("pM")])
        CR = cumrow[par]
        CRp = cumrow[1 - par]
        init = 0.0 if (ti == 0 or sample) else CRp[:, TW - 1:TW]
        d0 = segm if sample else ones2
        P.op("dve", lambda e: e.tensor_tensor_scan(
            out=CR[:, 0:N], data0=d0[:, 0:N], data1=pM[0:2, 0:N], initial=init, op0=ALU.mult, op1=ALU.add),
            reads=[buf("pM"), buf("ones2"), buf("segm"), buf(f"cumrow{1 - par}")], writes=[buf(f"cumrow{par}")])
        P.op("dve", lambda e: e.tensor_copy(out=cumb[:, 0:N], in_=CR[:, 0:N]),
             reads=[buf(f"cumrow{par}")], writes=[buf("cumb")])
        for h in range(2):
            P.dma("pool", qt_d[h, 64:65, t0:t0 + N], cumb[h:h + 1, 0:N],
                  reads=[buf("cumb")], writes=[buf(f"qt_d{ti}")])
        for s in range(NS):
            P.op("pe", lambda e, s=s: e.transpose(out=pS[:, 16 + 2 * s:16 + 2 * s + 2],
                                                   in_=CR[:, s * 128:(s + 1) * 128], identity=identf[0:2, 0:2]),
                 reads=[buf(f"cumrow{par}"), buf("identf")], writes=[buf("pS")])
        P.op("dve", lambda e: e.tensor_scalar(
            out=NCK[:, j0:j0 + NS, :], in0=pS[:, 16:16 + 2 * NS].rearrange("p (s h) -> p s h", h=2),
            scalar1=-1.0, scalar2=None, op0=ALU.mult),
            reads=[buf("pS")], writes=[buf("NCK")])
        yield "back"
        iga = proj(384)
        igs = proj(640)
        silu2_from_psum([(iga, sgb[0], buf("sgb0")), (igs, sgb[1], buf("sgb1"))], N,
                        [(lnb, "lnb0", rr, "rr0"), (lnbB, "kst", rrB, "vst")])
        P.dma("pool", sgs_d[:, t0:t0 + N], sgb[1][:, 0:N], reads=[buf("sgb1")], writes=[buf(f"sgs_d{ti}")])
        for h in range(2):
            P.dma("pool", sga_d[h, :, t0:t0 + N], sgb[0][h * 64:(h + 1) * 64, 0:N],
                  reads=[buf("sgb0")], writes=[buf(f"sga_d{ti}")])
        yield "back"
        i = proj(512)
        uo = 0 if sample else (ti % 4) * 64
        P.op("act", lambda e, i=i: e.activation(out=WK["ub"][:, :, uo:uo + N // 8],
                                                in_=pP[i][:, 0:N].rearrange("p (c j) -> p j c", j=8), func=AF.Copy),
             reads=[buf(f"pP{i}")], writes=[buf("s5ub")], deps=(setup_done if ti == 0 else ()))
        yield "back"
        if sample:
            s5_sample(S5S, WK)
            if "X" in stages:
                ag1(TP // PZ)
        elif ti % 4 == 3 or ti == NTILE - 1:
            sti = ti // 4
            tok0 = sti * 2048
            ntok = t0 + TW - tok0
            tf = min(L_REAL, TP)
            fk = None
            if (tf - 1) // 2048 == sti:
                fk = (tf - tok0) // 8
            s5_supertile(S5S, WK, sti, ntok, tok0, fk)
            if "X" in stages and (tok0 + ntok) % PZ == 0:
                ag1(tok0 // PZ)

    if "A" in stages:
        S5S = s5_alloc_weights()
        XR = 4
        xr = [aa([128, D_MODEL], F32) for i in range(XR)]
        junk = aa([128, D_MODEL], BF16)
        xs = [aa([128, D_MODEL], BF16) for i in range(2)]
        xT = [aa([128, 8, TW], BF16) for i in range(2)]
        ss = aa([128, 4], F32)
        lnt = aa([128, 4], F32)
        rstd = [aa([128, 4], F32) for i in range(2)]
        sq = aa([128, TW], BF16)
        lnb = aa([128, TW], F32)
        rr = aa([128, TW], F32)
        qnb = aa([128, TW], BF16)
        knf = aa([128, TW], F32)
        knb = aa([128, TW], BF16)
        kst = aa([128, 4, 128], F32)
        vst = aa([128, 4, 128], F32)
        ef = aa([128, 4, 2], F32)
        lf = aa([128, 4, 2], F32)
        cumb = aa([2, TW], BF16)
        sgb = [aa([128, TW], BF16) for i in range(2)]
        sqB = sgb[0]
        lnbB = kst.rearrange("p s f -> p (s f)")
        rrB = vst.rearrange("p s f -> p (s f)")

        mark2 = ar_off[0]
        s5_setup(S5S)
        setup_done = [buf("s5tmp").w, buf("Wb").w]
        ar_off[0] = mark2
        WK = s5_alloc_work()
        gens = [phase_a_tile(ti) for ti in range(NTILE)] + [phase_a_tile(NTILE, sample=True)]
        front_done = [False] * len(gens)

        def step(gi):
            try:
                r = next(gens[gi])
            except StopIteration:
                return False
            if r == "FRONT_DONE":
                front_done[gi] = True
            return True

        while not front_done[0]:
            step(0)
        for gi in range(len(gens)):
            alive = True
            while alive:
                alive = step(gi)
                if gi + 1 < len(gens) and not front_done[gi + 1]:
                    step(gi + 1)
            if gi + 1 < len(gens):
                while not front_done[gi + 1]:
                    step(gi + 1)
        out_dmas.append(P.dma("sp", s5fin_o[:, :], S5FIN, reads=[buf("s5fin")]))
        out_dmas.append(P.dma("sp", s5fins_o[:, :, :], S5FINS, reads=[buf("s5fins")]))

    n_real = min(L_REAL, TP)
    qbs = []
    q0 = 0
    while q0 < n_real:
        ql = min(QB, n_real - q0)
        qbs.append((q0, ql))
        q0 += ql

    def tiles_of(a, b):
        return range(a // TW, (b + TW - 1) // TW)

    step = [0]

    def attend(qi, h, q0, qlen):
        par = qi % 2
        nkb = (q0 + qlen + 127) // 128
        halves = [(a, min(a + 512, qlen)) for a in range(0, qlen, 512)]
        pO = Z[2]
        bQ = buf(f"QA{par}{h}")
        bKT = buf(f"KT{h}")
        base = step[0]
        step[0] += nkb

        def clo(j):
            return max(0, 128 * j - q0)

        def zbufs(zi, lo, hi):
            return [buf(f"pZ{zi}{hf}") for hf in range(2) if lo < (hf + 1) * 512 and hi > hf * 512]

        def qk(j):
            zi = (base + j) % 2
            c_lo = clo(j)
            for (a, b) in halves:
                lo = max(a, c_lo)
                if lo >= b:
                    continue
                diag = (128 * j >= q0) and (a <= c_lo < b)
                P.op("pe", lambda e, zi=zi, lo=lo, b=b, diag=diag: e.matmul(
                    Z[zi][:, lo:b], lhsT=KT[h][:, 128 * j:128 * j + 128], rhs=QA[par][h][:, lo:b],
                    start=True, stop=not diag),
                    reads=[bKT, bQ], writes=zbufs(zi, lo, b))
                if diag:
                    w = min(128, qlen - c_lo)
                    P.op("pe", lambda e, zi=zi, c_lo=c_lo, w=w: e.matmul(
                        Z[zi][:, c_lo:c_lo + w], lhsT=identb, rhs=MN[:, 0:w], start=False, stop=True),
                        reads=[buf("identb"), buf("MN")], writes=zbufs(zi, c_lo, c_lo + w))

        def ex(j):
            zi = (base + j) % 2
            pi = (base + j) % 3
            c_lo = clo(j)
            P.op("act", lambda e, zi=zi, pi=pi, c_lo=c_lo: e.activation(
                out=PT[pi][:, c_lo:qlen], in_=Z[zi][:, c_lo:qlen], func=AF.Exp,
                bias=NCK[:, j, h:h + 1], scale=1.0),
                reads=zbufs(zi, c_lo, qlen) + [buf("NCK")], writes=[buf(f"PT{pi}")], cost=0.25 + (qlen - c_lo) / 1200.0)

        def pv(j):
            pi = (base + j) % 3
            c_lo = clo(j)
            for (a, b) in halves:
                lo = max(a, c_lo)
                if lo >= b:
                    continue
                j_last = min(nkb - 1, (q0 + b - 1) // 128)
                P.op("pe", lambda e, pi=pi, lo=lo, b=b, j_last=j_last: e.matmul(
                    pO[0:65, lo:b], lhsT=VP[:, j, h, :], rhs=PT[pi][:, lo:b],
                    start=(j == 0), stop=(j == j_last)),
                    reads=[buf("VP"), buf(f"PT{pi}")], writes=zbufs(2, lo, b))

        qk(0)
        for j in range(nkb):
            if j + 1 < nkb:
                qk(j + 1)
            ex(j)
            pv(j)
        ob = osb[h]
        bob = buf(f"osb{h}")
        P.op("dve", lambda e: e.tensor_copy(out=ob[:, 0:qlen], in_=pO[0:65, 0:qlen]),
             reads=zbufs(2, 0, qlen), writes=[bob])
        P.op("dve", lambda e: e.reciprocal(out=ob[64:65, 0:qlen], in_=ob[64:65, 0:qlen]),
             reads=[bob], writes=[bob])
        for (a, b) in halves:
            P.op("pe", lambda e, a=a, b=b: e.matmul(pO[0:64, a:b], lhsT=onesP[64:65, 0:64], rhs=ob[64:65, a:b],
                                                    start=True, stop=True),
                 reads=[bob, buf("onesP")], writes=zbufs(2, a, b))
        P.op("dve", lambda e: e.tensor_tensor(out=ob[0:64, 0:qlen], in0=ob[0:64, 0:qlen], in1=pO[0:64, 0:qlen],
                                              op=ALU.mult),
             reads=[bob] + zbufs(2, 0, qlen), writes=[bob])
        P.op("dve", lambda e: e.tensor_tensor(out=attg[h][:, 0:qlen], in0=ob[0:64, 0:qlen],
                                              in1=SG[par][h][:, 0:qlen], op=ALU.mult),
             reads=[bob, buf(f"SG{par}{h}")], writes=[buf(f"attg{h}")])
        P.dma("pool", mixin_ap(h * 64, (h + 1) * 64, q0, qlen), attg[h][:, 0:qlen],
              reads=[buf(f"attg{h}")], writes=[buf(f"mixinP{q0 // PM_}")])

    def sample_attention():
        ar_off[0] = 0
        clf = aa([32, 1024], F32)
        ccum = aa([32, 1024], F32)
        NCKc = aa([128, 8, 32], F32)
        MSK = aa([128, 8, 16], BF16)
        negt = aa([128, 16], F32)
        KTn = [aa([65, 256], BF16) for h in range(2)]
        QAs = [aa([65, 256], BF16) for h in range(2)]
        SGs = [aa([64, 256], BF16) for h in range(2)]
        kst_ = [aa([64, 1024], F32) for i in range(8)]
        vst_ = [aa([128, 8, 64], F32) for i in range(8)]
        KTc = [aa([65, 1024], BF16) for i in range(8)]
        VSc = [aa([128, 8, 65], BF16) for i in range(8)]
        PTs = [aa([128, 9, 16], BF16) for i in range(8)]
        obs_l = [aa([65, 16], F32) for i in range(8)]
        attS = [aa([64, 256], BF16) for h in range(2)]
        P.dma("sp", clf, clf_d[:, :], writes=[buf("clf")])
        P.op("dve", lambda e: e.tensor_tensor_scan(out=ccum, data0=onesP[0:32, 0:1].to_broadcast([32, 1024]), data1=clf,
                                                   initial=0.0, op0=ALU.mult, op1=ALU.add),
             reads=[buf("clf"), buf("onesP")], writes=[buf("ccum")])
        P.op("dve", lambda e: e.tensor_scalar(out=clf, in0=ccum, scalar1=ccum[:, 1023:1024], scalar2=-1.0,
                                              op0=ALU.subtract, op1=ALU.mult),
             reads=[buf("ccum")], writes=[buf("clf")])
        for blk in range(8):
            P.op("pe", lambda e, blk=blk: e.transpose(out=pM[:, blk * 32:(blk + 1) * 32], in_=clf[:, blk * 128:(blk + 1) * 128],
                                                       identity=identf[0:32, 0:32]),
                 reads=[buf("clf"), buf("identf")], writes=[buf("pM")])
        P.op("dve", lambda e: e.tensor_copy(out=NCKc.rearrange("p b r -> p (b r)"), in_=pM[:, 0:256]),
             reads=[buf("pM")], writes=[buf("NCKc")])
        for qq in range(8):
            P.op("pool", lambda e: e.memset(negt, NEG), writes=[buf("negt")])
            P.op("pool", lambda e, qq=qq: e.affine_select(out=negt, in_=negt, pattern=[[0, 16]], compare_op=ALU.is_ge,
                                                          fill=0.0, base=16 * qq - 1, channel_multiplier=-1),
                 reads=[buf("negt")], writes=[buf("negt")])
            P.op("pool", lambda e, qq=qq: e.tensor_copy(out=MSK[:, qq, :], in_=negt), reads=[buf("negt")], writes=[buf("MSK")])
            P.op("pool", lambda e: e.memset(negt, NEG), writes=[buf("negt")])
            P.op("pool", lambda e, qq=qq: e.affine_select(out=negt, in_=negt, pattern=[[-1, 16]], compare_op=ALU.is_gt,
                                                          fill=0.0, base=-16 * qq, channel_multiplier=1),
                 reads=[buf("negt")], writes=[buf("negt")])
            P.op("pool", lambda e, qq=qq: e.tensor_tensor(out=negt, in0=negt, in1=MSK[:, qq, :], op=ALU.add),
                 reads=[buf("negt"), buf("MSK")], writes=[buf("negt")])
            P.op("pool", lambda e, qq=qq: e.tensor_copy(out=MSK[:, qq, :], in_=negt), reads=[buf("negt")], writes=[buf("MSK")])
        for h in range(2):
            P.dma("sp", KTn[h][0:64, :], kt_d[h, :, TP:TP + 256], reads=[buf(f"kt_d{NTILE}")], writes=[buf(f"KTn{h}")])
            P.op("pool", lambda e, h=h: e.memset(KTn[h][64:65, :], 1.0), writes=[buf(f"KTn{h}")])
            P.dma("sp", QAs[h], qt_d[h, :, TP:TP + 256], reads=[buf(f"qt_d{NTILE}")], writes=[buf(f"QAs{h}")])
            P.dma("sp", SGs[h], sga_d[h, :, TP:TP + 256], reads=[buf(f"sga_d{NTILE}")], writes=[buf(f"SGs{h}")])
            for i in range(2):
                pass
        for i in range(8):
            P.op("pool", lambda e, i=i: e.memset(KTc[i][64:65, :], 1.0), writes=[buf(f"KTc{i}")])
            P.op("pool", lambda e, i=i: e.memset(VSc[i][:, :, 64:65], 1.0), writes=[buf(f"VSc{i}")])
        def one_qh(q, h, i):
            if True:
                r = q * 2 + h
                obs = obs_l[i]
                bobs = buf(f"obs{i}")
                P.dma("sp", kst_[i], kc_d[q, h, :, :], writes=[buf(f"kst_{i}")])
                P.dma("sp", vst_[i], vc_d[q, h, :, :].rearrange("(b p) d -> p b d", p=128), writes=[buf(f"vst_{i}")])
                P.op("dve", lambda e, i=i: e.tensor_copy(out=KTc[i][0:64, :], in_=kst_[i]),
                     reads=[buf(f"kst_{i}")], writes=[buf(f"KTc{i}")])
                P.op("pool", lambda e, i=i: e.tensor_copy(out=VSc[i][:, :, 0:64], in_=vst_[i]),
                     reads=[buf(f"vst_{i}")], writes=[buf(f"VSc{i}")])
                bank = zb(i // 2, i % 2)
                bnk = f"pZ{i // 2}{i % 2}"
                Sps = bank[:, 0:144].rearrange("p (b t) -> p b t", t=16)
                qs = slice(q * 16, (q + 1) * 16)
                sb_, qq = q // 8, q % 8
                for blk in range(8):
                    P.op("pe", lambda e, blk=blk, i=i, Sps=Sps, qs=qs: e.matmul(
                        Sps[:, blk, :], lhsT=KTc[i][:, blk * 128:(blk + 1) * 128], rhs=QAs[h][:, qs], start=True, stop=True),
                        reads=[buf(f"KTc{i}"), buf(f"QAs{h}")], writes=[buf(bnk)])
                P.op("pe", lambda e, Sps=Sps, qs=qs, sb_=sb_: e.matmul(
                    Sps[:, 8, :], lhsT=KTn[h][:, sb_ * 128:(sb_ + 1) * 128], rhs=QAs[h][:, qs], start=True, stop=False),
                    reads=[buf(f"KTn{h}"), buf(f"QAs{h}")], writes=[buf(bnk)])
                P.op("pe", lambda e, Sps=Sps, qq=qq: e.matmul(Sps[:, 8, :], lhsT=identb, rhs=MSK[:, qq, :], start=False, stop=True),
                     reads=[buf("identb"), buf("MSK")], writes=[buf(bnk)])
                for blk in range(9):
                    bias = NCKc[:, blk, r:r + 1] if blk < 8 else NCK[:, NBLK + sb_, h:h + 1]
                    P.op("act", lambda e, blk=blk, i=i, Sps=Sps, bias=bias: e.activation(
                        out=PTs[i][:, blk, :], in_=Sps[:, blk, :], func=AF.Exp, bias=bias, scale=1.0),
                        reads=[buf(bnk), buf("NCKc"), buf("NCK")], writes=[buf(f"PTs{i}")])
                pO = bank[0:65, 256:272]
                for blk in range(9):
                    lhs = VSc[i][:, blk, :] if blk < 8 else VP[:, NBLK + sb_, h, :]
                    P.op("pe", lambda e, blk=blk, i=i, lhs=lhs, pO=pO: e.matmul(pO, lhsT=lhs, rhs=PTs[i][:, blk, :],
                                                                            start=(blk == 0), stop=(blk == 8)),
                         reads=[buf(f"VSc{i}"), buf("VP"), buf(f"PTs{i}")], writes=[buf(bnk)])
                P.op("dve", lambda e, pO=pO: e.tensor_copy(out=obs, in_=pO), reads=[buf(bnk)], writes=[bobs])
                P.op("dve", lambda e: e.reciprocal(out=obs[64:65, :], in_=obs[64:65, :]), reads=[bobs], writes=[bobs])
                P.op("pe", lambda e, i=i: e.matmul(bank[0:64, 256:272], lhsT=onesP[64:65, 0:64], rhs=obs[64:65, :],
                                                   start=True, stop=True),
                     reads=[bobs, buf("onesP")], writes=[buf(bnk)])
                P.op("dve", lambda e, i=i: e.tensor_tensor(out=obs[0:64, :], in0=obs[0:64, :], in1=bank[0:64, 256:272], op=ALU.mult),
                     reads=[bobs, buf(bnk)], writes=[bobs])
                P.op("dve", lambda e, qs=qs: e.tensor_tensor(out=attS[h][:, qs], in0=obs[0:64, :], in1=SGs[h][:, qs], op=ALU.mult),
                     reads=[bobs, buf(f"SGs{h}")], writes=[buf(f"attS{h}")])
        it = 0
        for q in range(16):
            for h in range(2):
                one_qh(q, h, it % 8)
                it += 1
        for h in range(2):
            P.dma("pool", mixin_ap(h * 64, (h + 1) * 64, TP, 256), attS[h], reads=[buf(f"attS{h}")],
                  writes=[buf(f"mixinP{TP // PM_}")])

    tiles_x = [(ti * TW, TW) for ti in range(NTILE)] + [(TP, 256)]

    def g_alloc():
        G_ = {}
        G_['wgs'] = aa([128, 4, 128], F32)
        G_['wg'] = aa([128, 4, 128], BF16)
        G_['bg'] = aa([128, 1], F32)
        G_['nbg'] = aa([128, 1], F32)
        G_['zl'] = [aa([128, 4, TW], BF16) for i in range(2)]
        G_['zm'] = [aa([128, TW], BF16) for i in range(2)]
        G_['sgm'] = [aa([128, TW], BF16) for i in range(2)]
        G_['tg'] = aa([128, TW], F32)
        G_['tg2'] = aa([128, TW], F32)
        G_['s5o'] = [aa([128, TW], BF16) for i in range(2)]
        return G_

    def g_setup(G_):
        wgs, wg, bg, nbg = G_["wgs"], G_["wg"], G_["bg"], G_["nbg"]
        P.dma("sp", wgs, wglu_d.rearrange("(kc p) f -> p kc f", p=128), writes=[buf("wgs")])
        P.dma("sp", bg, bglu_d[:, :], writes=[buf("bg")])
        P.op("dve", lambda e: e.tensor_copy(out=wg, in_=wgs), reads=[buf("wgs")], writes=[buf("wg")])
        P.op("dve", lambda e: e.tensor_scalar(out=nbg, in0=bg, scalar1=-1.0, scalar2=None, op0=ALU.mult),
             reads=[buf("bg")], writes=[buf("nbg")])

    def g_tile(G_, k):
        wg, nbg, zl, zm, sgm, tg, tg2, s5o = (G_[x] for x in ("wg", "nbg", "zl", "zm", "sgm", "tg", "tg2", "s5o"))
        pG = zb(3, 0)
        c0, n = tiles_x[k]
        if True:
            i = k % 2
            P.dma("sp", zl[i][:, :, 0:n], zall_ap(c0, n).rearrange("(kc p) t -> p kc t", p=128),
                  reads=[buf(f"zallP{c0 // PZ}")], writes=[buf(f"zl{i}")])
            P.dma("sp", zm[i][:, 0:n], zin_ap(c0, n), reads=[buf(f"zinP{c0 // PZ}")], writes=[buf(f"zm{i}")])
            P.dma("sp", sgm[i][:, 0:n], sgs_d[:, c0:c0 + n], reads=[buf(f"sgs_d{k}")], writes=[buf(f"sgm{i}")])
            for kc in range(4):
                P.op("pe", lambda e, i=i, kc=kc, n=n: e.matmul(pG[:, 0:n], lhsT=wg[:, kc, :], rhs=zl[i][:, kc, 0:n],
                                                               start=(kc == 0), stop=(kc == 3)),
                     reads=[buf(f"zl{i}"), buf("wg")], writes=[buf("pZ30")])
            P.op("act", lambda e, i=i, n=n: e.activation(out=tg[:, 0:n], in_=pG[:, 0:n], func=AF.Exp, bias=nbg[:, 0:1],
                                                         scale=-1.0), reads=[buf("pZ30"), buf("nbg")], writes=[buf("tg")])
            P.op("act", lambda e, n=n: e.activation(out=tg[:, 0:n], in_=tg[:, 0:n], func=AF.Ln, bias=1.0, scale=1.0),
                 reads=[buf("tg")], writes=[buf("tg")])
            P.op("act", lambda e, n=n: e.activation(out=tg[:, 0:n], in_=tg[:, 0:n], func=AF.Exp, scale=-1.0),
                 reads=[buf("tg")], writes=[buf("tg")])
            P.op("dve", lambda e, i=i, n=n: e.tensor_tensor(out=tg2[:, 0:n], in0=zm[i][:, 0:n], in1=tg[:, 0:n], op=ALU.mult),
                 reads=[buf("tg"), buf(f"zm{i}")], writes=[buf("tg2")])
            P.op("dve", lambda e, i=i, n=n: e.tensor_tensor(out=s5o[i][:, 0:n], in0=tg2[:, 0:n], in1=sgm[i][:, 0:n], op=ALU.mult),
                 reads=[buf("tg2"), buf(f"sgm{i}")], writes=[buf(f"s5o{i}")])
            P.dma("pool", mixin_ap(128, 256, c0, n), s5o[i][:, 0:n], reads=[buf(f"s5o{i}")], writes=[buf(f"mixinP{c0 // PM_}")])

    def c_alloc():
        C_ = {}
        C_['wos'] = [aa([128, 256], F32) for i in range(2)]
        C_['wo'] = aa([128, 8, 256], BF16)
        C_['ml'] = [aa([128, 8, TW], BF16) for i in range(2)]
        C_['xc'] = [aa([128, 4, 256], F32) for i in range(1)]
        C_['yst'] = [aa([128, 4, 256], F32) for i in range(1)]
        return C_

    def c_setup(C_):
        wos, wo = C_['wos'], C_['wo']
        for kc in range(8):
            s = kc % 2
            P.dma("sp", wos[s], wout_d[kc * 128:(kc + 1) * 128, :], writes=[buf(f"wos{s}")])
            P.op("dve", lambda e, kc=kc, s=s: e.tensor_copy(out=wo[:, kc, :], in_=wos[s]),
                 reads=[buf(f"wos{s}")], writes=[buf("wo")])

    def c_tile(C_, k):
        wo, ml, xc, yst = C_['wo'], C_['ml'], C_['xc'], C_['yst']
        pC = zb(3, 1)
        c0, n = tiles_x[k]
        if True:
            i = k % 2
            ns = n // 128
            P.dma("sp", ml[i][:, :, 0:n], mixall_ap(c0, n).rearrange("(kc p) t -> p kc t", p=128),
                  reads=[buf(f"mixallP{c0 // PM_}")], writes=[buf(f"ml{i}")])
            P.dma("sp", xc[0][:, 0:ns, :], xc_d[c0:c0 + n, :].rearrange("(s p) c -> p s c", p=128), writes=[buf("xc0")])
            for s2 in range(0, ns, 2):
                for s in range(s2, min(s2 + 2, ns)):
                    for kc in range(8):
                        P.op("pe", lambda e, i=i, s=s, kc=kc: e.matmul(
                            pC[:, (s % 2) * 256:(s % 2) * 256 + 256], lhsT=ml[i][:, kc, s * 128:(s + 1) * 128], rhs=wo[:, kc, :],
                            start=(kc == 0), stop=(kc == 7)),
                            reads=[buf(f"ml{i}"), buf("wo")], writes=[buf("pZ31")])
                w2 = min(2, ns - s2)
                P.op("dve", lambda e, s2=s2, w2=w2: e.tensor_tensor(
                    out=yst[0][:, s2:s2 + w2, :], in0=pC.rearrange("p (s c) -> p s c", s=2)[:, 0:w2, :],
                    in1=xc[0][:, s2:s2 + w2, :], op=ALU.add),
                    reads=[buf("pZ31"), buf("xc0")], writes=[buf("yst0")])
            out_dmas.append(P.dma("pool", y_o[c0:c0 + n, :].rearrange("(s p) c -> p s c", p=128), yst[0][:, 0:ns, :],
                                  reads=[buf("yst0")]))

    def ag2(pi):
        P.coll(lambda e: e.collective_compute("AllGather", ALU.bypass, replica_groups=GROUPS,
                                              ins=[mixin_p[pi]], outs=[mixall_p[pi]]),
               reads=[buf(f"mixinP{pi}")], writes=[buf(f"mixallP{pi}")])

    if "SAMP" in stages:
        P.barrier()
        sample_attention()
    P.barrier()
    ar_off[0] = 0
    KT = [aa([65, TP], BF16) for h in range(2)]
    QA = [[aa([65, QB], BF16) for h in range(2)] for p in range(2)]
    SG = [[aa([64, QB], BF16) for h in range(2)] for p in range(2)]
    PT = [aa([128, QB], BF16) for i in range(3)]
    osb = [aa([65, QB], F32) for h in range(2)]
    attg = [aa([64, QB], BF16) for h in range(2)]
    if "ATT" in stages:
        do_x = "X" in stages
        if do_x:
            G_ = g_alloc()
            C_ = c_alloc()
            g_setup(G_)
            c_setup(C_)
        ntl = (n_real + TW - 1) // TW
        for h in range(2):
            P.dma("sp", KT[h][0:64, 0:ntl * TW], kt_d[h, :, 0:ntl * TW],
                  reads=[buf(f"kt_d{ti}") for ti in range(ntl)], writes=[buf(f"KT{h}")])
            P.op("pool", lambda e, h=h: e.memset(KT[h][64:65, :], 1.0), writes=[buf(f"KT{h}")])
        ntile_x = len(tiles_x)
        g_next = [0]
        c_queue = []
        ag2_done = [0]

        def after_unit(u, last):
            if not do_x:
                return
            while g_next[0] < ntile_x and (g_next[0] <= u or last):
                g_tile(G_, g_next[0])
                g_next[0] += 1
            while ag2_done[0] < npm:
                pi = ag2_done[0]
                tok_end = min((pi + 1) * PM_, TX)
                need_tiles = [k for k, (c0, n) in enumerate(tiles_x) if c0 < tok_end]
                need_q = [qi for qi, (q0, ql) in enumerate(qbs) if q0 < tok_end]
                if (max(need_tiles) < g_next[0]) and (max(need_q) * 2 + 1 <= u or last):
                    ag2(pi)
                    ag2_done[0] += 1
                    c_queue.extend([(k, u + 4) for k, (c0, n) in enumerate(tiles_x) if pi * PM_ <= c0 < tok_end])
                else:
                    break
            if c_queue and (c_queue[0][1] <= u or last):
                n_emit = len(c_queue) if last else 1
                for _ in range(n_emit):
                    k, _u = c_queue.pop(0)
                    c_tile(C_, k)

        u = 0
        nunits = 2 * len(qbs)
        for qi, (q0, qlen) in enumerate(qbs):
            par = qi % 2
            for h in range(2):
                rd = [buf(f"qt_d{ti}") for ti in tiles_of(q0, q0 + qlen)]
                P.dma("sp", QA[par][h][:, 0:qlen], qt_d[h, :, q0:q0 + qlen], reads=rd, writes=[buf(f"QA{par}{h}")])
                rd = [buf(f"sga_d{ti}") for ti in tiles_of(q0, q0 + qlen)]
                P.dma("sp", SG[par][h][:, 0:qlen], sga_d[h, :, q0:q0 + qlen], reads=rd, writes=[buf(f"SG{par}{h}")])
            for h in range(2):
                attend(qi, h, q0, qlen)
                after_unit(u, u == nunits - 1)
                u += 1

    P.barrier()
    P.wait("sp", out_dmas)
    if os.environ.get("MK_RESCHED", "1") == "1":
        est = P.reschedule(window=int(os.environ.get('MK_WIN', '128')), hop=float(os.environ.get('MK_HOP', '1.5')))
    stats = P.emit(stack)
    stack.close()
    return nc, stats


_CACHE = {}


def _prep_core(c, I):
    b, hp = c // 4, c % 4
    f32 = np.float32
    x = np.zeros((TP, D_MODEL), f32)
    x[:N_META] = I["meta_tokens"]
    nreal = min(L_REAL, TP)
    x[N_META:nreal] = I["x_prompt"][b][:nreal - N_META]
    w = I["w_in"][0]
    cols = np.concatenate([
        np.arange(128 * hp, 128 * hp + 128), 512 + np.arange(128 * hp, 128 * hp + 128),
        1024 + np.arange(128 * hp, 128 * hp + 128), 1544 + np.arange(128 * hp, 128 * hp + 128),
        2056 + np.arange(128 * hp, 128 * hp + 128), 2568 + np.arange(128 * hp, 128 * hp + 128),
        1536 + np.arange(2 * hp, 2 * hp + 2)])
    m = {
        "x": x,
        "w_in_c": np.ascontiguousarray(w[:, cols]),
        "norm_g": np.ascontiguousarray(I["norm_g"][0].reshape(8, 128).T),
        "b_f": np.ascontiguousarray(np.broadcast_to(I["b_f"][0, 2 * hp:2 * hp + 2][None, :], (128, 2))),
        "qg": np.ascontiguousarray(np.tile(I["q_norm_g"][0], 2)[:, None]),
        "kg": np.ascontiguousarray(np.tile(I["k_norm_g"][0], 2)[:, None]),
    }
    G = slice(8 * hp, 8 * hp + 8)
    are, aim = I["s5_a_re"][0, G], I["s5_a_im"][0, G]
    bre = I["s5_b_re"][0, G].transpose(1, 0, 2)
    bim = I["s5_b_im"][0, G].transpose(1, 0, 2)
    cre = I["s5_c_re"][0, G].transpose(2, 0, 1)
    cim = I["s5_c_im"][0, G].transpose(2, 0, 1)
    m.update({
        "s5_are": np.tile(are.T, (2, 1)),
        "s5_aim": np.tile(aim.T, (2, 1)),
        "s5_ldt": np.broadcast_to(I["s5_log_dt"][0, G][None, :], (128, 8)),
        "s5_d": I["s5_d"][0, G].reshape(128, 1),
        "s5_x1": np.concatenate([bre, bim], 0),
        "s5_x2": np.concatenate([bim, bre], 0),
        "s5_cx1": np.concatenate([cre, cim], 0),
        "s5_cx2": np.concatenate([cim, cre], 0),
    })
    Q = slice(16 * b, 16 * b + 16)
    H2 = slice(2 * hp, 2 * hp + 2)
    xsmp = I["x_sample"][Q].reshape(256, D_MODEL)
    ocols = slice(256 * hp, 256 * hp + 256)
    wo = I["w_out"][0]
    rows = np.concatenate([np.concatenate([np.arange(128 * r, 128 * r + 128), 512 + np.arange(128 * r, 128 * r + 128)])
                           for r in range(4)])
    sre = I["state_s5_re"][0, Q, G, :].transpose(2, 1, 0)
    sim_ = I["state_s5_im"][0, Q, G, :].transpose(2, 1, 0)
    m.update({
        "xsmp": xsmp,
        "clf": I["cache_logf"][0, Q, :, H2].transpose(0, 2, 1).reshape(32, 1024),
        "kcT": I["cache_k"][0, Q, :, H2, :].transpose(0, 2, 3, 1),
        "vc": I["cache_v"][0, Q, :, H2, :].transpose(0, 2, 1, 3),
        "wglu_c": I["w_glu"][0][:, 128 * hp:128 * hp + 128],
        "bglu_c": I["b_glu"][0, 128 * hp:128 * hp + 128][:, None],
        "wout_c": wo[rows][:, ocols],
        "x_c": np.concatenate([x[:, ocols], xsmp[:, ocols]], 0),
        "s5x0": np.concatenate([sre, sim_], 0),
    })
    return {k: np.ascontiguousarray(v, dtype=f32) for k, v in m.items()}


def kernel(**inputs):
    I = {k: np.asarray(v) for k, v in inputs.items()}
    if "nc" not in _CACHE:
        _CACHE["nc"] = build_program()[0]
    nc = _CACHE["nc"]
    in_maps = [_prep_core(c, I) for c in range(8)]
    res = run_bass_kernel_spmd(nc, in_maps, core_ids=list(range(8)))
    R = res.results
    f32 = np.float32
    y_p = np.zeros((2, SEQ, D_MODEL), f32)
    y_s = np.zeros((32, 16, D_MODEL), f32)
    k_p = np.zeros((1, 2, L_REAL, 8, 64), f32)
    v_p = np.zeros((1, 2, L_REAL, 8, 64), f32)
    lf_p = np.zeros((1, 2, L_REAL, 8), f32)
    sr_p = np.zeros((1, 2, 32, 64), f32)
    si_p = np.zeros((1, 2, 32, 64), f32)
    k_s = np.zeros((1, 32, 16, 8, 64), f32)
    v_s = np.zeros((1, 32, 16, 8, 64), f32)
    lf_s = np.zeros((1, 32, 16, 8), f32)
    sr_s = np.zeros((1, 32, 32, 64), f32)
    si_s = np.zeros((1, 32, 32, 64), f32)
    for c in range(8):
        b, hp = c // 4, c % 4
        r = R[c]
        nr = min(L_REAL, TP)
        k_p[0, b, :nr, 2 * hp:2 * hp + 2, :] = r["k_out"][:nr].reshape(nr, 2, 64)
        v_p[0, b, :nr, 2 * hp:2 * hp + 2, :] = r["v_out"][:nr].reshape(nr, 2, 64)
        lf_p[0, b, :nr, 2 * hp:2 * hp + 2] = r["logf_out"][:nr]
        if "y_out" in r:
            cols = slice(256 * hp, 256 * hp + 256)
            G = slice(8 * hp, 8 * hp + 8)
            Q = slice(16 * b, 16 * b + 16)
            yo = r["y_out"]
            y_p[b, :nr - N_META, cols] = yo[N_META:nr]
            y_s[Q, :, cols] = yo[TP:TP + 256].reshape(16, 16, 256)
            k_s[0, Q, :, 2 * hp:2 * hp + 2, :] = r["ks_out"].reshape(16, 16, 2, 64)
            v_s[0, Q, :, 2 * hp:2 * hp + 2, :] = r["vs_out"].reshape(16, 16, 2, 64)
            lf_s[0, Q, :, 2 * hp:2 * hp + 2] = r["lfs_out"].reshape(16, 16, 2)
            fin = r["s5fin_out"]
            sr_p[0, b, G, :] = fin[0:64, :].T
            si_p[0, b, G, :] = fin[64:128, :].T
            fs = r["s5fins_out"]
            sr_s[0, Q, G, :] = fs[0:64].transpose(2, 1, 0)
            si_s[0, Q, G, :] = fs[64:128].transpose(2, 1, 0)
    _CACHE["last"] = R
    return (y_p, y_s, k_p, v_p, lf_p, sr_p, si_p, k_s, v_s, lf_s, sr_s, si_s)
```

```python
import math
import numpy as np
import ml_dtypes
from contextlib import ExitStack
import concourse.bass as bass
import concourse.mybir as mybir
from concourse.bass_utils import run_bass_kernel_spmd

F32 = mybir.dt.float32
BF16 = mybir.dt.bfloat16
AF = mybir.ActivationFunctionType
ALU = mybir.AluOpType

D_MODEL = 1024
SEQ = 16384
N_META = 16
L_REAL = SEQ + N_META
TW = 512
import os
NTILE = int(os.environ.get("MK_NTILE", "33"))
TP = NTILE * TW
NBLK = TP // 128
TX = TP + 256
NCOL = 770
EPS = 1e-6
NEG = -30000.0


class Buf:
    __slots__ = ("name", "w", "rs", "const", "excl")

    def __init__(self, name, const=False, excl=False):
        self.name = name
        self.w = None
        self.rs = []
        self.const = const
        self.excl = excl


class Node:
    __slots__ = ("eng", "fn", "deps", "kind", "sem", "val", "used", "idx", "fuse", "cost", "seg", "fin", "prio")

    def __init__(self, eng, fn, kind):
        self.eng = eng
        self.fn = fn
        self.kind = kind
        self.deps = []
        self.sem = None
        self.val = 0
        self.used = False
        self.fuse = True
        self.cost = None
        self.seg = 0
        self.fin = 0.0
        self.prio = 0


ENGS = ("pe", "act", "dve", "pool", "sp")
N_DMA_SEMS = 48
SEM_ROLL = 30000


class Prog:
    def __init__(self, nc):
        self.nc = nc
        self.q = {e: [] for e in ENGS}
        self.nodes = []
        self.dma_i = 0
        self.dma_j = 0
        self.seg = 0
        self.cur_prio = 0
        self.cur_deps = ()
        self.dma_last = [None] * N_DMA_SEMS

    def _mk(self, eng, fn, kind, reads, writes, extra):
        n = Node(eng, fn, kind)
        seen = set()
        ex = [b for b in reads if b.excl]
        if ex:
            reads = [b for b in reads if not b.excl]
            writes = list(writes) + [b for b in ex if b not in writes]

        def add(d, k):
            if d is None or (id(d), k) in seen:
                return
            seen.add((id(d), k))
            n.deps.append((d, k))
        for b in reads:
            add(b.w, "raw")
        for b in writes:
            add(b.w, "waw")
            for r in b.rs:
                add(r, "war")
        for d in extra:
            add(d, "raw")
        for d in self.cur_deps:
            add(d, "raw")
        for b in reads:
            if not b.const:
                b.rs.append(n)
        for b in writes:
            b.w = n
            b.rs = []
        n.idx = len(self.nodes)
        n.seg = self.seg
        n.prio = self.cur_prio
        self.nodes.append(n)
        self.q[eng].append(n)
        return n

    def op(self, eng, fn, reads=(), writes=(), deps=(), fuse=True, cost=None):
        n = self._mk(eng, fn, "c", reads, writes, deps)
        n.fuse = fuse
        n.cost = cost
        return n

    def dma(self, eng, out, in_, reads=(), writes=(), deps=()):
        half = N_DMA_SEMS // 2
        if eng == "pool":
            k = half + self.dma_j % half
            self.dma_j += 1
        else:
            k = self.dma_i % half
            self.dma_i += 1
        extra = list(deps)
        if self.dma_last[k] is not None:
            extra.append(self.dma_last[k])
        n = self._mk(eng, lambda e: e.dma_start(out=out, in_=in_), "d", reads, writes, extra)
        n.sem = k
        self.dma_last[k] = n
        return n

    def coll(self, fn, reads=(), writes=()):
        return self._mk("pool", fn, "x", reads, writes, ())

    def wait(self, eng, deps):
        return self._mk(eng, None, "w", (), (), deps)

    def barrier(self):
        deps = []
        for e in ENGS:
            for n in reversed(self.q[e]):
                if n.kind == "c":
                    deps.append(n)
                    break
        deps += [n for n in self.dma_last if n is not None]
        deps += [n for n in self.nodes if n.kind == "x"]
        self.seg += 1
        for e in ENGS:
            self.wait(e, deps)
        self.seg += 1

    DEF_COST = {"pe": 0.25, "act": 0.75, "dve": 0.75, "pool": 0.8, "sp": 0.1}

    def reschedule(self, window=48, hop=1.2):
        segs = {}
        for n in self.nodes:
            segs.setdefault(n.seg, []).append(n)
        clock = {e: 0.0 for e in ENGS}
        newq = {e: [] for e in ENGS}
        done = set()
        for sg in sorted(segs):
            if all(n.kind == "w" for n in segs[sg]):
                lastc = []
                for e in ENGS:
                    for m in reversed(newq[e]):
                        if m.kind == "c":
                            lastc.append(m)
                            break
                for n in segs[sg]:
                    keep = [(d, k) for d, k in n.deps if d.kind in ("d", "x")]
                    n.deps = keep + [(m, "raw") for m in lastc]
            pend = {e: [n for n in segs[sg] if n.eng == e] for e in ENGS}
            left = sum(len(v) for v in pend.values())
            while left:
                best = None
                for e in ENGS:
                    cand = pend[e]
                    seen = 0
                    for ci in range(len(cand)):
                        n = cand[ci]
                        if not n.prio:
                            seen += 1
                            if seen > window:
                                break
                        ok = True
                        rdy = 0.0
                        for d, k in n.deps:
                            if id(d) not in done:
                                ok = False
                                break
                            t = d.fin + (0.05 if (d.eng == e and d.kind == "c") else hop)
                            if t > rdy:
                                rdy = t
                        if not ok:
                            continue
                        st = max(clock[e], rdy)
                        key = (st + (0.8 if n.prio else 0.0), n.idx)
                        if best is None or key < best[0]:
                            best = (key, e, ci, n, st)
                        if rdy <= clock[e] and not n.prio:
                            break
                assert best is not None, "scheduler stuck"
                _, e, ci, n, st = best
                pend[e].pop(ci)
                left -= 1
                c = n.cost
                if c is None:
                    c = 0.0 if n.kind == "w" else (2.5 if n.kind in ("d", "x") else self.DEF_COST[e])
                if n.kind in ("d", "x"):
                    clock[e] = st + 0.3
                    n.fin = st + c
                else:
                    clock[e] = st + c
                    n.fin = st + c
                done.add(id(n))
                newq[e].append(n)
        self.q = newq
        return max(clock.values())

    def emit(self, stack):
        nc = self.nc
        for n in self.nodes:
            for d, k in n.deps:
                if d.kind in ("d", "x"):
                    d.used = True
                elif d.eng != n.eng:
                    d.used = True
                elif n.eng != "pe":
                    d.used = True
                elif n.kind == "d":
                    d.used = True
        esems = {e: [stack.enter_context(nc.semaphore(f"s_{e}0"))] for e in ENGS}
        ecnt = {e: 0 for e in ENGS}
        dsems = [stack.enter_context(nc.semaphore(f"s_dma{i}")) for i in range(N_DMA_SEMS)]
        dcnt = [0] * N_DMA_SEMS
        for n in self.nodes:
            if n.kind == "x":
                n.sem = stack.enter_context(nc.semaphore(f"s_cc{n.idx}"))
                n.val = 1
            elif n.kind == "d":
                dcnt[n.sem] += 16
                n.val = dcnt[n.sem]
                n.sem = dsems[n.sem]
        for e in ENGS:
            for n in self.q[e]:
                if n.kind == "c" and n.used:
                    if ecnt[e] >= SEM_ROLL:
                        esems[e].append(stack.enter_context(nc.semaphore(f"s_{e}{len(esems[e])}")))
                        ecnt[e] = 0
                    ecnt[e] += 1
                    n.val = ecnt[e]
                    n.sem = esems[e][-1]
        block = stack.enter_context(nc.Block())
        handles = {"pe": block.tensor, "act": block.scalar, "dve": block.vector,
                   "pool": block.gpsimd, "sp": block.sync}
        stats = {}
        for e in ENGS:
            queue = self.q[e]

            def body(eng, queue=queue, e=e):
                waited = {}
                nw = 0
                for n in queue:
                    pend = []
                    for d, k in n.deps:
                        if d.kind not in ("d", "x"):
                            if d.eng == e and e == "pe" and n.kind != "d":
                                continue
                        if d.sem is None:
                            continue
                        key = id(d.sem)
                        if waited.get(key, 0) >= d.val:
                            continue
                        waited[key] = d.val
                        pend.append((d.sem, d.val))
                        nw += 1
                    best = {}
                    for s_, v_ in pend:
                        if id(s_) not in best or best[id(s_)][1] < v_:
                            best[id(s_)] = (s_, v_)
                    pend = list(best.values())
                    fuse = None
                    if pend and n.kind == "c" and n.fuse and e in ("act", "dve", "pool"):
                        fuse = pend.pop()
                    for s_, v_ in pend:
                        eng.wait_ge(s_, v_)
                    if n.kind == "w":
                        continue
                    ins = n.fn(eng)
                    if fuse is not None:
                        ins._wait_ge(fuse[0], fuse[1])
                    if n.kind == "x":
                        ins.then_inc(n.sem, 1)
                    elif n.kind == "d":
                        ins.then_inc(n.sem, 16)
                    elif n.used:
                        ins.then_inc(n.sem, 1)
                stats[e] = (len(queue), nw)
            handles[e](body)
        return stats


QB = 1024
ARENA_BYTES = 161 * 1024
STAGES = os.environ.get("MK_STAGES", "A,ATT,SAMP,X")


def build_program(debug=False):
    nc = bass.Bass("TRN2", target_bir_lowering=False)
    P = Prog(nc)
    stack = ExitStack()
    stages = set(STAGES.split(","))

    def din(name, shape, dt=F32):
        return nc.dram_tensor(name, list(shape), dt, kind="ExternalInput").ap()

    def dout(name, shape, dt=F32):
        return nc.dram_tensor(name, list(shape), dt, kind="ExternalOutput").ap()

    def dscr(name, shape, dt):
        return nc.dram_tensor(name, list(shape), dt).ap()

    def sb(name, shape, dt):
        return stack.enter_context(nc.sbuf_tensor("sb_" + name, list(shape), dt))[:]

    def ps(name, shape, dt):
        return stack.enter_context(nc.psum_tensor("ps_" + name, list(shape), dt))[:]

    arena = sb("arena", [128, ARENA_BYTES // 4], F32)
    ar_off = [0]

    def aa(shape, dt):
        nfree = int(np.prod(shape[1:]))
        esz = 2 if dt == BF16 else 4
        nbytes = (nfree * esz + 31) // 32 * 32
        w0 = ar_off[0] // 4
        ar_off[0] += nbytes
        assert ar_off[0] <= ARENA_BYTES, ("arena overflow", ar_off[0])
        v = arena[0:shape[0], w0:w0 + nbytes // 4]
        if dt != F32:
            v = v.bitcast(dt)
        v = v[:, 0:nfree]
        if len(shape) == 3:
            v = v.rearrange("p (a b) -> p a b", a=shape[1])
        elif len(shape) == 4:
            v = v.rearrange("p (a b c) -> p a b c", a=shape[1], b=shape[2])
        return v

    x_d = din("x", [TP, D_MODEL])
    w_d = din("w_in_c", [D_MODEL, NCOL])
    ng_d = din("norm_g", [128, 8])
    bf_d = din("b_f", [128, 2])
    qg_d = din("qg", [128, 1])
    kg_d = din("kg", [128, 1])

    S5D = {}
    for nm in ("s5_are", "s5_aim", "s5_ldt"):
        S5D[nm] = din(nm, [128, 8])
    S5D["s5_d"] = din("s5_d", [128, 1])
    for nm in ("s5_x1", "s5_x2", "s5_cx1", "s5_cx2"):
        S5D[nm] = din(nm, [128, 8, 16])
    s5fin_o = dout("s5fin_out", [128, 8])
    PZ, PM_ = 4096, 2048
    npz = (TX + PZ - 1) // PZ
    npm = (TX + PM_ - 1) // PM_
    zin_p = [dscr(f"zin{i}", [128, min(PZ, TX - i * PZ)], BF16) for i in range(npz)]
    zall_p = [dscr(f"zall{i}", [512, min(PZ, TX - i * PZ)], BF16) for i in range(npz)]
    mixin_p = [dscr(f"mixin{i}", [256, min(PM_, TX - i * PM_)], BF16) for i in range(npm)]
    mixall_p = [dscr(f"mixall{i}", [1024, min(PM_, TX - i * PM_)], BF16) for i in range(npm)]

    def zin_ap(c0, n):
        return zin_p[c0 // PZ][:, c0 % PZ:c0 % PZ + n]

    def zall_ap(c0, n):
        return zall_p[c0 // PZ][:, c0 % PZ:c0 % PZ + n]

    def mixin_ap(r0, r1, c0, n):
        return mixin_p[c0 // PM_][r0:r1, c0 % PM_:c0 % PM_ + n]

    def mixall_ap(c0, n):
        return mixall_p[c0 // PM_][:, c0 % PM_:c0 % PM_ + n]
    sgs_d = dscr("sgs_scr", [128, TX], BF16)

    xs_d = din("xsmp", [256, D_MODEL])
    clf_d = din("clf", [32, 1024])
    kc_d = din("kcT", [16, 2, 64, 1024])
    vc_d = din("vc", [16, 2, 1024, 64])
    wglu_d = din("wglu_c", [512, 128])
    bglu_d = din("bglu_c", [128, 1])
    wout_d = din("wout_c", [1024, 256])
    xc_d = din("x_c", [TX, 256])
    s5x0_d = din("s5x0", [128, 8, 16])
    y_o = dout("y_out", [TX, 256])
    ks_o = dout("ks_out", [256, 128])
    vs_o = dout("vs_out", [256, 128])
    lfs_o = dout("lfs_out", [256, 2])
    s5fins_o = dout("s5fins_out", [128, 8, 16])

    k_o = dout("k_out", [TP, 128])
    v_o = dout("v_out", [TP, 128])
    lf_o = dout("logf_out", [TP, 2])

    qt_d = dscr("qt_scr", [2, 65, TX], BF16)
    kt_d = dscr("kt_scr", [2, 64, TX], BF16)
    sga_d = dscr("sga_scr", [2, 64, TX], BF16)

    ng = sb("ng", [128, 8], F32)
    bfp = sb("bfp", [128, 2], F32)
    nbf = sb("nbf", [128, 2], F32)
    qg = sb("qg", [128, 1], F32)
    kg = sb("kg", [128, 1], F32)
    qg8 = sb("qg8", [128, 1], F32)
    onesf = sb("onesf", [128, 128], F32)
    identf = sb("identf", [128, 128], F32)
    identb = sb("identb", [128, 128], BF16)
    BO = sb("BO", [128, 128], BF16)
    MN = sb("MN", [128, 128], BF16)
    ones2 = sb("ones2", [2, TW], F32)
    onesP = sb("onesP", [128, 64], F32)
    VP = sb("VP", [128, NBLK + 2, 2, 65], BF16)
    NCK = sb("NCK", [128, NBLK + 2, 2], F32)
    cumrow = [sb(f"cumrow{i}", [2, TW], F32) for i in range(2)]
    S5FIN = sb("s5fin", [128, 8], F32)
    S5FINS = sb("s5fins", [128, 8, 16], F32)
    S5X0 = sb("s5x0s", [128, 8, 16], F32)
    segm = sb("segm", [2, TW], F32)

    Z = [ps(f"Z{i}", [128, 1024], F32) for i in range(4)]

    def zb(i, half):
        return Z[i][:, half * 512:(half + 1) * 512]

    pT = [zb(0, h).bitcast(BF16).rearrange("p (k t) -> p k t", k=8) for h in range(2)]
    pP = [zb(1, 0), zb(1, 1)]
    pM = zb(2, 0)
    pV = zb(2, 1).rearrange("p (s f) -> p s f", s=4)
    pK = zb(3, 0).rearrange("p (s f) -> p s f", s=4)
    pS = zb(3, 1)
    PALIAS = {"pT0": "pZ00", "pT1": "pZ01", "pP0": "pZ10", "pP1": "pZ11", "pM": "pZ20", "pV": "pZ21",
              "pK": "pZ30", "pS": "pZ31"}

    B = {}

    def buf(name, const=False):
        name = PALIAS.get(name, name)
        name = {"lnb": "lnb0", "rr": "rr0", "sq": "sq0"}.get(name, name)
        if name not in B:
            B[name] = Buf(name, const, excl=name.startswith("pZ"))
        return B[name]

    P.dma("sp", ng, ng_d[:, :], writes=[buf("ng")])
    P.dma("sp", bfp, bf_d[:, :], writes=[buf("bfp")])
    P.dma("sp", qg, qg_d[:, :], writes=[buf("qg")])
    P.dma("sp", kg, kg_d[:, :], writes=[buf("kg")])
    P.op("dve", lambda e: e.tensor_scalar(out=nbf, in0=bfp, scalar1=-1.0, scalar2=None, op0=ALU.mult),
         reads=[buf("bfp")], writes=[buf("nbf")])
    P.op("dve", lambda e: e.tensor_scalar(out=qg8, in0=qg, scalar1=0.125, scalar2=None, op0=ALU.mult),
         reads=[buf("qg")], writes=[buf("qg8")])
    P.op("pool", lambda e: e.memset(onesf, 1.0), writes=[buf("onesf")])
    P.op("pool", lambda e: e.memset(ones2, 1.0), writes=[buf("ones2")])
    P.op("pool", lambda e: e.memset(onesP, 1.0), writes=[buf("onesP")])
    P.op("pool", lambda e: e.memset(segm, 1.0), writes=[buf("segm")])
    P.op("pool", lambda e: e.memset(segm.rearrange("p (q t) -> p q t", t=16)[:, :, 0:1], 0.0), writes=[buf("segm")])
    P.op("pool", lambda e: e.affine_select(out=identf, in_=onesf, pattern=[[-1, 128]],
                                           compare_op=ALU.is_equal, fill=0.0, base=0, channel_multiplier=1),
         reads=[buf("onesf")], writes=[buf("identf")])
    P.op("pool", lambda e: e.tensor_copy(out=identb, in_=identf), reads=[buf("identf")], writes=[buf("identb")])
    P.op("pool", lambda e: e.memset(onesf, NEG), reads=[buf("identf")], writes=[buf("onesf")])
    P.op("pool", lambda e: e.affine_select(out=MN, in_=onesf, pattern=[[-1, 128]],
                                           compare_op=ALU.is_gt, fill=0.0, base=0, channel_multiplier=1),
         reads=[buf("onesf")], writes=[buf("MN")])
    P.op("pool", lambda e: e.memset(BO, 0.0), writes=[buf("BO")])
    P.op("pool", lambda e: e.memset(BO[0:64, 0:64], 1.0 / 64), writes=[buf("BO")])
    P.op("pool", lambda e: e.memset(BO[64:128, 64:128], 1.0 / 64), writes=[buf("BO")])
    P.op("pool", lambda e: e.memset(VP[:, :, :, 64:65], 1.0), writes=[buf("VP")])

    out_dmas = []

    ar_off[0] = 0
    Wb = aa([128, 8, NCOL], BF16)
    s5_mark = [0]
    setup_extent = [0]
    def wb_setup():
      wst = [aa([128, NCOL], F32) for i in range(2)]
      for kc in range(8):
        s = kc % 2
        P.dma("sp", wst[s], w_d[kc * 128:(kc + 1) * 128, :], writes=[buf(f"wst{s}")])
        P.op("dve", lambda e, kc=kc, s=s: e.tensor_scalar(
            out=Wb[:, kc, :], in0=wst[s], scalar1=ng[:, kc:kc + 1], scalar2=None, op0=ALU.mult),
            reads=[buf(f"wst{s}"), buf("ng")], writes=[buf("Wb")])

    def silu2_from_psum(items, N, tmps):
        for (i, o, ob), (l, lb, r, rb) in zip(items, tmps):
            P.op("act", lambda e, i=i, l=l: e.activation(out=l[:, 0:N], in_=pP[i][:, 0:N], func=AF.Exp, scale=-1.0),
                 reads=[buf(f"pP{i}")], writes=[buf(lb)])
        for (i, o, ob), (l, lb, r, rb) in zip(items, tmps):
            P.op("act", lambda e, l=l: e.activation(out=l[:, 0:N], in_=l[:, 0:N], func=AF.Ln, bias=1.0, scale=1.0),
                 reads=[buf(lb)], writes=[buf(lb)])
        for (i, o, ob), (l, lb, r, rb) in zip(items, tmps):
            P.op("act", lambda e, l=l, r=r: e.activation(out=r[:, 0:N], in_=l[:, 0:N], func=AF.Exp, scale=-1.0),
                 reads=[buf(lb)], writes=[buf(rb)])
        for (i, o, ob), (l, lb, r, rb) in zip(items, tmps):
            P.op("dve", lambda e, i=i, o=o, r=r: e.tensor_tensor(out=o[:, 0:N], in0=pP[i][:, 0:N], in1=r[:, 0:N], op=ALU.mult),
                 reads=[buf(f"pP{i}"), buf(rb)], writes=[ob])

    def silu_from_psum(i, out_ap, out_buf, N=TW):
        P.op("act", lambda e: e.activation(out=lnb[:, 0:N], in_=pP[i][:, 0:N], func=AF.Exp, scale=-1.0),
             reads=[buf(f"pP{i}")], writes=[buf("lnb")])
        P.op("act", lambda e: e.activation(out=lnb[:, 0:N], in_=lnb[:, 0:N], func=AF.Ln, bias=1.0, scale=1.0),
             reads=[buf("lnb")], writes=[buf("lnb")])
        P.op("act", lambda e: e.activation(out=rr[:, 0:N], in_=lnb[:, 0:N], func=AF.Exp, scale=-1.0),
             reads=[buf("lnb")], writes=[buf("rr")])
        P.op("dve", lambda e: e.tensor_tensor(out=out_ap[:, 0:N], in0=pP[i][:, 0:N], in1=rr[:, 0:N], op=ALU.mult),
             reads=[buf(f"pP{i}"), buf("rr")], writes=[out_buf])

    def s5_alloc_weights():
        S = {}
        S["are"] = aa([128, 8], F32)
        S["aim"] = aa([128, 8], F32)
        S["ldt"] = aa([128, 8], F32)
        S["dvec"] = aa([128, 1], F32)
        S["rho8"] = aa([128, 8], F32)
        S["ar8"] = aa([128, 8], F32)
        S["ai8"] = aa([128, 8], F32)
        S["PiT"] = aa([128, 128], F32)
        S["Wfir"] = aa([128, 8, 128], BF16)
        S["Wst"] = aa([128, 8, 8, 128], BF16)
        S["Wint"] = aa([128, 8, 8, 128], BF16)
        S["COS"] = aa([128, 8, 256], F32)
        S["SIN"] = aa([128, 8, 256], F32)
        S["X0"] = [aa([128, 8], F32) for i in range(2)]
        return S

    def s5_setup(S):
        wb_setup()
        P.cur_prio = 1
        for nm, dn in (("are", "s5_are"), ("aim", "s5_aim"), ("ldt", "s5_ldt"), ("dvec", "s5_d")):
            P.dma("sp", S[nm], S5D[dn][:, :], writes=[buf("s5" + nm)])
        X1 = aa([128, 8, 16], F32)
        X2 = aa([128, 8, 16], F32)
        CX1 = aa([128, 8, 16], F32)
        CX2 = aa([128, 8, 16], F32)
        P.dma("sp", X1, S5D["s5_x1"][:, :, :], writes=[buf("s5X1")])
        P.dma("sp", X2, S5D["s5_x2"][:, :, :], writes=[buf("s5X2")])
        P.dma("sp", CX1, S5D["s5_cx1"][:, :, :], writes=[buf("s5CX1")])
        P.dma("sp", CX2, S5D["s5_cx2"][:, :, :], writes=[buf("s5CX2")])
        sg1 = aa([128, 1], F32)
        sg2 = aa([128, 1], F32)
        dt = aa([128, 8], F32)
        lr = aa([128, 8], F32)
        th = aa([128, 8], F32)
        mlr = aa([128, 8, 9], F32)
        mth = aa([128, 8, 9], F32)
        mcol = aa([128, 9], F32)
        mag = aa([128, 8, 9], F32)
        sn = aa([128, 8, 9], F32)
        cs = aa([128, 8, 9], F32)
        AR = aa([128, 8, 9], F32)
        AI = aa([128, 8, 9], F32)
        t8a = aa([128, 8], F32)
        t8b = aa([128, 8], F32)
        t8c = aa([128, 8], F32)
        cr = aa([128, 8], F32)
        ci = aa([128, 8], F32)
        Bst = aa([128, 8, 16], F32)
        Bsw = aa([128, 8, 16], F32)
        tB = aa([128, 8, 16], F32)
        CA = aa([128, 9, 8, 16], F32)
        Bpad = aa([128, 8, 128], F32)
        ABp = aa([128, 128], F32)
        iot = aa([128, 256], F32)
        ang = aa([128, 256], F32)
        wk = [aa([128, 256], F32) for _ in range(4)]
        wki = aa([128, 256], mybir.dt.int32)
        bT = "s5tmp"

        def dv(fn, extra_r=(), extra_w=()):
            P.op("dve", fn, reads=[buf(bT)] + list(extra_r), writes=[buf(bT)] + list(extra_w))

        def sincos(ang_ap, sin_out, cos_out, n):
            y, kf, f, g_ = wk[0][:, 0:n], wk[1][:, 0:n], wk[2][:, 0:n], wk[3][:, 0:n]
            ki = wki[:, 0:n]
            dv(lambda e: e.tensor_scalar(out=y, in0=ang_ap, scalar1=1.0 / (2 * math.pi), scalar2=None, op0=ALU.mult))
            dv(lambda e: e.tensor_copy(out=ki, in_=y))
            dv(lambda e: e.tensor_copy(out=kf, in_=ki))
            dv(lambda e: e.tensor_tensor(out=f, in0=y, in1=kf, op=ALU.subtract))
            for shift, dst in ((0.0, sin_out), (0.25, cos_out)):
                if shift:
                    dv(lambda e: e.tensor_scalar(out=f, in0=f, scalar1=shift, scalar2=None, op0=ALU.add))
                dv(lambda e: e.tensor_scalar(out=g_, in0=f, scalar1=0.5, scalar2=None, op0=ALU.is_gt))
                dv(lambda e: e.tensor_tensor(out=f, in0=f, in1=g_, op=ALU.subtract))
                dv(lambda e: e.tensor_scalar(out=g_, in0=f, scalar1=-0.5, scalar2=None, op0=ALU.is_lt))
                dv(lambda e: e.tensor_tensor(out=f, in0=f, in1=g_, op=ALU.add))
                P.op("act", lambda e, dst=dst: e.activation(out=dst, in_=f, func=AF.Sin, scale=2 * math.pi),
                     reads=[buf(bT)], writes=[buf(bT)])

        rd_in = [buf("s5are"), buf("s5aim"), buf("s5ldt"), buf("s5dvec"), buf("s5X1"), buf("s5X2"),
                 buf("s5CX1"), buf("s5CX2"), buf("identf")]
        P.op("pool", lambda e: e.memset(sg1[0:64, :], 1.0), reads=rd_in, writes=[buf(bT)])
        P.op("pool", lambda e: e.memset(sg1[64:128, :], -1.0), writes=[buf(bT)])
        P.op("pool", lambda e: e.memset(sg2[0:64, :], -1.0), writes=[buf(bT)])
        P.op("pool", lambda e: e.memset(sg2[64:128, :], 1.0), writes=[buf(bT)])
        P.op("pool", lambda e: e.memset(S["PiT"], 0.0), writes=[buf(bT), buf("s5PiT")])
        P.op("pool", lambda e: e.memset(Bpad, 0.0), writes=[buf(bT)])
        P.op("pool", lambda e: e.memset(S["Wint"], 0.0), writes=[buf(bT), buf("s5Wint")])
        P.op("pool", lambda e: e.iota(mcol, pattern=[[1, 9]], base=0, channel_multiplier=0,
                                      allow_small_or_imprecise_dtypes=True), writes=[buf(bT)])
        P.op("pool", lambda e: e.iota(iot, pattern=[[1, 256]], base=1, channel_multiplier=0,
                                      allow_small_or_imprecise_dtypes=True), writes=[buf(bT)])
        dv(lambda e: e.tensor_copy(out=S["PiT"][0:64, 64:128], in_=identf[0:64, 0:64]), extra_w=[buf("s5PiT")])
        dv(lambda e: e.tensor_scalar(out=S["PiT"][64:128, 0:64], in0=identf[64:128, 64:128], scalar1=-1.0,
                                     scalar2=None, op0=ALU.mult), extra_w=[buf("s5PiT")])
        P.op("act", lambda e: e.activation(out=dt, in_=S["ldt"], func=AF.Exp), reads=[buf(bT)], writes=[buf(bT)])
        dv(lambda e: e.tensor_tensor(out=lr, in0=S["are"], in1=dt, op=ALU.mult))
        dv(lambda e: e.tensor_tensor(out=th, in0=S["aim"], in1=dt, op=ALU.mult))
        for g in range(8):
            dv(lambda e, g=g: e.tensor_scalar(out=mlr[:, g, :], in0=mcol, scalar1=lr[:, g:g + 1], scalar2=None,
                                              op0=ALU.mult))
            dv(lambda e, g=g: e.tensor_scalar(out=mth[:, g, :], in0=mcol, scalar1=th[:, g:g + 1], scalar2=None,
                                              op0=ALU.mult))
        fl = "p g m -> p (g m)"
        P.op("act", lambda e: e.activation(out=mag.rearrange(fl), in_=mlr.rearrange(fl), func=AF.Exp),
             reads=[buf(bT)], writes=[buf(bT)])
        sincos(mth.rearrange(fl), sn.rearrange(fl), cs.rearrange(fl), 72)
        dv(lambda e: e.tensor_tensor(out=AR.rearrange(fl), in0=mag.rearrange(fl), in1=cs.rearrange(fl), op=ALU.mult))
        dv(lambda e: e.tensor_tensor(out=AI.rearrange(fl), in0=mag.rearrange(fl), in1=sn.rearrange(fl), op=ALU.mult))
        dv(lambda e: e.tensor_copy(out=S["rho8"], in_=mag[:, :, 8]), extra_w=[buf("s5rho8")])
        dv(lambda e: e.tensor_copy(out=S["ar8"], in_=AR[:, :, 8]), extra_w=[buf("s5ar8")])
        dv(lambda e: e.tensor_copy(out=S["ai8"], in_=AI[:, :, 8]), extra_w=[buf("s5ai8")])
        dv(lambda e: e.tensor_scalar(out=t8a, in0=AR[:, :, 1], scalar1=-1.0, scalar2=None, op0=ALU.add))
        dv(lambda e: e.tensor_tensor(out=t8b, in0=S["are"], in1=S["are"], op=ALU.mult))
        dv(lambda e: e.tensor_tensor(out=t8c, in0=S["aim"], in1=S["aim"], op=ALU.mult))
        dv(lambda e: e.tensor_tensor(out=t8b, in0=t8b, in1=t8c, op=ALU.add))
        dv(lambda e: e.reciprocal(out=t8b, in_=t8b))
        dv(lambda e: e.tensor_tensor(out=cr, in0=t8a, in1=S["are"], op=ALU.mult))
        dv(lambda e: e.tensor_tensor(out=t8c, in0=AI[:, :, 1], in1=S["aim"], op=ALU.mult))
        dv(lambda e: e.tensor_tensor(out=cr, in0=cr, in1=t8c, op=ALU.add))
        dv(lambda e: e.tensor_tensor(out=cr, in0=cr, in1=t8b, op=ALU.mult))
        dv(lambda e: e.tensor_tensor(out=ci, in0=AI[:, :, 1], in1=S["are"], op=ALU.mult))
        dv(lambda e: e.tensor_tensor(out=t8c, in0=t8a, in1=S["aim"], op=ALU.mult))
        dv(lambda e: e.tensor_tensor(out=ci, in0=ci, in1=t8c, op=ALU.subtract))
        dv(lambda e: e.tensor_tensor(out=ci, in0=ci, in1=t8b, op=ALU.mult))
        dv(lambda e: e.tensor_scalar(out=X2.rearrange("p g h -> p (g h)"), in0=X2.rearrange("p g h -> p (g h)"),
                                     scalar1=sg2[:, 0:1], scalar2=None, op0=ALU.mult))
        dv(lambda e: e.tensor_scalar(out=CX1.rearrange("p g h -> p (g h)"), in0=CX1.rearrange("p g h -> p (g h)"),
                                     scalar1=sg1[:, 0:1], scalar2=None, op0=ALU.mult))
        for g in range(8):
            dv(lambda e, g=g: e.tensor_scalar(out=tB[:, g, :], in0=X2[:, g, :], scalar1=ci[:, g:g + 1], scalar2=None,
                                              op0=ALU.mult))
            dv(lambda e, g=g: e.scalar_tensor_tensor(out=Bst[:, g, :], in0=X1[:, g, :], scalar=cr[:, g:g + 1],
                                                     in1=tB[:, g, :], op0=ALU.mult, op1=ALU.add))
            dv(lambda e, g=g: e.tensor_scalar(out=tB[:, g, :], in0=X1[:, g, :], scalar1=ci[:, g:g + 1], scalar2=None,
                                              op0=ALU.mult))
            dv(lambda e, g=g: e.scalar_tensor_tensor(out=Bsw[:, g, :], in0=X2[:, g, :], scalar=cr[:, g:g + 1],
                                                     in1=tB[:, g, :], op0=ALU.mult, op1=ALU.subtract))
            dv(lambda e, g=g: e.tensor_copy(out=Bpad[:, g, 16 * g:16 * g + 16], in_=Bst[:, g, :]))
            for m in range(9):
                dv(lambda e, g=g, m=m: e.tensor_scalar(out=tB[:, g, :], in0=CX2[:, g, :], scalar1=AI[:, g, m:m + 1],
                                                       scalar2=None, op0=ALU.mult))
                dv(lambda e, g=g, m=m: e.scalar_tensor_tensor(out=CA[:, m, g, :], in0=CX1[:, g, :],
                                                              scalar=AR[:, g, m:m + 1], in1=tB[:, g, :],
                                                              op0=ALU.mult, op1=ALU.subtract))
        for j in range(8):
            for g in range(8):
                dv(lambda e, j=j, g=g: e.tensor_copy(out=S["Wint"][:, j, g, 16 * g:16 * g + 16], in_=CA[:, j + 1, g, :]),
                   extra_w=[buf("s5Wint")])
        for tau in range(8):
            for g in range(8):
                P.op("pe", lambda e, tau=tau, g=g: e.matmul(pM[:, 16 * g:16 * g + 16], lhsT=Bpad[:, g, :],
                                                            rhs=CA[:, tau, g, :], start=True, stop=True),
                     reads=[buf(bT)], writes=[buf("pM")])
            if tau == 0:
                P.op("dve", lambda e: e.scalar_tensor_tensor(out=S["Wfir"][:, 0, :], in0=identf, scalar=S["dvec"][:, 0:1],
                                                             in1=pM[:, 0:128], op0=ALU.mult, op1=ALU.add),
                     reads=[buf("pM"), buf(bT)], writes=[buf("s5Wfir")])
            else:
                P.op("dve", lambda e, tau=tau: e.tensor_copy(out=S["Wfir"][:, tau, :], in_=pM[:, 0:128]),
                     reads=[buf("pM")], writes=[buf("s5Wfir")])
        for s in range(8):
            m = 7 - s
            for g in range(8):
                dv(lambda e, g=g, m=m: e.tensor_scalar(out=tB[:, g, :], in0=Bsw[:, g, :], scalar1=AI[:, g, m:m + 1],
                                                       scalar2=None, op0=ALU.mult))
                dv(lambda e, g=g: e.memset(ABp, 0.0))
                dv(lambda e, g=g, m=m: e.scalar_tensor_tensor(out=ABp[:, 16 * g:16 * g + 16], in0=Bst[:, g, :],
                                                              scalar=AR[:, g, m:m + 1], in1=tB[:, g, :],
                                                              op0=ALU.mult, op1=ALU.add))
                P.op("pe", lambda e: e.transpose(out=pM[:, 0:128], in_=ABp, identity=identf),
                     reads=[buf(bT), buf("identf")], writes=[buf("pM")])
                P.op("dve", lambda e, s=s, g=g: e.tensor_copy(out=S["Wst"][:, s, g, :], in_=pM[:, 0:128]),
                     reads=[buf("pM")], writes=[buf("s5Wst"), buf(bT)])
        for g in range(8):
            dv(lambda e, g=g: e.tensor_scalar(out=ang, in0=iot, scalar1=mth[:, g, 8:9], scalar2=None, op0=ALU.mult))
            sincos(ang, S["SIN"][:, g, :], S["COS"][:, g, :], 256)
        P.op("dve", lambda e: e.tensor_copy(out=wk[0], in_=wk[0]), reads=[buf(bT)],
             writes=[buf(bT), buf("s5SIN"), buf("s5COS")])
        P.op("dve", lambda e: e.memset(S["X0"][0], 0.0), writes=[buf("s5X00")])
        P.cur_prio = 0
        return S

    def s5_alloc_work():
        Wk = {}
        Wk["Ssb"] = aa([128, 256], F32)
        Wk["t1"] = aa([128, 256], F32)
        Wk["t2"] = aa([128, 256], F32)
        Wk["Wsc"] = aa([128, 256], F32)
        Wk["Xn"] = aa([128, 256], F32)
        Wk["Xin"] = aa([128, 8, 256], BF16)
        for nm in ("S4", "T1", "T2"):
            Wk[nm] = aa([128, 4, 256], F32)
        Wk["W4"] = Wk["S4"]
        Wk["X4"] = Wk["T2"]
        Wk["ysb"] = aa([128, 2, 256], F32)
        Wk["g1"] = aa([128, 2, 256], F32)
        Wk["g2"] = aa([128, 2, 256], F32)
        Wk["zfull"] = aa([128, 2048], BF16)
        assert ar_off[0] >= setup_extent[0], (ar_off[0], setup_extent[0])
        Wk["ub"] = aa([128, 8, 256], BF16)
        return Wk

    GC = math.sqrt(2.0 / math.pi)
    YBANK = [zb(2, 0), zb(2, 1), zb(3, 0), zb(3, 1)]
    YBUF = ["pZ20", "pZ21", "pZ30", "pZ31"]

    def s5_fir_inter(S, Wk, nc_, dst_ap, dst_buf):
        ub, Xin, zfull = Wk["ub"], Wk["Xin"], Wk["zfull"]
        zv = zfull.rearrange("p (c j) -> p j c", j=8)
        for j in range(8):
            bk = j // 2
            Yj = YBANK[bk][:, (j % 2) * 256:(j % 2) * 256 + nc_]
            for tau in range(j + 1):
                P.op("pe", lambda e, Yj=Yj, tau=tau, j=j: e.matmul(Yj, lhsT=S["Wfir"][:, tau, :], rhs=ub[:, j - tau, 0:nc_],
                                                                 start=(tau == 0), stop=False),
                     reads=[buf("s5ub"), buf("s5Wfir")], writes=[buf(YBUF[bk])])
            for g in range(8):
                P.op("pe", lambda e, Yj=Yj, j=j, g=g: e.matmul(Yj, lhsT=S["Wint"][:, j, g, :], rhs=Xin[:, g, 0:nc_],
                                                             start=False, stop=(g == 7)),
                     reads=[buf("s5Xin"), buf("s5Wint")], writes=[buf(YBUF[bk])])
            if j % 2 == 1:
                Yb = YBANK[bk].rearrange("p (j c) -> p j c", j=2)[:, :, 0:nc_]
                ysb, g1, g2 = Wk["ysb"][:, :, 0:nc_], Wk["g1"][:, :, 0:nc_], Wk["g2"][:, :, 0:nc_]
                P.op("act", lambda e, Yb=Yb: e.activation(out=ysb, in_=Yb, func=AF.Copy),
                     reads=[buf(YBUF[bk])], writes=[buf("s5ysb")])
                P.op("dve", lambda e: e.tensor_tensor(out=g1, in0=ysb, in1=ysb, op=ALU.mult),
                     reads=[buf("s5ysb")], writes=[buf("s5g1")])
                P.op("dve", lambda e: e.tensor_scalar(out=g1, in0=g1, scalar1=0.044715, scalar2=1.0, op0=ALU.mult,
                                                      op1=ALU.add), reads=[buf("s5g1")], writes=[buf("s5g1")])
                P.op("dve", lambda e: e.tensor_tensor(out=g1, in0=g1, in1=ysb, op=ALU.mult),
                     reads=[buf("s5g1"), buf("s5ysb")], writes=[buf("s5g1")])
                P.op("dve", lambda e: e.tensor_scalar(out=g1, in0=g1, scalar1=-18.0, scalar2=None, op0=ALU.max),
                     reads=[buf("s5g1")], writes=[buf("s5g1")])
                P.op("act", lambda e: e.activation(out=g2, in_=g1, func=AF.Exp, scale=-2.0 * GC),
                     reads=[buf("s5g1")], writes=[buf("s5g2")])
                P.op("act", lambda e: e.activation(out=g2, in_=g2, func=AF.Ln, bias=1.0, scale=1.0),
                     reads=[buf("s5g2")], writes=[buf("s5g2")])
                P.op("act", lambda e: e.activation(out=g2, in_=g2, func=AF.Exp, scale=-1.0),
                     reads=[buf("s5g2")], writes=[buf("s5g2")])
                P.op("dve", lambda e, j=j: e.tensor_tensor(out=zv[:, j - 1:j + 1, 0:nc_], in0=ysb, in1=g2, op=ALU.mult),
                     reads=[buf("s5ysb"), buf("s5g2")], writes=[buf("s5zfull")])
        P.dma("pool", dst_ap, zfull[:, 0:nc_ * 8], reads=[buf("s5zfull")], writes=[dst_buf])

    def s5_states(S, Wk, nc_, g, dstps, dstbuf):
        for s in range(8):
            P.op("pe", lambda e, s=s: e.matmul(dstps[:, 0:nc_], lhsT=S["Wst"][:, s, g, :], rhs=Wk["ub"][:, s, 0:nc_],
                                                 start=(s == 0), stop=(s == 7)),
                 reads=[buf("s5ub"), buf("s5Wst")], writes=[buf(dstbuf)])

    def s5_supertile(S, Wk, sti, ntok, tok0, final_k=None):
        nc_ = ntok // 8
        X0 = S["X0"][sti % 2]
        X0n = S["X0"][(sti + 1) % 2]
        bX0, bX0n = buf(f"s5X0{sti % 2}"), buf(f"s5X0{(sti + 1) % 2}")
        Xin = Wk["Xin"]
        S4, T1, T2, W4, X4 = (Wk[k] for k in ("S4", "T1", "T2", "W4", "X4"))
        ZA = Z[0].rearrange("p (g c) -> p g c", g=4)
        ZB = Z[1].rearrange("p (g c) -> p g c", g=4)
        bZA = [buf("pZ00"), buf("pZ01")]
        bZB = [buf("pZ10"), buf("pZ11")]
        for hf in range(2):
            gs = slice(4 * hf, 4 * hf + 4)
            for gl in range(4):
                g = 4 * hf + gl
                for s in range(8):
                    P.op("pe", lambda e, s=s, g=g, gl=gl: e.matmul(ZA[:, gl, 0:nc_], lhsT=S["Wst"][:, s, g, :],
                                                                   rhs=Wk["ub"][:, s, 0:nc_], start=(s == 0), stop=(s == 7)),
                         reads=[buf("s5ub"), buf("s5Wst")], writes=[bZA[gl // 2]])
            P.op("dve", lambda e: e.tensor_copy(out=S4[:, :, 0:nc_], in_=ZA[:, :, 0:nc_]), reads=bZA, writes=[buf("s5S4")],
                 deps=(setup_done if (sti == 0 and hf == 0) else ()))
            for gl in range(4):
                P.op("pe", lambda e, gl=gl: e.matmul(ZB[:, gl, 0:nc_], lhsT=S["PiT"], rhs=S4[:, gl, 0:nc_], start=True, stop=True),
                     reads=[buf("s5S4"), buf("s5PiT")], writes=[bZB[gl // 2]])
            P.op("dve", lambda e, gs=gs: e.tensor_tensor(out=T1[:, :, 0:nc_], in0=S["SIN"][:, gs, 0:nc_], in1=ZB[:, :, 0:nc_],
                                                         op=ALU.mult), reads=bZB + [buf("s5SIN")], writes=[buf("s5T1")])
            P.op("dve", lambda e, gs=gs: e.tensor_tensor(out=T2[:, :, 0:nc_], in0=S["COS"][:, gs, 0:nc_], in1=S4[:, :, 0:nc_],
                                                         op=ALU.mult), reads=[buf("s5S4"), buf("s5COS")], writes=[buf("s5T2")])
            P.op("dve", lambda e: e.tensor_tensor(out=T2[:, :, 0:nc_], in0=T2[:, :, 0:nc_], in1=T1[:, :, 0:nc_], op=ALU.subtract),
                 reads=[buf("s5T1"), buf("s5T2")], writes=[buf("s5T2")])
            for gl in range(4):
                g = 4 * hf + gl
                P.op("dve", lambda e, g=g, gl=gl: e.tensor_tensor_scan(
                    out=W4[:, gl, 0:nc_], data0=S["rho8"][:, g:g + 1].to_broadcast([128, nc_]), data1=T2[:, gl, 0:nc_],
                    initial=X0[:, g:g + 1], op0=ALU.mult, op1=ALU.add),
                    reads=[buf("s5T2"), buf("s5rho8"), bX0], writes=[buf("s5S4")])
            for gl in range(4):
                P.op("pe", lambda e, gl=gl: e.matmul(ZB[:, gl, 0:nc_], lhsT=S["PiT"], rhs=W4[:, gl, 0:nc_], start=True, stop=True),
                     reads=[buf("s5S4"), buf("s5PiT")], writes=[bZB[gl // 2]])
            P.op("dve", lambda e, gs=gs: e.tensor_tensor(out=T1[:, :, 0:nc_], in0=S["SIN"][:, gs, 0:nc_], in1=ZB[:, :, 0:nc_],
                                                         op=ALU.mult), reads=bZB + [buf("s5SIN")], writes=[buf("s5T1")])
            P.op("dve", lambda e, gs=gs: e.tensor_tensor(out=X4[:, :, 0:nc_], in0=S["COS"][:, gs, 0:nc_], in1=W4[:, :, 0:nc_],
                                                         op=ALU.mult), reads=[buf("s5S4"), buf("s5COS")], writes=[buf("s5T2")])
            P.op("dve", lambda e: e.tensor_tensor(out=X4[:, :, 0:nc_], in0=X4[:, :, 0:nc_], in1=T1[:, :, 0:nc_], op=ALU.add),
                 reads=[buf("s5T1"), buf("s5T2")], writes=[buf("s5T2")])
            P.op("dve", lambda e, gs=gs: e.tensor_copy(out=Xin[:, gs, 0:1], in_=X0[:, gs].rearrange("p (g o) -> p g o", o=1)),
                 reads=[bX0], writes=[buf("s5Xin")])
            if nc_ > 1:
                P.op("dve", lambda e, gs=gs: e.tensor_copy(out=Xin[:, gs, 1:nc_], in_=X4[:, :, 0:nc_ - 1]),
                     reads=[buf("s5T2")], writes=[buf("s5Xin")])
            P.op("dve", lambda e, gs=gs: e.tensor_copy(out=X0n[:, gs].rearrange("p (g o) -> p g o", o=1), in_=X4[:, :, nc_ - 1:nc_]),
                 reads=[buf("s5T2")], writes=[bX0n])
            if final_k is not None:
                P.op("dve", lambda e, gs=gs: e.tensor_copy(out=S5FIN[:, gs].rearrange("p (g o) -> p g o", o=1),
                                                           in_=X4[:, :, final_k - 1:final_k]),
                     reads=[buf("s5T2")], writes=[buf("s5fin")])
        s5_fir_inter(S, Wk, nc_, zin_ap(tok0, ntok), buf(f"zinP{tok0 // PZ}"))

    def s5_sample(S, Wk):
        nc_ = 32
        Ssb, t1, t2, Xn, Xin = (Wk[k] for k in ("Ssb", "t1", "t2", "Xn", "Xin"))
        pA, pB, bA, bB = zb(1, 0), zb(1, 1), "pZ10", "pZ11"
        P.dma("sp", S5X0, s5x0_d[:, :, :], writes=[buf("s5x0s")])
        Sv = Ssb[:, 0:32].rearrange("p (q c) -> p q c", c=2)
        Xv = Xin[:, :, 0:32].rearrange("p g (q c) -> p g q c", c=2)
        for g in range(8):
            s5_states(S, Wk, nc_, g, pA, bA)
            P.op("dve", lambda e: e.tensor_copy(out=Ssb[:, 0:nc_], in_=pA[:, 0:nc_]), reads=[buf(bA)], writes=[buf("s5Ssb")])
            Z0 = S5X0[:, g, :]
            X1 = Xn[:, 0:16]
            Fn = Xn[:, 16:32]
            P.op("pe", lambda e, Z0=Z0: e.matmul(pB[:, 0:16], lhsT=S["PiT"], rhs=Z0, start=True, stop=True),
                 reads=[buf("s5x0s"), buf("s5PiT")], writes=[buf(bB)])
            P.op("dve", lambda e, g=g: e.tensor_scalar(out=t1[:, 0:16], in0=pB[:, 0:16], scalar1=S["ai8"][:, g:g + 1], scalar2=None,
                                                       op0=ALU.mult), reads=[buf(bB), buf("s5ai8")], writes=[buf("s5t1")])
            P.op("dve", lambda e, g=g, Z0=Z0: e.scalar_tensor_tensor(out=X1, in0=Z0, scalar=S["ar8"][:, g:g + 1], in1=t1[:, 0:16],
                                                                      op0=ALU.mult, op1=ALU.add),
                 reads=[buf("s5x0s"), buf("s5t1"), buf("s5ar8")], writes=[buf("s5Xn")])
            P.op("dve", lambda e: e.tensor_tensor(out=X1, in0=X1, in1=Sv[:, :, 0], op=ALU.add),
                 reads=[buf("s5Xn"), buf("s5Ssb")], writes=[buf("s5Xn")])
            P.op("pe", lambda e: e.matmul(pB[:, 0:16], lhsT=S["PiT"], rhs=X1, start=True, stop=True),
                 reads=[buf("s5Xn"), buf("s5PiT")], writes=[buf(bB)])
            P.op("dve", lambda e, g=g: e.tensor_scalar(out=t1[:, 0:16], in0=pB[:, 0:16], scalar1=S["ai8"][:, g:g + 1], scalar2=None,
                                                       op0=ALU.mult), reads=[buf(bB), buf("s5ai8")], writes=[buf("s5t1")])
            P.op("dve", lambda e, g=g: e.scalar_tensor_tensor(out=Fn, in0=X1, scalar=S["ar8"][:, g:g + 1], in1=t1[:, 0:16],
                                                              op0=ALU.mult, op1=ALU.add),
                 reads=[buf("s5Xn"), buf("s5t1"), buf("s5ar8")], writes=[buf("s5Xn")])
            P.op("dve", lambda e, g=g: e.tensor_tensor(out=S5FINS[:, g, :], in0=Fn, in1=Sv[:, :, 1], op=ALU.add),
                 reads=[buf("s5Xn"), buf("s5Ssb")], writes=[buf("s5fins")])
            P.op("dve", lambda e, g=g, Z0=Z0: e.tensor_copy(out=Xv[:, g, :, 0], in_=Z0), reads=[buf("s5x0s")], writes=[buf("s5Xin")])
            P.op("dve", lambda e, g=g: e.tensor_copy(out=Xv[:, g, :, 1], in_=X1), reads=[buf("s5Xn")], writes=[buf("s5Xin")])
        s5_fir_inter(S, Wk, nc_, zin_ap(TP, 256), buf(f"zinP{TP // PZ}"))

    GROUPS = [[0, 1, 2, 3], [4, 5, 6, 7]]

    def ag1(pi):
        P.coll(lambda e: e.collective_compute("AllGather", ALU.bypass, replica_groups=GROUPS,
                                              ins=[zin_p[pi]], outs=[zall_p[pi]]),
               reads=[buf(f"zinP{pi}")], writes=[buf(f"zallP{pi}")])

    def phase_a_tile(ti, sample=False):
        NS = 2 if sample else 4
        N = NS * 128
        t0 = TP if sample else ti * TW
        j0 = NBLK if sample else ti * 4
        par = ti % 2
        xsrc = xs_d if sample else x_d
        xrow0 = 0 if sample else t0
        ko, vo, lo = (ks_o, vs_o, lfs_o) if sample else (k_o, v_o, lf_o)
        orow0 = 0 if sample else t0
        XT = xT[par]
        bXT = buf(f"xT{par}")
        for s in range(NS):
            slot = (ti * 4 + s) % XR
            bx = buf(f"xr{slot}")
            P.dma("sp", xr[slot], xsrc[xrow0 + s * 128:xrow0 + (s + 1) * 128, :], writes=[bx])
            P.op("act", lambda e, slot=slot, s=s: e.activation(
                out=junk, in_=xr[slot], func=AF.Square, accum_out=ss[:, s:s + 1]),
                reads=[bx], writes=[buf("junk"), buf("ss")], fuse=False, cost=1.2)
        P.op("act", lambda e: e.activation(out=lnt[:, 0:NS], in_=ss[:, 0:NS], func=AF.Ln, bias=EPS, scale=1.0 / D_MODEL),
             reads=[buf("ss")], writes=[buf("lnt")])
        RS = rstd[par]
        bRS = buf(f"rstd{par}")
        P.op("act", lambda e: e.activation(out=RS[:, 0:NS], in_=lnt[:, 0:NS], func=AF.Exp, scale=-0.5),
             reads=[buf("lnt")], writes=[bRS])
        for s in range(NS):
            slot = (ti * 4 + s) % XR
            bx = buf(f"xr{slot}")
            xp = s % 2
            P.op("dve", lambda e, slot=slot, s=s, xp=xp: e.tensor_scalar(
                out=xs[xp], in0=xr[slot], scalar1=RS[:, s:s + 1], scalar2=None, op0=ALU.mult),
                reads=[bx, bRS], writes=[buf(f"xs{xp}")])
            for kc in range(8):
                P.op("pe", lambda e, xp=xp, kc=kc: e.transpose(
                    out=pT[xp][:, kc, :], in_=xs[xp][:, kc * 128:(kc + 1) * 128], identity=identb),
                    reads=[buf(f"xs{xp}"), buf("identb")], writes=[buf(f"pT{xp}")])
            if s % 2 == 0:
                P.op("dve", lambda e, xp=xp, s=s: e.tensor_copy(out=XT[:, :, s * 128:(s + 1) * 128], in_=pT[xp]),
                     reads=[buf(f"pT{xp}")], writes=[bXT])
            else:
                P.op("act", lambda e, xp=xp, s=s: e.activation(
                    out=XT[:, :, s * 128:(s + 1) * 128], in_=pT[xp], func=AF.Copy),
                    reads=[buf(f"pT{xp}")], writes=[bXT])
            yield "front"
        yield "FRONT_DONE"

        pp_i = [0]

        def proj(col0):
            i = pp_i[0] % 2
            pp_i[0] += 1
            for kc in range(8):
                P.op("pe", lambda e, kc=kc, i=i: e.matmul(
                    pP[i][:, 0:N], lhsT=Wb[:, kc, col0:col0 + 128], rhs=XT[:, kc, 0:N], start=(kc == 0), stop=(kc == 7)),
                    reads=[bXT, buf("Wb")], writes=[buf(f"pP{i}")])
            return i

        def headnorm2(items):
            for k_, (i, gain, gain_buf, out_ap, out_buf) in enumerate(items):
                P.op("act", lambda e, i=i, k_=k_: e.activation(out=sq2[k_][:, 0:N], in_=pP[i][:, 0:N], func=AF.Square),
                     reads=[buf(f"pP{i}")], writes=[buf(SQN[k_])])
            for k_, (i, gain, gain_buf, out_ap, out_buf) in enumerate(items):
                P.op("pe", lambda e, k_=k_: e.matmul(pMs[k_][:, 0:N], lhsT=BO, rhs=sq2[k_][:, 0:N], start=True, stop=True),
                     reads=[buf(SQN[k_]), buf("BO")], writes=[buf(pMn[k_])])
            for k_, (i, gain, gain_buf, out_ap, out_buf) in enumerate(items):
                P.op("act", lambda e, k_=k_: e.activation(out=ln2[k_][:, 0:N], in_=pMs[k_][:, 0:N], func=AF.Ln, bias=EPS, scale=1.0),
                     reads=[buf(pMn[k_])], writes=[buf(LNN[k_])])
            for k_, (i, gain, gain_buf, out_ap, out_buf) in enumerate(items):
                P.op("act", lambda e, k_=k_: e.activation(out=rr2[k_][:, 0:N], in_=ln2[k_][:, 0:N], func=AF.Exp, scale=-0.5),
                     reads=[buf(LNN[k_])], writes=[buf(RRN[k_])])
            for k_, (i, gain, gain_buf, out_ap, out_buf) in enumerate(items):
                P.op("dve", lambda e, i=i, k_=k_, gain=gain, out_ap=out_ap: e.scalar_tensor_tensor(
                    out=out_ap, in0=pP[i][:, 0:N], scalar=gain, in1=rr2[k_][:, 0:N], op0=ALU.mult, op1=ALU.mult),
                    reads=[buf(f"pP{i}"), buf(RRN[k_]), gain_buf], writes=[out_buf])

        sq2, ln2, rr2 = [sq, sqB], [lnb, lnbB], [rr, rrB]
        SQN, LNN, RRN = ["sq0", "sgb0"], ["lnb0", "kst"], ["rr0", "vst"]
        pMs, pMn = [pM, pK.rearrange("p s f -> p (s f)")], ["pM", "pK"]
        iq = proj(0)
        ik = proj(128)
        headnorm2([(iq, qg8[:, 0:1], buf("qg8"), qnb[:, 0:N], buf("qnb")),
                   (ik, kg[:, 0:1], buf("kg"), knf[:, 0:N], buf("knf"))])
        for h in range(2):
            P.dma("pool", qt_d[h, 0:64, t0:t0 + N], qnb[h * 64:(h + 1) * 64, 0:N],
                  reads=[buf("qnb")], writes=[buf(f"qt_d{ti}")])
        yield "back"
        P.op("act", lambda e: e.activation(out=knb[:, 0:N], in_=knf[:, 0:N], func=AF.Copy),
             reads=[buf("knf")], writes=[buf("knb")])
        for h in range(2):
            P.dma("pool", kt_d[h, :, t0:t0 + N], knb[h * 64:(h + 1) * 64, 0:N],
                  reads=[buf("knb")], writes=[buf(f"kt_d{ti}")])
        for s in range(NS):
            P.op("pe", lambda e, s=s: e.transpose(out=pK[:, s, :], in_=knf[:, s * 128:(s + 1) * 128], identity=identf),
                 reads=[buf("knf"), buf("identf")], writes=[buf("pK")])
        P.op("dve", lambda e: e.tensor_copy(out=kst[:, 0:NS, :], in_=pK[:, 0:NS, :]), reads=[buf("pK")], writes=[buf("kst")])
        out_dmas.append(P.dma("pool", ko[orow0:orow0 + N, :].rearrange("(s p) f -> p s f", p=128), kst[:, 0:NS, :],
                              reads=[buf("kst")]))
        yield "back"
        for s in range(NS):
            for kc in range(8):
                P.op("pe", lambda e, s=s, kc=kc: e.matmul(
                    pV[:, s, :], lhsT=XT[:, kc, s * 128:(s + 1) * 128], rhs=Wb[:, kc, 256:384],
                    start=(kc == 0), stop=(kc == 7)),
                    reads=[bXT, buf("Wb")], writes=[buf("pV")])
        for s in range(NS):
            for kc in range(8):
                P.op("pe", lambda e, s=s, kc=kc: e.matmul(
                    pS[:, 2 * s:2 * s + 2], lhsT=XT[:, kc, s * 128:(s + 1) * 128], rhs=Wb[:, kc, 768:770],
                    start=(kc == 0), stop=(kc == 7)),
                    reads=[bXT, buf("Wb")], writes=[buf("pS")])
        P.op("act", lambda e: e.activation(out=vst[:, 0:NS, :], in_=pV[:, 0:NS, :], func=AF.Copy),
             reads=[buf("pV")], writes=[buf("vst")])
        out_dmas.append(P.dma("pool", vo[orow0:orow0 + N, :].rearrange("(s p) f -> p s f", p=128), vst[:, 0:NS, :],
                              reads=[buf("vst")]))
        P.op("dve", lambda e: e.tensor_copy(
            out=VP[:, j0:j0 + NS, :, 0:64], in_=pV[:, 0:NS, :].rearrange("p s (h d) -> p s h d", h=2)),
            reads=[buf("pV")], writes=[buf("VP")])
        yield "back"
        pSv = pS[:, 0:2 * NS].rearrange("p (s h) -> p s h", h=2)
        for h in range(2):
            P.op("act", lambda e, h=h: e.activation(
                out=ef[:, 0:NS, h], in_=pSv[:, :, h], func=AF.Exp, bias=nbf[:, h:h + 1], scale=-1.0),
                reads=[buf("pS"), buf("nbf")], writes=[buf("ef")])
        P.op("act", lambda e: e.activation(out=lf[:, 0:NS, :], in_=ef[:, 0:NS, :], func=AF.Ln, bias=1.0, scale=1.0),
             reads=[buf("ef")], writes=[buf("lf")])
        P.op("dve", lambda e: e.tensor_scalar(out=lf[:, 0:NS, :], in0=lf[:, 0:NS, :], scalar1=-1.0, scalar2=None, op0=ALU.mult),
             reads=[buf("lf")], writes=[buf("lf")])
        out_dmas.append(P.dma("pool", lo[orow0:orow0 + N, :].rearrange("(s p) h -> p s h", p=128), lf[:, 0:NS, :],
                              reads=[buf("lf")]))
        yield "back"
        for s in range(NS):
            P.op("pe", lambda e, s=s: e.transpose(out=pM[0:2, s * 128:(s + 1) * 128], in_=lf[:, s, :],
                                                   identity=identf),
                 reads=[buf("lf"), buf("identf")], writes=[buf("pM")])
        CR = cumrow[par]
        CRp = cumrow[1 - par]
        init = 0.0 if (ti == 0 or sample) else CRp[:, TW - 1:TW]
        d0 = segm if sample else ones2
        P.op("dve", lambda e: e.tensor_tensor_scan(
            out=CR[:, 0:N], data0=d0[:, 0:N], data1=pM[0:2, 0:N], initial=init, op0=ALU.mult, op1=ALU.add),
            reads=[buf("pM"), buf("ones2"), buf("segm"), buf(f"cumrow{1 - par}")], writes=[buf(f"cumrow{par}")])
        P.op("dve", lambda e: e.tensor_copy(out=cumb[:, 0:N], in_=CR[:, 0:N]),
             reads=[buf(f"cumrow{par}")], writes=[buf("cumb")])
        for h in range(2):
            P.dma("pool", qt_d[h, 64:65, t0:t0 + N], cumb[h:h + 1, 0:N],
                  reads=[buf("cumb")], writes=[buf(f"qt_d{ti}")])
        for s in range(NS):
            P.op("pe", lambda e, s=s: e.transpose(out=pS[:, 16 + 2 * s:16 + 2 * s + 2],
                                                   in_=CR[:, s * 128:(s + 1) * 128], identity=identf[0:2, 0:2]),
                 reads=[buf(f"cumrow{par}"), buf("identf")], writes=[buf("pS")])
        P.op("dve", lambda e: e.tensor_scalar(
            out=NCK[:, j0:j0 + NS, :], in0=pS[:, 16:16 + 2 * NS].rearrange("p (s h) -> p s h", h=2),
            scalar1=-1.0, scalar2=None, op0=ALU.mult),
            reads=[buf("pS")], writes=[buf("NCK")])
        yield "back"
        iga = proj(384)
        igs = proj(640)
        silu2_from_psum([(iga, sgb[0], buf("sgb0")), (igs, sgb[1], buf("sgb1"))], N,
                        [(lnb, "lnb0", rr, "rr0"), (lnbB, "kst", rrB, "vst")])
        P.dma("pool", sgs_d[:, t0:t0 + N], sgb[1][:, 0:N], reads=[buf("sgb1")], writes=[buf(f"sgs_d{ti}")])
        for h in range(2):
            P.dma("pool", sga_d[h, :, t0:t0 + N], sgb[0][h * 64:(h + 1) * 64, 0:N],
                  reads=[buf("sgb0")], writes=[buf(f"sga_d{ti}")])
        yield "back"
        i = proj(512)
        uo = 0 if sample else (ti % 4) * 64
        P.op("act", lambda e, i=i: e.activation(out=WK["ub"][:, :, uo:uo + N // 8],
                                                in_=pP[i][:, 0:N].rearrange("p (c j) -> p j c", j=8), func=AF.Copy),
             reads=[buf(f"pP{i}")], writes=[buf("s5ub")])
        yield "back"
        if sample:
            P.cur_deps = tuple(setup_done)
            s5_sample(S5S, WK)
            P.cur_deps = ()
            if "X" in stages:
                ag1(TP // PZ)
        elif ti % 4 == 3 or ti == NTILE - 1:
            sti = ti // 4
            tok0 = sti * 2048
            ntok = t0 + TW - tok0
            tf = min(L_REAL, TP)
            fk = None
            if (tf - 1) // 2048 == sti:
                fk = (tf - tok0) // 8
            if sti == 0:
                P.cur_deps = tuple(setup_done)
            s5_supertile(S5S, WK, sti, ntok, tok0, fk)
            P.cur_deps = ()
            if "X" in stages and (tok0 + ntok) % PZ == 0:
                ag1(tok0 // PZ)

    if "A" in stages:
        S5S = s5_alloc_weights()
        XR = 4
        xr = [aa([128, D_MODEL], F32) for i in range(XR)]
        junk = aa([128, D_MODEL], BF16)
        xs = [aa([128, D_MODEL], BF16) for i in range(2)]
        xT = [aa([128, 8, TW], BF16) for i in range(2)]
        ss = aa([128, 4], F32)
        lnt = aa([128, 4], F32)
        rstd = [aa([128, 4], F32) for i in range(2)]
        sq = aa([128, TW], BF16)
        lnb = aa([128, TW], F32)
        rr = aa([128, TW], F32)
        qnb = aa([128, TW], BF16)
        knf = aa([128, TW], F32)
        knb = aa([128, TW], BF16)
        kst = aa([128, 4, 128], F32)
        vst = aa([128, 4, 128], F32)
        ef = aa([128, 4, 2], F32)
        lf = aa([128, 4, 2], F32)
        cumb = aa([2, TW], BF16)
        sgb = [aa([128, TW], BF16) for i in range(2)]
        sqB = sgb[0]
        lnbB = kst.rearrange("p s f -> p (s f)")
        rrB = vst.rearrange("p s f -> p (s f)")

        mark2 = ar_off[0]
        s5_setup(S5S)
        setup_extent[0] = ar_off[0]
        setup_done = [buf("s5tmp").w, buf("Wb").w]
        ar_off[0] = mark2
        WK = s5_alloc_work()
        gens = [phase_a_tile(ti) for ti in range(NTILE)] + [phase_a_tile(NTILE, sample=True)]
        front_done = [False] * len(gens)

        def step(gi):
            try:
                r = next(gens[gi])
            except StopIteration:
                return False
            if r == "FRONT_DONE":
                front_done[gi] = True
            return True

        while not front_done[0]:
            step(0)
        for gi in range(len(gens)):
            alive = True
            while alive:
                alive = step(gi)
                if gi + 1 < len(gens) and not front_done[gi + 1]:
                    step(gi + 1)
            if gi + 1 < len(gens):
                while not front_done[gi + 1]:
                    step(gi + 1)
        out_dmas.append(P.dma("sp", s5fin_o[:, :], S5FIN, reads=[buf("s5fin")]))
        out_dmas.append(P.dma("sp", s5fins_o[:, :, :], S5FINS, reads=[buf("s5fins")]))

    n_real = min(L_REAL, TP)
    qbs = []
    q0 = 0
    while q0 < n_real:
        ql = min(QB, n_real - q0)
        qbs.append((q0, ql))
        q0 += ql

    def tiles_of(a, b):
        return range(a // TW, (b + TW - 1) // TW)

    step = [0]

    def attend(qi, h, q0, qlen):
        par = qi % 2
        nkb = (q0 + qlen + 127) // 128
        halves = [(a, min(a + 512, qlen)) for a in range(0, qlen, 512)]
        pO = Z[2]
        bQ = buf(f"QA{par}{h}")
        bKT = buf(f"KT{h}")
        base = step[0]
        step[0] += nkb

        def clo(j):
            return max(0, 128 * j - q0)

        def zbufs(zi, lo, hi):
            return [buf(f"pZ{zi}{hf}") for hf in range(2) if lo < (hf + 1) * 512 and hi > hf * 512]

        def qk(j):
            zi = (base + j) % 2
            c_lo = clo(j)
            for (a, b) in halves:
                lo = max(a, c_lo)
                if lo >= b:
                    continue
                diag = (128 * j >= q0) and (a <= c_lo < b)
                P.op("pe", lambda e, zi=zi, lo=lo, b=b, diag=diag: e.matmul(
                    Z[zi][:, lo:b], lhsT=KT[h][:, 128 * j:128 * j + 128], rhs=QA[par][h][:, lo:b],
                    start=True, stop=not diag),
                    reads=[bKT, bQ], writes=zbufs(zi, lo, b))
                if diag:
                    w = min(128, qlen - c_lo)
                    P.op("pe", lambda e, zi=zi, c_lo=c_lo, w=w: e.matmul(
                        Z[zi][:, c_lo:c_lo + w], lhsT=identb, rhs=MN[:, 0:w], start=False, stop=True),
                        reads=[buf("identb"), buf("MN")], writes=zbufs(zi, c_lo, c_lo + w))

        def ex(j):
            zi = (base + j) % 2
            pi = (base + j) % 3
            c_lo = clo(j)
            P.op("act", lambda e, zi=zi, pi=pi, c_lo=c_lo: e.activation(
                out=PT[pi][:, c_lo:qlen], in_=Z[zi][:, c_lo:qlen], func=AF.Exp,
                bias=NCK[:, j, h:h + 1], scale=1.0),
                reads=zbufs(zi, c_lo, qlen) + [buf("NCK")], writes=[buf(f"PT{pi}")], cost=0.25 + (qlen - c_lo) / 1200.0)

        def pv(j):
            pi = (base + j) % 3
            c_lo = clo(j)
            for (a, b) in halves:
                lo = max(a, c_lo)
                if lo >= b:
                    continue
                j_last = min(nkb - 1, (q0 + b - 1) // 128)
                P.op("pe", lambda e, pi=pi, lo=lo, b=b, j_last=j_last: e.matmul(
                    pO[0:65, lo:b], lhsT=VP[:, j, h, :], rhs=PT[pi][:, lo:b],
                    start=(j == 0), stop=(j == j_last)),
                    reads=[buf("VP"), buf(f"PT{pi}")], writes=zbufs(2, lo, b))

        qk(0)
        for j in range(nkb):
            if j + 1 < nkb:
                qk(j + 1)
            ex(j)
            pv(j)
        ob = osb[h]
        bob = buf(f"osb{h}")
        P.op("dve", lambda e: e.tensor_copy(out=ob[:, 0:qlen], in_=pO[0:65, 0:qlen]),
             reads=zbufs(2, 0, qlen), writes=[bob])
        P.op("dve", lambda e: e.reciprocal(out=ob[64:65, 0:qlen], in_=ob[64:65, 0:qlen]),
             reads=[bob], writes=[bob])
        for (a, b) in halves:
            P.op("pe", lambda e, a=a, b=b: e.matmul(pO[0:64, a:b], lhsT=onesP[64:65, 0:64], rhs=ob[64:65, a:b],
                                                    start=True, stop=True),
                 reads=[bob, buf("onesP")], writes=zbufs(2, a, b))
        P.op("dve", lambda e: e.tensor_tensor(out=ob[0:64, 0:qlen], in0=ob[0:64, 0:qlen], in1=pO[0:64, 0:qlen],
                                              op=ALU.mult),
             reads=[bob] + zbufs(2, 0, qlen), writes=[bob])
        P.op("dve", lambda e: e.tensor_tensor(out=attg[h][:, 0:qlen], in0=ob[0:64, 0:qlen],
                                              in1=SG[par][h][:, 0:qlen], op=ALU.mult),
             reads=[bob, buf(f"SG{par}{h}")], writes=[buf(f"attg{h}")])
        P.dma("pool", mixin_ap(h * 64, (h + 1) * 64, q0, qlen), attg[h][:, 0:qlen],
              reads=[buf(f"attg{h}")], writes=[buf(f"mixinP{q0 // PM_}")])

    def sample_attention():
        ar_off[0] = 0
        clf = aa([32, 1024], F32)
        ccum = aa([32, 1024], F32)
        NCKc = aa([128, 8, 32], F32)
        MSK = aa([128, 8, 16], BF16)
        negt = aa([128, 16], F32)
        KTn = [aa([65, 256], BF16) for h in range(2)]
        QAs = [aa([65, 256], BF16) for h in range(2)]
        SGs = [aa([64, 256], BF16) for h in range(2)]
        kst_ = [aa([64, 1024], F32) for i in range(4)]
        vst_ = [aa([128, 8, 64], F32) for i in range(4)]
        KTc = [aa([65, 1024], BF16) for i in range(4)]
        VSc = [aa([128, 8, 65], BF16) for i in range(4)]
        PTs = [aa([128, 9, 16], BF16) for i in range(4)]
        obs_l = [aa([65, 16], F32) for i in range(4)]
        attS = [aa([64, 256], BF16) for h in range(2)]
        P.dma("sp", clf, clf_d[:, :], writes=[buf("clf")])
        P.op("dve", lambda e: e.tensor_tensor_scan(out=ccum, data0=onesP[0:32, 0:1].to_broadcast([32, 1024]), data1=clf,
                                                   initial=0.0, op0=ALU.mult, op1=ALU.add),
             reads=[buf("clf"), buf("onesP")], writes=[buf("ccum")])
        P.op("dve", lambda e: e.tensor_scalar(out=clf, in0=ccum, scalar1=ccum[:, 1023:1024], scalar2=-1.0,
                                              op0=ALU.subtract, op1=ALU.mult),
             reads=[buf("ccum")], writes=[buf("clf")])
        for blk in range(8):
            P.op("pe", lambda e, blk=blk: e.transpose(out=pM[:, blk * 32:(blk + 1) * 32], in_=clf[:, blk * 128:(blk + 1) * 128],
                                                       identity=identf[0:32, 0:32]),
                 reads=[buf("clf"), buf("identf")], writes=[buf("pM")])
        P.op("dve", lambda e: e.tensor_copy(out=NCKc.rearrange("p b r -> p (b r)"), in_=pM[:, 0:256]),
             reads=[buf("pM")], writes=[buf("NCKc")])
        for qq in range(8):
            P.op("pool", lambda e: e.memset(negt, NEG), writes=[buf("negt")])
            P.op("pool", lambda e, qq=qq: e.affine_select(out=negt, in_=negt, pattern=[[0, 16]], compare_op=ALU.is_ge,
                                                          fill=0.0, base=16 * qq - 1, channel_multiplier=-1),
                 reads=[buf("negt")], writes=[buf("negt")])
            P.op("pool", lambda e, qq=qq: e.tensor_copy(out=MSK[:, qq, :], in_=negt), reads=[buf("negt")], writes=[buf("MSK")])
            P.op("pool", lambda e: e.memset(negt, NEG), writes=[buf("negt")])
            P.op("pool", lambda e, qq=qq: e.affine_select(out=negt, in_=negt, pattern=[[-1, 16]], compare_op=ALU.is_gt,
                                                          fill=0.0, base=-16 * qq, channel_multiplier=1),
                 reads=[buf("negt")], writes=[buf("negt")])
            P.op("pool", lambda e, qq=qq: e.tensor_tensor(out=negt, in0=negt, in1=MSK[:, qq, :], op=ALU.add),
                 reads=[buf("negt"), buf("MSK")], writes=[buf("negt")])
            P.op("pool", lambda e, qq=qq: e.tensor_copy(out=MSK[:, qq, :], in_=negt), reads=[buf("negt")], writes=[buf("MSK")])
        for h in range(2):
            P.dma("sp", KTn[h][0:64, :], kt_d[h, :, TP:TP + 256], reads=[buf(f"kt_d{NTILE}")], writes=[buf(f"KTn{h}")])
            P.op("pool", lambda e, h=h: e.memset(KTn[h][64:65, :], 1.0), writes=[buf(f"KTn{h}")])
            P.dma("sp", QAs[h], qt_d[h, :, TP:TP + 256], reads=[buf(f"qt_d{NTILE}")], writes=[buf(f"QAs{h}")])
            P.dma("sp", SGs[h], sga_d[h, :, TP:TP + 256], reads=[buf(f"sga_d{NTILE}")], writes=[buf(f"SGs{h}")])
            for i in range(2):
                pass
        for i in range(4):
            P.op("pool", lambda e, i=i: e.memset(KTc[i][64:65, :], 1.0), writes=[buf(f"KTc{i}")])
            P.op("pool", lambda e, i=i: e.memset(VSc[i][:, :, 64:65], 1.0), writes=[buf(f"VSc{i}")])
        def one_qh(q, h, i):
            if True:
                r = q * 2 + h
                obs = obs_l[i]
                bobs = buf(f"obs{i}")
                P.dma("sp", kst_[i], kc_d[q, h, :, :], writes=[buf(f"kst_{i}")])
                P.dma("sp", vst_[i], vc_d[q, h, :, :].rearrange("(b p) d -> p b d", p=128), writes=[buf(f"vst_{i}")])
                P.op("dve", lambda e, i=i: e.tensor_copy(out=KTc[i][0:64, :], in_=kst_[i]),
                     reads=[buf(f"kst_{i}")], writes=[buf(f"KTc{i}")])
                P.op("pool", lambda e, i=i: e.tensor_copy(out=VSc[i][:, :, 0:64], in_=vst_[i]),
                     reads=[buf(f"vst_{i}")], writes=[buf(f"VSc{i}")])
                zi = i
                Sps = Z[zi][:, 0:144].rearrange("p (b t) -> p b t", t=16)
                qs = slice(q * 16, (q + 1) * 16)
                sb_, qq = q // 8, q % 8
                for blk in range(8):
                    P.op("pe", lambda e, blk=blk, i=i, Sps=Sps, qs=qs: e.matmul(
                        Sps[:, blk, :], lhsT=KTc[i][:, blk * 128:(blk + 1) * 128], rhs=QAs[h][:, qs], start=True, stop=True),
                        reads=[buf(f"KTc{i}"), buf(f"QAs{h}")], writes=[buf(f"pZ{zi}0")])
                P.op("pe", lambda e, Sps=Sps, qs=qs, sb_=sb_: e.matmul(
                    Sps[:, 8, :], lhsT=KTn[h][:, sb_ * 128:(sb_ + 1) * 128], rhs=QAs[h][:, qs], start=True, stop=False),
                    reads=[buf(f"KTn{h}"), buf(f"QAs{h}")], writes=[buf(f"pZ{zi}0")])
                P.op("pe", lambda e, Sps=Sps, qq=qq: e.matmul(Sps[:, 8, :], lhsT=identb, rhs=MSK[:, qq, :], start=False, stop=True),
                     reads=[buf("identb"), buf("MSK")], writes=[buf(f"pZ{zi}0")])
                for blk in range(9):
                    bias = NCKc[:, blk, r:r + 1] if blk < 8 else NCK[:, NBLK + sb_, h:h + 1]
                    P.op("act", lambda e, blk=blk, i=i, Sps=Sps, bias=bias: e.activation(
                        out=PTs[i][:, blk, :], in_=Sps[:, blk, :], func=AF.Exp, bias=bias, scale=1.0),
                        reads=[buf(f"pZ{zi}0"), buf("NCKc"), buf("NCK")], writes=[buf(f"PTs{i}")])
                pO = Z[i][0:65, 512:528]
                for blk in range(9):
                    lhs = VSc[i][:, blk, :] if blk < 8 else VP[:, NBLK + sb_, h, :]
                    P.op("pe", lambda e, blk=blk, i=i, lhs=lhs, pO=pO: e.matmul(pO, lhsT=lhs, rhs=PTs[i][:, blk, :],
                                                                            start=(blk == 0), stop=(blk == 8)),
                         reads=[buf(f"VSc{i}"), buf("VP"), buf(f"PTs{i}")], writes=[buf(f"pZ{i}1")])
                P.op("dve", lambda e, pO=pO: e.tensor_copy(out=obs, in_=pO), reads=[buf(f"pZ{i}1")], writes=[bobs])
                P.op("dve", lambda e: e.reciprocal(out=obs[64:65, :], in_=obs[64:65, :]), reads=[bobs], writes=[bobs])
                P.op("pe", lambda e, i=i: e.matmul(Z[i][0:64, 512:528], lhsT=onesP[64:65, 0:64], rhs=obs[64:65, :],
                                                   start=True, stop=True),
                     reads=[bobs, buf("onesP")], writes=[buf(f"pZ{i}1")])
                P.op("dve", lambda e, i=i: e.tensor_tensor(out=obs[0:64, :], in0=obs[0:64, :], in1=Z[i][0:64, 512:528], op=ALU.mult),
                     reads=[bobs, buf(f"pZ{i}1")], writes=[bobs])
                P.op("dve", lambda e, qs=qs: e.tensor_tensor(out=attS[h][:, qs], in0=obs[0:64, :], in1=SGs[h][:, qs], op=ALU.mult),
                     reads=[bobs, buf(f"SGs{h}")], writes=[buf(f"attS{h}")])
        it = 0
        for q in range(16):
            for h in range(2):
                one_qh(q, h, it % 4)
                it += 1
        for h in range(2):
            P.dma("pool", mixin_ap(h * 64, (h + 1) * 64, TP, 256), attS[h], reads=[buf(f"attS{h}")],
                  writes=[buf(f"mixinP{TP // PM_}")])

    tiles_x = [(ti * TW, TW) for ti in range(NTILE)] + [(TP, 256)]

    def g_alloc():
        G_ = {}
        G_['wgs'] = aa([128, 4, 128], F32)
        G_['wg'] = aa([128, 4, 128], BF16)
        G_['bg'] = aa([128, 1], F32)
        G_['nbg'] = aa([128, 1], F32)
        G_['zl'] = [aa([128, 4, TW], BF16) for i in range(2)]
        G_['zm'] = [aa([128, TW], BF16) for i in range(2)]
        G_['sgm'] = [aa([128, TW], BF16) for i in range(2)]
        G_['tg'] = aa([128, TW], F32)
        G_['tg2'] = aa([128, TW], F32)
        G_['s5o'] = [aa([128, TW], BF16) for i in range(2)]
        return G_

    def g_setup(G_):
        wgs, wg, bg, nbg = G_["wgs"], G_["wg"], G_["bg"], G_["nbg"]
        P.dma("sp", wgs, wglu_d.rearrange("(kc p) f -> p kc f", p=128), writes=[buf("wgs")])
        P.dma("sp", bg, bglu_d[:, :], writes=[buf("bg")])
        P.op("dve", lambda e: e.tensor_copy(out=wg, in_=wgs), reads=[buf("wgs")], writes=[buf("wg")])
        P.op("dve", lambda e: e.tensor_scalar(out=nbg, in0=bg, scalar1=-1.0, scalar2=None, op0=ALU.mult),
             reads=[buf("bg")], writes=[buf("nbg")])

    def g_tile(G_, k):
        wg, nbg, zl, zm, sgm, tg, tg2, s5o = (G_[x] for x in ("wg", "nbg", "zl", "zm", "sgm", "tg", "tg2", "s5o"))
        pG = zb(3, 0)
        c0, n = tiles_x[k]
        if True:
            i = k % 2
            P.dma("sp", zl[i][:, :, 0:n], zall_ap(c0, n).rearrange("(kc p) t -> p kc t", p=128),
                  reads=[buf(f"zallP{c0 // PZ}")], writes=[buf(f"zl{i}")])
            P.dma("sp", zm[i][:, 0:n], zin_ap(c0, n), reads=[buf(f"zinP{c0 // PZ}")], writes=[buf(f"zm{i}")])
            P.dma("sp", sgm[i][:, 0:n], sgs_d[:, c0:c0 + n], reads=[buf(f"sgs_d{k}")], writes=[buf(f"sgm{i}")])
            for kc in range(4):
                P.op("pe", lambda e, i=i, kc=kc, n=n: e.matmul(pG[:, 0:n], lhsT=wg[:, kc, :], rhs=zl[i][:, kc, 0:n],
                                                               start=(kc == 0), stop=(kc == 3)),
                     reads=[buf(f"zl{i}"), buf("wg")], writes=[buf("pZ30")])
            P.op("act", lambda e, i=i, n=n: e.activation(out=tg[:, 0:n], in_=pG[:, 0:n], func=AF.Exp, bias=nbg[:, 0:1],
                                                         scale=-1.0), reads=[buf("pZ30"), buf("nbg")], writes=[buf("tg")])
            P.op("act", lambda e, n=n: e.activation(out=tg[:, 0:n], in_=tg[:, 0:n], func=AF.Ln, bias=1.0, scale=1.0),
                 reads=[buf("tg")], writes=[buf("tg")])
            P.op("act", lambda e, n=n: e.activation(out=tg[:, 0:n], in_=tg[:, 0:n], func=AF.Exp, scale=-1.0),
                 reads=[buf("tg")], writes=[buf("tg")])
            P.op("dve", lambda e, i=i, n=n: e.tensor_tensor(out=tg2[:, 0:n], in0=zm[i][:, 0:n], in1=tg[:, 0:n], op=ALU.mult),
                 reads=[buf("tg"), buf(f"zm{i}")], writes=[buf("tg2")])
            P.op("dve", lambda e, i=i, n=n: e.tensor_tensor(out=s5o[i][:, 0:n], in0=tg2[:, 0:n], in1=sgm[i][:, 0:n], op=ALU.mult),
                 reads=[buf("tg2"), buf(f"sgm{i}")], writes=[buf(f"s5o{i}")])
            P.dma("pool", mixin_ap(128, 256, c0, n), s5o[i][:, 0:n], reads=[buf(f"s5o{i}")], writes=[buf(f"mixinP{c0 // PM_}")])

    def c_alloc():
        C_ = {}
        C_['wos'] = [aa([128, 256], F32) for i in range(2)]
        C_['wo'] = aa([128, 8, 256], BF16)
        C_['ml'] = [aa([128, 8, TW], BF16) for i in range(2)]
        C_['xc'] = [aa([128, 4, 256], F32) for i in range(1)]
        C_['yst'] = [aa([128, 4, 256], F32) for i in range(1)]
        return C_

    def c_setup(C_):
        wos, wo = C_['wos'], C_['wo']
        for kc in range(8):
            s = kc % 2
            P.dma("sp", wos[s], wout_d[kc * 128:(kc + 1) * 128, :], writes=[buf(f"wos{s}")])
            P.op("dve", lambda e, kc=kc, s=s: e.tensor_copy(out=wo[:, kc, :], in_=wos[s]),
                 reads=[buf(f"wos{s}")], writes=[buf("wo")])

    def c_tile(C_, k):
        wo, ml, xc, yst = C_['wo'], C_['ml'], C_['xc'], C_['yst']
        pC = zb(3, 1)
        c0, n = tiles_x[k]
        if True:
            i = k % 2
            ns = n // 128
            P.dma("sp", ml[i][:, :, 0:n], mixall_ap(c0, n).rearrange("(kc p) t -> p kc t", p=128),
                  reads=[buf(f"mixallP{c0 // PM_}")], writes=[buf(f"ml{i}")])
            P.dma("sp", xc[0][:, 0:ns, :], xc_d[c0:c0 + n, :].rearrange("(s p) c -> p s c", p=128), writes=[buf("xc0")])
            for s2 in range(0, ns, 2):
                for s in range(s2, min(s2 + 2, ns)):
                    for kc in range(8):
                        P.op("pe", lambda e, i=i, s=s, kc=kc: e.matmul(
                            pC[:, (s % 2) * 256:(s % 2) * 256 + 256], lhsT=ml[i][:, kc, s * 128:(s + 1) * 128], rhs=wo[:, kc, :],
                            start=(kc == 0), stop=(kc == 7)),
                            reads=[buf(f"ml{i}"), buf("wo")], writes=[buf("pZ31")])
                w2 = min(2, ns - s2)
                P.op("dve", lambda e, s2=s2, w2=w2: e.tensor_tensor(
                    out=yst[0][:, s2:s2 + w2, :], in0=pC.rearrange("p (s c) -> p s c", s=2)[:, 0:w2, :],
                    in1=xc[0][:, s2:s2 + w2, :], op=ALU.add),
                    reads=[buf("pZ31"), buf("xc0")], writes=[buf("yst0")])
            out_dmas.append(P.dma("pool", y_o[c0:c0 + n, :].rearrange("(s p) c -> p s c", p=128), yst[0][:, 0:ns, :],
                                  reads=[buf("yst0")]))

    def ag2(pi):
        P.coll(lambda e: e.collective_compute("AllGather", ALU.bypass, replica_groups=GROUPS,
                                              ins=[mixin_p[pi]], outs=[mixall_p[pi]]),
               reads=[buf(f"mixinP{pi}")], writes=[buf(f"mixallP{pi}")])

    if "SAMP" in stages:
        P.barrier()
        sample_attention()
    P.barrier()
    ar_off[0] = 0
    KT = [aa([65, TP], BF16) for h in range(2)]
    QA = [[aa([65, QB], BF16) for h in range(2)] for p in range(2)]
    SG = [[aa([64, QB], BF16) for h in range(2)] for p in range(2)]
    PT = [aa([128, QB], BF16) for i in range(3)]
    osb = [aa([65, QB], F32) for h in range(2)]
    attg = [aa([64, QB], BF16) for h in range(2)]
    if "ATT" in stages:
        do_x = "X" in stages
        if do_x:
            G_ = g_alloc()
            C_ = c_alloc()
            g_setup(G_)
            c_setup(C_)
        ntl = (n_real + TW - 1) // TW
        for h in range(2):
            P.dma("sp", KT[h][0:64, 0:ntl * TW], kt_d[h, :, 0:ntl * TW],
                  reads=[buf(f"kt_d{ti}") for ti in range(ntl)], writes=[buf(f"KT{h}")])
            P.op("pool", lambda e, h=h: e.memset(KT[h][64:65, :], 1.0), writes=[buf(f"KT{h}")])
        ntile_x = len(tiles_x)
        g_next = [0]
        c_queue = []
        ag2_done = [0]

        def after_unit(u, last):
            if not do_x:
                return
            while g_next[0] < ntile_x and (g_next[0] <= u or last):
                g_tile(G_, g_next[0])
                g_next[0] += 1
            while ag2_done[0] < npm:
                pi = ag2_done[0]
                tok_end = min((pi + 1) * PM_, TX)
                need_tiles = [k for k, (c0, n) in enumerate(tiles_x) if c0 < tok_end]
                need_q = [qi for qi, (q0, ql) in enumerate(qbs) if q0 < tok_end]
                if (max(need_tiles) < g_next[0]) and (max(need_q) * 2 + 1 <= u or last):
                    ag2(pi)
                    ag2_done[0] += 1
                    c_queue.extend([(k, u + 4) for k, (c0, n) in enumerate(tiles_x) if pi * PM_ <= c0 < tok_end])
                else:
                    break
            if c_queue and (c_queue[0][1] <= u or last):
                n_emit = len(c_queue) if last else 1
                for _ in range(n_emit):
                    k, _u = c_queue.pop(0)
                    c_tile(C_, k)

        u = 0
        nunits = 2 * len(qbs)
        for qi, (q0, qlen) in enumerate(qbs):
            par = qi % 2
            for h in range(2):
                rd = [buf(f"qt_d{ti}") for ti in tiles_of(q0, q0 + qlen)]
                P.dma("sp", QA[par][h][:, 0:qlen], qt_d[h, :, q0:q0 + qlen], reads=rd, writes=[buf(f"QA{par}{h}")])
                rd = [buf(f"sga_d{ti}") for ti in tiles_of(q0, q0 + qlen)]
                P.dma("sp", SG[par][h][:, 0:qlen], sga_d[h, :, q0:q0 + qlen], reads=rd, writes=[buf(f"SG{par}{h}")])
            for h in range(2):
                attend(qi, h, q0, qlen)
                after_unit(u, u == nunits - 1)
                u += 1

    P.barrier()
    P.wait("sp", out_dmas)
    if os.environ.get("MK_RESCHED", "1") == "1":
        est = P.reschedule(window=int(os.environ.get('MK_WIN', '128')), hop=float(os.environ.get('MK_HOP', '1.5')))
    stats = P.emit(stack)
    stack.close()
    return nc, stats


_CACHE = {}


def _prep_core(c, I):
    b, hp = c // 4, c % 4
    f32 = np.float32
    x = np.zeros((TP, D_MODEL), f32)
    x[:N_META] = I["meta_tokens"]
    nreal = min(L_REAL, TP)
    x[N_META:nreal] = I["x_prompt"][b][:nreal - N_META]
    w = I["w_in"][0]
    cols = np.concatenate([
        np.arange(128 * hp, 128 * hp + 128), 512 + np.arange(128 * hp, 128 * hp + 128),
        1024 + np.arange(128 * hp, 128 * hp + 128), 1544 + np.arange(128 * hp, 128 * hp + 128),
        2056 + np.arange(128 * hp, 128 * hp + 128), 2568 + np.arange(128 * hp, 128 * hp + 128),
        1536 + np.arange(2 * hp, 2 * hp + 2)])
    m = {
        "x": x,
        "w_in_c": np.ascontiguousarray(w[:, cols]),
        "norm_g": np.ascontiguousarray(I["norm_g"][0].reshape(8, 128).T),
        "b_f": np.ascontiguousarray(np.broadcast_to(I["b_f"][0, 2 * hp:2 * hp + 2][None, :], (128, 2))),
        "qg": np.ascontiguousarray(np.tile(I["q_norm_g"][0], 2)[:, None]),
        "kg": np.ascontiguousarray(np.tile(I["k_norm_g"][0], 2)[:, None]),
    }
    G = slice(8 * hp, 8 * hp + 8)
    are, aim = I["s5_a_re"][0, G], I["s5_a_im"][0, G]
    bre = I["s5_b_re"][0, G].transpose(1, 0, 2)
    bim = I["s5_b_im"][0, G].transpose(1, 0, 2)
    cre = I["s5_c_re"][0, G].transpose(2, 0, 1)
    cim = I["s5_c_im"][0, G].transpose(2, 0, 1)
    m.update({
        "s5_are": np.tile(are.T, (2, 1)),
        "s5_aim": np.tile(aim.T, (2, 1)),
        "s5_ldt": np.broadcast_to(I["s5_log_dt"][0, G][None, :], (128, 8)),
        "s5_d": I["s5_d"][0, G].reshape(128, 1),
        "s5_x1": np.concatenate([bre, bim], 0),
        "s5_x2": np.concatenate([bim, bre], 0),
        "s5_cx1": np.concatenate([cre, cim], 0),
        "s5_cx2": np.concatenate([cim, cre], 0),
    })
    Q = slice(16 * b, 16 * b + 16)
    H2 = slice(2 * hp, 2 * hp + 2)
    xsmp = I["x_sample"][Q].reshape(256, D_MODEL)
    ocols = slice(256 * hp, 256 * hp + 256)
    wo = I["w_out"][0]
    rows = np.concatenate([np.concatenate([np.arange(128 * r, 128 * r + 128), 512 + np.arange(128 * r, 128 * r + 128)])
                           for r in range(4)])
    sre = I["state_s5_re"][0, Q, G, :].transpose(2, 1, 0)
    sim_ = I["state_s5_im"][0, Q, G, :].transpose(2, 1, 0)
    m.update({
        "xsmp": xsmp,
        "clf": I["cache_logf"][0, Q, :, H2].transpose(0, 2, 1).reshape(32, 1024),
        "kcT": I["cache_k"][0, Q, :, H2, :].transpose(0, 2, 3, 1),
        "vc": I["cache_v"][0, Q, :, H2, :].transpose(0, 2, 1, 3),
        "wglu_c": I["w_glu"][0][:, 128 * hp:128 * hp + 128],
        "bglu_c": I["b_glu"][0, 128 * hp:128 * hp + 128][:, None],
        "wout_c": wo[rows][:, ocols],
        "x_c": np.concatenate([x[:, ocols], xsmp[:, ocols]], 0),
        "s5x0": np.concatenate([sre, sim_], 0),
    })
    return {k: np.ascontiguousarray(v, dtype=f32) for k, v in m.items()}


def kernel(**inputs):
    I = {k: np.asarray(v) for k, v in inputs.items()}
    if "nc" not in _CACHE:
        _CACHE["nc"] = build_program()[0]
    nc = _CACHE["nc"]
    in_maps = [_prep_core(c, I) for c in range(8)]
    res = run_bass_kernel_spmd(nc, in_maps, core_ids=list(range(8)))
    R = res.results
    f32 = np.float32
    y_p = np.zeros((2, SEQ, D_MODEL), f32)
    y_s = np.zeros((32, 16, D_MODEL), f32)
    k_p = np.zeros((1, 2, L_REAL, 8, 64), f32)
    v_p = np.zeros((1, 2, L_REAL, 8, 64), f32)
    lf_p = np.zeros((1, 2, L_REAL, 8), f32)
    sr_p = np.zeros((1, 2, 32, 64), f32)
    si_p = np.zeros((1, 2, 32, 64), f32)
    k_s = np.zeros((1, 32, 16, 8, 64), f32)
    v_s = np.zeros((1, 32, 16, 8, 64), f32)
    lf_s = np.zeros((1, 32, 16, 8), f32)
    sr_s = np.zeros((1, 32, 32, 64), f32)
    si_s = np.zeros((1, 32, 32, 64), f32)
    for c in range(8):
        b, hp = c // 4, c % 4
        r = R[c]
        nr = min(L_REAL, TP)
        k_p[0, b, :nr, 2 * hp:2 * hp + 2, :] = r["k_out"][:nr].reshape(nr, 2, 64)
        v_p[0, b, :nr, 2 * hp:2 * hp + 2, :] = r["v_out"][:nr].reshape(nr, 2, 64)
        lf_p[0, b, :nr, 2 * hp:2 * hp + 2] = r["logf_out"][:nr]
        if "y_out" in r:
            cols = slice(256 * hp, 256 * hp + 256)
            G = slice(8 * hp, 8 * hp + 8)
            Q = slice(16 * b, 16 * b + 16)
            yo = r["y_out"]
            y_p[b, :nr - N_META, cols] = yo[N_META:nr]
            y_s[Q, :, cols] = yo[TP:TP + 256].reshape(16, 16, 256)
            k_s[0, Q, :, 2 * hp:2 * hp + 2, :] = r["ks_out"].reshape(16, 16, 2, 64)
            v_s[0, Q, :, 2 * hp:2 * hp + 2, :] = r["vs_out"].reshape(16, 16, 2, 64)
            lf_s[0, Q, :, 2 * hp:2 * hp + 2] = r["lfs_out"].reshape(16, 16, 2)
            fin = r["s5fin_out"]
            sr_p[0, b, G, :] = fin[0:64, :].T
            si_p[0, b, G, :] = fin[64:128, :].T
            fs = r["s5fins_out"]
            sr_s[0, Q, G, :] = fs[0:64].transpose(2, 1, 0)
            si_s[0, Q, G, :] = fs[64:128].transpose(2, 1, 0)
    _CACHE["last"] = R
    return (y_p, y_s, k_p, v_p, lf_p, sr_p, si_p, k_s, v_s, lf_s, sr_s, si_s)
```

```python
import math
import numpy as np
import ml_dtypes
from contextlib import ExitStack
import concourse.bass as bass
import concourse.mybir as mybir
from concourse.bass_utils import run_bass_kernel_spmd

F32 = mybir.dt.float32
BF16 = mybir.dt.bfloat16
AF = mybir.ActivationFunctionType
ALU = mybir.AluOpType

D_MODEL = 1024
SEQ = 16384
N_META = 16
L_REAL = SEQ + N_META
TW = 512
import os
NTILE = int(os.environ.get("MK_NTILE", "33"))
TP = NTILE * TW
NBLK = TP // 128
TX = TP + 256
NCOL = 770
EPS = 1e-6
NEG = -30000.0


class Buf:
    __slots__ = ("name", "w", "rs", "const", "excl")

    def __init__(self, name, const=False, excl=False):
        self.name = name
        self.w = None
        self.rs = []
        self.const = const
        self.excl = excl


class Node:
    __slots__ = ("eng", "fn", "deps", "kind", "sem", "val", "used", "idx", "fuse", "cost", "seg", "fin", "prio")

    def __init__(self, eng, fn, kind):
        self.eng = eng
        self.fn = fn
        self.kind = kind
        self.deps = []
        self.sem = None
        self.val = 0
        self.used = False
        self.fuse = True
        self.cost = None
        self.seg = 0
        self.fin = 0.0
        self.prio = 0


ENGS = ("pe", "act", "dve", "pool", "sp")
N_DMA_SEMS = 48
SEM_ROLL = 30000


class Prog:
    def __init__(self, nc):
        self.nc = nc
        self.q = {e: [] for e in ENGS}
        self.nodes = []
        self.dma_i = 0
        self.dma_j = 0
        self.seg = 0
        self.cur_prio = 0
        self.cur_deps = ()
        self.dma_last = [None] * N_DMA_SEMS

    def _mk(self, eng, fn, kind, reads, writes, extra):
        n = Node(eng, fn, kind)
        seen = set()
        ex = [b for b in reads if b.excl]
        if ex:
            reads = [b for b in reads if not b.excl]
            writes = list(writes) + [b for b in ex if b not in writes]

        def add(d, k):
            if d is None or (id(d), k) in seen:
                return
            seen.add((id(d), k))
            n.deps.append((d, k))
        for b in reads:
            add(b.w, "raw")
        for b in writes:
            add(b.w, "waw")
            for r in b.rs:
                add(r, "war")
        for d in extra:
            add(d, "raw")
        for d in self.cur_deps:
            add(d, "raw")
        for b in reads:
            if not b.const:
                b.rs.append(n)
        for b in writes:
            b.w = n
            b.rs = []
        n.idx = len(self.nodes)
        n.seg = self.seg
        n.prio = self.cur_prio
        self.nodes.append(n)
        self.q[eng].append(n)
        return n

    def op(self, eng, fn, reads=(), writes=(), deps=(), fuse=True, cost=None):
        n = self._mk(eng, fn, "c", reads, writes, deps)
        n.fuse = fuse
        n.cost = cost
        return n

    def dma(self, eng, out, in_, reads=(), writes=(), deps=()):
        half = N_DMA_SEMS // 2
        if eng == "pool":
            k = half + self.dma_j % half
            self.dma_j += 1
        else:
            k = self.dma_i % half
            self.dma_i += 1
        extra = list(deps)
        if self.dma_last[k] is not None:
            extra.append(self.dma_last[k])
        n = self._mk(eng, lambda e: e.dma_start(out=out, in_=in_), "d", reads, writes, extra)
        n.sem = k
        self.dma_last[k] = n
        return n

    def coll(self, fn, reads=(), writes=()):
        return self._mk("pool", fn, "x", reads, writes, ())

    def wait(self, eng, deps):
        return self._mk(eng, None, "w", (), (), deps)

    def barrier(self):
        deps = []
        for e in ENGS:
            for n in reversed(self.q[e]):
                if n.kind == "c":
                    deps.append(n)
                    break
        deps += [n for n in self.dma_last if n is not None]
        deps += [n for n in self.nodes if n.kind == "x"]
        self.seg += 1
        for e in ENGS:
            self.wait(e, deps)
        self.seg += 1

    DEF_COST = {"pe": 0.25, "act": 0.75, "dve": 0.75, "pool": 0.8, "sp": 0.1}

    def reschedule(self, window=48, hop=1.2):
        segs = {}
        for n in self.nodes:
            segs.setdefault(n.seg, []).append(n)
        def c_of(n):
            if n.cost is not None:
                return n.cost
            return 0.0 if n.kind == "w" else (2.5 if n.kind in ("d", "x") else self.DEF_COST[n.eng])
        cp = {}
        for n in reversed(self.nodes):
            cp.setdefault(id(n), c_of(n))
            base = cp[id(n)]
            for d, k in n.deps:
                v = base + (0.05 if d.eng == n.eng else hop) + c_of(d)
                if cp.get(id(d), 0.0) < v:
                    cp[id(d)] = v
        use_cp = os.environ.get("MK_CP", "1") == "1"
        clock = {e: 0.0 for e in ENGS}
        newq = {e: [] for e in ENGS}
        done = set()
        for sg in sorted(segs):
            if all(n.kind == "w" for n in segs[sg]):
                lastc = []
                for e in ENGS:
                    for m in reversed(newq[e]):
                        if m.kind == "c":
                            lastc.append(m)
                            break
                for n in segs[sg]:
                    keep = [(d, k) for d, k in n.deps if d.kind in ("d", "x")]
                    n.deps = keep + [(m, "raw") for m in lastc]
            pend = {e: [n for n in segs[sg] if n.eng == e] for e in ENGS}
            left = sum(len(v) for v in pend.values())
            while left:
                best = None
                for e in ENGS:
                    cand = pend[e]
                    seen = 0
                    for ci in range(len(cand)):
                        n = cand[ci]
                        if not n.prio:
                            seen += 1
                            if seen > window:
                                break
                        ok = True
                        rdy = 0.0
                        for d, k in n.deps:
                            if id(d) not in done:
                                ok = False
                                break
                            t = d.fin + (0.05 if (d.eng == e and d.kind == "c") else hop)
                            if t > rdy:
                                rdy = t
                        if not ok:
                            continue
                        st = max(clock[e], rdy)
                        if use_cp:
                            key = (int((st + (0.8 if n.prio else 0.0)) / float(os.environ.get("MK_BKT", "1.0"))), -cp[id(n)], n.idx)
                        else:
                            key = (st + (0.8 if n.prio else 0.0), n.idx)
                        if best is None or key < best[0]:
                            best = (key, e, ci, n, st)
                        if rdy <= clock[e] and not n.prio and not use_cp:
                            break
                assert best is not None, "scheduler stuck"
                _, e, ci, n, st = best
                pend[e].pop(ci)
                left -= 1
                c = n.cost
                if c is None:
                    c = 0.0 if n.kind == "w" else (2.5 if n.kind in ("d", "x") else self.DEF_COST[e])
                if n.kind in ("d", "x"):
                    clock[e] = st + 0.3
                    n.fin = st + c
                else:
                    clock[e] = st + c
                    n.fin = st + c
                done.add(id(n))
                newq[e].append(n)
        self.q = newq
        return max(clock.values())

    def emit(self, stack):
        nc = self.nc
        for n in self.nodes:
            for d, k in n.deps:
                if d.kind in ("d", "x"):
                    d.used = True
                elif d.eng != n.eng:
                    d.used = True
                elif n.eng != "pe":
                    d.used = True
                elif n.kind == "d":
                    d.used = True
        esems = {e: [stack.enter_context(nc.semaphore(f"s_{e}0"))] for e in ENGS}
        ecnt = {e: 0 for e in ENGS}
        dsems = [stack.enter_context(nc.semaphore(f"s_dma{i}")) for i in range(N_DMA_SEMS)]
        dcnt = [0] * N_DMA_SEMS
        for n in self.nodes:
            if n.kind == "x":
                n.sem = stack.enter_context(nc.semaphore(f"s_cc{n.idx}"))
                n.val = 1
            elif n.kind == "d":
                dcnt[n.sem] += 16
                n.val = dcnt[n.sem]
                n.sem = dsems[n.sem]
        for e in ENGS:
            for n in self.q[e]:
                if n.kind == "c" and n.used:
                    if ecnt[e] >= SEM_ROLL:
                        esems[e].append(stack.enter_context(nc.semaphore(f"s_{e}{len(esems[e])}")))
                        ecnt[e] = 0
                    ecnt[e] += 1
                    n.val = ecnt[e]
                    n.sem = esems[e][-1]
        block = stack.enter_context(nc.Block())
        handles = {"pe": block.tensor, "act": block.scalar, "dve": block.vector,
                   "pool": block.gpsimd, "sp": block.sync}
        stats = {}
        for e in ENGS:
            queue = self.q[e]

            def body(eng, queue=queue, e=e):
                waited = {}
                nw = 0
                for n in queue:
                    pend = []
                    for d, k in n.deps:
                        if d.kind not in ("d", "x"):
                            if d.eng == e and e == "pe" and n.kind != "d":
                                continue
                        if d.sem is None:
                            continue
                        key = id(d.sem)
                        if waited.get(key, 0) >= d.val:
                            continue
                        waited[key] = d.val
                        pend.append((d.sem, d.val))
                        nw += 1
                    best = {}
                    for s_, v_ in pend:
                        if id(s_) not in best or best[id(s_)][1] < v_:
                            best[id(s_)] = (s_, v_)
                    pend = list(best.values())
                    fuse = None
                    if pend and n.kind == "c" and n.fuse and e in ("act", "dve", "pool"):
                        fuse = pend.pop()
                    for s_, v_ in pend:
                        eng.wait_ge(s_, v_)
                    if n.kind == "w":
                        continue
                    ins = n.fn(eng)
                    if fuse is not None:
                        ins._wait_ge(fuse[0], fuse[1])
                    if n.kind == "x":
                        ins.then_inc(n.sem, 1)
                    elif n.kind == "d":
                        ins.then_inc(n.sem, 16)
                    elif n.used:
                        ins.then_inc(n.sem, 1)
                stats[e] = (len(queue), nw)
            handles[e](body)
        return stats


QB = 1024
ARENA_BYTES = 161 * 1024
STAGES = os.environ.get("MK_STAGES", "A,ATT,SAMP,X")


def build_program(debug=False):
    nc = bass.Bass("TRN2", target_bir_lowering=False)
    P = Prog(nc)
    stack = ExitStack()
    stages = set(STAGES.split(","))

    def din(name, shape, dt=F32):
        return nc.dram_tensor(name, list(shape), dt, kind="ExternalInput").ap()

    def dout(name, shape, dt=F32):
        return nc.dram_tensor(name, list(shape), dt, kind="ExternalOutput").ap()

    def dscr(name, shape, dt):
        return nc.dram_tensor(name, list(shape), dt).ap()

    def sb(name, shape, dt):
        return stack.enter_context(nc.sbuf_tensor("sb_" + name, list(shape), dt))[:]

    def ps(name, shape, dt):
        return stack.enter_context(nc.psum_tensor("ps_" + name, list(shape), dt))[:]

    arena = sb("arena", [128, ARENA_BYTES // 4], F32)
    ar_off = [0]

    def aa(shape, dt):
        nfree = int(np.prod(shape[1:]))
        esz = 2 if dt == BF16 else 4
        nbytes = (nfree * esz + 31) // 32 * 32
        w0 = ar_off[0] // 4
        ar_off[0] += nbytes
        assert ar_off[0] <= ARENA_BYTES, ("arena overflow", ar_off[0])
        v = arena[0:shape[0], w0:w0 + nbytes // 4]
        if dt != F32:
            v = v.bitcast(dt)
        v = v[:, 0:nfree]
        if len(shape) == 3:
            v = v.rearrange("p (a b) -> p a b", a=shape[1])
        elif len(shape) == 4:
            v = v.rearrange("p (a b c) -> p a b c", a=shape[1], b=shape[2])
        return v

    x_d = din("x", [TP, D_MODEL])
    w_d = din("w_in_c", [D_MODEL, NCOL])
    ng_d = din("norm_g", [128, 8])
    bf_d = din("b_f", [128, 2])
    qg_d = din("qg", [128, 1])
    kg_d = din("kg", [128, 1])

    S5D = {}
    for nm in ("s5_are", "s5_aim", "s5_ldt"):
        S5D[nm] = din(nm, [128, 8])
    S5D["s5_d"] = din("s5_d", [128, 1])
    for nm in ("s5_x1", "s5_x2", "s5_cx1", "s5_cx2"):
        S5D[nm] = din(nm, [128, 8, 16])
    s5fin_o = dout("s5fin_out", [128, 8])
    PZ, PM_ = 4096, 2048
    npz = (TX + PZ - 1) // PZ
    npm = (TX + PM_ - 1) // PM_
    zin_p = [dscr(f"zin{i}", [128, min(PZ, TX - i * PZ)], BF16) for i in range(npz)]
    zall_p = [dscr(f"zall{i}", [512, min(PZ, TX - i * PZ)], BF16) for i in range(npz)]
    mixin_p = [dscr(f"mixin{i}", [256, min(PM_, TX - i * PM_)], BF16) for i in range(npm)]
    mixall_p = [dscr(f"mixall{i}", [1024, min(PM_, TX - i * PM_)], BF16) for i in range(npm)]

    def zin_ap(c0, n):
        return zin_p[c0 // PZ][:, c0 % PZ:c0 % PZ + n]

    def zall_ap(c0, n):
        return zall_p[c0 // PZ][:, c0 % PZ:c0 % PZ + n]

    def mixin_ap(r0, r1, c0, n):
        return mixin_p[c0 // PM_][r0:r1, c0 % PM_:c0 % PM_ + n]

    def mixall_ap(c0, n):
        return mixall_p[c0 // PM_][:, c0 % PM_:c0 % PM_ + n]
    sgs_d = dscr("sgs_scr", [128, TX], BF16)

    xs_d = din("xsmp", [256, D_MODEL])
    clf_d = din("clf", [32, 1024])
    kc_d = din("kcT", [16, 2, 64, 1024])
    vc_d = din("vc", [16, 2, 1024, 64])
    wglu_d = din("wglu_c", [512, 128])
    bglu_d = din("bglu_c", [128, 1])
    wout_d = din("wout_c", [1024, 256])
    xc_d = din("x_c", [TX, 256])
    s5x0_d = din("s5x0", [128, 8, 16])
    y_o = dout("y_out", [TX, 256])
    ks_o = dout("ks_out", [256, 128])
    vs_o = dout("vs_out", [256, 128])
    lfs_o = dout("lfs_out", [256, 2])
    s5fins_o = dout("s5fins_out", [128, 8, 16])

    k_o = dout("k_out", [TP, 128])
    v_o = dout("v_out", [TP, 128])
    lf_o = dout("logf_out", [TP, 2])

    qt_d = dscr("qt_scr", [2, 65, TX], BF16)
    kt_d = dscr("kt_scr", [2, 64, TX], BF16)
    sga_d = dscr("sga_scr", [2, 64, TX], BF16)

    ng = sb("ng", [128, 8], F32)
    bfp = sb("bfp", [128, 2], F32)
    nbf = sb("nbf", [128, 2], F32)
    qg = sb("qg", [128, 1], F32)
    kg = sb("kg", [128, 1], F32)
    qg8 = sb("qg8", [128, 1], F32)
    onesf = sb("onesf", [128, 128], F32)
    identf = sb("identf", [128, 128], F32)
    identb = sb("identb", [128, 128], BF16)
    BO = sb("BO", [128, 128], BF16)
    MN = sb("MN", [128, 128], BF16)
    ones2 = sb("ones2", [2, TW], F32)
    onesP = sb("onesP", [128, 64], F32)
    VP = sb("VP", [128, NBLK + 2, 2, 65], BF16)
    NCK = sb("NCK", [128, NBLK + 2, 2], F32)
    cumrow = [sb(f"cumrow{i}", [2, TW], F32) for i in range(2)]
    S5FIN = sb("s5fin", [128, 8], F32)
    S5FINS = sb("s5fins", [128, 8, 16], F32)
    S5X0 = sb("s5x0s", [128, 8, 16], F32)
    segm = sb("segm", [2, TW], F32)

    Z = [ps(f"Z{i}", [128, 1024], F32) for i in range(4)]

    def zb(i, half):
        return Z[i][:, half * 512:(half + 1) * 512]

    pT = [zb(0, h).bitcast(BF16).rearrange("p (k t) -> p k t", k=8) for h in range(2)]
    pP = [zb(1, 0), zb(1, 1)]
    pM = zb(2, 0)
    pV = zb(2, 1).rearrange("p (s f) -> p s f", s=4)
    pK = zb(3, 0).rearrange("p (s f) -> p s f", s=4)
    pS = zb(3, 1)
    PALIAS = {"pT0": "pZ00", "pT1": "pZ01", "pP0": "pZ10", "pP1": "pZ11", "pM": "pZ20", "pV": "pZ21",
              "pK": "pZ30", "pS": "pZ31"}

    B = {}

    def buf(name, const=False):
        name = PALIAS.get(name, name)
        name = {"lnb": "lnb0", "rr": "rr0", "sq": "sq0"}.get(name, name)
        if name not in B:
            B[name] = Buf(name, const, excl=name.startswith("pZ"))
        return B[name]

    P.dma("sp", ng, ng_d[:, :], writes=[buf("ng")])
    P.dma("sp", bfp, bf_d[:, :], writes=[buf("bfp")])
    P.dma("sp", qg, qg_d[:, :], writes=[buf("qg")])
    P.dma("sp", kg, kg_d[:, :], writes=[buf("kg")])
    P.op("dve", lambda e: e.tensor_scalar(out=nbf, in0=bfp, scalar1=-1.0, scalar2=None, op0=ALU.mult),
         reads=[buf("bfp")], writes=[buf("nbf")])
    P.op("dve", lambda e: e.tensor_scalar(out=qg8, in0=qg, scalar1=0.125, scalar2=None, op0=ALU.mult),
         reads=[buf("qg")], writes=[buf("qg8")])
    P.op("pool", lambda e: e.memset(onesf, 1.0), writes=[buf("onesf")])
    P.op("pool", lambda e: e.memset(ones2, 1.0), writes=[buf("ones2")])
    P.op("pool", lambda e: e.memset(onesP, 1.0), writes=[buf("onesP")])
    P.op("pool", lambda e: e.memset(segm, 1.0), writes=[buf("segm")])
    P.op("pool", lambda e: e.memset(segm.rearrange("p (q t) -> p q t", t=16)[:, :, 0:1], 0.0), writes=[buf("segm")])
    P.op("pool", lambda e: e.affine_select(out=identf, in_=onesf, pattern=[[-1, 128]],
                                           compare_op=ALU.is_equal, fill=0.0, base=0, channel_multiplier=1),
         reads=[buf("onesf")], writes=[buf("identf")])
    P.op("pool", lambda e: e.tensor_copy(out=identb, in_=identf), reads=[buf("identf")], writes=[buf("identb")])
    P.op("pool", lambda e: e.memset(onesf, NEG), reads=[buf("identf")], writes=[buf("onesf")])
    P.op("pool", lambda e: e.affine_select(out=MN, in_=onesf, pattern=[[-1, 128]],
                                           compare_op=ALU.is_gt, fill=0.0, base=0, channel_multiplier=1),
         reads=[buf("onesf")], writes=[buf("MN")])
    P.op("pool", lambda e: e.memset(BO, 0.0), writes=[buf("BO")])
    P.op("pool", lambda e: e.memset(BO[0:64, 0:64], 1.0 / 64), writes=[buf("BO")])
    P.op("pool", lambda e: e.memset(BO[64:128, 64:128], 1.0 / 64), writes=[buf("BO")])
    P.op("pool", lambda e: e.memset(VP[:, :, :, 64:65], 1.0), writes=[buf("VP")])

    out_dmas = []

    ar_off[0] = 0
    Wb = aa([128, 8, NCOL], BF16)
    s5_mark = [0]
    setup_extent = [0]
    def wb_setup():
      wst = [aa([128, NCOL], F32) for i in range(2)]
      for kc in range(8):
        s = kc % 2
        P.dma("sp", wst[s], w_d[kc * 128:(kc + 1) * 128, :], writes=[buf(f"wst{s}")])
        P.op("dve", lambda e, kc=kc, s=s: e.tensor_scalar(
            out=Wb[:, kc, :], in0=wst[s], scalar1=ng[:, kc:kc + 1], scalar2=None, op0=ALU.mult),
            reads=[buf(f"wst{s}"), buf("ng")], writes=[buf("Wb")])

    def silu2_from_psum(items, N, tmps):
        for (i, o, ob), (l, lb, r, rb) in zip(items, tmps):
            P.op("act", lambda e, i=i, l=l: e.activation(out=l[:, 0:N], in_=pP[i][:, 0:N], func=AF.Exp, scale=-1.0),
                 reads=[buf(f"pP{i}")], writes=[buf(lb)])
        for (i, o, ob), (l, lb, r, rb) in zip(items, tmps):
            P.op("act", lambda e, l=l: e.activation(out=l[:, 0:N], in_=l[:, 0:N], func=AF.Ln, bias=1.0, scale=1.0),
                 reads=[buf(lb)], writes=[buf(lb)])
        for (i, o, ob), (l, lb, r, rb) in zip(items, tmps):
            P.op("act", lambda e, l=l, r=r: e.activation(out=r[:, 0:N], in_=l[:, 0:N], func=AF.Exp, scale=-1.0),
                 reads=[buf(lb)], writes=[buf(rb)])
        for (i, o, ob), (l, lb, r, rb) in zip(items, tmps):
            P.op("dve", lambda e, i=i, o=o, r=r: e.tensor_tensor(out=o[:, 0:N], in0=pP[i][:, 0:N], in1=r[:, 0:N], op=ALU.mult),
                 reads=[buf(f"pP{i}"), buf(rb)], writes=[ob])

    def silu_from_psum(i, out_ap, out_buf, N=TW):
        P.op("act", lambda e: e.activation(out=lnb[:, 0:N], in_=pP[i][:, 0:N], func=AF.Exp, scale=-1.0),
             reads=[buf(f"pP{i}")], writes=[buf("lnb")])
        P.op("act", lambda e: e.activation(out=lnb[:, 0:N], in_=lnb[:, 0:N], func=AF.Ln, bias=1.0, scale=1.0),
             reads=[buf("lnb")], writes=[buf("lnb")])
        P.op("act", lambda e: e.activation(out=rr[:, 0:N], in_=lnb[:, 0:N], func=AF.Exp, scale=-1.0),
             reads=[buf("lnb")], writes=[buf("rr")])
        P.op("dve", lambda e: e.tensor_tensor(out=out_ap[:, 0:N], in0=pP[i][:, 0:N], in1=rr[:, 0:N], op=ALU.mult),
             reads=[buf(f"pP{i}"), buf("rr")], writes=[out_buf])

    def s5_alloc_weights():
        S = {}
        S["are"] = aa([128, 8], F32)
        S["aim"] = aa([128, 8], F32)
        S["ldt"] = aa([128, 8], F32)
        S["dvec"] = aa([128, 1], F32)
        S["rho8"] = aa([128, 8], F32)
        S["ar8"] = aa([128, 8], F32)
        S["ai8"] = aa([128, 8], F32)
        S["PiT"] = aa([128, 128], F32)
        S["Wfir"] = aa([128, 8, 128], BF16)
        S["Wst"] = aa([128, 8, 8, 128], BF16)
        S["Wint"] = aa([128, 8, 8, 128], BF16)
        S["COS"] = aa([128, 8, 256], F32)
        S["SIN"] = aa([128, 8, 256], F32)
        S["X0"] = [aa([128, 8], F32) for i in range(2)]
        return S

    def s5_setup(S):
        wb_setup()
        P.cur_prio = 1
        for nm, dn in (("are", "s5_are"), ("aim", "s5_aim"), ("ldt", "s5_ldt"), ("dvec", "s5_d")):
            P.dma("sp", S[nm], S5D[dn][:, :], writes=[buf("s5" + nm)])
        X1 = aa([128, 8, 16], F32)
        X2 = aa([128, 8, 16], F32)
        CX1 = aa([128, 8, 16], F32)
        CX2 = aa([128, 8, 16], F32)
        P.dma("sp", X1, S5D["s5_x1"][:, :, :], writes=[buf("s5X1")])
        P.dma("sp", X2, S5D["s5_x2"][:, :, :], writes=[buf("s5X2")])
        P.dma("sp", CX1, S5D["s5_cx1"][:, :, :], writes=[buf("s5CX1")])
        P.dma("sp", CX2, S5D["s5_cx2"][:, :, :], writes=[buf("s5CX2")])
        sg1 = aa([128, 1], F32)
        sg2 = aa([128, 1], F32)
        dt = aa([128, 8], F32)
        lr = aa([128, 8], F32)
        th = aa([128, 8], F32)
        mlr = aa([128, 8, 9], F32)
        mth = aa([128, 8, 9], F32)
        mcol = aa([128, 9], F32)
        mag = aa([128, 8, 9], F32)
        sn = aa([128, 8, 9], F32)
        cs = aa([128, 8, 9], F32)
        AR = aa([128, 8, 9], F32)
        AI = aa([128, 8, 9], F32)
        t8a = aa([128, 8], F32)
        t8b = aa([128, 8], F32)
        t8c = aa([128, 8], F32)
        cr = aa([128, 8], F32)
        ci = aa([128, 8], F32)
        Bst = aa([128, 8, 16], F32)
        Bsw = aa([128, 8, 16], F32)
        tB = aa([128, 8, 16], F32)
        CA = aa([128, 9, 8, 16], F32)
        Bpad = aa([128, 8, 128], F32)
        ABp = aa([128, 128], F32)
        iot = aa([128, 256], F32)
        ang = aa([128, 256], F32)
        wk = [aa([128, 256], F32) for _ in range(4)]
        wki = aa([128, 256], mybir.dt.int32)
        bT = "s5tmp"

        def dv(fn, extra_r=(), extra_w=()):
            P.op("dve", fn, reads=[buf(bT)] + list(extra_r), writes=[buf(bT)] + list(extra_w))

        def sincos(ang_ap, sin_out, cos_out, n):
            y, kf, f, g_ = wk[0][:, 0:n], wk[1][:, 0:n], wk[2][:, 0:n], wk[3][:, 0:n]
            ki = wki[:, 0:n]
            dv(lambda e: e.tensor_scalar(out=y, in0=ang_ap, scalar1=1.0 / (2 * math.pi), scalar2=None, op0=ALU.mult))
            dv(lambda e: e.tensor_copy(out=ki, in_=y))
            dv(lambda e: e.tensor_copy(out=kf, in_=ki))
            dv(lambda e: e.tensor_tensor(out=f, in0=y, in1=kf, op=ALU.subtract))
            for shift, dst in ((0.0, sin_out), (0.25, cos_out)):
                if shift:
                    dv(lambda e: e.tensor_scalar(out=f, in0=f, scalar1=shift, scalar2=None, op0=ALU.add))
                dv(lambda e: e.tensor_scalar(out=g_, in0=f, scalar1=0.5, scalar2=None, op0=ALU.is_gt))
                dv(lambda e: e.tensor_tensor(out=f, in0=f, in1=g_, op=ALU.subtract))
                dv(lambda e: e.tensor_scalar(out=g_, in0=f, scalar1=-0.5, scalar2=None, op0=ALU.is_lt))
                dv(lambda e: e.tensor_tensor(out=f, in0=f, in1=g_, op=ALU.add))
                P.op("act", lambda e, dst=dst: e.activation(out=dst, in_=f, func=AF.Sin, scale=2 * math.pi),
                     reads=[buf(bT)], writes=[buf(bT)])

        rd_in = [buf("s5are"), buf("s5aim"), buf("s5ldt"), buf("s5dvec"), buf("s5X1"), buf("s5X2"),
                 buf("s5CX1"), buf("s5CX2"), buf("identf")]
        P.op("pool", lambda e: e.memset(sg1[0:64, :], 1.0), reads=rd_in, writes=[buf(bT)])
        P.op("pool", lambda e: e.memset(sg1[64:128, :], -1.0), writes=[buf(bT)])
        P.op("pool", lambda e: e.memset(sg2[0:64, :], -1.0), writes=[buf(bT)])
        P.op("pool", lambda e: e.memset(sg2[64:128, :], 1.0), writes=[buf(bT)])
        P.op("pool", lambda e: e.memset(S["PiT"], 0.0), writes=[buf(bT), buf("s5PiT")])
        P.op("pool", lambda e: e.memset(Bpad, 0.0), writes=[buf(bT)])
        P.op("pool", lambda e: e.memset(S["Wint"], 0.0), writes=[buf(bT), buf("s5Wint")])
        P.op("pool", lambda e: e.iota(mcol, pattern=[[1, 9]], base=0, channel_multiplier=0,
                                      allow_small_or_imprecise_dtypes=True), writes=[buf(bT)])
        P.op("pool", lambda e: e.iota(iot, pattern=[[1, 256]], base=1, channel_multiplier=0,
                                      allow_small_or_imprecise_dtypes=True), writes=[buf(bT)])
        dv(lambda e: e.tensor_copy(out=S["PiT"][0:64, 64:128], in_=identf[0:64, 0:64]), extra_w=[buf("s5PiT")])
        dv(lambda e: e.tensor_scalar(out=S["PiT"][64:128, 0:64], in0=identf[64:128, 64:128], scalar1=-1.0,
                                     scalar2=None, op0=ALU.mult), extra_w=[buf("s5PiT")])
        P.op("act", lambda e: e.activation(out=dt, in_=S["ldt"], func=AF.Exp), reads=[buf(bT)], writes=[buf(bT)])
        dv(lambda e: e.tensor_tensor(out=lr, in0=S["are"], in1=dt, op=ALU.mult))
        dv(lambda e: e.tensor_tensor(out=th, in0=S["aim"], in1=dt, op=ALU.mult))
        for g in range(8):
            dv(lambda e, g=g: e.tensor_scalar(out=mlr[:, g, :], in0=mcol, scalar1=lr[:, g:g + 1], scalar2=None,
                                              op0=ALU.mult))
            dv(lambda e, g=g: e.tensor_scalar(out=mth[:, g, :], in0=mcol, scalar1=th[:, g:g + 1], scalar2=None,
                                              op0=ALU.mult))
        fl = "p g m -> p (g m)"
        P.op("act", lambda e: e.activation(out=mag.rearrange(fl), in_=mlr.rearrange(fl), func=AF.Exp),
             reads=[buf(bT)], writes=[buf(bT)])
        sincos(mth.rearrange(fl), sn.rearrange(fl), cs.rearrange(fl), 72)
        dv(lambda e: e.tensor_tensor(out=AR.rearrange(fl), in0=mag.rearrange(fl), in1=cs.rearrange(fl), op=ALU.mult))
        dv(lambda e: e.tensor_tensor(out=AI.rearrange(fl), in0=mag.rearrange(fl), in1=sn.rearrange(fl), op=ALU.mult))
        dv(lambda e: e.tensor_copy(out=S["rho8"], in_=mag[:, :, 8]), extra_w=[buf("s5rho8")])
        dv(lambda e: e.tensor_copy(out=S["ar8"], in_=AR[:, :, 8]), extra_w=[buf("s5ar8")])
        dv(lambda e: e.tensor_copy(out=S["ai8"], in_=AI[:, :, 8]), extra_w=[buf("s5ai8")])
        dv(lambda e: e.tensor_scalar(out=t8a, in0=AR[:, :, 1], scalar1=-1.0, scalar2=None, op0=ALU.add))
        dv(lambda e: e.tensor_tensor(out=t8b, in0=S["are"], in1=S["are"], op=ALU.mult))
        dv(lambda e: e.tensor_tensor(out=t8c, in0=S["aim"], in1=S["aim"], op=ALU.mult))
        dv(lambda e: e.tensor_tensor(out=t8b, in0=t8b, in1=t8c, op=ALU.add))
        dv(lambda e: e.reciprocal(out=t8b, in_=t8b))
        dv(lambda e: e.tensor_tensor(out=cr, in0=t8a, in1=S["are"], op=ALU.mult))
        dv(lambda e: e.tensor_tensor(out=t8c, in0=AI[:, :, 1], in1=S["aim"], op=ALU.mult))
        dv(lambda e: e.tensor_tensor(out=cr, in0=cr, in1=t8c, op=ALU.add))
        dv(lambda e: e.tensor_tensor(out=cr, in0=cr, in1=t8b, op=ALU.mult))
        dv(lambda e: e.tensor_tensor(out=ci, in0=AI[:, :, 1], in1=S["are"], op=ALU.mult))
        dv(lambda e: e.tensor_tensor(out=t8c, in0=t8a, in1=S["aim"], op=ALU.mult))
        dv(lambda e: e.tensor_tensor(out=ci, in0=ci, in1=t8c, op=ALU.subtract))
        dv(lambda e: e.tensor_tensor(out=ci, in0=ci, in1=t8b, op=ALU.mult))
        dv(lambda e: e.tensor_scalar(out=X2.rearrange("p g h -> p (g h)"), in0=X2.rearrange("p g h -> p (g h)"),
                                     scalar1=sg2[:, 0:1], scalar2=None, op0=ALU.mult))
        dv(lambda e: e.tensor_scalar(out=CX1.rearrange("p g h -> p (g h)"), in0=CX1.rearrange("p g h -> p (g h)"),
                                     scalar1=sg1[:, 0:1], scalar2=None, op0=ALU.mult))
        for g in range(8):
            dv(lambda e, g=g: e.tensor_scalar(out=tB[:, g, :], in0=X2[:, g, :], scalar1=ci[:, g:g + 1], scalar2=None,
                                              op0=ALU.mult))
            dv(lambda e, g=g: e.scalar_tensor_tensor(out=Bst[:, g, :], in0=X1[:, g, :], scalar=cr[:, g:g + 1],
                                                     in1=tB[:, g, :], op0=ALU.mult, op1=ALU.add))
            dv(lambda e, g=g: e.tensor_scalar(out=tB[:, g, :], in0=X1[:, g, :], scalar1=ci[:, g:g + 1], scalar2=None,
                                              op0=ALU.mult))
            dv(lambda e, g=g: e.scalar_tensor_tensor(out=Bsw[:, g, :], in0=X2[:, g, :], scalar=cr[:, g:g + 1],
                                                     in1=tB[:, g, :], op0=ALU.mult, op1=ALU.subtract))
            dv(lambda e, g=g: e.tensor_copy(out=Bpad[:, g, 16 * g:16 * g + 16], in_=Bst[:, g, :]))
            for m in range(9):
                dv(lambda e, g=g, m=m: e.tensor_scalar(out=tB[:, g, :], in0=CX2[:, g, :], scalar1=AI[:, g, m:m + 1],
                                                       scalar2=None, op0=ALU.mult))
                dv(lambda e, g=g, m=m: e.scalar_tensor_tensor(out=CA[:, m, g, :], in0=CX1[:, g, :],
                                                              scalar=AR[:, g, m:m + 1], in1=tB[:, g, :],
                                                              op0=ALU.mult, op1=ALU.subtract))
        for j in range(8):
            for g in range(8):
                dv(lambda e, j=j, g=g: e.tensor_copy(out=S["Wint"][:, j, g, 16 * g:16 * g + 16], in_=CA[:, j + 1, g, :]),
                   extra_w=[buf("s5Wint")])
        for tau in range(8):
            for g in range(8):
                P.op("pe", lambda e, tau=tau, g=g: e.matmul(pM[:, 16 * g:16 * g + 16], lhsT=Bpad[:, g, :],
                                                            rhs=CA[:, tau, g, :], start=True, stop=True),
                     reads=[buf(bT)], writes=[buf("pM")])
            if tau == 0:
                P.op("dve", lambda e: e.scalar_tensor_tensor(out=S["Wfir"][:, 0, :], in0=identf, scalar=S["dvec"][:, 0:1],
                                                             in1=pM[:, 0:128], op0=ALU.mult, op1=ALU.add),
                     reads=[buf("pM"), buf(bT)], writes=[buf("s5Wfir")])
            else:
                P.op("dve", lambda e, tau=tau: e.tensor_copy(out=S["Wfir"][:, tau, :], in_=pM[:, 0:128]),
                     reads=[buf("pM")], writes=[buf("s5Wfir")])
        for s in range(8):
            m = 7 - s
            for g in range(8):
                dv(lambda e, g=g, m=m: e.tensor_scalar(out=tB[:, g, :], in0=Bsw[:, g, :], scalar1=AI[:, g, m:m + 1],
                                                       scalar2=None, op0=ALU.mult))
                dv(lambda e, g=g: e.memset(ABp, 0.0))
                dv(lambda e, g=g, m=m: e.scalar_tensor_tensor(out=ABp[:, 16 * g:16 * g + 16], in0=Bst[:, g, :],
                                                              scalar=AR[:, g, m:m + 1], in1=tB[:, g, :],
                                                              op0=ALU.mult, op1=ALU.add))
                P.op("pe", lambda e: e.transpose(out=pM[:, 0:128], in_=ABp, identity=identf),
                     reads=[buf(bT), buf("identf")], writes=[buf("pM")])
                P.op("dve", lambda e, s=s, g=g: e.tensor_copy(out=S["Wst"][:, s, g, :], in_=pM[:, 0:128]),
                     reads=[buf("pM")], writes=[buf("s5Wst"), buf(bT)])
        for g in range(8):
            dv(lambda e, g=g: e.tensor_scalar(out=ang, in0=iot, scalar1=mth[:, g, 8:9], scalar2=None, op0=ALU.mult))
            sincos(ang, S["SIN"][:, g, :], S["COS"][:, g, :], 256)
        P.op("dve", lambda e: e.tensor_copy(out=wk[0], in_=wk[0]), reads=[buf(bT)],
             writes=[buf(bT), buf("s5SIN"), buf("s5COS")])
        P.op("dve", lambda e: e.memset(S["X0"][0], 0.0), writes=[buf("s5X00")])
        P.cur_prio = 0
        return S

    def s5_alloc_work():
        Wk = {}
        Wk["Ssb"] = aa([128, 256], F32)
        Wk["t1"] = aa([128, 256], F32)
        Wk["t2"] = aa([128, 256], F32)
        Wk["Wsc"] = aa([128, 256], F32)
        Wk["Xn"] = aa([128, 256], F32)
        Wk["Xin"] = aa([128, 8, 256], BF16)
        for nm in ("S4", "T1", "T2"):
            Wk[nm] = aa([128, 4, 256], F32)
        Wk["W4"] = Wk["S4"]
        Wk["X4"] = Wk["T2"]
        Wk["ysb"] = aa([128, 2, 256], F32)
        Wk["g1"] = aa([128, 2, 256], F32)
        Wk["g2"] = aa([128, 2, 256], F32)
        Wk["zfull"] = aa([128, 2048], BF16)
        assert ar_off[0] >= setup_extent[0], (ar_off[0], setup_extent[0])
        Wk["ub"] = aa([128, 8, 256], BF16)
        return Wk

    GC = math.sqrt(2.0 / math.pi)
    YBANK = [zb(2, 0), zb(2, 1), zb(3, 0), zb(3, 1)]
    YBUF = ["pZ20", "pZ21", "pZ30", "pZ31"]

    def s5_fir_inter(S, Wk, nc_, dst_ap, dst_buf):
        ub, Xin, zfull = Wk["ub"], Wk["Xin"], Wk["zfull"]
        zv = zfull.rearrange("p (c j) -> p j c", j=8)
        for j in range(8):
            bk = j // 2
            Yj = YBANK[bk][:, (j % 2) * 256:(j % 2) * 256 + nc_]
            for tau in range(j + 1):
                P.op("pe", lambda e, Yj=Yj, tau=tau, j=j: e.matmul(Yj, lhsT=S["Wfir"][:, tau, :], rhs=ub[:, j - tau, 0:nc_],
                                                                 start=(tau == 0), stop=False),
                     reads=[buf("s5ub"), buf("s5Wfir")], writes=[buf(YBUF[bk])])
            for g in range(8):
                P.op("pe", lambda e, Yj=Yj, j=j, g=g: e.matmul(Yj, lhsT=S["Wint"][:, j, g, :], rhs=Xin[:, g, 0:nc_],
                                                             start=False, stop=(g == 7)),
                     reads=[buf("s5Xin"), buf("s5Wint")], writes=[buf(YBUF[bk])])
            if j % 2 == 1:
                Yb = YBANK[bk].rearrange("p (j c) -> p j c", j=2)[:, :, 0:nc_]
                ysb, g1, g2 = Wk["ysb"][:, :, 0:nc_], Wk["g1"][:, :, 0:nc_], Wk["g2"][:, :, 0:nc_]
                P.op("act", lambda e, Yb=Yb: e.activation(out=ysb, in_=Yb, func=AF.Copy),
                     reads=[buf(YBUF[bk])], writes=[buf("s5ysb")])
                P.op("dve", lambda e: e.tensor_tensor(out=g1, in0=ysb, in1=ysb, op=ALU.mult),
                     reads=[buf("s5ysb")], writes=[buf("s5g1")])
                P.op("dve", lambda e: e.tensor_scalar(out=g1, in0=g1, scalar1=0.044715, scalar2=1.0, op0=ALU.mult,
                                                      op1=ALU.add), reads=[buf("s5g1")], writes=[buf("s5g1")])
                P.op("dve", lambda e: e.tensor_tensor(out=g1, in0=g1, in1=ysb, op=ALU.mult),
                     reads=[buf("s5g1"), buf("s5ysb")], writes=[buf("s5g1")])
                P.op("dve", lambda e: e.tensor_scalar(out=g1, in0=g1, scalar1=-18.0, scalar2=None, op0=ALU.max),
                     reads=[buf("s5g1")], writes=[buf("s5g1")])
                P.op("act", lambda e: e.activation(out=g2, in_=g1, func=AF.Exp, scale=-2.0 * GC),
                     reads=[buf("s5g1")], writes=[buf("s5g2")])
                P.op("act", lambda e: e.activation(out=g2, in_=g2, func=AF.Ln, bias=1.0, scale=1.0),
                     reads=[buf("s5g2")], writes=[buf("s5g2")])
                P.op("act", lambda e: e.activation(out=g2, in_=g2, func=AF.Exp, scale=-1.0),
                     reads=[buf("s5g2")], writes=[buf("s5g2")])
                P.op("dve", lambda e, j=j: e.tensor_tensor(out=zv[:, j - 1:j + 1, 0:nc_], in0=ysb, in1=g2, op=ALU.mult),
                     reads=[buf("s5ysb"), buf("s5g2")], writes=[buf("s5zfull")])
        P.dma("pool", dst_ap, zfull[:, 0:nc_ * 8], reads=[buf("s5zfull")], writes=[dst_buf])

    def s5_states(S, Wk, nc_, g, dstps, dstbuf):
        for s in range(8):
            P.op("pe", lambda e, s=s: e.matmul(dstps[:, 0:nc_], lhsT=S["Wst"][:, s, g, :], rhs=Wk["ub"][:, s, 0:nc_],
                                                 start=(s == 0), stop=(s == 7)),
                 reads=[buf("s5ub"), buf("s5Wst")], writes=[buf(dstbuf)])

    def s5_supertile(S, Wk, sti, ntok, tok0, final_k=None):
        nc_ = ntok // 8
        X0 = S["X0"][sti % 2]
        X0n = S["X0"][(sti + 1) % 2]
        bX0, bX0n = buf(f"s5X0{sti % 2}"), buf(f"s5X0{(sti + 1) % 2}")
        Xin = Wk["Xin"]
        S4, T1, T2, W4, X4 = (Wk[k] for k in ("S4", "T1", "T2", "W4", "X4"))
        ZA = Z[0].rearrange("p (g c) -> p g c", g=4)
        ZB = Z[1].rearrange("p (g c) -> p g c", g=4)
        bZA = [buf("pZ00"), buf("pZ01")]
        bZB = [buf("pZ10"), buf("pZ11")]
        for hf in range(2):
            gs = slice(4 * hf, 4 * hf + 4)
            for gl in range(4):
                g = 4 * hf + gl
                for s in range(8):
                    P.op("pe", lambda e, s=s, g=g, gl=gl: e.matmul(ZA[:, gl, 0:nc_], lhsT=S["Wst"][:, s, g, :],
                                                                   rhs=Wk["ub"][:, s, 0:nc_], start=(s == 0), stop=(s == 7)),
                         reads=[buf("s5ub"), buf("s5Wst")], writes=[bZA[gl // 2]])
            P.op("dve", lambda e: e.tensor_copy(out=S4[:, :, 0:nc_], in_=ZA[:, :, 0:nc_]), reads=bZA, writes=[buf("s5S4")],
                 deps=(setup_done if (sti == 0 and hf == 0) else ()))
            for gl in range(4):
                P.op("pe", lambda e, gl=gl: e.matmul(ZB[:, gl, 0:nc_], lhsT=S["PiT"], rhs=S4[:, gl, 0:nc_], start=True, stop=True),
                     reads=[buf("s5S4"), buf("s5PiT")], writes=[bZB[gl // 2]])
            P.op("dve", lambda e, gs=gs: e.tensor_tensor(out=T1[:, :, 0:nc_], in0=S["SIN"][:, gs, 0:nc_], in1=ZB[:, :, 0:nc_],
                                                         op=ALU.mult), reads=bZB + [buf("s5SIN")], writes=[buf("s5T1")])
            P.op("dve", lambda e, gs=gs: e.tensor_tensor(out=T2[:, :, 0:nc_], in0=S["COS"][:, gs, 0:nc_], in1=S4[:, :, 0:nc_],
                                                         op=ALU.mult), reads=[buf("s5S4"), buf("s5COS")], writes=[buf("s5T2")])
            P.op("dve", lambda e: e.tensor_tensor(out=T2[:, :, 0:nc_], in0=T2[:, :, 0:nc_], in1=T1[:, :, 0:nc_], op=ALU.subtract),
                 reads=[buf("s5T1"), buf("s5T2")], writes=[buf("s5T2")])
            for gl in range(4):
                g = 4 * hf + gl
                P.op("dve", lambda e, g=g, gl=gl: e.tensor_tensor_scan(
                    out=W4[:, gl, 0:nc_], data0=S["rho8"][:, g:g + 1].to_broadcast([128, nc_]), data1=T2[:, gl, 0:nc_],
                    initial=X0[:, g:g + 1], op0=ALU.mult, op1=ALU.add),
                    reads=[buf("s5T2"), buf("s5rho8"), bX0], writes=[buf("s5S4")])
            for gl in range(4):
                P.op("pe", lambda e, gl=gl: e.matmul(ZB[:, gl, 0:nc_], lhsT=S["PiT"], rhs=W4[:, gl, 0:nc_], start=True, stop=True),
                     reads=[buf("s5S4"), buf("s5PiT")], writes=[bZB[gl // 2]])
            P.op("dve", lambda e, gs=gs: e.tensor_tensor(out=T1[:, :, 0:nc_], in0=S["SIN"][:, gs, 0:nc_], in1=ZB[:, :, 0:nc_],
                                                         op=ALU.mult), reads=bZB + [buf("s5SIN")], writes=[buf("s5T1")])
            P.op("dve", lambda e, gs=gs: e.tensor_tensor(out=X4[:, :, 0:nc_], in0=S["COS"][:, gs, 0:nc_], in1=W4[:, :, 0:nc_],
                                                         op=ALU.mult), reads=[buf("s5S4"), buf("s5COS")], writes=[buf("s5T2")])
            P.op("dve", lambda e: e.tensor_tensor(out=X4[:, :, 0:nc_], in0=X4[:, :, 0:nc_], in1=T1[:, :, 0:nc_], op=ALU.add),
                 reads=[buf("s5T1"), buf("s5T2")], writes=[buf("s5T2")])
            P.op("dve", lambda e, gs=gs: e.tensor_copy(out=Xin[:, gs, 0:1], in_=X0[:, gs].rearrange("p (g o) -> p g o", o=1)),
                 reads=[bX0], writes=[buf("s5Xin")])
            if nc_ > 1:
                P.op("dve", lambda e, gs=gs: e.tensor_copy(out=Xin[:, gs, 1:nc_], in_=X4[:, :, 0:nc_ - 1]),
                     reads=[buf("s5T2")], writes=[buf("s5Xin")])
            P.op("dve", lambda e, gs=gs: e.tensor_copy(out=X0n[:, gs].rearrange("p (g o) -> p g o", o=1), in_=X4[:, :, nc_ - 1:nc_]),
                 reads=[buf("s5T2")], writes=[bX0n])
            if final_k is not None:
                P.op("dve", lambda e, gs=gs: e.tensor_copy(out=S5FIN[:, gs].rearrange("p (g o) -> p g o", o=1),
                                                           in_=X4[:, :, final_k - 1:final_k]),
                     reads=[buf("s5T2")], writes=[buf("s5fin")])
        s5_fir_inter(S, Wk, nc_, zin_ap(tok0, ntok), buf(f"zinP{tok0 // PZ}"))

    def s5_sample(S, Wk):
        nc_ = 32
        Ssb, t1, t2, Xn, Xin = (Wk[k] for k in ("Ssb", "t1", "t2", "Xn", "Xin"))
        pA, pB, bA, bB = zb(1, 0), zb(1, 1), "pZ10", "pZ11"
        P.dma("sp", S5X0, s5x0_d[:, :, :], writes=[buf("s5x0s")])
        Sv = Ssb[:, 0:32].rearrange("p (q c) -> p q c", c=2)
        Xv = Xin[:, :, 0:32].rearrange("p g (q c) -> p g q c", c=2)
        for g in range(8):
            s5_states(S, Wk, nc_, g, pA, bA)
            P.op("dve", lambda e: e.tensor_copy(out=Ssb[:, 0:nc_], in_=pA[:, 0:nc_]), reads=[buf(bA)], writes=[buf("s5Ssb")])
            Z0 = S5X0[:, g, :]
            X1 = Xn[:, 0:16]
            Fn = Xn[:, 16:32]
            P.op("pe", lambda e, Z0=Z0: e.matmul(pB[:, 0:16], lhsT=S["PiT"], rhs=Z0, start=True, stop=True),
                 reads=[buf("s5x0s"), buf("s5PiT")], writes=[buf(bB)])
            P.op("dve", lambda e, g=g: e.tensor_scalar(out=t1[:, 0:16], in0=pB[:, 0:16], scalar1=S["ai8"][:, g:g + 1], scalar2=None,
                                                       op0=ALU.mult), reads=[buf(bB), buf("s5ai8")], writes=[buf("s5t1")])
            P.op("dve", lambda e, g=g, Z0=Z0: e.scalar_tensor_tensor(out=X1, in0=Z0, scalar=S["ar8"][:, g:g + 1], in1=t1[:, 0:16],
                                                                      op0=ALU.mult, op1=ALU.add),
                 reads=[buf("s5x0s"), buf("s5t1"), buf("s5ar8")], writes=[buf("s5Xn")])
            P.op("dve", lambda e: e.tensor_tensor(out=X1, in0=X1, in1=Sv[:, :, 0], op=ALU.add),
                 reads=[buf("s5Xn"), buf("s5Ssb")], writes=[buf("s5Xn")])
            P.op("pe", lambda e: e.matmul(pB[:, 0:16], lhsT=S["PiT"], rhs=X1, start=True, stop=True),
                 reads=[buf("s5Xn"), buf("s5PiT")], writes=[buf(bB)])
            P.op("dve", lambda e, g=g: e.tensor_scalar(out=t1[:, 0:16], in0=pB[:, 0:16], scalar1=S["ai8"][:, g:g + 1], scalar2=None,
                                                       op0=ALU.mult), reads=[buf(bB), buf("s5ai8")], writes=[buf("s5t1")])
            P.op("dve", lambda e, g=g: e.scalar_tensor_tensor(out=Fn, in0=X1, scalar=S["ar8"][:, g:g + 1], in1=t1[:, 0:16],
                                                              op0=ALU.mult, op1=ALU.add),
                 reads=[buf("s5Xn"), buf("s5t1"), buf("s5ar8")], writes=[buf("s5Xn")])
            P.op("dve", lambda e, g=g: e.tensor_tensor(out=S5FINS[:, g, :], in0=Fn, in1=Sv[:, :, 1], op=ALU.add),
                 reads=[buf("s5Xn"), buf("s5Ssb")], writes=[buf("s5fins")])
            P.op("dve", lambda e, g=g, Z0=Z0: e.tensor_copy(out=Xv[:, g, :, 0], in_=Z0), reads=[buf("s5x0s")], writes=[buf("s5Xin")])
            P.op("dve", lambda e, g=g: e.tensor_copy(out=Xv[:, g, :, 1], in_=X1), reads=[buf("s5Xn")], writes=[buf("s5Xin")])
        s5_fir_inter(S, Wk, nc_, zin_ap(TP, 256), buf(f"zinP{TP // PZ}"))

    GROUPS = [[0, 1, 2, 3], [4, 5, 6, 7]]

    def ag1(pi):
        P.coll(lambda e: e.collective_compute("AllGather", ALU.bypass, replica_groups=GROUPS,
                                              ins=[zin_p[pi]], outs=[zall_p[pi]]),
               reads=[buf(f"zinP{pi}")], writes=[buf(f"zallP{pi}")])

    def phase_a_tile(ti, sample=False):
        NS = 2 if sample else 4
        N = NS * 128
        t0 = TP if sample else ti * TW
        j0 = NBLK if sample else ti * 4
        par = ti % 2
        xsrc = xs_d if sample else x_d
        xrow0 = 0 if sample else t0
        ko, vo, lo = (ks_o, vs_o, lfs_o) if sample else (k_o, v_o, lf_o)
        orow0 = 0 if sample else t0
        XT = xT[par]
        bXT = buf(f"xT{par}")
        for s in range(NS):
            slot = (ti * 4 + s) % XR
            bx = buf(f"xr{slot}")
            P.dma("sp", xr[slot], xsrc[xrow0 + s * 128:xrow0 + (s + 1) * 128, :], writes=[bx])
            P.op("act", lambda e, slot=slot, s=s: e.activation(
                out=junk, in_=xr[slot], func=AF.Square, accum_out=ss[:, s:s + 1]),
                reads=[bx], writes=[buf("junk"), buf("ss")], fuse=False, cost=1.2)
        P.op("act", lambda e: e.activation(out=lnt[:, 0:NS], in_=ss[:, 0:NS], func=AF.Ln, bias=EPS, scale=1.0 / D_MODEL),
             reads=[buf("ss")], writes=[buf("lnt")])
        RS = rstd[par]
        bRS = buf(f"rstd{par}")
        P.op("act", lambda e: e.activation(out=RS[:, 0:NS], in_=lnt[:, 0:NS], func=AF.Exp, scale=-0.5),
             reads=[buf("lnt")], writes=[bRS])
        for s in range(NS):
            slot = (ti * 4 + s) % XR
            bx = buf(f"xr{slot}")
            xp = s % 2
            P.op("dve", lambda e, slot=slot, s=s, xp=xp: e.tensor_scalar(
                out=xs[xp], in0=xr[slot], scalar1=RS[:, s:s + 1], scalar2=None, op0=ALU.mult),
                reads=[bx, bRS], writes=[buf(f"xs{xp}")])
            for kc in range(8):
                P.op("pe", lambda e, xp=xp, kc=kc: e.transpose(
                    out=pT[xp][:, kc, :], in_=xs[xp][:, kc * 128:(kc + 1) * 128], identity=identb),
                    reads=[buf(f"xs{xp}"), buf("identb")], writes=[buf(f"pT{xp}")])
            if s % 2 == 0:
                P.op("dve", lambda e, xp=xp, s=s: e.tensor_copy(out=XT[:, :, s * 128:(s + 1) * 128], in_=pT[xp]),
                     reads=[buf(f"pT{xp}")], writes=[bXT])
            else:
                P.op("act", lambda e, xp=xp, s=s: e.activation(
                    out=XT[:, :, s * 128:(s + 1) * 128], in_=pT[xp], func=AF.Copy),
                    reads=[buf(f"pT{xp}")], writes=[bXT])
            yield "front"
        yield "FRONT_DONE"

        pp_i = [0]

        def proj(col0):
            i = pp_i[0] % 2
            pp_i[0] += 1
            for kc in range(8):
                P.op("pe", lambda e, kc=kc, i=i: e.matmul(
                    pP[i][:, 0:N], lhsT=Wb[:, kc, col0:col0 + 128], rhs=XT[:, kc, 0:N], start=(kc == 0), stop=(kc == 7)),
                    reads=[bXT, buf("Wb")], writes=[buf(f"pP{i}")])
            return i

        def headnorm2(items):
            for k_, (i, gain, gain_buf, out_ap, out_buf) in enumerate(items):
                P.op("act", lambda e, i=i, k_=k_: e.activation(out=sq2[k_][:, 0:N], in_=pP[i][:, 0:N], func=AF.Square),
                     reads=[buf(f"pP{i}")], writes=[buf(SQN[k_])])
            for k_, (i, gain, gain_buf, out_ap, out_buf) in enumerate(items):
                P.op("pe", lambda e, k_=k_: e.matmul(pMs[k_][:, 0:N], lhsT=BO, rhs=sq2[k_][:, 0:N], start=True, stop=True),
                     reads=[buf(SQN[k_]), buf("BO")], writes=[buf(pMn[k_])])
            for k_, (i, gain, gain_buf, out_ap, out_buf) in enumerate(items):
                P.op("act", lambda e, k_=k_: e.activation(out=ln2[k_][:, 0:N], in_=pMs[k_][:, 0:N], func=AF.Ln, bias=EPS, scale=1.0),
                     reads=[buf(pMn[k_])], writes=[buf(LNN[k_])])
            for k_, (i, gain, gain_buf, out_ap, out_buf) in enumerate(items):
                P.op("act", lambda e, k_=k_: e.activation(out=rr2[k_][:, 0:N], in_=ln2[k_][:, 0:N], func=AF.Exp, scale=-0.5),
                     reads=[buf(LNN[k_])], writes=[buf(RRN[k_])])
            for k_, (i, gain, gain_buf, out_ap, out_buf) in enumerate(items):
                P.op("dve", lambda e, i=i, k_=k_, gain=gain, out_ap=out_ap: e.scalar_tensor_tensor(
                    out=out_ap, in0=pP[i][:, 0:N], scalar=gain, in1=rr2[k_][:, 0:N], op0=ALU.mult, op1=ALU.mult),
                    reads=[buf(f"pP{i}"), buf(RRN[k_]), gain_buf], writes=[out_buf])

        sq2, ln2, rr2 = [sq, sqB], [lnb, lnbB], [rr, rrB]
        SQN, LNN, RRN = ["sq0", "sgb0"], ["lnb0", "kst"], ["rr0", "vst"]
        pMs, pMn = [pM, pK.rearrange("p s f -> p (s f)")], ["pM", "pK"]
        iq = proj(0)
        ik = proj(128)
        headnorm2([(iq, qg8[:, 0:1], buf("qg8"), qnb[:, 0:N], buf("qnb")),
                   (ik, kg[:, 0:1], buf("kg"), knf[:, 0:N], buf("knf"))])
        for h in range(2):
            P.dma("pool", qt_d[h, 0:64, t0:t0 + N], qnb[h * 64:(h + 1) * 64, 0:N],
                  reads=[buf("qnb")], writes=[buf(f"qt_d{ti}")])
        yield "back"
        P.op("act", lambda e: e.activation(out=knb[:, 0:N], in_=knf[:, 0:N], func=AF.Copy),
             reads=[buf("knf")], writes=[buf("knb")])
        for h in range(2):
            P.dma("pool", kt_d[h, :, t0:t0 + N], knb[h * 64:(h + 1) * 64, 0:N],
                  reads=[buf("knb")], writes=[buf(f"kt_d{ti}")])
        for s in range(NS):
            P.op("pe", lambda e, s=s: e.transpose(out=pK[:, s, :], in_=knf[:, s * 128:(s + 1) * 128], identity=identf),
                 reads=[buf("knf"), buf("identf")], writes=[buf("pK")])
        P.op("dve", lambda e: e.tensor_copy(out=kst[:, 0:NS, :], in_=pK[:, 0:NS, :]), reads=[buf("pK")], writes=[buf("kst")])
        out_dmas.append(P.dma("pool", ko[orow0:orow0 + N, :].rearrange("(s p) f -> p s f", p=128), kst[:, 0:NS, :],
                              reads=[buf("kst")]))
        yield "back"
        for s in range(NS):
            for kc in range(8):
                P.op("pe", lambda e, s=s, kc=kc: e.matmul(
                    pV[:, s, :], lhsT=XT[:, kc, s * 128:(s + 1) * 128], rhs=Wb[:, kc, 256:384],
                    start=(kc == 0), stop=(kc == 7)),
                    reads=[bXT, buf("Wb")], writes=[buf("pV")])
        for s in range(NS):
            for kc in range(8):
                P.op("pe", lambda e, s=s, kc=kc: e.matmul(
                    pS[:, 2 * s:2 * s + 2], lhsT=XT[:, kc, s * 128:(s + 1) * 128], rhs=Wb[:, kc, 768:770],
                    start=(kc == 0), stop=(kc == 7)),
                    reads=[bXT, buf("Wb")], writes=[buf("pS")])
        P.op("act", lambda e: e.activation(out=vst[:, 0:NS, :], in_=pV[:, 0:NS, :], func=AF.Copy),
             reads=[buf("pV")], writes=[buf("vst")])
        out_dmas.append(P.dma("pool", vo[orow0:orow0 + N, :].rearrange("(s p) f -> p s f", p=128), vst[:, 0:NS, :],
                              reads=[buf("vst")]))
        P.op("dve", lambda e: e.tensor_copy(
            out=VP[:, j0:j0 + NS, :, 0:64], in_=pV[:, 0:NS, :].rearrange("p s (h d) -> p s h d", h=2)),
            reads=[buf("pV")], writes=[buf("VP")])
        yield "back"
        pSv = pS[:, 0:2 * NS].rearrange("p (s h) -> p s h", h=2)
        for h in range(2):
            P.op("act", lambda e, h=h: e.activation(
                out=ef[:, 0:NS, h], in_=pSv[:, :, h], func=AF.Exp, bias=nbf[:, h:h + 1], scale=-1.0),
                reads=[buf("pS"), buf("nbf")], writes=[buf("ef")])
        P.op("act", lambda e: e.activation(out=lf[:, 0:NS, :], in_=ef[:, 0:NS, :], func=AF.Ln, bias=1.0, scale=1.0),
             reads=[buf("ef")], writes=[buf("lf")])
        P.op("dve", lambda e: e.tensor_scalar(out=lf[:, 0:NS, :], in0=lf[:, 0:NS, :], scalar1=-1.0, scalar2=None, op0=ALU.mult),
             reads=[buf("lf")], writes=[buf("lf")])
        out_dmas.append(P.dma("pool", lo[orow0:orow0 + N, :].rearrange("(s p) h -> p s h", p=128), lf[:, 0:NS, :],
                              reads=[buf("lf")]))
        yield "back"
        for s in range(NS):
            P.op("pe", lambda e, s=s: e.transpose(out=pM[0:2, s * 128:(s + 1) * 128], in_=lf[:, s, :],
                                                   identity=identf),
                 reads=[buf("lf"), buf("identf")], writes=[buf("pM")])
        CR = cumrow[par]
        CRp = cumrow[1 - par]
        init = 0.0 if (ti == 0 or sample) else CRp[:, TW - 1:TW]
        d0 = segm if sample else ones2
        P.op("dve", lambda e: e.tensor_tensor_scan(
            out=CR[:, 0:N], data0=d0[:, 0:N], data1=pM[0:2, 0:N], initial=init, op0=ALU.mult, op1=ALU.add),
            reads=[buf("pM"), buf("ones2"), buf("segm"), buf(f"cumrow{1 - par}")], writes=[buf(f"cumrow{par}")])
        P.op("dve", lambda e: e.tensor_copy(out=cumb[:, 0:N], in_=CR[:, 0:N]),
             reads=[buf(f"cumrow{par}")], writes=[buf("cumb")])
        for h in range(2):
            P.dma("pool", qt_d[h, 64:65, t0:t0 + N], cumb[h:h + 1, 0:N],
                  reads=[buf("cumb")], writes=[buf(f"qt_d{ti}")])
        for s in range(NS):
            P.op("pe", lambda e, s=s: e.transpose(out=pS[:, 16 + 2 * s:16 + 2 * s + 2],
                                                   in_=CR[:, s * 128:(s + 1) * 128], identity=identf[0:2, 0:2]),
                 reads=[buf(f"cumrow{par}"), buf("identf")], writes=[buf("pS")])
        P.op("dve", lambda e: e.tensor_scalar(
            out=NCK[:, j0:j0 + NS, :], in0=pS[:, 16:16 + 2 * NS].rearrange("p (s h) -> p s h", h=2),
            scalar1=-1.0, scalar2=None, op0=ALU.mult),
            reads=[buf("pS")], writes=[buf("NCK")])
        yield "back"
        iga = proj(384)
        igs = proj(640)
        silu2_from_psum([(iga, sgb[0], buf("sgb0")), (igs, sgb[1], buf("sgb1"))], N,
                        [(lnb, "lnb0", rr, "rr0"), (lnbB, "kst", rrB, "vst")])
        P.dma("pool", sgs_d[:, t0:t0 + N], sgb[1][:, 0:N], reads=[buf("sgb1")], writes=[buf(f"sgs_d{ti}")])
        for h in range(2):
            P.dma("pool", sga_d[h, :, t0:t0 + N], sgb[0][h * 64:(h + 1) * 64, 0:N],
                  reads=[buf("sgb0")], writes=[buf(f"sga_d{ti}")])
        yield "back"
        i = proj(512)
        uo = 0 if sample else (ti % 4) * 64
        P.op("act", lambda e, i=i: e.activation(out=WK["ub"][:, :, uo:uo + N // 8],
                                                in_=pP[i][:, 0:N].rearrange("p (c j) -> p j c", j=8), func=AF.Copy),
             reads=[buf(f"pP{i}")], writes=[buf("s5ub")])
        yield "back"
        if sample:
            P.cur_deps = tuple(setup_done)
            s5_sample(S5S, WK)
            P.cur_deps = ()
            if "X" in stages:
                ag1(TP // PZ)
        elif ti % 4 == 3 or ti == NTILE - 1:
            sti = ti // 4
            tok0 = sti * 2048
            ntok = t0 + TW - tok0
            tf = min(L_REAL, TP)
            fk = None
            if (tf - 1) // 2048 == sti:
                fk = (tf - tok0) // 8
            if sti == 0:
                P.cur_deps = tuple(setup_done)
            s5_supertile(S5S, WK, sti, ntok, tok0, fk)
            P.cur_deps = ()
            if "X" in stages and (tok0 + ntok) % PZ == 0:
                ag1(tok0 // PZ)

    if "A" in stages:
        S5S = s5_alloc_weights()
        XR = 4
        xr = [aa([128, D_MODEL], F32) for i in range(XR)]
        junk = aa([128, D_MODEL], BF16)
        xs = [aa([128, D_MODEL], BF16) for i in range(2)]
        xT = [aa([128, 8, TW], BF16) for i in range(2)]
        ss = aa([128, 4], F32)
        lnt = aa([128, 4], F32)
        rstd = [aa([128, 4], F32) for i in range(2)]
        sq = aa([128, TW], BF16)
        lnb = aa([128, TW], F32)
        rr = aa([128, TW], F32)
        qnb = aa([128, TW], BF16)
        knf = aa([128, TW], F32)
        knb = aa([128, TW], BF16)
        kst = aa([128, 4, 128], F32)
        vst = aa([128, 4, 128], F32)
        ef = aa([128, 4, 2], F32)
        lf = aa([128, 4, 2], F32)
        cumb = aa([2, TW], BF16)
        sgb = [aa([128, TW], BF16) for i in range(2)]
        sqB = sgb[0]
        lnbB = kst.rearrange("p s f -> p (s f)")
        rrB = vst.rearrange("p s f -> p (s f)")

        mark2 = ar_off[0]
        s5_setup(S5S)
        setup_extent[0] = ar_off[0]
        setup_done = [buf("s5tmp").w, buf("Wb").w]
        ar_off[0] = mark2
        WK = s5_alloc_work()
        gens = [phase_a_tile(ti) for ti in range(NTILE)] + [phase_a_tile(NTILE, sample=True)]
        front_done = [False] * len(gens)

        def step(gi):
            try:
                r = next(gens[gi])
            except StopIteration:
                return False
            if r == "FRONT_DONE":
                front_done[gi] = True
            return True

        while not front_done[0]:
            step(0)
        for gi in range(len(gens)):
            alive = True
            while alive:
                alive = step(gi)
                if gi + 1 < len(gens) and not front_done[gi + 1]:
                    step(gi + 1)
            if gi + 1 < len(gens):
                while not front_done[gi + 1]:
                    step(gi + 1)
        out_dmas.append(P.dma("sp", s5fin_o[:, :], S5FIN, reads=[buf("s5fin")]))
        out_dmas.append(P.dma("sp", s5fins_o[:, :, :], S5FINS, reads=[buf("s5fins")]))

    n_real = min(L_REAL, TP)
    qbs = []
    q0 = 0
    while q0 < n_real:
        ql = min(QB, n_real - q0)
        qbs.append((q0, ql))
        q0 += ql

    def tiles_of(a, b):
        return range(a // TW, (b + TW - 1) // TW)

    step = [0]

    def attend(qi, h, q0, qlen):
        par = qi % 2
        nkb = (q0 + qlen + 127) // 128
        halves = [(a, min(a + 512, qlen)) for a in range(0, qlen, 512)]
        pO = Z[2]
        bQ = buf(f"QA{par}{h}")
        bKT = buf(f"KT{h}")
        base = step[0]
        step[0] += nkb

        def clo(j):
            return max(0, 128 * j - q0)

        def zbufs(zi, lo, hi):
            return [buf(f"pZ{zi}{hf}") for hf in range(2) if lo < (hf + 1) * 512 and hi > hf * 512]

        def qk(j):
            zi = (base + j) % 2
            c_lo = clo(j)
            for (a, b) in halves:
                lo = max(a, c_lo)
                if lo >= b:
                    continue
                diag = (128 * j >= q0) and (a <= c_lo < b)
                P.op("pe", lambda e, zi=zi, lo=lo, b=b, diag=diag: e.matmul(
                    Z[zi][:, lo:b], lhsT=KT[h][:, 128 * j:128 * j + 128], rhs=QA[par][h][:, lo:b],
                    start=True, stop=not diag),
                    reads=[bKT, bQ], writes=zbufs(zi, lo, b))
                if diag:
                    w = min(128, qlen - c_lo)
                    P.op("pe", lambda e, zi=zi, c_lo=c_lo, w=w: e.matmul(
                        Z[zi][:, c_lo:c_lo + w], lhsT=identb, rhs=MN[:, 0:w], start=False, stop=True),
                        reads=[buf("identb"), buf("MN")], writes=zbufs(zi, c_lo, c_lo + w))

        def ex(j):
            zi = (base + j) % 2
            pi = (base + j) % 3
            c_lo = clo(j)
            P.op("act", lambda e, zi=zi, pi=pi, c_lo=c_lo: e.activation(
                out=PT[pi][:, c_lo:qlen], in_=Z[zi][:, c_lo:qlen], func=AF.Exp,
                bias=NCK[:, j, h:h + 1], scale=1.0),
                reads=zbufs(zi, c_lo, qlen) + [buf("NCK")], writes=[buf(f"PT{pi}")], cost=0.25 + (qlen - c_lo) / 1200.0)

        def pv(j):
            pi = (base + j) % 3
            c_lo = clo(j)
            for (a, b) in halves:
                lo = max(a, c_lo)
                if lo >= b:
                    continue
                j_last = min(nkb - 1, (q0 + b - 1) // 128)
                P.op("pe", lambda e, pi=pi, lo=lo, b=b, j_last=j_last: e.matmul(
                    pO[0:65, lo:b], lhsT=VP[:, j, h, :], rhs=PT[pi][:, lo:b],
                    start=(j == 0), stop=(j == j_last)),
                    reads=[buf("VP"), buf(f"PT{pi}")], writes=zbufs(2, lo, b))

        qk(0)
        for j in range(nkb):
            if j + 1 < nkb:
                qk(j + 1)
            ex(j)
            pv(j)
        ob = osb[h]
        bob = buf(f"osb{h}")
        P.op("dve", lambda e: e.tensor_copy(out=ob[:, 0:qlen], in_=pO[0:65, 0:qlen]),
             reads=zbufs(2, 0, qlen), writes=[bob])
        P.op("dve", lambda e: e.reciprocal(out=ob[64:65, 0:qlen], in_=ob[64:65, 0:qlen]),
             reads=[bob], writes=[bob])
        for (a, b) in halves:
            P.op("pe", lambda e, a=a, b=b: e.matmul(pO[0:64, a:b], lhsT=onesP[64:65, 0:64], rhs=ob[64:65, a:b],
                                                    start=True, stop=True),
                 reads=[bob, buf("onesP")], writes=zbufs(2, a, b))
        P.op("dve", lambda e: e.tensor_tensor(out=ob[0:64, 0:qlen], in0=ob[0:64, 0:qlen], in1=pO[0:64, 0:qlen],
                                              op=ALU.mult),
             reads=[bob] + zbufs(2, 0, qlen), writes=[bob])
        P.op("dve", lambda e: e.tensor_tensor(out=attg[h][:, 0:qlen], in0=ob[0:64, 0:qlen],
                                              in1=SG[par][h][:, 0:qlen], op=ALU.mult),
             reads=[bob, buf(f"SG{par}{h}")], writes=[buf(f"attg{h}")])
        P.dma("pool", mixin_ap(h * 64, (h + 1) * 64, q0, qlen), attg[h][:, 0:qlen],
              reads=[buf(f"attg{h}")], writes=[buf(f"mixinP{q0 // PM_}")])

    def sample_attention():
        ar_off[0] = 0
        clf = aa([32, 1024], F32)
        ccum = aa([32, 1024], F32)
        NCKc = aa([128, 8, 32], F32)
        MSK = aa([128, 8, 16], BF16)
        negt = aa([128, 16], F32)
        KTn = [aa([65, 256], BF16) for h in range(2)]
        QAs = [aa([65, 256], BF16) for h in range(2)]
        SGs = [aa([64, 256], BF16) for h in range(2)]
        kst_ = [aa([64, 1024], F32) for i in range(4)]
        vst_ = [aa([128, 8, 64], F32) for i in range(4)]
        KTc = [aa([65, 1024], BF16) for i in range(4)]
        VSc = [aa([128, 8, 65], BF16) for i in range(4)]
        PTs = [aa([128, 9, 16], BF16) for i in range(4)]
        obs_l = [aa([65, 16], F32) for i in range(4)]
        attS = [aa([64, 256], BF16) for h in range(2)]
        P.dma("sp", clf, clf_d[:, :], writes=[buf("clf")])
        P.op("dve", lambda e: e.tensor_tensor_scan(out=ccum, data0=onesP[0:32, 0:1].to_broadcast([32, 1024]), data1=clf,
                                                   initial=0.0, op0=ALU.mult, op1=ALU.add),
             reads=[buf("clf"), buf("onesP")], writes=[buf("ccum")])
        P.op("dve", lambda e: e.tensor_scalar(out=clf, in0=ccum, scalar1=ccum[:, 1023:1024], scalar2=-1.0,
                                              op0=ALU.subtract, op1=ALU.mult),
             reads=[buf("ccum")], writes=[buf("clf")])
        for blk in range(8):
            P.op("pe", lambda e, blk=blk: e.transpose(out=pM[:, blk * 32:(blk + 1) * 32], in_=clf[:, blk * 128:(blk + 1) * 128],
                                                       identity=identf[0:32, 0:32]),
                 reads=[buf("clf"), buf("identf")], writes=[buf("pM")])
        P.op("dve", lambda e: e.tensor_copy(out=NCKc.rearrange("p b r -> p (b r)"), in_=pM[:, 0:256]),
             reads=[buf("pM")], writes=[buf("NCKc")])
        for qq in range(8):
            P.op("pool", lambda e: e.memset(negt, NEG), writes=[buf("negt")])
            P.op("pool", lambda e, qq=qq: e.affine_select(out=negt, in_=negt, pattern=[[0, 16]], compare_op=ALU.is_ge,
                                                          fill=0.0, base=16 * qq - 1, channel_multiplier=-1),
                 reads=[buf("negt")], writes=[buf("negt")])
            P.op("pool", lambda e, qq=qq: e.tensor_copy(out=MSK[:, qq, :], in_=negt), reads=[buf("negt")], writes=[buf("MSK")])
            P.op("pool", lambda e: e.memset(negt, NEG), writes=[buf("negt")])
            P.op("pool", lambda e, qq=qq: e.affine_select(out=negt, in_=negt, pattern=[[-1, 16]], compare_op=ALU.is_gt,
                                                          fill=0.0, base=-16 * qq, channel_multiplier=1),
                 reads=[buf("negt")], writes=[buf("negt")])
            P.op("pool", lambda e, qq=qq: e.tensor_tensor(out=negt, in0=negt, in1=MSK[:, qq, :], op=ALU.add),
                 reads=[buf("negt"), buf("MSK")], writes=[buf("negt")])
            P.op("pool", lambda e, qq=qq: e.tensor_copy(out=MSK[:, qq, :], in_=negt), reads=[buf("negt")], writes=[buf("MSK")])
        for h in range(2):
            P.dma("sp", KTn[h][0:64, :], kt_d[h, :, TP:TP + 256], reads=[buf(f"kt_d{NTILE}")], writes=[buf(f"KTn{h}")])
            P.op("pool", lambda e, h=h: e.memset(KTn[h][64:65, :], 1.0), writes=[buf(f"KTn{h}")])
            P.dma("sp", QAs[h], qt_d[h, :, TP:TP + 256], reads=[buf(f"qt_d{NTILE}")], writes=[buf(f"QAs{h}")])
            P.dma("sp", SGs[h], sga_d[h, :, TP:TP + 256], reads=[buf(f"sga_d{NTILE}")], writes=[buf(f"SGs{h}")])
            for i in range(2):
                pass
        for i in range(4):
            P.op("pool", lambda e, i=i: e.memset(KTc[i][64:65, :], 1.0), writes=[buf(f"KTc{i}")])
            P.op("pool", lambda e, i=i: e.memset(VSc[i][:, :, 64:65], 1.0), writes=[buf(f"VSc{i}")])
        def one_qh(q, h, i):
            if True:
                r = q * 2 + h
                obs = obs_l[i]
                bobs = buf(f"obs{i}")
                P.dma("sp", kst_[i], kc_d[q, h, :, :], writes=[buf(f"kst_{i}")])
                P.dma("sp", vst_[i], vc_d[q, h, :, :].rearrange("(b p) d -> p b d", p=128), writes=[buf(f"vst_{i}")])
                P.op("dve", lambda e, i=i: e.tensor_copy(out=KTc[i][0:64, :], in_=kst_[i]),
                     reads=[buf(f"kst_{i}")], writes=[buf(f"KTc{i}")])
                P.op("pool", lambda e, i=i: e.tensor_copy(out=VSc[i][:, :, 0:64], in_=vst_[i]),
                     reads=[buf(f"vst_{i}")], writes=[buf(f"VSc{i}")])
                zi = i
                Sps = Z[zi][:, 0:144].rearrange("p (b t) -> p b t", t=16)
                qs = slice(q * 16, (q + 1) * 16)
                sb_, qq = q // 8, q % 8
                for blk in range(8):
                    P.op("pe", lambda e, blk=blk, i=i, Sps=Sps, qs=qs: e.matmul(
                        Sps[:, blk, :], lhsT=KTc[i][:, blk * 128:(blk + 1) * 128], rhs=QAs[h][:, qs], start=True, stop=True),
                        reads=[buf(f"KTc{i}"), buf(f"QAs{h}")], writes=[buf(f"pZ{zi}0")])
                P.op("pe", lambda e, Sps=Sps, qs=qs, sb_=sb_: e.matmul(
                    Sps[:, 8, :], lhsT=KTn[h][:, sb_ * 128:(sb_ + 1) * 128], rhs=QAs[h][:, qs], start=True, stop=False),
                    reads=[buf(f"KTn{h}"), buf(f"QAs{h}")], writes=[buf(f"pZ{zi}0")])
                P.op("pe", lambda e, Sps=Sps, qq=qq: e.matmul(Sps[:, 8, :], lhsT=identb, rhs=MSK[:, qq, :], start=False, stop=True),
                     reads=[buf("identb"), buf("MSK")], writes=[buf(f"pZ{zi}0")])
                for blk in range(9):
                    bias = NCKc[:, blk, r:r + 1] if blk < 8 else NCK[:, NBLK + sb_, h:h + 1]
                    P.op("act", lambda e, blk=blk, i=i, Sps=Sps, bias=bias: e.activation(
                        out=PTs[i][:, blk, :], in_=Sps[:, blk, :], func=AF.Exp, bias=bias, scale=1.0),
                        reads=[buf(f"pZ{zi}0"), buf("NCKc"), buf("NCK")], writes=[buf(f"PTs{i}")])
                pO = Z[i][0:65, 512:528]
                for blk in range(9):
                    lhs = VSc[i][:, blk, :] if blk < 8 else VP[:, NBLK + sb_, h, :]
                    P.op("pe", lambda e, blk=blk, i=i, lhs=lhs, pO=pO: e.matmul(pO, lhsT=lhs, rhs=PTs[i][:, blk, :],
                                                                            start=(blk == 0), stop=(blk == 8)),
                         reads=[buf(f"VSc{i}"), buf("VP"), buf(f"PTs{i}")], writes=[buf(f"pZ{i}1")])
                P.op("dve", lambda e, pO=pO: e.tensor_copy(out=obs, in_=pO), reads=[buf(f"pZ{i}1")], writes=[bobs])
                P.op("dve", lambda e: e.reciprocal(out=obs[64:65, :], in_=obs[64:65, :]), reads=[bobs], writes=[bobs])
                P.op("pe", lambda e, i=i: e.matmul(Z[i][0:64, 512:528], lhsT=onesP[64:65, 0:64], rhs=obs[64:65, :],
                                                   start=True, stop=True),
                     reads=[bobs, buf("onesP")], writes=[buf(f"pZ{i}1")])
                P.op("dve", lambda e, i=i: e.tensor_tensor(out=obs[0:64, :], in0=obs[0:64, :], in1=Z[i][0:64, 512:528], op=ALU.mult),
                     reads=[bobs, buf(f"pZ{i}1")], writes=[bobs])
                P.op("dve", lambda e, qs=qs: e.tensor_tensor(out=attS[h][:, qs], in0=obs[0:64, :], in1=SGs[h][:, qs], op=ALU.mult),
                     reads=[bobs, buf(f"SGs{h}")], writes=[buf(f"attS{h}")])
        it = 0
        for q in range(16):
            for h in range(2):
                one_qh(q, h, it % 4)
                it += 1
        for h in range(2):
            P.dma("pool", mixin_ap(h * 64, (h + 1) * 64, TP, 256), attS[h], reads=[buf(f"attS{h}")],
                  writes=[buf(f"mixinP{TP // PM_}")])

    tiles_x = [(ti * TW, TW) for ti in range(NTILE)] + [(TP, 256)]

    def g_alloc():
        G_ = {}
        G_['wgs'] = aa([128, 4, 128], F32)
        G_['wg'] = aa([128, 4, 128], BF16)
        G_['bg'] = aa([128, 1], F32)
        G_['nbg'] = aa([128, 1], F32)
        G_['zl'] = [aa([128, 4, TW], BF16) for i in range(2)]
        G_['zm'] = [aa([128, TW], BF16) for i in range(2)]
        G_['sgm'] = [aa([128, TW], BF16) for i in range(2)]
        G_['tg'] = aa([128, TW], F32)
        G_['tg2'] = aa([128, TW], F32)
        G_['s5o'] = [aa([128, TW], BF16) for i in range(2)]
        return G_

    def g_setup(G_):
        wgs, wg, bg, nbg = G_["wgs"], G_["wg"], G_["bg"], G_["nbg"]
        P.dma("sp", wgs, wglu_d.rearrange("(kc p) f -> p kc f", p=128), writes=[buf("wgs")])
        P.dma("sp", bg, bglu_d[:, :], writes=[buf("bg")])
        P.op("dve", lambda e: e.tensor_copy(out=wg, in_=wgs), reads=[buf("wgs")], writes=[buf("wg")])
        P.op("dve", lambda e: e.tensor_scalar(out=nbg, in0=bg, scalar1=-1.0, scalar2=None, op0=ALU.mult),
             reads=[buf("bg")], writes=[buf("nbg")])

    def g_tile(G_, k):
        wg, nbg, zl, zm, sgm, tg, tg2, s5o = (G_[x] for x in ("wg", "nbg", "zl", "zm", "sgm", "tg", "tg2", "s5o"))
        pG = zb(3, 0)
        c0, n = tiles_x[k]
        if True:
            i = k % 2
            P.dma("sp", zl[i][:, :, 0:n], zall_ap(c0, n).rearrange("(kc p) t -> p kc t", p=128),
                  reads=[buf(f"zallP{c0 // PZ}")], writes=[buf(f"zl{i}")])
            P.dma("sp", zm[i][:, 0:n], zin_ap(c0, n), reads=[buf(f"zinP{c0 // PZ}")], writes=[buf(f"zm{i}")])
            P.dma("sp", sgm[i][:, 0:n], sgs_d[:, c0:c0 + n], reads=[buf(f"sgs_d{k}")], writes=[buf(f"sgm{i}")])
            for kc in range(4):
                P.op("pe", lambda e, i=i, kc=kc, n=n: e.matmul(pG[:, 0:n], lhsT=wg[:, kc, :], rhs=zl[i][:, kc, 0:n],
                                                               start=(kc == 0), stop=(kc == 3)),
                     reads=[buf(f"zl{i}"), buf("wg")], writes=[buf("pZ30")])
            P.op("act", lambda e, i=i, n=n: e.activation(out=tg[:, 0:n], in_=pG[:, 0:n], func=AF.Exp, bias=nbg[:, 0:1],
                                                         scale=-1.0), reads=[buf("pZ30"), buf("nbg")], writes=[buf("tg")])
            P.op("act", lambda e, n=n: e.activation(out=tg[:, 0:n], in_=tg[:, 0:n], func=AF.Ln, bias=1.0, scale=1.0),
                 reads=[buf("tg")], writes=[buf("tg")])
            P.op("act", lambda e, n=n: e.activation(out=tg[:, 0:n], in_=tg[:, 0:n], func=AF.Exp, scale=-1.0),
                 reads=[buf("tg")], writes=[buf("tg")])
            P.op("dve", lambda e, i=i, n=n: e.tensor_tensor(out=tg2[:, 0:n], in0=zm[i][:, 0:n], in1=tg[:, 0:n], op=ALU.mult),
                 reads=[buf("tg"), buf(f"zm{i}")], writes=[buf("tg2")])
            P.op("dve", lambda e, i=i, n=n: e.tensor_tensor(out=s5o[i][:, 0:n], in0=tg2[:, 0:n], in1=sgm[i][:, 0:n], op=ALU.mult),
                 reads=[buf("tg2"), buf(f"sgm{i}")], writes=[buf(f"s5o{i}")])
            P.dma("pool", mixin_ap(128, 256, c0, n), s5o[i][:, 0:n], reads=[buf(f"s5o{i}")], writes=[buf(f"mixinP{c0 // PM_}")])

    def c_alloc():
        C_ = {}
        C_['wos'] = [aa([128, 256], F32) for i in range(2)]
        C_['wo'] = aa([128, 8, 256], BF16)
        C_['ml'] = [aa([128, 8, TW], BF16) for i in range(2)]
        C_['xc'] = [aa([128, 4, 256], F32) for i in range(1)]
        C_['yst'] = [aa([128, 4, 256], F32) for i in range(1)]
        return C_

    def c_setup(C_):
        wos, wo = C_['wos'], C_['wo']
        for kc in range(8):
            s = kc % 2
            P.dma("sp", wos[s], wout_d[kc * 128:(kc + 1) * 128, :], writes=[buf(f"wos{s}")])
            P.op("dve", lambda e, kc=kc, s=s: e.tensor_copy(out=wo[:, kc, :], in_=wos[s]),
                 reads=[buf(f"wos{s}")], writes=[buf("wo")])

    def c_tile(C_, k):
        wo, ml, xc, yst = C_['wo'], C_['ml'], C_['xc'], C_['yst']
        pC = zb(3, 1)
        c0, n = tiles_x[k]
        if True:
            i = k % 2
            ns = n // 128
            P.dma("sp", ml[i][:, :, 0:n], mixall_ap(c0, n).rearrange("(kc p) t -> p kc t", p=128),
                  reads=[buf(f"mixallP{c0 // PM_}")], writes=[buf(f"ml{i}")])
            P.dma("sp", xc[0][:, 0:ns, :], xc_d[c0:c0 + n, :].rearrange("(s p) c -> p s c", p=128), writes=[buf("xc0")])
            for s2 in range(0, ns, 2):
                for s in range(s2, min(s2 + 2, ns)):
                    for kc in range(8):
                        P.op("pe", lambda e, i=i, s=s, kc=kc: e.matmul(
                            pC[:, (s % 2) * 256:(s % 2) * 256 + 256], lhsT=ml[i][:, kc, s * 128:(s + 1) * 128], rhs=wo[:, kc, :],
                            start=(kc == 0), stop=(kc == 7)),
                            reads=[buf(f"ml{i}"), buf("wo")], writes=[buf("pZ31")])
                w2 = min(2, ns - s2)
                P.op("dve", lambda e, s2=s2, w2=w2: e.tensor_tensor(
                    out=yst[0][:, s2:s2 + w2, :], in0=pC.rearrange("p (s c) -> p s c", s=2)[:, 0:w2, :],
                    in1=xc[0][:, s2:s2 + w2, :], op=ALU.add),
                    reads=[buf("pZ31"), buf("xc0")], writes=[buf("yst0")])
            out_dmas.append(P.dma("pool", y_o[c0:c0 + n, :].rearrange("(s p) c -> p s c", p=128), yst[0][:, 0:ns, :],
                                  reads=[buf("yst0")]))

    def ag2(pi):
        P.coll(lambda e: e.collective_compute("AllGather", ALU.bypass, replica_groups=GROUPS,
                                              ins=[mixin_p[pi]], outs=[mixall_p[pi]]),
               reads=[buf(f"mixinP{pi}")], writes=[buf(f"mixallP{pi}")])

    if "SAMP" in stages:
        P.barrier()
        sample_attention()
    P.barrier()
    ar_off[0] = 0
    KT = [aa([65, TP], BF16) for h in range(2)]
    QA = [[aa([65, QB], BF16) for h in range(2)] for p in range(2)]
    SG = [[aa([64, QB], BF16) for h in range(2)] for p in range(2)]
    PT = [aa([128, QB], BF16) for i in range(3)]
    osb = [aa([65, QB], F32) for h in range(2)]
    attg = [aa([64, QB], BF16) for h in range(2)]
    if "ATT" in stages:
        do_x = "X" in stages
        if do_x:
            G_ = g_alloc()
            C_ = c_alloc()
            g_setup(G_)
            c_setup(C_)
        ntl = (n_real + TW - 1) // TW
        for h in range(2):
            P.dma("sp", KT[h][0:64, 0:ntl * TW], kt_d[h, :, 0:ntl * TW],
                  reads=[buf(f"kt_d{ti}") for ti in range(ntl)], writes=[buf(f"KT{h}")])
            P.op("pool", lambda e, h=h: e.memset(KT[h][64:65, :], 1.0), writes=[buf(f"KT{h}")])
        ntile_x = len(tiles_x)
        g_next = [0]
        c_queue = []
        ag2_done = [0]

        def after_unit(u, last):
            if not do_x:
                return
            while g_next[0] < ntile_x and (g_next[0] <= u or last):
                g_tile(G_, g_next[0])
                g_next[0] += 1
            while ag2_done[0] < npm:
                pi = ag2_done[0]
                tok_end = min((pi + 1) * PM_, TX)
                need_tiles = [k for k, (c0, n) in enumerate(tiles_x) if c0 < tok_end]
                need_q = [qi for qi, (q0, ql) in enumerate(qbs) if q0 < tok_end]
                if (max(need_tiles) < g_next[0]) and (max(need_q) * 2 + 1 <= u or last):
                    ag2(pi)
                    ag2_done[0] += 1
                    c_queue.extend([(k, u + 4) for k, (c0, n) in enumerate(tiles_x) if pi * PM_ <= c0 < tok_end])
                else:
                    break
            if c_queue and (c_queue[0][1] <= u or last):
                n_emit = len(c_queue) if last else 1
                for _ in range(n_emit):
                    k, _u = c_queue.pop(0)
                    c_tile(C_, k)

        u = 0
        nunits = 2 * len(qbs)
        for qi, (q0, qlen) in enumerate(qbs):
            par = qi % 2
            for h in range(2):
                rd = [buf(f"qt_d{ti}") for ti in tiles_of(q0, q0 + qlen)]
                P.dma("sp", QA[par][h][:, 0:qlen], qt_d[h, :, q0:q0 + qlen], reads=rd, writes=[buf(f"QA{par}{h}")])
                rd = [buf(f"sga_d{ti}") for ti in tiles_of(q0, q0 + qlen)]
                P.dma("sp", SG[par][h][:, 0:qlen], sga_d[h, :, q0:q0 + qlen], reads=rd, writes=[buf(f"SG{par}{h}")])
            for h in range(2):
                attend(qi, h, q0, qlen)
                after_unit(u, u == nunits - 1)
                u += 1

    P.barrier()
    P.wait("sp", out_dmas)
    if os.environ.get("MK_RESCHED", "1") == "1":
        est = P.reschedule(window=int(os.environ.get('MK_WIN', '128')), hop=float(os.environ.get('MK_HOP', '1.5')))
    stats = P.emit(stack)
    stack.close()
    return nc, stats


_CACHE = {}


def _prep_core(c, I):
    b, hp = c // 4, c % 4
    f32 = np.float32
    x = np.zeros((TP, D_MODEL), f32)
    x[:N_META] = I["meta_tokens"]
    nreal = min(L_REAL, TP)
    x[N_META:nreal] = I["x_prompt"][b][:nreal - N_META]
    w = I["w_in"][0]
    cols = np.concatenate([
        np.arange(128 * hp, 128 * hp + 128), 512 + np.arange(128 * hp, 128 * hp + 128),
        1024 + np.arange(128 * hp, 128 * hp + 128), 1544 + np.arange(128 * hp, 128 * hp + 128),
        2056 + np.arange(128 * hp, 128 * hp + 128), 2568 + np.arange(128 * hp, 128 * hp + 128),
        1536 + np.arange(2 * hp, 2 * hp + 2)])
    m = {
        "x": x,
        "w_in_c": np.ascontiguousarray(w[:, cols]),
        "norm_g": np.ascontiguousarray(I["norm_g"][0].reshape(8, 128).T),
        "b_f": np.ascontiguousarray(np.broadcast_to(I["b_f"][0, 2 * hp:2 * hp + 2][None, :], (128, 2))),
        "qg": np.ascontiguousarray(np.tile(I["q_norm_g"][0], 2)[:, None]),
        "kg": np.ascontiguousarray(np.tile(I["k_norm_g"][0], 2)[:, None]),
    }
    G = slice(8 * hp, 8 * hp + 8)
    are, aim = I["s5_a_re"][0, G], I["s5_a_im"][0, G]
    bre = I["s5_b_re"][0, G].transpose(1, 0, 2)
    bim = I["s5_b_im"][0, G].transpose(1, 0, 2)
    cre = I["s5_c_re"][0, G].transpose(2, 0, 1)
    cim = I["s5_c_im"][0, G].transpose(2, 0, 1)
    m.update({
        "s5_are": np.tile(are.T, (2, 1)),
        "s5_aim": np.tile(aim.T, (2, 1)),
        "s5_ldt": np.broadcast_to(I["s5_log_dt"][0, G][None, :], (128, 8)),
        "s5_d": I["s5_d"][0, G].reshape(128, 1),
        "s5_x1": np.concatenate([bre, bim], 0),
        "s5_x2": np.concatenate([bim, bre], 0),
        "s5_cx1": np.concatenate([cre, cim], 0),
        "s5_cx2": np.concatenate([cim, cre], 0),
    })
    Q = slice(16 * b, 16 * b + 16)
    H2 = slice(2 * hp, 2 * hp + 2)
    xsmp = I["x_sample"][Q].reshape(256, D_MODEL)
    ocols = slice(256 * hp, 256 * hp + 256)
    wo = I["w_out"][0]
    rows = np.concatenate([np.concatenate([np.arange(128 * r, 128 * r + 128), 512 + np.arange(128 * r, 128 * r + 128)])
                           for r in range(4)])
    sre = I["state_s5_re"][0, Q, G, :].transpose(2, 1, 0)
    sim_ = I["state_s5_im"][0, Q, G, :].transpose(2, 1, 0)
    m.update({
        "xsmp": xsmp,
        "clf": I["cache_logf"][0, Q, :, H2].transpose(0, 2, 1).reshape(32, 1024),
        "kcT": I["cache_k"][0, Q, :, H2, :].transpose(0, 2, 3, 1),
        "vc": I["cache_v"][0, Q, :, H2, :].transpose(0, 2, 1, 3),
        "wglu_c": I["w_glu"][0][:, 128 * hp:128 * hp + 128],
        "bglu_c": I["b_glu"][0, 128 * hp:128 * hp + 128][:, None],
        "wout_c": wo[rows][:, ocols],
        "x_c": np.concatenate([x[:, ocols], xsmp[:, ocols]], 0),
        "s5x0": np.concatenate([sre, sim_], 0),
    })
    return {k: np.ascontiguousarray(v, dtype=f32) for k, v in m.items()}


def kernel(**inputs):
    I = {k: np.asarray(v) for k, v in inputs.items()}
    if "nc" not in _CACHE:
        _CACHE["nc"] = build_program()[0]
    nc = _CACHE["nc"]
    in_maps = [_prep_core(c, I) for c in range(8)]
    res = run_bass_kernel_spmd(nc, in_maps, core_ids=list(range(8)))
    R = res.results
    f32 = np.float32
    y_p = np.zeros((2, SEQ, D_MODEL), f32)
    y_s = np.zeros((32, 16, D_MODEL), f32)
    k_p = np.zeros((1, 2, L_REAL, 8, 64), f32)
    v_p = np.zeros((1, 2, L_REAL, 8, 64), f32)
    lf_p = np.zeros((1, 2, L_REAL, 8), f32)
    sr_p = np.zeros((1, 2, 32, 64), f32)
    si_p = np.zeros((1, 2, 32, 64), f32)
    k_s = np.zeros((1, 32, 16, 8, 64), f32)
    v_s = np.zeros((1, 32, 16, 8, 64), f32)
    lf_s = np.zeros((1, 32, 16, 8), f32)
    sr_s = np.zeros((1, 32, 32, 64), f32)
    si_s = np.zeros((1, 32, 32, 64), f32)
    for c in range(8):
        b, hp = c // 4, c % 4
        r = R[c]
        nr = min(L_REAL, TP)
        k_p[0, b, :nr, 2 * hp:2 * hp + 2, :] = r["k_out"][:nr].reshape(nr, 2, 64)
        v_p[0, b, :nr, 2 * hp:2 * hp + 2, :] = r["v_out"][:nr].reshape(nr, 2, 64)
        lf_p[0, b, :nr, 2 * hp:2 * hp + 2] = r["logf_out"][:nr]
        if "y_out" in r:
            cols = slice(256 * hp, 256 * hp + 256)
            G = slice(8 * hp, 8 * hp + 8)
            Q = slice(16 * b, 16 * b + 16)
            yo = r["y_out"]
            y_p[b, :nr - N_META, cols] = yo[N_META:nr]
            y_s[Q, :, cols] = yo[TP:TP + 256].reshape(16, 16, 256)
            k_s[0, Q, :, 2 * hp:2 * hp + 2, :] = r["ks_out"].reshape(16, 16, 2, 64)
            v_s[0, Q, :, 2 * hp:2 * hp + 2, :] = r["vs_out"].reshape(16, 16, 2, 64)
            lf_s[0, Q, :, 2 * hp:2 * hp + 2] = r["lfs_out"].reshape(16, 16, 2)
            fin = r["s5fin_out"]
            sr_p[0, b, G, :] = fin[0:64, :].T
            si_p[0, b, G, :] = fin[64:128, :].T
            fs = r["s5fins_out"]
            sr_s[0, Q, G, :] = fs[0:64].transpose(2, 1, 0)
            si_s[0, Q, G, :] = fs[64:128].transpose(2, 1, 0)
    _CACHE["last"] = R
    return (y_p, y_s, k_p, v_p, lf_p, sr_p, si_p, k_s, v_s, lf_s, sr_s, si_s)
```

```python
import math
import numpy as np
import ml_dtypes
from contextlib import ExitStack
import concourse.bass as bass
import concourse.mybir as mybir
from concourse.bass_utils import run_bass_kernel_spmd

F32 = mybir.dt.float32
BF16 = mybir.dt.bfloat16
AF = mybir.ActivationFunctionType
ALU = mybir.AluOpType

D_MODEL = 1024
SEQ = 16384
N_META = 16
L_REAL = SEQ + N_META
TW = 512
import os
NTILE = int(os.environ.get("MK_NTILE", "33"))
TP = NTILE * TW
NBLK = TP // 128
TX = TP + 256
NCOL = 770
EPS = 1e-6
NEG = -30000.0


class Buf:
    __slots__ = ("name", "w", "rs", "const", "excl")

    def __init__(self, name, const=False, excl=False):
        self.name = name
        self.w = None
        self.rs = []
        self.const = const
        self.excl = excl


class Node:
    __slots__ = ("eng", "fn", "deps", "kind", "sem", "val", "used", "idx", "fuse", "cost", "seg", "fin", "prio")

    def __init__(self, eng, fn, kind):
        self.eng = eng
        self.fn = fn
        self.kind = kind
        self.deps = []
        self.sem = None
        self.val = 0
        self.used = False
        self.fuse = True
        self.cost = None
        self.seg = 0
        self.fin = 0.0
        self.prio = 0


ENGS = ("pe", "act", "dve", "pool", "sp")
N_DMA_SEMS = 48
SEM_ROLL = 30000


class Prog:
    def __init__(self, nc):
        self.nc = nc
        self.q = {e: [] for e in ENGS}
        self.nodes = []
        self.dma_i = 0
        self.dma_j = 0
        self.seg = 0
        self.cur_prio = 0
        self.cur_deps = ()
        self.dma_last = [None] * N_DMA_SEMS

    def _mk(self, eng, fn, kind, reads, writes, extra):
        n = Node(eng, fn, kind)
        seen = set()
        ex = [b for b in reads if b.excl]
        if ex:
            reads = [b for b in reads if not b.excl]
            writes = list(writes) + [b for b in ex if b not in writes]

        def add(d, k):
            if d is None or (id(d), k) in seen:
                return
            seen.add((id(d), k))
            n.deps.append((d, k))
        for b in reads:
            add(b.w, "raw")
        for b in writes:
            add(b.w, "waw")
            for r in b.rs:
                add(r, "war")
        for d in extra:
            add(d, "raw")
        for d in self.cur_deps:
            add(d, "raw")
        for b in reads:
            if not b.const:
                b.rs.append(n)
        for b in writes:
            b.w = n
            b.rs = []
        n.idx = len(self.nodes)
        n.seg = self.seg
        n.prio = self.cur_prio
        self.nodes.append(n)
        self.q[eng].append(n)
        return n

    def op(self, eng, fn, reads=(), writes=(), deps=(), fuse=True, cost=None):
        n = self._mk(eng, fn, "c", reads, writes, deps)
        n.fuse = fuse
        n.cost = cost
        return n

    def dma(self, eng, out, in_, reads=(), writes=(), deps=()):
        half = N_DMA_SEMS // 2
        if eng == "pool":
            k = half + self.dma_j % half
            self.dma_j += 1
        else:
            k = self.dma_i % half
            self.dma_i += 1
        extra = list(deps)
        if self.dma_last[k] is not None:
            extra.append(self.dma_last[k])
        n = self._mk(eng, lambda e: e.dma_start(out=out, in_=in_), "d", reads, writes, extra)
        n.sem = k
        self.dma_last[k] = n
        return n

    def coll(self, fn, reads=(), writes=()):
        return self._mk("pool", fn, "x", reads, writes, ())

    def wait(self, eng, deps):
        return self._mk(eng, None, "w", (), (), deps)

    def barrier(self):
        deps = []
        for e in ENGS:
            for n in reversed(self.q[e]):
                if n.kind == "c":
                    deps.append(n)
                    break
        deps += [n for n in self.dma_last if n is not None]
        deps += [n for n in self.nodes if n.kind == "x"]
        self.seg += 1
        for e in ENGS:
            self.wait(e, deps)
        self.seg += 1

    DEF_COST = {"pe": 0.25, "act": 0.75, "dve": 0.75, "pool": 0.8, "sp": 0.1}

    def reschedule(self, window=48, hop=1.2):
        segs = {}
        for n in self.nodes:
            segs.setdefault(n.seg, []).append(n)
        def c_of(n):
            if n.cost is not None:
                return n.cost
            return 0.0 if n.kind == "w" else (2.5 if n.kind in ("d", "x") else self.DEF_COST[n.eng])
        cp = {}
        for n in reversed(self.nodes):
            cp.setdefault(id(n), c_of(n))
            base = cp[id(n)]
            for d, k in n.deps:
                v = base + (0.05 if d.eng == n.eng else hop) + c_of(d)
                if cp.get(id(d), 0.0) < v:
                    cp[id(d)] = v
        use_cp = os.environ.get("MK_CP", "1") == "1"
        clock = {e: 0.0 for e in ENGS}
        newq = {e: [] for e in ENGS}
        done = set()
        for sg in sorted(segs):
            if all(n.kind == "w" for n in segs[sg]):
                lastc = []
                for e in ENGS:
                    for m in reversed(newq[e]):
                        if m.kind == "c":
                            lastc.append(m)
                            break
                for n in segs[sg]:
                    keep = [(d, k) for d, k in n.deps if d.kind in ("d", "x")]
                    n.deps = keep + [(m, "raw") for m in lastc]
            pend = {e: [n for n in segs[sg] if n.eng == e] for e in ENGS}
            left = sum(len(v) for v in pend.values())
            while left:
                best = None
                for e in ENGS:
                    cand = pend[e]
                    seen = 0
                    for ci in range(len(cand)):
                        n = cand[ci]
                        if not n.prio:
                            seen += 1
                            if seen > window:
                                break
                        ok = True
                        rdy = 0.0
                        for d, k in n.deps:
                            if id(d) not in done:
                                ok = False
                                break
                            t = d.fin + (0.05 if (d.eng == e and d.kind == "c") else hop)
                            if t > rdy:
                                rdy = t
                        if not ok:
                            continue
                        st = max(clock[e], rdy)
                        if use_cp:
                            key = (int((st + (0.8 if n.prio else 0.0)) / float(os.environ.get("MK_BKT", "0.2"))), -cp[id(n)], n.idx)
                        else:
                            key = (st + (0.8 if n.prio else 0.0), n.idx)
                        if best is None or key < best[0]:
                            best = (key, e, ci, n, st)
                        if rdy <= clock[e] and not n.prio and not use_cp:
                            break
                assert best is not None, "scheduler stuck"
                _, e, ci, n, st = best
                pend[e].pop(ci)
                left -= 1
                c = n.cost
                if c is None:
                    c = 0.0 if n.kind == "w" else (2.5 if n.kind in ("d", "x") else self.DEF_COST[e])
                if n.kind in ("d", "x"):
                    clock[e] = st + 0.3
                    n.fin = st + c
                else:
                    clock[e] = st + c
                    n.fin = st + c
                done.add(id(n))
                newq[e].append(n)
        self.q = newq
        return max(clock.values())

    def emit(self, stack):
        nc = self.nc
        for n in self.nodes:
            for d, k in n.deps:
                if d.kind in ("d", "x"):
                    d.used = True
                elif d.eng != n.eng:
                    d.used = True
                elif n.eng != "pe":
                    d.used = True
                elif n.kind == "d":
                    d.used = True
        esems = {e: [stack.enter_context(nc.semaphore(f"s_{e}0"))] for e in ENGS}
        ecnt = {e: 0 for e in ENGS}
        dsems = [stack.enter_context(nc.semaphore(f"s_dma{i}")) for i in range(N_DMA_SEMS)]
        dcnt = [0] * N_DMA_SEMS
        for n in self.nodes:
            if n.kind == "x":
                n.sem = stack.enter_context(nc.semaphore(f"s_cc{n.idx}"))
                n.val = 1
            elif n.kind == "d":
                dcnt[n.sem] += 16
                n.val = dcnt[n.sem]
                n.sem = dsems[n.sem]
        for e in ENGS:
            for n in self.q[e]:
                if n.kind == "c" and n.used:
                    if ecnt[e] >= SEM_ROLL:
                        esems[e].append(stack.enter_context(nc.semaphore(f"s_{e}{len(esems[e])}")))
                        ecnt[e] = 0
                    ecnt[e] += 1
                    n.val = ecnt[e]
                    n.sem = esems[e][-1]
        block = stack.enter_context(nc.Block())
        handles = {"pe": block.tensor, "act": block.scalar, "dve": block.vector,
                   "pool": block.gpsimd, "sp": block.sync}
        stats = {}
        for e in ENGS:
            queue = self.q[e]

            def body(eng, queue=queue, e=e):
                waited = {}
                nw = 0
                for n in queue:
                    pend = []
                    for d, k in n.deps:
                        if d.kind not in ("d", "x"):
                            if d.eng == e and e == "pe" and n.kind != "d":
                                continue
                        if d.sem is None:
                            continue
                        key = id(d.sem)
                        if waited.get(key, 0) >= d.val:
                            continue
                        waited[key] = d.val
                        pend.append((d.sem, d.val))
                        nw += 1
                    best = {}
                    for s_, v_ in pend:
                        if id(s_) not in best or best[id(s_)][1] < v_:
                            best[id(s_)] = (s_, v_)
                    pend = list(best.values())
                    fuse = None
                    if pend and n.kind == "c" and n.fuse and e in ("act", "dve", "pool"):
                        fuse = pend.pop()
                    for s_, v_ in pend:
                        eng.wait_ge(s_, v_)
                    if n.kind == "w":
                        continue
                    ins = n.fn(eng)
                    if fuse is not None:
                        ins._wait_ge(fuse[0], fuse[1])
                    if n.kind == "x":
                        ins.then_inc(n.sem, 1)
                    elif n.kind == "d":
                        ins.then_inc(n.sem, 16)
                    elif n.used:
                        ins.then_inc(n.sem, 1)
                stats[e] = (len(queue), nw)
            handles[e](body)
        return stats


QB = 1024
ARENA_BYTES = 161 * 1024
STAGES = os.environ.get("MK_STAGES", "A,ATT,SAMP,X")


def build_program(debug=False):
    nc = bass.Bass("TRN2", target_bir_lowering=False)
    P = Prog(nc)
    stack = ExitStack()
    stages = set(STAGES.split(","))

    def din(name, shape, dt=F32):
        return nc.dram_tensor(name, list(shape), dt, kind="ExternalInput").ap()

    def dout(name, shape, dt=F32):
        return nc.dram_tensor(name, list(shape), dt, kind="ExternalOutput").ap()

    def dscr(name, shape, dt):
        return nc.dram_tensor(name, list(shape), dt).ap()

    def sb(name, shape, dt):
        return stack.enter_context(nc.sbuf_tensor("sb_" + name, list(shape), dt))[:]

    def ps(name, shape, dt):
        return stack.enter_context(nc.psum_tensor("ps_" + name, list(shape), dt))[:]

    arena = sb("arena", [128, ARENA_BYTES // 4], F32)
    ar_off = [0]

    def aa(shape, dt):
        nfree = int(np.prod(shape[1:]))
        esz = 2 if dt == BF16 else 4
        nbytes = (nfree * esz + 31) // 32 * 32
        w0 = ar_off[0] // 4
        ar_off[0] += nbytes
        assert ar_off[0] <= ARENA_BYTES, ("arena overflow", ar_off[0])
        v = arena[0:shape[0], w0:w0 + nbytes // 4]
        if dt != F32:
            v = v.bitcast(dt)
        v = v[:, 0:nfree]
        if len(shape) == 3:
            v = v.rearrange("p (a b) -> p a b", a=shape[1])
        elif len(shape) == 4:
            v = v.rearrange("p (a b c) -> p a b c", a=shape[1], b=shape[2])
        return v

    x_d = din("x", [TP, D_MODEL])
    w_d = din("w_in_c", [D_MODEL, NCOL])
    ng_d = din("norm_g", [128, 8])
    bf_d = din("b_f", [128, 2])
    qg_d = din("qg", [128, 1])
    kg_d = din("kg", [128, 1])

    S5D = {}
    for nm in ("s5_are", "s5_aim", "s5_ldt"):
        S5D[nm] = din(nm, [128, 8])
    S5D["s5_d"] = din("s5_d", [128, 1])
    for nm in ("s5_x1", "s5_x2", "s5_cx1", "s5_cx2"):
        S5D[nm] = din(nm, [128, 8, 16])
    s5fin_o = dout("s5fin_out", [128, 8])
    PZ, PM_ = 4096, 2048
    npz = (TX + PZ - 1) // PZ
    npm = (TX + PM_ - 1) // PM_
    zin_p = [dscr(f"zin{i}", [128, min(PZ, TX - i * PZ)], BF16) for i in range(npz)]
    zall_p = [dscr(f"zall{i}", [512, min(PZ, TX - i * PZ)], BF16) for i in range(npz)]
    mixin_p = [dscr(f"mixin{i}", [256, min(PM_, TX - i * PM_)], BF16) for i in range(npm)]
    mixall_p = [dscr(f"mixall{i}", [1024, min(PM_, TX - i * PM_)], BF16) for i in range(npm)]

    def zin_ap(c0, n):
        return zin_p[c0 // PZ][:, c0 % PZ:c0 % PZ + n]

    def zall_ap(c0, n):
        return zall_p[c0 // PZ][:, c0 % PZ:c0 % PZ + n]

    def mixin_ap(r0, r1, c0, n):
        return mixin_p[c0 // PM_][r0:r1, c0 % PM_:c0 % PM_ + n]

    def mixall_ap(c0, n):
        return mixall_p[c0 // PM_][:, c0 % PM_:c0 % PM_ + n]
    sgs_d = dscr("sgs_scr", [128, TX], BF16)

    xs_d = din("xsmp", [256, D_MODEL])
    clf_d = din("clf", [32, 1024])
    kc_d = din("kcT", [16, 2, 64, 1024])
    vc_d = din("vc", [16, 2, 1024, 64])
    wglu_d = din("wglu_c", [512, 128])
    bglu_d = din("bglu_c", [128, 1])
    wout_d = din("wout_c", [1024, 256])
    xc_d = din("x_c", [TX, 256])
    s5x0_d = din("s5x0", [128, 8, 16])
    y_o = dout("y_out", [TX, 256])
    ks_o = dout("ks_out", [256, 128])
    vs_o = dout("vs_out", [256, 128])
    lfs_o = dout("lfs_out", [256, 2])
    s5fins_o = dout("s5fins_out", [128, 8, 16])

    k_o = dout("k_out", [TP, 128])
    v_o = dout("v_out", [TP, 128])
    lf_o = dout("logf_out", [TP, 2])

    qt_d = dscr("qt_scr", [2, 65, TX], BF16)
    kt_d = dscr("kt_scr", [2, 64, TX], BF16)
    sga_d = dscr("sga_scr", [2, 64, TX], BF16)

    ng = sb("ng", [128, 8], F32)
    bfp = sb("bfp", [128, 2], F32)
    nbf = sb("nbf", [128, 2], F32)
    qg = sb("qg", [128, 1], F32)
    kg = sb("kg", [128, 1], F32)
    qg8 = sb("qg8", [128, 1], F32)
    onesf = sb("onesf", [128, 128], F32)
    identf = sb("identf", [128, 128], F32)
    identb = sb("identb", [128, 128], BF16)
    BO = sb("BO", [128, 128], BF16)
    MN = sb("MN", [128, 128], BF16)
    ones2 = sb("ones2", [2, TW], F32)
    onesP = sb("onesP", [128, 64], F32)
    VP = sb("VP", [128, NBLK + 2, 2, 65], BF16)
    NCK = sb("NCK", [128, NBLK + 2, 2], F32)
    cumrow = [sb(f"cumrow{i}", [2, TW], F32) for i in range(2)]
    S5FIN = sb("s5fin", [128, 8], F32)
    S5FINS = sb("s5fins", [128, 8, 16], F32)
    S5X0 = sb("s5x0s", [128, 8, 16], F32)
    segm = sb("segm", [2, TW], F32)

    Z = [ps(f"Z{i}", [128, 1024], F32) for i in range(4)]

    def zb(i, half):
        return Z[i][:, half * 512:(half + 1) * 512]

    pT = [zb(0, h).bitcast(BF16).rearrange("p (k t) -> p k t", k=8) for h in range(2)]
    pP = [zb(1, 0), zb(1, 1)]
    pM = zb(2, 0)
    pV = zb(2, 1).rearrange("p (s f) -> p s f", s=4)
    pK = zb(3, 0).rearrange("p (s f) -> p s f", s=4)
    pS = zb(3, 1)
    PALIAS = {"pT0": "pZ00", "pT1": "pZ01", "pP0": "pZ10", "pP1": "pZ11", "pM": "pZ20", "pV": "pZ21",
              "pK": "pZ30", "pS": "pZ31"}

    B = {}

    def buf(name, const=False):
        name = PALIAS.get(name, name)
        name = {"lnb": "lnb0", "rr": "rr0", "sq": "sq0"}.get(name, name)
        if name not in B:
            B[name] = Buf(name, const, excl=name.startswith("pZ"))
        return B[name]

    P.dma("sp", ng, ng_d[:, :], writes=[buf("ng")])
    P.dma("sp", bfp, bf_d[:, :], writes=[buf("bfp")])
    P.dma("sp", qg, qg_d[:, :], writes=[buf("qg")])
    P.dma("sp", kg, kg_d[:, :], writes=[buf("kg")])
    P.op("dve", lambda e: e.tensor_scalar(out=nbf, in0=bfp, scalar1=-1.0, scalar2=None, op0=ALU.mult),
         reads=[buf("bfp")], writes=[buf("nbf")])
    P.op("dve", lambda e: e.tensor_scalar(out=qg8, in0=qg, scalar1=0.125, scalar2=None, op0=ALU.mult),
         reads=[buf("qg")], writes=[buf("qg8")])
    P.op("pool", lambda e: e.memset(onesf, 1.0), writes=[buf("onesf")])
    P.op("pool", lambda e: e.memset(ones2, 1.0), writes=[buf("ones2")])
    P.op("pool", lambda e: e.memset(onesP, 1.0), writes=[buf("onesP")])
    P.op("pool", lambda e: e.memset(segm, 1.0), writes=[buf("segm")])
    P.op("pool", lambda e: e.memset(segm.rearrange("p (q t) -> p q t", t=16)[:, :, 0:1], 0.0), writes=[buf("segm")])
    P.op("pool", lambda e: e.affine_select(out=identf, in_=onesf, pattern=[[-1, 128]],
                                           compare_op=ALU.is_equal, fill=0.0, base=0, channel_multiplier=1),
         reads=[buf("onesf")], writes=[buf("identf")])
    P.op("pool", lambda e: e.tensor_copy(out=identb, in_=identf), reads=[buf("identf")], writes=[buf("identb")])
    P.op("pool", lambda e: e.memset(onesf, NEG), reads=[buf("identf")], writes=[buf("onesf")])
    P.op("pool", lambda e: e.affine_select(out=MN, in_=onesf, pattern=[[-1, 128]],
                                           compare_op=ALU.is_gt, fill=0.0, base=0, channel_multiplier=1),
         reads=[buf("onesf")], writes=[buf("MN")])
    P.op("pool", lambda e: e.memset(BO, 0.0), writes=[buf("BO")])
    P.op("pool", lambda e: e.memset(BO[0:64, 0:64], 1.0 / 64), writes=[buf("BO")])
    P.op("pool", lambda e: e.memset(BO[64:128, 64:128], 1.0 / 64), writes=[buf("BO")])
    P.op("pool", lambda e: e.memset(VP[:, :, :, 64:65], 1.0), writes=[buf("VP")])

    out_dmas = []

    ar_off[0] = 0
    Wb = aa([128, 8, NCOL], BF16)
    s5_mark = [0]
    setup_extent = [0]
    def wb_setup():
      wst = [aa([128, NCOL], F32) for i in range(2)]
      for kc in range(8):
        s = kc % 2
        P.dma("sp", wst[s], w_d[kc * 128:(kc + 1) * 128, :], writes=[buf(f"wst{s}")])
        P.op("dve", lambda e, kc=kc, s=s: e.tensor_scalar(
            out=Wb[:, kc, :], in0=wst[s], scalar1=ng[:, kc:kc + 1], scalar2=None, op0=ALU.mult),
            reads=[buf(f"wst{s}"), buf("ng")], writes=[buf("Wb")])

    def silu2_from_psum(items, N, tmps):
        for (i, o, ob), (l, lb, r, rb) in zip(items, tmps):
            P.op("act", lambda e, i=i, l=l: e.activation(out=l[:, 0:N], in_=pP[i][:, 0:N], func=AF.Exp, scale=-1.0),
                 reads=[buf(f"pP{i}")], writes=[buf(lb)])
        for (i, o, ob), (l, lb, r, rb) in zip(items, tmps):
            P.op("act", lambda e, l=l: e.activation(out=l[:, 0:N], in_=l[:, 0:N], func=AF.Ln, bias=1.0, scale=1.0),
                 reads=[buf(lb)], writes=[buf(lb)])
        for (i, o, ob), (l, lb, r, rb) in zip(items, tmps):
            P.op("act", lambda e, l=l, r=r: e.activation(out=r[:, 0:N], in_=l[:, 0:N], func=AF.Exp, scale=-1.0),
                 reads=[buf(lb)], writes=[buf(rb)])
        for (i, o, ob), (l, lb, r, rb) in zip(items, tmps):
            P.op("dve", lambda e, i=i, o=o, r=r: e.tensor_tensor(out=o[:, 0:N], in0=pP[i][:, 0:N], in1=r[:, 0:N], op=ALU.mult),
                 reads=[buf(f"pP{i}"), buf(rb)], writes=[ob])

    def silu_from_psum(i, out_ap, out_buf, N=TW):
        P.op("act", lambda e: e.activation(out=lnb[:, 0:N], in_=pP[i][:, 0:N], func=AF.Exp, scale=-1.0),
             reads=[buf(f"pP{i}")], writes=[buf("lnb")])
        P.op("act", lambda e: e.activation(out=lnb[:, 0:N], in_=lnb[:, 0:N], func=AF.Ln, bias=1.0, scale=1.0),
             reads=[buf("lnb")], writes=[buf("lnb")])
        P.op("act", lambda e: e.activation(out=rr[:, 0:N], in_=lnb[:, 0:N], func=AF.Exp, scale=-1.0),
             reads=[buf("lnb")], writes=[buf("rr")])
        P.op("dve", lambda e: e.tensor_tensor(out=out_ap[:, 0:N], in0=pP[i][:, 0:N], in1=rr[:, 0:N], op=ALU.mult),
             reads=[buf(f"pP{i}"), buf("rr")], writes=[out_buf])

    def s5_alloc_weights():
        S = {}
        S["are"] = aa([128, 8], F32)
        S["aim"] = aa([128, 8], F32)
        S["ldt"] = aa([128, 8], F32)
        S["dvec"] = aa([128, 1], F32)
        S["rho8"] = aa([128, 8], F32)
        S["ar8"] = aa([128, 8], F32)
        S["ai8"] = aa([128, 8], F32)
        S["PiT"] = aa([128, 128], F32)
        S["Wfir"] = aa([128, 8, 128], BF16)
        S["Wst"] = aa([128, 8, 8, 128], BF16)
        S["Wint"] = aa([128, 8, 8, 128], BF16)
        S["COS"] = aa([128, 8, 256], F32)
        S["SIN"] = aa([128, 8, 256], F32)
        S["X0"] = [aa([128, 8], F32) for i in range(2)]
        return S

    def s5_setup(S):
        wb_setup()
        P.cur_prio = 1
        for nm, dn in (("are", "s5_are"), ("aim", "s5_aim"), ("ldt", "s5_ldt"), ("dvec", "s5_d")):
            P.dma("sp", S[nm], S5D[dn][:, :], writes=[buf("s5" + nm)])
        X1 = aa([128, 8, 16], F32)
        X2 = aa([128, 8, 16], F32)
        CX1 = aa([128, 8, 16], F32)
        CX2 = aa([128, 8, 16], F32)
        P.dma("sp", X1, S5D["s5_x1"][:, :, :], writes=[buf("s5X1")])
        P.dma("sp", X2, S5D["s5_x2"][:, :, :], writes=[buf("s5X2")])
        P.dma("sp", CX1, S5D["s5_cx1"][:, :, :], writes=[buf("s5CX1")])
        P.dma("sp", CX2, S5D["s5_cx2"][:, :, :], writes=[buf("s5CX2")])
        sg1 = aa([128, 1], F32)
        sg2 = aa([128, 1], F32)
        dt = aa([128, 8], F32)
        lr = aa([128, 8], F32)
        th = aa([128, 8], F32)
        mlr = aa([128, 8, 9], F32)
        mth = aa([128, 8, 9], F32)
        mcol = aa([128, 9], F32)
        mag = aa([128, 8, 9], F32)
        sn = aa([128, 8, 9], F32)
        cs = aa([128, 8, 9], F32)
        AR = aa([128, 8, 9], F32)
        AI = aa([128, 8, 9], F32)
        t8a = aa([128, 8], F32)
        t8b = aa([128, 8], F32)
        t8c = aa([128, 8], F32)
        cr = aa([128, 8], F32)
        ci = aa([128, 8], F32)
        Bst = aa([128, 8, 16], F32)
        Bsw = aa([128, 8, 16], F32)
        tB = aa([128, 8, 16], F32)
        CA = aa([128, 9, 8, 16], F32)
        Bpad = aa([128, 8, 128], F32)
        ABp = aa([128, 128], F32)
        iot = aa([128, 256], F32)
        ang = aa([128, 256], F32)
        wk = [aa([128, 256], F32) for _ in range(4)]
        wki = aa([128, 256], mybir.dt.int32)
        bT = "s5tmp"

        def dv(fn, extra_r=(), extra_w=()):
            P.op("dve", fn, reads=[buf(bT)] + list(extra_r), writes=[buf(bT)] + list(extra_w))

        def sincos(ang_ap, sin_out, cos_out, n):
            y, kf, f, g_ = wk[0][:, 0:n], wk[1][:, 0:n], wk[2][:, 0:n], wk[3][:, 0:n]
            ki = wki[:, 0:n]
            dv(lambda e: e.tensor_scalar(out=y, in0=ang_ap, scalar1=1.0 / (2 * math.pi), scalar2=None, op0=ALU.mult))
            dv(lambda e: e.tensor_copy(out=ki, in_=y))
            dv(lambda e: e.tensor_copy(out=kf, in_=ki))
            dv(lambda e: e.tensor_tensor(out=f, in0=y, in1=kf, op=ALU.subtract))
            for shift, dst in ((0.0, sin_out), (0.25, cos_out)):
                if shift:
                    dv(lambda e: e.tensor_scalar(out=f, in0=f, scalar1=shift, scalar2=None, op0=ALU.add))
                dv(lambda e: e.tensor_scalar(out=g_, in0=f, scalar1=0.5, scalar2=None, op0=ALU.is_gt))
                dv(lambda e: e.tensor_tensor(out=f, in0=f, in1=g_, op=ALU.subtract))
                dv(lambda e: e.tensor_scalar(out=g_, in0=f, scalar1=-0.5, scalar2=None, op0=ALU.is_lt))
                dv(lambda e: e.tensor_tensor(out=f, in0=f, in1=g_, op=ALU.add))
                P.op("act", lambda e, dst=dst: e.activation(out=dst, in_=f, func=AF.Sin, scale=2 * math.pi),
                     reads=[buf(bT)], writes=[buf(bT)])

        rd_in = [buf("s5are"), buf("s5aim"), buf("s5ldt"), buf("s5dvec"), buf("s5X1"), buf("s5X2"),
                 buf("s5CX1"), buf("s5CX2"), buf("identf")]
        P.op("pool", lambda e: e.memset(sg1[0:64, :], 1.0), reads=rd_in, writes=[buf(bT)])
        P.op("pool", lambda e: e.memset(sg1[64:128, :], -1.0), writes=[buf(bT)])
        P.op("pool", lambda e: e.memset(sg2[0:64, :], -1.0), writes=[buf(bT)])
        P.op("pool", lambda e: e.memset(sg2[64:128, :], 1.0), writes=[buf(bT)])
        P.op("pool", lambda e: e.memset(S["PiT"], 0.0), writes=[buf(bT), buf("s5PiT")])
        P.op("pool", lambda e: e.memset(Bpad, 0.0), writes=[buf(bT)])
        P.op("pool", lambda e: e.memset(S["Wint"], 0.0), writes=[buf(bT), buf("s5Wint")])
        P.op("pool", lambda e: e.iota(mcol, pattern=[[1, 9]], base=0, channel_multiplier=0,
                                      allow_small_or_imprecise_dtypes=True), writes=[buf(bT)])
        P.op("pool", lambda e: e.iota(iot, pattern=[[1, 256]], base=1, channel_multiplier=0,
                                      allow_small_or_imprecise_dtypes=True), writes=[buf(bT)])
        dv(lambda e: e.tensor_copy(out=S["PiT"][0:64, 64:128], in_=identf[0:64, 0:64]), extra_w=[buf("s5PiT")])
        dv(lambda e: e.tensor_scalar(out=S["PiT"][64:128, 0:64], in0=identf[64:128, 64:128], scalar1=-1.0,
                                     scalar2=None, op0=ALU.mult), extra_w=[buf("s5PiT")])
        P.op("act", lambda e: e.activation(out=dt, in_=S["ldt"], func=AF.Exp), reads=[buf(bT)], writes=[buf(bT)])
        dv(lambda e: e.tensor_tensor(out=lr, in0=S["are"], in1=dt, op=ALU.mult))
        dv(lambda e: e.tensor_tensor(out=th, in0=S["aim"], in1=dt, op=ALU.mult))
        for g in range(8):
            dv(lambda e, g=g: e.tensor_scalar(out=mlr[:, g, :], in0=mcol, scalar1=lr[:, g:g + 1], scalar2=None,
                                              op0=ALU.mult))
            dv(lambda e, g=g: e.tensor_scalar(out=mth[:, g, :], in0=mcol, scalar1=th[:, g:g + 1], scalar2=None,
                                              op0=ALU.mult))
        fl = "p g m -> p (g m)"
        P.op("act", lambda e: e.activation(out=mag.rearrange(fl), in_=mlr.rearrange(fl), func=AF.Exp),
             reads=[buf(bT)], writes=[buf(bT)])
        sincos(mth.rearrange(fl), sn.rearrange(fl), cs.rearrange(fl), 72)
        dv(lambda e: e.tensor_tensor(out=AR.rearrange(fl), in0=mag.rearrange(fl), in1=cs.rearrange(fl), op=ALU.mult))
        dv(lambda e: e.tensor_tensor(out=AI.rearrange(fl), in0=mag.rearrange(fl), in1=sn.rearrange(fl), op=ALU.mult))
        dv(lambda e: e.tensor_copy(out=S["rho8"], in_=mag[:, :, 8]), extra_w=[buf("s5rho8")])
        dv(lambda e: e.tensor_copy(out=S["ar8"], in_=AR[:, :, 8]), extra_w=[buf("s5ar8")])
        dv(lambda e: e.tensor_copy(out=S["ai8"], in_=AI[:, :, 8]), extra_w=[buf("s5ai8")])
        dv(lambda e: e.tensor_scalar(out=t8a, in0=AR[:, :, 1], scalar1=-1.0, scalar2=None, op0=ALU.add))
        dv(lambda e: e.tensor_tensor(out=t8b, in0=S["are"], in1=S["are"], op=ALU.mult))
        dv(lambda e: e.tensor_tensor(out=t8c, in0=S["aim"], in1=S["aim"], op=ALU.mult))
        dv(lambda e: e.tensor_tensor(out=t8b, in0=t8b, in1=t8c, op=ALU.add))
        dv(lambda e: e.reciprocal(out=t8b, in_=t8b))
        dv(lambda e: e.tensor_tensor(out=cr, in0=t8a, in1=S["are"], op=ALU.mult))
        dv(lambda e: e.tensor_tensor(out=t8c, in0=AI[:, :, 1], in1=S["aim"], op=ALU.mult))
        dv(lambda e: e.tensor_tensor(out=cr, in0=cr, in1=t8c, op=ALU.add))
        dv(lambda e: e.tensor_tensor(out=cr, in0=cr, in1=t8b, op=ALU.mult))
        dv(lambda e: e.tensor_tensor(out=ci, in0=AI[:, :, 1], in1=S["are"], op=ALU.mult))
        dv(lambda e: e.tensor_tensor(out=t8c, in0=t8a, in1=S["aim"], op=ALU.mult))
        dv(lambda e: e.tensor_tensor(out=ci, in0=ci, in1=t8c, op=ALU.subtract))
        dv(lambda e: e.tensor_tensor(out=ci, in0=ci, in1=t8b, op=ALU.mult))
        dv(lambda e: e.tensor_scalar(out=X2.rearrange("p g h -> p (g h)"), in0=X2.rearrange("p g h -> p (g h)"),
                                     scalar1=sg2[:, 0:1], scalar2=None, op0=ALU.mult))
        dv(lambda e: e.tensor_scalar(out=CX1.rearrange("p g h -> p (g h)"), in0=CX1.rearrange("p g h -> p (g h)"),
                                     scalar1=sg1[:, 0:1], scalar2=None, op0=ALU.mult))
        for g in range(8):
            dv(lambda e, g=g: e.tensor_scalar(out=tB[:, g, :], in0=X2[:, g, :], scalar1=ci[:, g:g + 1], scalar2=None,
                                              op0=ALU.mult))
            dv(lambda e, g=g: e.scalar_tensor_tensor(out=Bst[:, g, :], in0=X1[:, g, :], scalar=cr[:, g:g + 1],
                                                     in1=tB[:, g, :], op0=ALU.mult, op1=ALU.add))
            dv(lambda e, g=g: e.tensor_scalar(out=tB[:, g, :], in0=X1[:, g, :], scalar1=ci[:, g:g + 1], scalar2=None,
                                              op0=ALU.mult))
            dv(lambda e, g=g: e.scalar_tensor_tensor(out=Bsw[:, g, :], in0=X2[:, g, :], scalar=cr[:, g:g + 1],
                                                     in1=tB[:, g, :], op0=ALU.mult, op1=ALU.subtract))
            dv(lambda e, g=g: e.tensor_copy(out=Bpad[:, g, 16 * g:16 * g + 16], in_=Bst[:, g, :]))
            for m in range(9):
                dv(lambda e, g=g, m=m: e.tensor_scalar(out=tB[:, g, :], in0=CX2[:, g, :], scalar1=AI[:, g, m:m + 1],
                                                       scalar2=None, op0=ALU.mult))
                dv(lambda e, g=g, m=m: e.scalar_tensor_tensor(out=CA[:, m, g, :], in0=CX1[:, g, :],
                                                              scalar=AR[:, g, m:m + 1], in1=tB[:, g, :],
                                                              op0=ALU.mult, op1=ALU.subtract))
        for j in range(8):
            for g in range(8):
                dv(lambda e, j=j, g=g: e.tensor_copy(out=S["Wint"][:, j, g, 16 * g:16 * g + 16], in_=CA[:, j + 1, g, :]),
                   extra_w=[buf("s5Wint")])
        for tau in range(8):
            for g in range(8):
                P.op("pe", lambda e, tau=tau, g=g: e.matmul(pM[:, 16 * g:16 * g + 16], lhsT=Bpad[:, g, :],
                                                            rhs=CA[:, tau, g, :], start=True, stop=True),
                     reads=[buf(bT)], writes=[buf("pM")])
            if tau == 0:
                P.op("dve", lambda e: e.scalar_tensor_tensor(out=S["Wfir"][:, 0, :], in0=identf, scalar=S["dvec"][:, 0:1],
                                                             in1=pM[:, 0:128], op0=ALU.mult, op1=ALU.add),
                     reads=[buf("pM"), buf(bT)], writes=[buf("s5Wfir")])
            else:
                P.op("dve", lambda e, tau=tau: e.tensor_copy(out=S["Wfir"][:, tau, :], in_=pM[:, 0:128]),
                     reads=[buf("pM")], writes=[buf("s5Wfir")])
        for s in range(8):
            m = 7 - s
            for g in range(8):
                dv(lambda e, g=g, m=m: e.tensor_scalar(out=tB[:, g, :], in0=Bsw[:, g, :], scalar1=AI[:, g, m:m + 1],
                                                       scalar2=None, op0=ALU.mult))
                dv(lambda e, g=g: e.memset(ABp, 0.0))
                dv(lambda e, g=g, m=m: e.scalar_tensor_tensor(out=ABp[:, 16 * g:16 * g + 16], in0=Bst[:, g, :],
                                                              scalar=AR[:, g, m:m + 1], in1=tB[:, g, :],
                                                              op0=ALU.mult, op1=ALU.add))
                P.op("pe", lambda e: e.transpose(out=pM[:, 0:128], in_=ABp, identity=identf),
                     reads=[buf(bT), buf("identf")], writes=[buf("pM")])
                P.op("dve", lambda e, s=s, g=g: e.tensor_copy(out=S["Wst"][:, s, g, :], in_=pM[:, 0:128]),
                     reads=[buf("pM")], writes=[buf("s5Wst"), buf(bT)])
        for g in range(8):
            dv(lambda e, g=g: e.tensor_scalar(out=ang, in0=iot, scalar1=mth[:, g, 8:9], scalar2=None, op0=ALU.mult))
            sincos(ang, S["SIN"][:, g, :], S["COS"][:, g, :], 256)
        P.op("dve", lambda e: e.tensor_copy(out=wk[0], in_=wk[0]), reads=[buf(bT)],
             writes=[buf(bT), buf("s5SIN"), buf("s5COS")])
        P.op("dve", lambda e: e.memset(S["X0"][0], 0.0), writes=[buf("s5X00")])
        P.cur_prio = 0
        return S

    def s5_alloc_work():
        Wk = {}
        Wk["Ssb"] = aa([128, 256], F32)
        Wk["t1"] = aa([128, 256], F32)
        Wk["t2"] = aa([128, 256], F32)
        Wk["Wsc"] = aa([128, 256], F32)
        Wk["Xn"] = aa([128, 256], F32)
        Wk["Xin"] = aa([128, 8, 256], BF16)
        for nm in ("S4", "T1", "T2"):
            Wk[nm] = aa([128, 4, 256], F32)
        Wk["W4"] = Wk["S4"]
        Wk["X4"] = Wk["T2"]
        Wk["ysb"] = aa([128, 2, 256], F32)
        Wk["g1"] = aa([128, 2, 256], F32)
        Wk["g2"] = aa([128, 2, 256], F32)
        Wk["zfull"] = aa([128, 2048], BF16)
        assert ar_off[0] >= setup_extent[0], (ar_off[0], setup_extent[0])
        Wk["ub"] = aa([128, 8, 256], BF16)
        return Wk

    GC = math.sqrt(2.0 / math.pi)
    YBANK = [zb(2, 0), zb(2, 1), zb(3, 0), zb(3, 1)]
    YBUF = ["pZ20", "pZ21", "pZ30", "pZ31"]

    def s5_fir_inter(S, Wk, nc_, dst_ap, dst_buf):
        ub, Xin, zfull = Wk["ub"], Wk["Xin"], Wk["zfull"]
        zv = zfull.rearrange("p (c j) -> p j c", j=8)
        for j in range(8):
            bk = j // 2
            Yj = YBANK[bk][:, (j % 2) * 256:(j % 2) * 256 + nc_]
            for tau in range(j + 1):
                P.op("pe", lambda e, Yj=Yj, tau=tau, j=j: e.matmul(Yj, lhsT=S["Wfir"][:, tau, :], rhs=ub[:, j - tau, 0:nc_],
                                                                 start=(tau == 0), stop=False),
                     reads=[buf("s5ub"), buf("s5Wfir")], writes=[buf(YBUF[bk])])
            for g in range(8):
                P.op("pe", lambda e, Yj=Yj, j=j, g=g: e.matmul(Yj, lhsT=S["Wint"][:, j, g, :], rhs=Xin[:, g, 0:nc_],
                                                             start=False, stop=(g == 7)),
                     reads=[buf("s5Xin"), buf("s5Wint")], writes=[buf(YBUF[bk])])
            if j % 2 == 1:
                Yb = YBANK[bk].rearrange("p (j c) -> p j c", j=2)[:, :, 0:nc_]
                ysb, g1, g2 = Wk["ysb"][:, :, 0:nc_], Wk["g1"][:, :, 0:nc_], Wk["g2"][:, :, 0:nc_]
                P.op("act", lambda e, Yb=Yb: e.activation(out=ysb, in_=Yb, func=AF.Copy),
                     reads=[buf(YBUF[bk])], writes=[buf("s5ysb")])
                P.op("dve", lambda e: e.tensor_tensor(out=g1, in0=ysb, in1=ysb, op=ALU.mult),
                     reads=[buf("s5ysb")], writes=[buf("s5g1")])
                P.op("dve", lambda e: e.tensor_scalar(out=g1, in0=g1, scalar1=0.044715, scalar2=1.0, op0=ALU.mult,
                                                      op1=ALU.add), reads=[buf("s5g1")], writes=[buf("s5g1")])
                P.op("dve", lambda e: e.tensor_tensor(out=g1, in0=g1, in1=ysb, op=ALU.mult),
                     reads=[buf("s5g1"), buf("s5ysb")], writes=[buf("s5g1")])
                P.op("dve", lambda e: e.tensor_scalar(out=g1, in0=g1, scalar1=-18.0, scalar2=None, op0=ALU.max),
                     reads=[buf("s5g1")], writes=[buf("s5g1")])
                P.op("act", lambda e: e.activation(out=g2, in_=g1, func=AF.Exp, scale=-2.0 * GC),
                     reads=[buf("s5g1")], writes=[buf("s5g2")])
                P.op("act", lambda e: e.activation(out=g2, in_=g2, func=AF.Ln, bias=1.0, scale=1.0),
                     reads=[buf("s5g2")], writes=[buf("s5g2")])
                P.op("act", lambda e: e.activation(out=g2, in_=g2, func=AF.Exp, scale=-1.0),
                     reads=[buf("s5g2")], writes=[buf("s5g2")])
                P.op("dve", lambda e, j=j: e.tensor_tensor(out=zv[:, j - 1:j + 1, 0:nc_], in0=ysb, in1=g2, op=ALU.mult),
                     reads=[buf("s5ysb"), buf("s5g2")], writes=[buf("s5zfull")])
        P.dma("pool", dst_ap, zfull[:, 0:nc_ * 8], reads=[buf("s5zfull")], writes=[dst_buf])

    def s5_states(S, Wk, nc_, g, dstps, dstbuf):
        for s in range(8):
            P.op("pe", lambda e, s=s: e.matmul(dstps[:, 0:nc_], lhsT=S["Wst"][:, s, g, :], rhs=Wk["ub"][:, s, 0:nc_],
                                                 start=(s == 0), stop=(s == 7)),
                 reads=[buf("s5ub"), buf("s5Wst")], writes=[buf(dstbuf)])

    def s5_supertile(S, Wk, sti, ntok, tok0, final_k=None):
        nc_ = ntok // 8
        X0 = S["X0"][sti % 2]
        X0n = S["X0"][(sti + 1) % 2]
        bX0, bX0n = buf(f"s5X0{sti % 2}"), buf(f"s5X0{(sti + 1) % 2}")
        Xin = Wk["Xin"]
        S4, T1, T2, W4, X4 = (Wk[k] for k in ("S4", "T1", "T2", "W4", "X4"))
        ZA = Z[0].rearrange("p (g c) -> p g c", g=4)
        ZB = Z[1].rearrange("p (g c) -> p g c", g=4)
        bZA = [buf("pZ00"), buf("pZ01")]
        bZB = [buf("pZ10"), buf("pZ11")]
        for hf in range(2):
            gs = slice(4 * hf, 4 * hf + 4)
            for gl in range(4):
                g = 4 * hf + gl
                for s in range(8):
                    P.op("pe", lambda e, s=s, g=g, gl=gl: e.matmul(ZA[:, gl, 0:nc_], lhsT=S["Wst"][:, s, g, :],
                                                                   rhs=Wk["ub"][:, s, 0:nc_], start=(s == 0), stop=(s == 7)),
                         reads=[buf("s5ub"), buf("s5Wst")], writes=[bZA[gl // 2]])
            P.op("dve", lambda e: e.tensor_copy(out=S4[:, :, 0:nc_], in_=ZA[:, :, 0:nc_]), reads=bZA, writes=[buf("s5S4")],
                 deps=(setup_done if (sti == 0 and hf == 0) else ()))
            for gl in range(4):
                P.op("pe", lambda e, gl=gl: e.matmul(ZB[:, gl, 0:nc_], lhsT=S["PiT"], rhs=S4[:, gl, 0:nc_], start=True, stop=True),
                     reads=[buf("s5S4"), buf("s5PiT")], writes=[bZB[gl // 2]])
            P.op("dve", lambda e, gs=gs: e.tensor_tensor(out=T1[:, :, 0:nc_], in0=S["SIN"][:, gs, 0:nc_], in1=ZB[:, :, 0:nc_],
                                                         op=ALU.mult), reads=bZB + [buf("s5SIN")], writes=[buf("s5T1")])
            P.op("dve", lambda e, gs=gs: e.tensor_tensor(out=T2[:, :, 0:nc_], in0=S["COS"][:, gs, 0:nc_], in1=S4[:, :, 0:nc_],
                                                         op=ALU.mult), reads=[buf("s5S4"), buf("s5COS")], writes=[buf("s5T2")])
            P.op("dve", lambda e: e.tensor_tensor(out=T2[:, :, 0:nc_], in0=T2[:, :, 0:nc_], in1=T1[:, :, 0:nc_], op=ALU.subtract),
                 reads=[buf("s5T1"), buf("s5T2")], writes=[buf("s5T2")])
            for gl in range(4):
                g = 4 * hf + gl
                P.op("dve", lambda e, g=g, gl=gl: e.tensor_tensor_scan(
                    out=W4[:, gl, 0:nc_], data0=S["rho8"][:, g:g + 1].to_broadcast([128, nc_]), data1=T2[:, gl, 0:nc_],
                    initial=X0[:, g:g + 1], op0=ALU.mult, op1=ALU.add),
                    reads=[buf("s5T2"), buf("s5rho8"), bX0], writes=[buf("s5S4")])
            for gl in range(4):
                P.op("pe", lambda e, gl=gl: e.matmul(ZB[:, gl, 0:nc_], lhsT=S["PiT"], rhs=W4[:, gl, 0:nc_], start=True, stop=True),
                     reads=[buf("s5S4"), buf("s5PiT")], writes=[bZB[gl // 2]])
            P.op("dve", lambda e, gs=gs: e.tensor_tensor(out=T1[:, :, 0:nc_], in0=S["SIN"][:, gs, 0:nc_], in1=ZB[:, :, 0:nc_],
                                                         op=ALU.mult), reads=bZB + [buf("s5SIN")], writes=[buf("s5T1")])
            P.op("dve", lambda e, gs=gs: e.tensor_tensor(out=X4[:, :, 0:nc_], in0=S["COS"][:, gs, 0:nc_], in1=W4[:, :, 0:nc_],
                                                         op=ALU.mult), reads=[buf("s5S4"), buf("s5COS")], writes=[buf("s5T2")])
            P.op("dve", lambda e: e.tensor_tensor(out=X4[:, :, 0:nc_], in0=X4[:, :, 0:nc_], in1=T1[:, :, 0:nc_], op=ALU.add),
                 reads=[buf("s5T1"), buf("s5T2")], writes=[buf("s5T2")])
            P.op("dve", lambda e, gs=gs: e.tensor_copy(out=Xin[:, gs, 0:1], in_=X0[:, gs].rearrange("p (g o) -> p g o", o=1)),
                 reads=[bX0], writes=[buf("s5Xin")])
            if nc_ > 1:
                P.op("dve", lambda e, gs=gs: e.tensor_copy(out=Xin[:, gs, 1:nc_], in_=X4[:, :, 0:nc_ - 1]),
                     reads=[buf("s5T2")], writes=[buf("s5Xin")])
            P.op("dve", lambda e, gs=gs: e.tensor_copy(out=X0n[:, gs].rearrange("p (g o) -> p g o", o=1), in_=X4[:, :, nc_ - 1:nc_]),
                 reads=[buf("s5T2")], writes=[bX0n])
            if final_k is not None:
                P.op("dve", lambda e, gs=gs: e.tensor_copy(out=S5FIN[:, gs].rearrange("p (g o) -> p g o", o=1),
                                                           in_=X4[:, :, final_k - 1:final_k]),
                     reads=[buf("s5T2")], writes=[buf("s5fin")])
        s5_fir_inter(S, Wk, nc_, zin_ap(tok0, ntok), buf(f"zinP{tok0 // PZ}"))

    def s5_sample(S, Wk):
        nc_ = 32
        Ssb, t1, t2, Xn, Xin = (Wk[k] for k in ("Ssb", "t1", "t2", "Xn", "Xin"))
        pA, pB, bA, bB = zb(1, 0), zb(1, 1), "pZ10", "pZ11"
        P.dma("sp", S5X0, s5x0_d[:, :, :], writes=[buf("s5x0s")])
        Sv = Ssb[:, 0:32].rearrange("p (q c) -> p q c", c=2)
        Xv = Xin[:, :, 0:32].rearrange("p g (q c) -> p g q c", c=2)
        for g in range(8):
            s5_states(S, Wk, nc_, g, pA, bA)
            P.op("dve", lambda e: e.tensor_copy(out=Ssb[:, 0:nc_], in_=pA[:, 0:nc_]), reads=[buf(bA)], writes=[buf("s5Ssb")])
            Z0 = S5X0[:, g, :]
            X1 = Xn[:, 0:16]
            Fn = Xn[:, 16:32]
            P.op("pe", lambda e, Z0=Z0: e.matmul(pB[:, 0:16], lhsT=S["PiT"], rhs=Z0, start=True, stop=True),
                 reads=[buf("s5x0s"), buf("s5PiT")], writes=[buf(bB)])
            P.op("dve", lambda e, g=g: e.tensor_scalar(out=t1[:, 0:16], in0=pB[:, 0:16], scalar1=S["ai8"][:, g:g + 1], scalar2=None,
                                                       op0=ALU.mult), reads=[buf(bB), buf("s5ai8")], writes=[buf("s5t1")])
            P.op("dve", lambda e, g=g, Z0=Z0: e.scalar_tensor_tensor(out=X1, in0=Z0, scalar=S["ar8"][:, g:g + 1], in1=t1[:, 0:16],
                                                                      op0=ALU.mult, op1=ALU.add),
                 reads=[buf("s5x0s"), buf("s5t1"), buf("s5ar8")], writes=[buf("s5Xn")])
            P.op("dve", lambda e: e.tensor_tensor(out=X1, in0=X1, in1=Sv[:, :, 0], op=ALU.add),
                 reads=[buf("s5Xn"), buf("s5Ssb")], writes=[buf("s5Xn")])
            P.op("pe", lambda e: e.matmul(pB[:, 0:16], lhsT=S["PiT"], rhs=X1, start=True, stop=True),
                 reads=[buf("s5Xn"), buf("s5PiT")], writes=[buf(bB)])
            P.op("dve", lambda e, g=g: e.tensor_scalar(out=t1[:, 0:16], in0=pB[:, 0:16], scalar1=S["ai8"][:, g:g + 1], scalar2=None,
                                                       op0=ALU.mult), reads=[buf(bB), buf("s5ai8")], writes=[buf("s5t1")])
            P.op("dve", lambda e, g=g: e.scalar_tensor_tensor(out=Fn, in0=X1, scalar=S["ar8"][:, g:g + 1], in1=t1[:, 0:16],
                                                              op0=ALU.mult, op1=ALU.add),
                 reads=[buf("s5Xn"), buf("s5t1"), buf("s5ar8")], writes=[buf("s5Xn")])
            P.op("dve", lambda e, g=g: e.tensor_tensor(out=S5FINS[:, g, :], in0=Fn, in1=Sv[:, :, 1], op=ALU.add),
                 reads=[buf("s5Xn"), buf("s5Ssb")], writes=[buf("s5fins")])
            P.op("dve", lambda e, g=g, Z0=Z0: e.tensor_copy(out=Xv[:, g, :, 0], in_=Z0), reads=[buf("s5x0s")], writes=[buf("s5Xin")])
            P.op("dve", lambda e, g=g: e.tensor_copy(out=Xv[:, g, :, 1], in_=X1), reads=[buf("s5Xn")], writes=[buf("s5Xin")])
        s5_fir_inter(S, Wk, nc_, zin_ap(TP, 256), buf(f"zinP{TP // PZ}"))

    GROUPS = [[0, 1, 2, 3], [4, 5, 6, 7]]

    def ag1(pi):
        P.coll(lambda e: e.collective_compute("AllGather", ALU.bypass, replica_groups=GROUPS,
                                              ins=[zin_p[pi]], outs=[zall_p[pi]]),
               reads=[buf(f"zinP{pi}")], writes=[buf(f"zallP{pi}")])

    def phase_a_tile(ti, sample=False):
        NS = 2 if sample else 4
        N = NS * 128
        t0 = TP if sample else ti * TW
        j0 = NBLK if sample else ti * 4
        par = ti % 2
        xsrc = xs_d if sample else x_d
        xrow0 = 0 if sample else t0
        ko, vo, lo = (ks_o, vs_o, lfs_o) if sample else (k_o, v_o, lf_o)
        orow0 = 0 if sample else t0
        XT = xT[par]
        bXT = buf(f"xT{par}")
        for s in range(NS):
            slot = (ti * 4 + s) % XR
            bx = buf(f"xr{slot}")
            P.dma("sp", xr[slot], xsrc[xrow0 + s * 128:xrow0 + (s + 1) * 128, :], writes=[bx])
            P.op("act", lambda e, slot=slot, s=s: e.activation(
                out=junk, in_=xr[slot], func=AF.Square, accum_out=ss[:, s:s + 1]),
                reads=[bx], writes=[buf("junk"), buf("ss")], fuse=False, cost=1.2)
        P.op("act", lambda e: e.activation(out=lnt[:, 0:NS], in_=ss[:, 0:NS], func=AF.Ln, bias=EPS, scale=1.0 / D_MODEL),
             reads=[buf("ss")], writes=[buf("lnt")])
        RS = rstd[par]
        bRS = buf(f"rstd{par}")
        P.op("act", lambda e: e.activation(out=RS[:, 0:NS], in_=lnt[:, 0:NS], func=AF.Exp, scale=-0.5),
             reads=[buf("lnt")], writes=[bRS])
        for s in range(NS):
            slot = (ti * 4 + s) % XR
            bx = buf(f"xr{slot}")
            xp = s % 2
            P.op("dve", lambda e, slot=slot, s=s, xp=xp: e.tensor_scalar(
                out=xs[xp], in0=xr[slot], scalar1=RS[:, s:s + 1], scalar2=None, op0=ALU.mult),
                reads=[bx, bRS], writes=[buf(f"xs{xp}")])
            for kc in range(8):
                P.op("pe", lambda e, xp=xp, kc=kc: e.transpose(
                    out=pT[xp][:, kc, :], in_=xs[xp][:, kc * 128:(kc + 1) * 128], identity=identb),
                    reads=[buf(f"xs{xp}"), buf("identb")], writes=[buf(f"pT{xp}")])
            if s % 2 == 0:
                P.op("dve", lambda e, xp=xp, s=s: e.tensor_copy(out=XT[:, :, s * 128:(s + 1) * 128], in_=pT[xp]),
                     reads=[buf(f"pT{xp}")], writes=[bXT])
            else:
                P.op("act", lambda e, xp=xp, s=s: e.activation(
                    out=XT[:, :, s * 128:(s + 1) * 128], in_=pT[xp], func=AF.Copy),
                    reads=[buf(f"pT{xp}")], writes=[bXT])
            yield "front"
        yield "FRONT_DONE"

        pp_i = [0]

        def proj(col0):
            i = pp_i[0] % 2
            pp_i[0] += 1
            for kc in range(8):
                P.op("pe", lambda e, kc=kc, i=i: e.matmul(
                    pP[i][:, 0:N], lhsT=Wb[:, kc, col0:col0 + 128], rhs=XT[:, kc, 0:N], start=(kc == 0), stop=(kc == 7)),
                    reads=[bXT, buf("Wb")], writes=[buf(f"pP{i}")])
            return i

        def headnorm2(items):
            for k_, (i, gain, gain_buf, out_ap, out_buf) in enumerate(items):
                P.op("act", lambda e, i=i, k_=k_: e.activation(out=sq2[k_][:, 0:N], in_=pP[i][:, 0:N], func=AF.Square),
                     reads=[buf(f"pP{i}")], writes=[buf(SQN[k_])])
            for k_, (i, gain, gain_buf, out_ap, out_buf) in enumerate(items):
                P.op("pe", lambda e, k_=k_: e.matmul(pMs[k_][:, 0:N], lhsT=BO, rhs=sq2[k_][:, 0:N], start=True, stop=True),
                     reads=[buf(SQN[k_]), buf("BO")], writes=[buf(pMn[k_])])
            for k_, (i, gain, gain_buf, out_ap, out_buf) in enumerate(items):
                P.op("act", lambda e, k_=k_: e.activation(out=ln2[k_][:, 0:N], in_=pMs[k_][:, 0:N], func=AF.Ln, bias=EPS, scale=1.0),
                     reads=[buf(pMn[k_])], writes=[buf(LNN[k_])])
            for k_, (i, gain, gain_buf, out_ap, out_buf) in enumerate(items):
                P.op("act", lambda e, k_=k_: e.activation(out=rr2[k_][:, 0:N], in_=ln2[k_][:, 0:N], func=AF.Exp, scale=-0.5),
                     reads=[buf(LNN[k_])], writes=[buf(RRN[k_])])
            for k_, (i, gain, gain_buf, out_ap, out_buf) in enumerate(items):
                P.op("dve", lambda e, i=i, k_=k_, gain=gain, out_ap=out_ap: e.scalar_tensor_tensor(
                    out=out_ap, in0=pP[i][:, 0:N], scalar=gain, in1=rr2[k_][:, 0:N], op0=ALU.mult, op1=ALU.mult),
                    reads=[buf(f"pP{i}"), buf(RRN[k_]), gain_buf], writes=[out_buf])

        sq2, ln2, rr2 = [sq, sqB], [lnb, lnbB], [rr, rrB]
        SQN, LNN, RRN = ["sq0", "sgb0"], ["lnb0", "kst"], ["rr0", "vst"]
        pMs, pMn = [pM, pK.rearrange("p s f -> p (s f)")], ["pM", "pK"]
        iq = proj(0)
        ik = proj(128)
        headnorm2([(iq, qg8[:, 0:1], buf("qg8"), qnb[:, 0:N], buf("qnb")),
                   (ik, kg[:, 0:1], buf("kg"), knf[:, 0:N], buf("knf"))])
        for h in range(2):
            P.dma("pool", qt_d[h, 0:64, t0:t0 + N], qnb[h * 64:(h + 1) * 64, 0:N],
                  reads=[buf("qnb")], writes=[buf(f"qt_d{ti}")])
        yield "back"
        P.op("act", lambda e: e.activation(out=knb[:, 0:N], in_=knf[:, 0:N], func=AF.Copy),
             reads=[buf("knf")], writes=[buf("knb")])
        for h in range(2):
            P.dma("pool", kt_d[h, :, t0:t0 + N], knb[h * 64:(h + 1) * 64, 0:N],
                  reads=[buf("knb")], writes=[buf(f"kt_d{ti}")])
        for s in range(NS):
            P.op("pe", lambda e, s=s: e.transpose(out=pK[:, s, :], in_=knf[:, s * 128:(s + 1) * 128], identity=identf),
                 reads=[buf("knf"), buf("identf")], writes=[buf("pK")])
        P.op("dve", lambda e: e.tensor_copy(out=kst[:, 0:NS, :], in_=pK[:, 0:NS, :]), reads=[buf("pK")], writes=[buf("kst")])
        out_dmas.append(P.dma("pool", ko[orow0:orow0 + N, :].rearrange("(s p) f -> p s f", p=128), kst[:, 0:NS, :],
                              reads=[buf("kst")]))
        yield "back"
        for s in range(NS):
            for kc in range(8):
                P.op("pe", lambda e, s=s, kc=kc: e.matmul(
                    pV[:, s, :], lhsT=XT[:, kc, s * 128:(s + 1) * 128], rhs=Wb[:, kc, 256:384],
                    start=(kc == 0), stop=(kc == 7)),
                    reads=[bXT, buf("Wb")], writes=[buf("pV")])
        for s in range(NS):
            for kc in range(8):
                P.op("pe", lambda e, s=s, kc=kc: e.matmul(
                    pS[:, 2 * s:2 * s + 2], lhsT=XT[:, kc, s * 128:(s + 1) * 128], rhs=Wb[:, kc, 768:770],
                    start=(kc == 0), stop=(kc == 7)),
                    reads=[bXT, buf("Wb")], writes=[buf("pS")])
        P.op("act", lambda e: e.activation(out=vst[:, 0:NS, :], in_=pV[:, 0:NS, :], func=AF.Copy),
             reads=[buf("pV")], writes=[buf("vst")])
        out_dmas.append(P.dma("pool", vo[orow0:orow0 + N, :].rearrange("(s p) f -> p s f", p=128), vst[:, 0:NS, :],
                              reads=[buf("vst")]))
        P.op("dve", lambda e: e.tensor_copy(
            out=VP[:, j0:j0 + NS, :, 0:64], in_=pV[:, 0:NS, :].rearrange("p s (h d) -> p s h d", h=2)),
            reads=[buf("pV")], writes=[buf("VP")])
        yield "back"
        pSv = pS[:, 0:2 * NS].rearrange("p (s h) -> p s h", h=2)
        for h in range(2):
            P.op("act", lambda e, h=h: e.activation(
                out=ef[:, 0:NS, h], in_=pSv[:, :, h], func=AF.Exp, bias=nbf[:, h:h + 1], scale=-1.0),
                reads=[buf("pS"), buf("nbf")], writes=[buf("ef")])
        P.op("act", lambda e: e.activation(out=lf[:, 0:NS, :], in_=ef[:, 0:NS, :], func=AF.Ln, bias=1.0, scale=1.0),
             reads=[buf("ef")], writes=[buf("lf")])
        P.op("dve", lambda e: e.tensor_scalar(out=lf[:, 0:NS, :], in0=lf[:, 0:NS, :], scalar1=-1.0, scalar2=None, op0=ALU.mult),
             reads=[buf("lf")], writes=[buf("lf")])
        out_dmas.append(P.dma("pool", lo[orow0:orow0 + N, :].rearrange("(s p) h -> p s h", p=128), lf[:, 0:NS, :],
                              reads=[buf("lf")]))
        yield "back"
        for s in range(NS):
            P.op("pe", lambda e, s=s: e.transpose(out=pM[0:2, s * 128:(s + 1) * 128], in_=lf[:, s, :],
                                                   identity=identf),
                 reads=[buf("lf"), buf("identf")], writes=[buf("pM")])
        CR = cumrow[par]
        CRp = cumrow[1 - par]
        init = 0.0 if (ti == 0 or sample) else CRp[:, TW - 1:TW]
        d0 = segm if sample else ones2
        P.op("dve", lambda e: e.tensor_tensor_scan(
            out=CR[:, 0:N], data0=d0[:, 0:N], data1=pM[0:2, 0:N], initial=init, op0=ALU.mult, op1=ALU.add),
            reads=[buf("pM"), buf("ones2"), buf("segm"), buf(f"cumrow{1 - par}")], writes=[buf(f"cumrow{par}")])
        P.op("dve", lambda e: e.tensor_copy(out=cumb[:, 0:N], in_=CR[:, 0:N]),
             reads=[buf(f"cumrow{par}")], writes=[buf("cumb")])
        for h in range(2):
            P.dma("pool", qt_d[h, 64:65, t0:t0 + N], cumb[h:h + 1, 0:N],
                  reads=[buf("cumb")], writes=[buf(f"qt_d{ti}")])
        for s in range(NS):
            P.op("pe", lambda e, s=s: e.transpose(out=pS[:, 16 + 2 * s:16 + 2 * s + 2],
                                                   in_=CR[:, s * 128:(s + 1) * 128], identity=identf[0:2, 0:2]),
                 reads=[buf(f"cumrow{par}"), buf("identf")], writes=[buf("pS")])
        P.op("dve", lambda e: e.tensor_scalar(
            out=NCK[:, j0:j0 + NS, :], in0=pS[:, 16:16 + 2 * NS].rearrange("p (s h) -> p s h", h=2),
            scalar1=-1.0, scalar2=None, op0=ALU.mult),
            reads=[buf("pS")], writes=[buf("NCK")])
        yield "back"
        iga = proj(384)
        igs = proj(640)
        silu2_from_psum([(iga, sgb[0], buf("sgb0")), (igs, sgb[1], buf("sgb1"))], N,
                        [(lnb, "lnb0", rr, "rr0"), (lnbB, "kst", rrB, "vst")])
        P.dma("pool", sgs_d[:, t0:t0 + N], sgb[1][:, 0:N], reads=[buf("sgb1")], writes=[buf(f"sgs_d{ti}")])
        for h in range(2):
            P.dma("pool", sga_d[h, :, t0:t0 + N], sgb[0][h * 64:(h + 1) * 64, 0:N],
                  reads=[buf("sgb0")], writes=[buf(f"sga_d{ti}")])
        yield "back"
        i = proj(512)
        uo = 0 if sample else (ti % 4) * 64
        P.op("act", lambda e, i=i: e.activation(out=WK["ub"][:, :, uo:uo + N // 8],
                                                in_=pP[i][:, 0:N].rearrange("p (c j) -> p j c", j=8), func=AF.Copy),
             reads=[buf(f"pP{i}")], writes=[buf("s5ub")])
        yield "back"
        if sample:
            P.cur_deps = tuple(setup_done)
            s5_sample(S5S, WK)
            P.cur_deps = ()
            if "X" in stages:
                ag1(TP // PZ)
        elif ti % 4 == 3 or ti == NTILE - 1:
            sti = ti // 4
            tok0 = sti * 2048
            ntok = t0 + TW - tok0
            tf = min(L_REAL, TP)
            fk = None
            if (tf - 1) // 2048 == sti:
                fk = (tf - tok0) // 8
            if sti == 0:
                P.cur_deps = tuple(setup_done)
            s5_supertile(S5S, WK, sti, ntok, tok0, fk)
            P.cur_deps = ()
            if "X" in stages and (tok0 + ntok) % PZ == 0:
                ag1(tok0 // PZ)

    if "A" in stages:
        S5S = s5_alloc_weights()
        XR = 4
        xr = [aa([128, D_MODEL], F32) for i in range(XR)]
        junk = aa([128, D_MODEL], BF16)
        xs = [aa([128, D_MODEL], BF16) for i in range(2)]
        xT = [aa([128, 8, TW], BF16) for i in range(2)]
        ss = aa([128, 4], F32)
        lnt = aa([128, 4], F32)
        rstd = [aa([128, 4], F32) for i in range(2)]
        sq = aa([128, TW], BF16)
        lnb = aa([128, TW], F32)
        rr = aa([128, TW], F32)
        qnb = aa([128, TW], BF16)
        knf = aa([128, TW], F32)
        knb = aa([128, TW], BF16)
        kst = aa([128, 4, 128], F32)
        vst = aa([128, 4, 128], F32)
        ef = aa([128, 4, 2], F32)
        lf = aa([128, 4, 2], F32)
        cumb = aa([2, TW], BF16)
        sgb = [aa([128, TW], BF16) for i in range(2)]
        sqB = sgb[0]
        lnbB = kst.rearrange("p s f -> p (s f)")
        rrB = vst.rearrange("p s f -> p (s f)")

        mark2 = ar_off[0]
        s5_setup(S5S)
        setup_extent[0] = ar_off[0]
        setup_done = [buf("s5tmp").w, buf("Wb").w]
        ar_off[0] = mark2
        WK = s5_alloc_work()
        gens = [phase_a_tile(ti) for ti in range(NTILE)] + [phase_a_tile(NTILE, sample=True)]
        front_done = [False] * len(gens)

        def step(gi):
            try:
                r = next(gens[gi])
            except StopIteration:
                return False
            if r == "FRONT_DONE":
                front_done[gi] = True
            return True

        while not front_done[0]:
            step(0)
        for gi in range(len(gens)):
            alive = True
            while alive:
                alive = step(gi)
                if gi + 1 < len(gens) and not front_done[gi + 1]:
                    step(gi + 1)
            if gi + 1 < len(gens):
                while not front_done[gi + 1]:
                    step(gi + 1)
        out_dmas.append(P.dma("sp", s5fin_o[:, :], S5FIN, reads=[buf("s5fin")]))
        out_dmas.append(P.dma("sp", s5fins_o[:, :, :], S5FINS, reads=[buf("s5fins")]))

    n_real = min(L_REAL, TP)
    qbs = []
    q0 = 0
    while q0 < n_real:
        ql = min(QB, n_real - q0)
        qbs.append((q0, ql))
        q0 += ql

    def tiles_of(a, b):
        return range(a // TW, (b + TW - 1) // TW)

    step = [0]

    def attend(qi, h, q0, qlen):
        par = qi % 2
        nkb = (q0 + qlen + 127) // 128
        halves = [(a, min(a + 512, qlen)) for a in range(0, qlen, 512)]
        pO = Z[2]
        bQ = buf(f"QA{par}{h}")
        bKT = buf(f"KT{h}")
        base = step[0]
        step[0] += nkb

        def clo(j):
            return max(0, 128 * j - q0)

        def zbufs(zi, lo, hi):
            return [buf(f"pZ{zi}{hf}") for hf in range(2) if lo < (hf + 1) * 512 and hi > hf * 512]

        def qk(j):
            zi = (base + j) % 2
            c_lo = clo(j)
            for (a, b) in halves:
                lo = max(a, c_lo)
                if lo >= b:
                    continue
                diag = (128 * j >= q0) and (a <= c_lo < b)
                P.op("pe", lambda e, zi=zi, lo=lo, b=b, diag=diag: e.matmul(
                    Z[zi][:, lo:b], lhsT=KT[h][:, 128 * j:128 * j + 128], rhs=QA[par][h][:, lo:b],
                    start=True, stop=not diag),
                    reads=[bKT, bQ], writes=zbufs(zi, lo, b))
                if diag:
                    w = min(128, qlen - c_lo)
                    P.op("pe", lambda e, zi=zi, c_lo=c_lo, w=w: e.matmul(
                        Z[zi][:, c_lo:c_lo + w], lhsT=identb, rhs=MN[:, 0:w], start=False, stop=True),
                        reads=[buf("identb"), buf("MN")], writes=zbufs(zi, c_lo, c_lo + w))

        def ex(j):
            zi = (base + j) % 2
            pi = (base + j) % 3
            c_lo = clo(j)
            P.op("act", lambda e, zi=zi, pi=pi, c_lo=c_lo: e.activation(
                out=PT[pi][:, c_lo:qlen], in_=Z[zi][:, c_lo:qlen], func=AF.Exp,
                bias=NCK[:, j, h:h + 1], scale=1.0),
                reads=zbufs(zi, c_lo, qlen) + [buf("NCK")], writes=[buf(f"PT{pi}")], cost=0.25 + (qlen - c_lo) / 1200.0)

        def pv(j):
            pi = (base + j) % 3
            c_lo = clo(j)
            for (a, b) in halves:
                lo = max(a, c_lo)
                if lo >= b:
                    continue
                j_last = min(nkb - 1, (q0 + b - 1) // 128)
                P.op("pe", lambda e, pi=pi, lo=lo, b=b, j_last=j_last: e.matmul(
                    pO[0:65, lo:b], lhsT=VP[:, j, h, :], rhs=PT[pi][:, lo:b],
                    start=(j == 0), stop=(j == j_last)),
                    reads=[buf("VP"), buf(f"PT{pi}")], writes=zbufs(2, lo, b))

        qk(0)
        for j in range(nkb):
            if j + 1 < nkb:
                qk(j + 1)
            ex(j)
            pv(j)
        ob = osb[h]
        bob = buf(f"osb{h}")
        P.op("dve", lambda e: e.tensor_copy(out=ob[:, 0:qlen], in_=pO[0:65, 0:qlen]),
             reads=zbufs(2, 0, qlen), writes=[bob])
        P.op("dve", lambda e: e.reciprocal(out=ob[64:65, 0:qlen], in_=ob[64:65, 0:qlen]),
             reads=[bob], writes=[bob])
        for (a, b) in halves:
            P.op("pe", lambda e, a=a, b=b: e.matmul(pO[0:64, a:b], lhsT=onesP[64:65, 0:64], rhs=ob[64:65, a:b],
                                                    start=True, stop=True),
                 reads=[bob, buf("onesP")], writes=zbufs(2, a, b))
        P.op("dve", lambda e: e.tensor_tensor(out=ob[0:64, 0:qlen], in0=ob[0:64, 0:qlen], in1=pO[0:64, 0:qlen],
                                              op=ALU.mult),
             reads=[bob] + zbufs(2, 0, qlen), writes=[bob])
        P.op("dve", lambda e: e.tensor_tensor(out=attg[h][:, 0:qlen], in0=ob[0:64, 0:qlen],
                                              in1=SG[par][h][:, 0:qlen], op=ALU.mult),
             reads=[bob, buf(f"SG{par}{h}")], writes=[buf(f"attg{h}")])
        P.dma("pool", mixin_ap(h * 64, (h + 1) * 64, q0, qlen), attg[h][:, 0:qlen],
              reads=[buf(f"attg{h}")], writes=[buf(f"mixinP{q0 // PM_}")])

    def sample_attention():
        ar_off[0] = 0
        clf = aa([32, 1024], F32)
        ccum = aa([32, 1024], F32)
        NCKc = aa([128, 8, 32], F32)
        MSK = aa([128, 8, 16], BF16)
        negt = aa([128, 16], F32)
        KTn = [aa([65, 256], BF16) for h in range(2)]
        QAs = [aa([65, 256], BF16) for h in range(2)]
        SGs = [aa([64, 256], BF16) for h in range(2)]
        kst_ = [aa([64, 1024], F32) for i in range(4)]
        vst_ = [aa([128, 8, 64], F32) for i in range(4)]
        KTc = [aa([65, 1024], BF16) for i in range(4)]
        VSc = [aa([128, 8, 65], BF16) for i in range(4)]
        PTs = [aa([128, 9, 16], BF16) for i in range(4)]
        obs_l = [aa([65, 16], F32) for i in range(4)]
        attS = [aa([64, 256], BF16) for h in range(2)]
        P.dma("sp", clf, clf_d[:, :], writes=[buf("clf")])
        P.op("dve", lambda e: e.tensor_tensor_scan(out=ccum, data0=onesP[0:32, 0:1].to_broadcast([32, 1024]), data1=clf,
                                                   initial=0.0, op0=ALU.mult, op1=ALU.add),
             reads=[buf("clf"), buf("onesP")], writes=[buf("ccum")])
        P.op("dve", lambda e: e.tensor_scalar(out=clf, in0=ccum, scalar1=ccum[:, 1023:1024], scalar2=-1.0,
                                              op0=ALU.subtract, op1=ALU.mult),
             reads=[buf("ccum")], writes=[buf("clf")])
        for blk in range(8):
            P.op("pe", lambda e, blk=blk: e.transpose(out=pM[:, blk * 32:(blk + 1) * 32], in_=clf[:, blk * 128:(blk + 1) * 128],
                                                       identity=identf[0:32, 0:32]),
                 reads=[buf("clf"), buf("identf")], writes=[buf("pM")])
        P.op("dve", lambda e: e.tensor_copy(out=NCKc.rearrange("p b r -> p (b r)"), in_=pM[:, 0:256]),
             reads=[buf("pM")], writes=[buf("NCKc")])
        for qq in range(8):
            P.op("pool", lambda e: e.memset(negt, NEG), writes=[buf("negt")])
            P.op("pool", lambda e, qq=qq: e.affine_select(out=negt, in_=negt, pattern=[[0, 16]], compare_op=ALU.is_ge,
                                                          fill=0.0, base=16 * qq - 1, channel_multiplier=-1),
                 reads=[buf("negt")], writes=[buf("negt")])
            P.op("pool", lambda e, qq=qq: e.tensor_copy(out=MSK[:, qq, :], in_=negt), reads=[buf("negt")], writes=[buf("MSK")])
            P.op("pool", lambda e: e.memset(negt, NEG), writes=[buf("negt")])
            P.op("pool", lambda e, qq=qq: e.affine_select(out=negt, in_=negt, pattern=[[-1, 16]], compare_op=ALU.is_gt,
                                                          fill=0.0, base=-16 * qq, channel_multiplier=1),
                 reads=[buf("negt")], writes=[buf("negt")])
            P.op("pool", lambda e, qq=qq: e.tensor_tensor(out=negt, in0=negt, in1=MSK[:, qq, :], op=ALU.add),
                 reads=[buf("negt"), buf("MSK")], writes=[buf("negt")])
            P.op("pool", lambda e, qq=qq: e.tensor_copy(out=MSK[:, qq, :], in_=negt), reads=[buf("negt")], writes=[buf("MSK")])
        for h in range(2):
            P.dma("sp", KTn[h][0:64, :], kt_d[h, :, TP:TP + 256], reads=[buf(f"kt_d{NTILE}")], writes=[buf(f"KTn{h}")])
            P.op("pool", lambda e, h=h: e.memset(KTn[h][64:65, :], 1.0), writes=[buf(f"KTn{h}")])
            P.dma("sp", QAs[h], qt_d[h, :, TP:TP + 256], reads=[buf(f"qt_d{NTILE}")], writes=[buf(f"QAs{h}")])
            P.dma("sp", SGs[h], sga_d[h, :, TP:TP + 256], reads=[buf(f"sga_d{NTILE}")], writes=[buf(f"SGs{h}")])
            for i in range(2):
                pass
        for i in range(4):
            P.op("pool", lambda e, i=i: e.memset(KTc[i][64:65, :], 1.0), writes=[buf(f"KTc{i}")])
            P.op("pool", lambda e, i=i: e.memset(VSc[i][:, :, 64:65], 1.0), writes=[buf(f"VSc{i}")])
        def one_qh(q, h, i):
            if True:
                r = q * 2 + h
                obs = obs_l[i]
                bobs = buf(f"obs{i}")
                P.dma("sp", kst_[i], kc_d[q, h, :, :], writes=[buf(f"kst_{i}")])
                P.dma("sp", vst_[i], vc_d[q, h, :, :].rearrange("(b p) d -> p b d", p=128), writes=[buf(f"vst_{i}")])
                P.op("dve", lambda e, i=i: e.tensor_copy(out=KTc[i][0:64, :], in_=kst_[i]),
                     reads=[buf(f"kst_{i}")], writes=[buf(f"KTc{i}")])
                P.op("pool", lambda e, i=i: e.tensor_copy(out=VSc[i][:, :, 0:64], in_=vst_[i]),
                     reads=[buf(f"vst_{i}")], writes=[buf(f"VSc{i}")])
                zi = i
                Sps = Z[zi][:, 0:144].rearrange("p (b t) -> p b t", t=16)
                qs = slice(q * 16, (q + 1) * 16)
                sb_, qq = q // 8, q % 8
                for blk in range(8):
                    P.op("pe", lambda e, blk=blk, i=i, Sps=Sps, qs=qs: e.matmul(
                        Sps[:, blk, :], lhsT=KTc[i][:, blk * 128:(blk + 1) * 128], rhs=QAs[h][:, qs], start=True, stop=True),
                        reads=[buf(f"KTc{i}"), buf(f"QAs{h}")], writes=[buf(f"pZ{zi}0")])
                P.op("pe", lambda e, Sps=Sps, qs=qs, sb_=sb_: e.matmul(
                    Sps[:, 8, :], lhsT=KTn[h][:, sb_ * 128:(sb_ + 1) * 128], rhs=QAs[h][:, qs], start=True, stop=False),
                    reads=[buf(f"KTn{h}"), buf(f"QAs{h}")], writes=[buf(f"pZ{zi}0")])
                P.op("pe", lambda e, Sps=Sps, qq=qq: e.matmul(Sps[:, 8, :], lhsT=identb, rhs=MSK[:, qq, :], start=False, stop=True),
                     reads=[buf("identb"), buf("MSK")], writes=[buf(f"pZ{zi}0")])
                for blk in range(9):
                    bias = NCKc[:, blk, r:r + 1] if blk < 8 else NCK[:, NBLK + sb_, h:h + 1]
                    P.op("act", lambda e, blk=blk, i=i, Sps=Sps, bias=bias: e.activation(
                        out=PTs[i][:, blk, :], in_=Sps[:, blk, :], func=AF.Exp, bias=bias, scale=1.0),
                        reads=[buf(f"pZ{zi}0"), buf("NCKc"), buf("NCK")], writes=[buf(f"PTs{i}")])
                pO = Z[i][0:65, 512:528]
                for blk in range(9):
                    lhs = VSc[i][:, blk, :] if blk < 8 else VP[:, NBLK + sb_, h, :]
                    P.op("pe", lambda e, blk=blk, i=i, lhs=lhs, pO=pO: e.matmul(pO, lhsT=lhs, rhs=PTs[i][:, blk, :],
                                                                            start=(blk == 0), stop=(blk == 8)),
                         reads=[buf(f"VSc{i}"), buf("VP"), buf(f"PTs{i}")], writes=[buf(f"pZ{i}1")])
                P.op("dve", lambda e, pO=pO: e.tensor_copy(out=obs, in_=pO), reads=[buf(f"pZ{i}1")], writes=[bobs])
                P.op("dve", lambda e: e.reciprocal(out=obs[64:65, :], in_=obs[64:65, :]), reads=[bobs], writes=[bobs])
                P.op("pe", lambda e, i=i: e.matmul(Z[i][0:64, 512:528], lhsT=onesP[64:65, 0:64], rhs=obs[64:65, :],
                                                   start=True, stop=True),
                     reads=[bobs, buf("onesP")], writes=[buf(f"pZ{i}1")])
                P.op("dve", lambda e, i=i: e.tensor_tensor(out=obs[0:64, :], in0=obs[0:64, :], in1=Z[i][0:64, 512:528], op=ALU.mult),
                     reads=[bobs, buf(f"pZ{i}1")], writes=[bobs])
                P.op("dve", lambda e, qs=qs: e.tensor_tensor(out=attS[h][:, qs], in0=obs[0:64, :], in1=SGs[h][:, qs], op=ALU.mult),
                     reads=[bobs, buf(f"SGs{h}")], writes=[buf(f"attS{h}")])
        it = 0
        for q in range(16):
            for h in range(2):
                one_qh(q, h, it % 4)
                it += 1
        for h in range(2):
            P.dma("pool", mixin_ap(h * 64, (h + 1) * 64, TP, 256), attS[h], reads=[buf(f"attS{h}")],
                  writes=[buf(f"mixinP{TP // PM_}")])

    tiles_x = [(ti * TW, TW) for ti in range(NTILE)] + [(TP, 256)]

    def g_alloc():
        G_ = {}
        G_['wgs'] = aa([128, 4, 128], F32)
        G_['wg'] = aa([128, 4, 128], BF16)
        G_['bg'] = aa([128, 1], F32)
        G_['nbg'] = aa([128, 1], F32)
        G_['zl'] = [aa([128, 4, TW], BF16) for i in range(2)]
        G_['zm'] = [aa([128, TW], BF16) for i in range(2)]
        G_['sgm'] = [aa([128, TW], BF16) for i in range(2)]
        G_['tg'] = aa([128, TW], F32)
        G_['tg2'] = aa([128, TW], F32)
        G_['s5o'] = [aa([128, TW], BF16) for i in range(2)]
        return G_

    def g_setup(G_):
        wgs, wg, bg, nbg = G_["wgs"], G_["wg"], G_["bg"], G_["nbg"]
        P.dma("sp", wgs, wglu_d.rearrange("(kc p) f -> p kc f", p=128), writes=[buf("wgs")])
        P.dma("sp", bg, bglu_d[:, :], writes=[buf("bg")])
        P.op("dve", lambda e: e.tensor_copy(out=wg, in_=wgs), reads=[buf("wgs")], writes=[buf("wg")])
        P.op("dve", lambda e: e.tensor_scalar(out=nbg, in0=bg, scalar1=-1.0, scalar2=None, op0=ALU.mult),
             reads=[buf("bg")], writes=[buf("nbg")])

    def g_tile(G_, k):
        wg, nbg, zl, zm, sgm, tg, tg2, s5o = (G_[x] for x in ("wg", "nbg", "zl", "zm", "sgm", "tg", "tg2", "s5o"))
        pG = zb(3, 0)
        c0, n = tiles_x[k]
        if True:
            i = k % 2
            P.dma("sp", zl[i][:, :, 0:n], zall_ap(c0, n).rearrange("(kc p) t -> p kc t", p=128),
                  reads=[buf(f"zallP{c0 // PZ}")], writes=[buf(f"zl{i}")])
            P.dma("sp", zm[i][:, 0:n], zin_ap(c0, n), reads=[buf(f"zinP{c0 // PZ}")], writes=[buf(f"zm{i}")])
            P.dma("sp", sgm[i][:, 0:n], sgs_d[:, c0:c0 + n], reads=[buf(f"sgs_d{k}")], writes=[buf(f"sgm{i}")])
            for kc in range(4):
                P.op("pe", lambda e, i=i, kc=kc, n=n: e.matmul(pG[:, 0:n], lhsT=wg[:, kc, :], rhs=zl[i][:, kc, 0:n],
                                                               start=(kc == 0), stop=(kc == 3)),
                     reads=[buf(f"zl{i}"), buf("wg")], writes=[buf("pZ30")])
            P.op("act", lambda e, i=i, n=n: e.activation(out=tg[:, 0:n], in_=pG[:, 0:n], func=AF.Exp, bias=nbg[:, 0:1],
                                                         scale=-1.0), reads=[buf("pZ30"), buf("nbg")], writes=[buf("tg")])
            P.op("act", lambda e, n=n: e.activation(out=tg[:, 0:n], in_=tg[:, 0:n], func=AF.Ln, bias=1.0, scale=1.0),
                 reads=[buf("tg")], writes=[buf("tg")])
            P.op("act", lambda e, n=n: e.activation(out=tg[:, 0:n], in_=tg[:, 0:n], func=AF.Exp, scale=-1.0),
                 reads=[buf("tg")], writes=[buf("tg")])
            P.op("dve", lambda e, i=i, n=n: e.tensor_tensor(out=tg2[:, 0:n], in0=zm[i][:, 0:n], in1=tg[:, 0:n], op=ALU.mult),
                 reads=[buf("tg"), buf(f"zm{i}")], writes=[buf("tg2")])
            P.op("dve", lambda e, i=i, n=n: e.tensor_tensor(out=s5o[i][:, 0:n], in0=tg2[:, 0:n], in1=sgm[i][:, 0:n], op=ALU.mult),
                 reads=[buf("tg2"), buf(f"sgm{i}")], writes=[buf(f"s5o{i}")])
            P.dma("pool", mixin_ap(128, 256, c0, n), s5o[i][:, 0:n], reads=[buf(f"s5o{i}")], writes=[buf(f"mixinP{c0 // PM_}")])

    def c_alloc():
        C_ = {}
        C_['wos'] = [aa([128, 256], F32) for i in range(2)]
        C_['wo'] = aa([128, 8, 256], BF16)
        C_['ml'] = [aa([128, 8, TW], BF16) for i in range(2)]
        C_['xc'] = [aa([128, 4, 256], F32) for i in range(1)]
        C_['yst'] = [aa([128, 4, 256], F32) for i in range(1)]
        return C_

    def c_setup(C_):
        wos, wo = C_['wos'], C_['wo']
        for kc in range(8):
            s = kc % 2
            P.dma("sp", wos[s], wout_d[kc * 128:(kc + 1) * 128, :], writes=[buf(f"wos{s}")])
            P.op("dve", lambda e, kc=kc, s=s: e.tensor_copy(out=wo[:, kc, :], in_=wos[s]),
                 reads=[buf(f"wos{s}")], writes=[buf("wo")])

    def c_tile(C_, k):
        wo, ml, xc, yst = C_['wo'], C_['ml'], C_['xc'], C_['yst']
        pC = zb(3, 1)
        c0, n = tiles_x[k]
        if True:
            i = k % 2
            ns = n // 128
            P.dma("sp", ml[i][:, :, 0:n], mixall_ap(c0, n).rearrange("(kc p) t -> p kc t", p=128),
                  reads=[buf(f"mixallP{c0 // PM_}")], writes=[buf(f"ml{i}")])
            P.dma("sp", xc[0][:, 0:ns, :], xc_d[c0:c0 + n, :].rearrange("(s p) c -> p s c", p=128), writes=[buf("xc0")])
            for s2 in range(0, ns, 2):
                for s in range(s2, min(s2 + 2, ns)):
                    for kc in range(8):
                        P.op("pe", lambda e, i=i, s=s, kc=kc: e.matmul(
                            pC[:, (s % 2) * 256:(s % 2) * 256 + 256], lhsT=ml[i][:, kc, s * 128:(s + 1) * 128], rhs=wo[:, kc, :],
                            start=(kc == 0), stop=(kc == 7)),
                            reads=[buf(f"ml{i}"), buf("wo")], writes=[buf("pZ31")])
                w2 = min(2, ns - s2)
                P.op("dve", lambda e, s2=s2, w2=w2: e.tensor_tensor(
                    out=yst[0][:, s2:s2 + w2, :], in0=pC.rearrange("p (s c) -> p s c", s=2)[:, 0:w2, :],
                    in1=xc[0][:, s2:s2 + w2, :], op=ALU.add),
                    reads=[buf("pZ31"), buf("xc0")], writes=[buf("yst0")])
            out_dmas.append(P.dma("pool", y_o[c0:c0 + n, :].rearrange("(s p) c -> p s c", p=128), yst[0][:, 0:ns, :],
                                  reads=[buf("yst0")]))

    def ag2(pi):
        P.coll(lambda e: e.collective_compute("AllGather", ALU.bypass, replica_groups=GROUPS,
                                              ins=[mixin_p[pi]], outs=[mixall_p[pi]]),
               reads=[buf(f"mixinP{pi}")], writes=[buf(f"mixallP{pi}")])

    if "SAMP" in stages:
        P.barrier()
        sample_attention()
    P.barrier()
    ar_off[0] = 0
    KT = [aa([65, TP], BF16) for h in range(2)]
    QA = [[aa([65, QB], BF16) for h in range(2)] for p in range(2)]
    SG = [[aa([64, QB], BF16) for h in range(2)] for p in range(2)]
    PT = [aa([128, QB], BF16) for i in range(3)]
    osb = [aa([65, QB], F32) for h in range(2)]
    attg = [aa([64, QB], BF16) for h in range(2)]
    if "ATT" in stages:
        do_x = "X" in stages
        if do_x:
            G_ = g_alloc()
            C_ = c_alloc()
            g_setup(G_)
            c_setup(C_)
        ntl = (n_real + TW - 1) // TW
        for h in range(2):
            P.dma("sp", KT[h][0:64, 0:ntl * TW], kt_d[h, :, 0:ntl * TW],
                  reads=[buf(f"kt_d{ti}") for ti in range(ntl)], writes=[buf(f"KT{h}")])
            P.op("pool", lambda e, h=h: e.memset(KT[h][64:65, :], 1.0), writes=[buf(f"KT{h}")])
        ntile_x = len(tiles_x)
        g_next = [0]
        c_queue = []
        ag2_done = [0]

        def after_unit(u, last):
            if not do_x:
                return
            while g_next[0] < ntile_x and (g_next[0] <= u or last):
                g_tile(G_, g_next[0])
                g_next[0] += 1
            while ag2_done[0] < npm:
                pi = ag2_done[0]
                tok_end = min((pi + 1) * PM_, TX)
                need_tiles = [k for k, (c0, n) in enumerate(tiles_x) if c0 < tok_end]
                need_q = [qi for qi, (q0, ql) in enumerate(qbs) if q0 < tok_end]
                if (max(need_tiles) < g_next[0]) and (max(need_q) * 2 + 1 <= u or last):
                    ag2(pi)
                    ag2_done[0] += 1
                    c_queue.extend([(k, u + 4) for k, (c0, n) in enumerate(tiles_x) if pi * PM_ <= c0 < tok_end])
                else:
                    break
            if c_queue and (c_queue[0][1] <= u or last):
                n_emit = len(c_queue) if last else 1
                for _ in range(n_emit):
                    k, _u = c_queue.pop(0)
                    c_tile(C_, k)

        u = 0
        nunits = 2 * len(qbs)
        for qi, (q0, qlen) in enumerate(qbs):
            par = qi % 2
            for h in range(2):
                rd = [buf(f"qt_d{ti}") for ti in tiles_of(q0, q0 + qlen)]
                P.dma("sp", QA[par][h][:, 0:qlen], qt_d[h, :, q0:q0 + qlen], reads=rd, writes=[buf(f"QA{par}{h}")])
                rd = [buf(f"sga_d{ti}") for ti in tiles_of(q0, q0 + qlen)]
                P.dma("sp", SG[par][h][:, 0:qlen], sga_d[h, :, q0:q0 + qlen], reads=rd, writes=[buf(f"SG{par}{h}")])
            for h in range(2):
                attend(qi, h, q0, qlen)
                after_unit(u, u == nunits - 1)
                u += 1

    P.barrier()
    P.wait("sp", out_dmas)
    if os.environ.get("MK_RESCHED", "1") == "1":
        est = P.reschedule(window=int(os.environ.get('MK_WIN', '128')), hop=float(os.environ.get('MK_HOP', '1.5')))
    stats = P.emit(stack)
    stack.close()
    return nc, stats


_CACHE = {}


def _prep_core(c, I):
    b, hp = c // 4, c % 4
    f32 = np.float32
    x = np.zeros((TP, D_MODEL), f32)
    x[:N_META] = I["meta_tokens"]
    nreal = min(L_REAL, TP)
    x[N_META:nreal] = I["x_prompt"][b][:nreal - N_META]
    w = I["w_in"][0]
    cols = np.concatenate([
        np.arange(128 * hp, 128 * hp + 128), 512 + np.arange(128 * hp, 128 * hp + 128),
        1024 + np.arange(128 * hp, 128 * hp + 128), 1544 + np.arange(128 * hp, 128 * hp + 128),
        2056 + np.arange(128 * hp, 128 * hp + 128), 2568 + np.arange(128 * hp, 128 * hp + 128),
        1536 + np.arange(2 * hp, 2 * hp + 2)])
    m = {
        "x": x,
        "w_in_c": np.ascontiguousarray(w[:, cols]),
        "norm_g": np.ascontiguousarray(I["norm_g"][0].reshape(8, 128).T),
        "b_f": np.ascontiguousarray(np.broadcast_to(I["b_f"][0, 2 * hp:2 * hp + 2][None, :], (128, 2))),
        "qg": np.ascontiguousarray(np.tile(I["q_norm_g"][0], 2)[:, None]),
        "kg": np.ascontiguousarray(np.tile(I["k_norm_g"][0], 2)[:, None]),
    }
    G = slice(8 * hp, 8 * hp + 8)
    are, aim = I["s5_a_re"][0, G], I["s5_a_im"][0, G]
    bre = I["s5_b_re"][0, G].transpose(1, 0, 2)
    bim = I["s5_b_im"][0, G].transpose(1, 0, 2)
    cre = I["s5_c_re"][0, G].transpose(2, 0, 1)
    cim = I["s5_c_im"][0, G].transpose(2, 0, 1)
    m.update({
        "s5_are": np.tile(are.T, (2, 1)),
        "s5_aim": np.tile(aim.T, (2, 1)),
        "s5_ldt": np.broadcast_to(I["s5_log_dt"][0, G][None, :], (128, 8)),
        "s5_d": I["s5_d"][0, G].reshape(128, 1),
        "s5_x1": np.concatenate([bre, bim], 0),
        "s5_x2": np.concatenate([bim, bre], 0),
        "s5_cx1": np.concatenate([cre, cim], 0),
        "s5_cx2": np.concatenate([cim, cre], 0),
    })
    Q = slice(16 * b, 16 * b + 16)
    H2 = slice(2 * hp, 2 * hp + 2)
    xsmp = I["x_sample"][Q].reshape(256, D_MODEL)
    ocols = slice(256 * hp, 256 * hp + 256)
    wo = I["w_out"][0]
    rows = np.concatenate([np.concatenate([np.arange(128 * r, 128 * r + 128), 512 + np.arange(128 * r, 128 * r + 128)])
                           for r in range(4)])
    sre = I["state_s5_re"][0, Q, G, :].transpose(2, 1, 0)
    sim_ = I["state_s5_im"][0, Q, G, :].transpose(2, 1, 0)
    m.update({
        "xsmp": xsmp,
        "clf": I["cache_logf"][0, Q, :, H2].transpose(0, 2, 1).reshape(32, 1024),
        "kcT": I["cache_k"][0, Q, :, H2, :].transpose(0, 2, 3, 1),
        "vc": I["cache_v"][0, Q, :, H2, :].transpose(0, 2, 1, 3),
        "wglu_c": I["w_glu"][0][:, 128 * hp:128 * hp + 128],
        "bglu_c": I["b_glu"][0, 128 * hp:128 * hp + 128][:, None],
        "wout_c": wo[rows][:, ocols],
        "x_c": np.concatenate([x[:, ocols], xsmp[:, ocols]], 0),
        "s5x0": np.concatenate([sre, sim_], 0),
    })
    return {k: np.ascontiguousarray(v, dtype=f32) for k, v in m.items()}


def kernel(**inputs):
    I = {k: np.asarray(v) for k, v in inputs.items()}
    if "nc" not in _CACHE:
        _CACHE["nc"] = build_program()[0]
    nc = _CACHE["nc"]
    in_maps = [_prep_core(c, I) for c in range(8)]
    res = run_bass_kernel_spmd(nc, in_maps, core_ids=list(range(8)))
    R = res.results
    f32 = np.float32
    y_p = np.zeros((2, SEQ, D_MODEL), f32)
    y_s = np.zeros((32, 16, D_MODEL), f32)
    k_p = np.zeros((1, 2, L_REAL, 8, 64), f32)
    v_p = np.zeros((1, 2, L_REAL, 8, 64), f32)
    lf_p = np.zeros((1, 2, L_REAL, 8), f32)
    sr_p = np.zeros((1, 2, 32, 64), f32)
    si_p = np.zeros((1, 2, 32, 64), f32)
    k_s = np.zeros((1, 32, 16, 8, 64), f32)
    v_s = np.zeros((1, 32, 16, 8, 64), f32)
    lf_s = np.zeros((1, 32, 16, 8), f32)
    sr_s = np.zeros((1, 32, 32, 64), f32)
    si_s = np.zeros((1, 32, 32, 64), f32)
    for c in range(8):
        b, hp = c // 4, c % 4
        r = R[c]
        nr = min(L_REAL, TP)
        k_p[0, b, :nr, 2 * hp:2 * hp + 2, :] = r["k_out"][:nr].reshape(nr, 2, 64)
        v_p[0, b, :nr, 2 * hp:2 * hp + 2, :] = r["v_out"][:nr].reshape(nr, 2, 64)
        lf_p[0, b, :nr, 2 * hp:2 * hp + 2] = r["logf_out"][:nr]
        if "y_out" in r:
            cols = slice(256 * hp, 256 * hp + 256)
            G = slice(8 * hp, 8 * hp + 8)
            Q = slice(16 * b, 16 * b + 16)
            yo = r["y_out"]
            y_p[b, :nr - N_META, cols] = yo[N_META:nr]
            y_s[Q, :, cols] = yo[TP:TP + 256].reshape(16, 16, 256)
            k_s[0, Q, :, 2 * hp:2 * hp + 2, :] = r["ks_out"].reshape(16, 16, 2, 64)
            v_s[0, Q, :, 2 * hp:2 * hp + 2, :] = r["vs_out"].reshape(16, 16, 2, 64)
            lf_s[0, Q, :, 2 * hp:2 * hp + 2] = r["lfs_out"].reshape(16, 16, 2)
            fin = r["s5fin_out"]
            sr_p[0, b, G, :] = fin[0:64, :].T
            si_p[0, b, G, :] = fin[64:128, :].T
            fs = r["s5fins_out"]
            sr_s[0, Q, G, :] = fs[0:64].transpose(2, 1, 0)
            si_s[0, Q, G, :] = fs[64:128].transpose(2, 1, 0)
    _CACHE["last"] = R
    return (y_p, y_s, k_p, v_p, lf_p, sr_p, si_p, k_s, v_s, lf_s, sr_s, si_s)
```

```python
import math
import numpy as np
import ml_dtypes
from contextlib import ExitStack
import concourse.bass as bass
import concourse.mybir as mybir
from concourse.bass_utils import run_bass_kernel_spmd

F32 = mybir.dt.float32
BF16 = mybir.dt.bfloat16
AF = mybir.ActivationFunctionType
ALU = mybir.AluOpType

D_MODEL = 1024
SEQ = 16384
N_META = 16
L_REAL = SEQ + N_META
TW = 512
import os
NTILE = int(os.environ.get("MK_NTILE", "33"))
TP = NTILE * TW
NBLK = TP // 128
TX = TP + 256
NCOL = 770
EPS = 1e-6
NEG = -30000.0


class Buf:
    __slots__ = ("name", "w", "rs", "const", "excl")

    def __init__(self, name, const=False, excl=False):
        self.name = name
        self.w = None
        self.rs = []
        self.const = const
        self.excl = excl


class Node:
    __slots__ = ("eng", "fn", "deps", "kind", "sem", "val", "used", "idx", "fuse", "cost", "seg", "fin", "prio")

    def __init__(self, eng, fn, kind):
        self.eng = eng
        self.fn = fn
        self.kind = kind
        self.deps = []
        self.sem = None
        self.val = 0
        self.used = False
        self.fuse = True
        self.cost = None
        self.seg = 0
        self.fin = 0.0
        self.prio = 0


ENGS = ("pe", "act", "dve", "pool", "sp")
N_DMA_SEMS = 48
SEM_ROLL = 30000


class Prog:
    def __init__(self, nc):
        self.nc = nc
        self.q = {e: [] for e in ENGS}
        self.nodes = []
        self.dma_i = 0
        self.dma_j = 0
        self.seg = 0
        self.cur_prio = 0
        self.cur_deps = ()
        self.dma_last = [None] * N_DMA_SEMS

    def _mk(self, eng, fn, kind, reads, writes, extra):
        n = Node(eng, fn, kind)
        seen = set()
        ex = [b for b in reads if b.excl]
        if ex:
            reads = [b for b in reads if not b.excl]
            writes = list(writes) + [b for b in ex if b not in writes]

        def add(d, k):
            if d is None or (id(d), k) in seen:
                return
            seen.add((id(d), k))
            n.deps.append((d, k))
        for b in reads:
            add(b.w, "raw")
        for b in writes:
            add(b.w, "waw")
            for r in b.rs:
                add(r, "war")
        for d in extra:
            add(d, "raw")
        for d in self.cur_deps:
            add(d, "raw")
        for b in reads:
            if not b.const:
                b.rs.append(n)
        for b in writes:
            b.w = n
            b.rs = []
        n.idx = len(self.nodes)
        n.seg = self.seg
        n.prio = self.cur_prio
        self.nodes.append(n)
        self.q[eng].append(n)
        return n

    def op(self, eng, fn, reads=(), writes=(), deps=(), fuse=True, cost=None):
        n = self._mk(eng, fn, "c", reads, writes, deps)
        n.fuse = fuse
        n.cost = cost
        return n

    def dma(self, eng, out, in_, reads=(), writes=(), deps=()):
        half = N_DMA_SEMS // 2
        if eng == "pool":
            k = half + self.dma_j % half
            self.dma_j += 1
        else:
            k = self.dma_i % half
            self.dma_i += 1
        extra = list(deps)
        if self.dma_last[k] is not None:
            extra.append(self.dma_last[k])
        n = self._mk(eng, lambda e: e.dma_start(out=out, in_=in_), "d", reads, writes, extra)
        n.sem = k
        self.dma_last[k] = n
        return n

    def coll(self, fn, reads=(), writes=()):
        return self._mk("pool", fn, "x", reads, writes, ())

    def wait(self, eng, deps):
        return self._mk(eng, None, "w", (), (), deps)

    def barrier(self):
        deps = []
        for e in ENGS:
            for n in reversed(self.q[e]):
                if n.kind == "c":
                    deps.append(n)
                    break
        deps += [n for n in self.dma_last if n is not None]
        deps += [n for n in self.nodes if n.kind == "x"]
        self.seg += 1
        for e in ENGS:
            self.wait(e, deps)
        self.seg += 1

    DEF_COST = {"pe": 0.25, "act": 0.75, "dve": 0.75, "pool": 0.8, "sp": 0.1}

    def reschedule(self, window=48, hop=1.2):
        segs = {}
        for n in self.nodes:
            segs.setdefault(n.seg, []).append(n)
        def c_of(n):
            if n.cost is not None:
                return n.cost
            return 0.0 if n.kind == "w" else (2.5 if n.kind in ("d", "x") else self.DEF_COST[n.eng])
        cp = {}
        for n in reversed(self.nodes):
            cp.setdefault(id(n), c_of(n))
            base = cp[id(n)]
            for d, k in n.deps:
                v = base + (0.05 if d.eng == n.eng else hop) + c_of(d)
                if cp.get(id(d), 0.0) < v:
                    cp[id(d)] = v
        use_cp = os.environ.get("MK_CP", "1") == "1"
        clock = {e: 0.0 for e in ENGS}
        newq = {e: [] for e in ENGS}
        done = set()
        for sg in sorted(segs):
            if all(n.kind == "w" for n in segs[sg]):
                lastc = []
                for e in ENGS:
                    for m in reversed(newq[e]):
                        if m.kind == "c":
                            lastc.append(m)
                            break
                for n in segs[sg]:
                    keep = [(d, k) for d, k in n.deps if d.kind in ("d", "x")]
                    n.deps = keep + [(m, "raw") for m in lastc]
            pend = {e: [n for n in segs[sg] if n.eng == e] for e in ENGS}
            left = sum(len(v) for v in pend.values())
            while left:
                best = None
                for e in ENGS:
                    cand = pend[e]
                    seen = 0
                    for ci in range(len(cand)):
                        n = cand[ci]
                        if not n.prio:
                            seen += 1
                            if seen > window:
                                break
                        ok = True
                        rdy = 0.0
                        for d, k in n.deps:
                            if id(d) not in done:
                                ok = False
                                break
                            t = d.fin + (0.05 if (d.eng == e and d.kind == "c") else hop)
                            if t > rdy:
                                rdy = t
                        if not ok:
                            continue
                        st = max(clock[e], rdy)
                        if use_cp:
                            key = (int((st + (0.8 if n.prio else 0.0)) / float(os.environ.get("MK_BKT", "0.1"))), -cp[id(n)], n.idx)
                        else:
                            key = (st + (0.8 if n.prio else 0.0), n.idx)
                        if best is None or key < best[0]:
                            best = (key, e, ci, n, st)
                        if rdy <= clock[e] and not n.prio and not use_cp:
                            break
                assert best is not None, "scheduler stuck"
                _, e, ci, n, st = best
                pend[e].pop(ci)
                left -= 1
                c = n.cost
                if c is None:
                    c = 0.0 if n.kind == "w" else (2.5 if n.kind in ("d", "x") else self.DEF_COST[e])
                if n.kind in ("d", "x"):
                    clock[e] = st + 0.3
                    n.fin = st + c
                else:
                    clock[e] = st + c
                    n.fin = st + c
                done.add(id(n))
                newq[e].append(n)
        self.q = newq
        return max(clock.values())

    def emit(self, stack):
        nc = self.nc
        for n in self.nodes:
            for d, k in n.deps:
                if d.kind in ("d", "x"):
                    d.used = True
                elif d.eng != n.eng:
                    d.used = True
                elif n.eng != "pe":
                    d.used = True
                elif n.kind == "d":
                    d.used = True
        esems = {e: [stack.enter_context(nc.semaphore(f"s_{e}0"))] for e in ENGS}
        ecnt = {e: 0 for e in ENGS}
        dsems = [stack.enter_context(nc.semaphore(f"s_dma{i}")) for i in range(N_DMA_SEMS)]
        dcnt = [0] * N_DMA_SEMS
        for n in self.nodes:
            if n.kind == "x":
                n.sem = stack.enter_context(nc.semaphore(f"s_cc{n.idx}"))
                n.val = 1
            elif n.kind == "d":
                dcnt[n.sem] += 16
                n.val = dcnt[n.sem]
                n.sem = dsems[n.sem]
        for e in ENGS:
            for n in self.q[e]:
                if n.kind == "c" and n.used:
                    if ecnt[e] >= SEM_ROLL:
                        esems[e].append(stack.enter_context(nc.semaphore(f"s_{e}{len(esems[e])}")))
                        ecnt[e] = 0
                    ecnt[e] += 1
                    n.val = ecnt[e]
                    n.sem = esems[e][-1]
        block = stack.enter_context(nc.Block())
        handles = {"pe": block.tensor, "act": block.scalar, "dve": block.vector,
                   "pool": block.gpsimd, "sp": block.sync}
        stats = {}
        for e in ENGS:
            queue = self.q[e]

            def body(eng, queue=queue, e=e):
                waited = {}
                nw = 0
                for n in queue:
                    pend = []
                    for d, k in n.deps:
                        if d.kind not in ("d", "x"):
                            if d.eng == e and e == "pe" and n.kind != "d":
                                continue
                        if d.sem is None:
                            continue
                        key = id(d.sem)
                        if waited.get(key, 0) >= d.val:
                            continue
                        waited[key] = d.val
                        pend.append((d.sem, d.val))
                        nw += 1
                    best = {}
                    for s_, v_ in pend:
                        if id(s_) not in best or best[id(s_)][1] < v_:
                            best[id(s_)] = (s_, v_)
                    pend = list(best.values())
                    fuse = None
                    if pend and n.kind == "c" and n.fuse and e in ("act", "dve", "pool"):
                        fuse = pend.pop()
                    for s_, v_ in pend:
                        eng.wait_ge(s_, v_)
                    if n.kind == "w":
                        continue
                    ins = n.fn(eng)
                    if fuse is not None:
                        ins._wait_ge(fuse[0], fuse[1])
                    if n.kind == "x":
                        ins.then_inc(n.sem, 1)
                    elif n.kind == "d":
                        ins.then_inc(n.sem, 16)
                    elif n.used:
                        ins.then_inc(n.sem, 1)
                stats[e] = (len(queue), nw)
            handles[e](body)
        return stats


QB = 1024
ARENA_BYTES = 161 * 1024
STAGES = os.environ.get("MK_STAGES", "A,ATT,SAMP,X")


def build_program(debug=False):
    nc = bass.Bass("TRN2", target_bir_lowering=False)
    P = Prog(nc)
    stack = ExitStack()
    stages = set(STAGES.split(","))

    def din(name, shape, dt=F32):
        return nc.dram_tensor(name, list(shape), dt, kind="ExternalInput").ap()

    def dout(name, shape, dt=F32):
        return nc.dram_tensor(name, list(shape), dt, kind="ExternalOutput").ap()

    def dscr(name, shape, dt):
        return nc.dram_tensor(name, list(shape), dt).ap()

    def sb(name, shape, dt):
        return stack.enter_context(nc.sbuf_tensor("sb_" + name, list(shape), dt))[:]

    def ps(name, shape, dt):
        return stack.enter_context(nc.psum_tensor("ps_" + name, list(shape), dt))[:]

    arena = sb("arena", [128, ARENA_BYTES // 4], F32)
    ar_off = [0]

    def aa(shape, dt):
        nfree = int(np.prod(shape[1:]))
        esz = 2 if dt == BF16 else 4
        nbytes = (nfree * esz + 31) // 32 * 32
        w0 = ar_off[0] // 4
        ar_off[0] += nbytes
        assert ar_off[0] <= ARENA_BYTES, ("arena overflow", ar_off[0])
        v = arena[0:shape[0], w0:w0 + nbytes // 4]
        if dt != F32:
            v = v.bitcast(dt)
        v = v[:, 0:nfree]
        if len(shape) == 3:
            v = v.rearrange("p (a b) -> p a b", a=shape[1])
        elif len(shape) == 4:
            v = v.rearrange("p (a b c) -> p a b c", a=shape[1], b=shape[2])
        return v

    x_d = din("x", [TP, D_MODEL])
    w_d = din("w_in_c", [D_MODEL, NCOL])
    ng_d = din("norm_g", [128, 8])
    bf_d = din("b_f", [128, 2])
    qg_d = din("qg", [128, 1])
    kg_d = din("kg", [128, 1])

    S5D = {}
    for nm in ("s5_are", "s5_aim", "s5_ldt"):
        S5D[nm] = din(nm, [128, 8])
    S5D["s5_d"] = din("s5_d", [128, 1])
    for nm in ("s5_x1", "s5_x2", "s5_cx1", "s5_cx2"):
        S5D[nm] = din(nm, [128, 8, 16])
    s5fin_o = dout("s5fin_out", [128, 8])
    PZ, PM_ = 4096, 2048
    npz = (TX + PZ - 1) // PZ
    npm = (TX + PM_ - 1) // PM_
    zin_p = [dscr(f"zin{i}", [128, min(PZ, TX - i * PZ)], BF16) for i in range(npz)]
    zall_p = [dscr(f"zall{i}", [512, min(PZ, TX - i * PZ)], BF16) for i in range(npz)]
    mixin_p = [dscr(f"mixin{i}", [256, min(PM_, TX - i * PM_)], BF16) for i in range(npm)]
    mixall_p = [dscr(f"mixall{i}", [1024, min(PM_, TX - i * PM_)], BF16) for i in range(npm)]

    def zin_ap(c0, n):
        return zin_p[c0 // PZ][:, c0 % PZ:c0 % PZ + n]

    def zall_ap(c0, n):
        return zall_p[c0 // PZ][:, c0 % PZ:c0 % PZ + n]

    def mixin_ap(r0, r1, c0, n):
        return mixin_p[c0 // PM_][r0:r1, c0 % PM_:c0 % PM_ + n]

    def mixall_ap(c0, n):
        return mixall_p[c0 // PM_][:, c0 % PM_:c0 % PM_ + n]
    sgs_d = dscr("sgs_scr", [128, TX], BF16)

    xs_d = din("xsmp", [256, D_MODEL])
    clf_d = din("clf", [32, 1024])
    kc_d = din("kcT", [16, 2, 64, 1024])
    vc_d = din("vc", [16, 2, 1024, 64])
    wglu_d = din("wglu_c", [512, 128])
    bglu_d = din("bglu_c", [128, 1])
    wout_d = din("wout_c", [1024, 256])
    xc_d = din("x_c", [TX, 256])
    s5x0_d = din("s5x0", [128, 8, 16])
    y_o = dout("y_out", [TX, 256])
    ks_o = dout("ks_out", [256, 128])
    vs_o = dout("vs_out", [256, 128])
    lfs_o = dout("lfs_out", [256, 2])
    s5fins_o = dout("s5fins_out", [128, 8, 16])

    k_o = dout("k_out", [TP, 128])
    v_o = dout("v_out", [TP, 128])
    lf_o = dout("logf_out", [TP, 2])

    qt_d = dscr("qt_scr", [2, 65, TX], BF16)
    kt_d = dscr("kt_scr", [2, 64, TX], BF16)
    sga_d = dscr("sga_scr", [2, 64, TX], BF16)

    ng = sb("ng", [128, 8], F32)
    bfp = sb("bfp", [128, 2], F32)
    nbf = sb("nbf", [128, 2], F32)
    qg = sb("qg", [128, 1], F32)
    kg = sb("kg", [128, 1], F32)
    qg8 = sb("qg8", [128, 1], F32)
    onesf = sb("onesf", [128, 128], F32)
    identf = sb("identf", [128, 128], F32)
    identb = sb("identb", [128, 128], BF16)
    BO = sb("BO", [128, 128], BF16)
    MN = sb("MN", [128, 128], BF16)
    ones2 = sb("ones2", [2, TW], F32)
    onesP = sb("onesP", [128, 64], F32)
    VP = sb("VP", [128, NBLK + 2, 2, 65], BF16)
    NCK = sb("NCK", [128, NBLK + 2, 2], F32)
    cumrow = [sb(f"cumrow{i}", [2, TW], F32) for i in range(2)]
    S5FIN = sb("s5fin", [128, 8], F32)
    S5FINS = sb("s5fins", [128, 8, 16], F32)
    S5X0 = sb("s5x0s", [128, 8, 16], F32)
    segm = sb("segm", [2, TW], F32)

    Z = [ps(f"Z{i}", [128, 1024], F32) for i in range(4)]

    def zb(i, half):
        return Z[i][:, half * 512:(half + 1) * 512]

    pT = [zb(0, h).bitcast(BF16).rearrange("p (k t) -> p k t", k=8) for h in range(2)]
    pP = [zb(1, 0), zb(1, 1)]
    pM = zb(2, 0)
    pV = zb(2, 1).rearrange("p (s f) -> p s f", s=4)
    pK = zb(3, 0).rearrange("p (s f) -> p s f", s=4)
    pS = zb(3, 1)
    PALIAS = {"pT0": "pZ00", "pT1": "pZ01", "pP0": "pZ10", "pP1": "pZ11", "pM": "pZ20", "pV": "pZ21",
              "pK": "pZ30", "pS": "pZ31"}

    B = {}

    def buf(name, const=False):
        name = PALIAS.get(name, name)
        name = {"lnb": "lnb0", "rr": "rr0", "sq": "sq0"}.get(name, name)
        if name not in B:
            B[name] = Buf(name, const, excl=name.startswith("pZ"))
        return B[name]

    P.dma("sp", ng, ng_d[:, :], writes=[buf("ng")])
    P.dma("sp", bfp, bf_d[:, :], writes=[buf("bfp")])
    P.dma("sp", qg, qg_d[:, :], writes=[buf("qg")])
    P.dma("sp", kg, kg_d[:, :], writes=[buf("kg")])
    P.op("dve", lambda e: e.tensor_scalar(out=nbf, in0=bfp, scalar1=-1.0, scalar2=None, op0=ALU.mult),
         reads=[buf("bfp")], writes=[buf("nbf")])
    P.op("dve", lambda e: e.tensor_scalar(out=qg8, in0=qg, scalar1=0.125, scalar2=None, op0=ALU.mult),
         reads=[buf("qg")], writes=[buf("qg8")])
    P.op("pool", lambda e: e.memset(onesf, 1.0), writes=[buf("onesf")])
    P.op("pool", lambda e: e.memset(ones2, 1.0), writes=[buf("ones2")])
    P.op("pool", lambda e: e.memset(onesP, 1.0), writes=[buf("onesP")])
    P.op("pool", lambda e: e.memset(segm, 1.0), writes=[buf("segm")])
    P.op("pool", lambda e: e.memset(segm.rearrange("p (q t) -> p q t", t=16)[:, :, 0:1], 0.0), writes=[buf("segm")])
    P.op("pool", lambda e: e.affine_select(out=identf, in_=onesf, pattern=[[-1, 128]],
                                           compare_op=ALU.is_equal, fill=0.0, base=0, channel_multiplier=1),
         reads=[buf("onesf")], writes=[buf("identf")])
    P.op("pool", lambda e: e.tensor_copy(out=identb, in_=identf), reads=[buf("identf")], writes=[buf("identb")])
    P.op("pool", lambda e: e.memset(onesf, NEG), reads=[buf("identf")], writes=[buf("onesf")])
    P.op("pool", lambda e: e.affine_select(out=MN, in_=onesf, pattern=[[-1, 128]],
                                           compare_op=ALU.is_gt, fill=0.0, base=0, channel_multiplier=1),
         reads=[buf("onesf")], writes=[buf("MN")])
    P.op("pool", lambda e: e.memset(BO, 0.0), writes=[buf("BO")])
    P.op("pool", lambda e: e.memset(BO[0:64, 0:64], 1.0 / 64), writes=[buf("BO")])
    P.op("pool", lambda e: e.memset(BO[64:128, 64:128], 1.0 / 64), writes=[buf("BO")])
    P.op("pool", lambda e: e.memset(VP[:, :, :, 64:65], 1.0), writes=[buf("VP")])

    out_dmas = []

    ar_off[0] = 0
    Wb = aa([128, 8, NCOL], BF16)
    s5_mark = [0]
    setup_extent = [0]
    def wb_setup():
      wst = [aa([128, NCOL], F32) for i in range(2)]
      for kc in range(8):
        s = kc % 2
        P.dma("sp", wst[s], w_d[kc * 128:(kc + 1) * 128, :], writes=[buf(f"wst{s}")])
        P.op("dve", lambda e, kc=kc, s=s: e.tensor_scalar(
            out=Wb[:, kc, :], in0=wst[s], scalar1=ng[:, kc:kc + 1], scalar2=None, op0=ALU.mult),
            reads=[buf(f"wst{s}"), buf("ng")], writes=[buf("Wb")])

    def silu2_from_psum(items, N, tmps):
        for (i, o, ob), (l, lb, r, rb) in zip(items, tmps):
            P.op("act", lambda e, i=i, l=l: e.activation(out=l[:, 0:N], in_=pP[i][:, 0:N], func=AF.Exp, scale=-1.0),
                 reads=[buf(f"pP{i}")], writes=[buf(lb)])
        for (i, o, ob), (l, lb, r, rb) in zip(items, tmps):
            P.op("act", lambda e, l=l: e.activation(out=l[:, 0:N], in_=l[:, 0:N], func=AF.Ln, bias=1.0, scale=1.0),
                 reads=[buf(lb)], writes=[buf(lb)])
        for (i, o, ob), (l, lb, r, rb) in zip(items, tmps):
            P.op("act", lambda e, l=l, r=r: e.activation(out=r[:, 0:N], in_=l[:, 0:N], func=AF.Exp, scale=-1.0),
                 reads=[buf(lb)], writes=[buf(rb)])
        for (i, o, ob), (l, lb, r, rb) in zip(items, tmps):
            P.op("dve", lambda e, i=i, o=o, r=r: e.tensor_tensor(out=o[:, 0:N], in0=pP[i][:, 0:N], in1=r[:, 0:N], op=ALU.mult),
                 reads=[buf(f"pP{i}"), buf(rb)], writes=[ob])

    def silu_from_psum(i, out_ap, out_buf, N=TW):
        P.op("act", lambda e: e.activation(out=lnb[:, 0:N], in_=pP[i][:, 0:N], func=AF.Exp, scale=-1.0),
             reads=[buf(f"pP{i}")], writes=[buf("lnb")])
        P.op("act", lambda e: e.activation(out=lnb[:, 0:N], in_=lnb[:, 0:N], func=AF.Ln, bias=1.0, scale=1.0),
             reads=[buf("lnb")], writes=[buf("lnb")])
        P.op("act", lambda e: e.activation(out=rr[:, 0:N], in_=lnb[:, 0:N], func=AF.Exp, scale=-1.0),
             reads=[buf("lnb")], writes=[buf("rr")])
        P.op("dve", lambda e: e.tensor_tensor(out=out_ap[:, 0:N], in0=pP[i][:, 0:N], in1=rr[:, 0:N], op=ALU.mult),
             reads=[buf(f"pP{i}"), buf("rr")], writes=[out_buf])

    def s5_alloc_weights():
        S = {}
        S["are"] = aa([128, 8], F32)
        S["aim"] = aa([128, 8], F32)
        S["ldt"] = aa([128, 8], F32)
        S["dvec"] = aa([128, 1], F32)
        S["rho8"] = aa([128, 8], F32)
        S["ar8"] = aa([128, 8], F32)
        S["ai8"] = aa([128, 8], F32)
        S["PiT"] = aa([128, 128], F32)
        S["Wfir"] = aa([128, 8, 128], BF16)
        S["Wst"] = aa([128, 8, 8, 128], BF16)
        S["Wint"] = aa([128, 8, 8, 128], BF16)
        S["COS"] = aa([128, 8, 256], F32)
        S["SIN"] = aa([128, 8, 256], F32)
        S["X0"] = [aa([128, 8], F32) for i in range(2)]
        return S

    def s5_setup(S):
        wb_setup()
        P.cur_prio = 1
        for nm, dn in (("are", "s5_are"), ("aim", "s5_aim"), ("ldt", "s5_ldt"), ("dvec", "s5_d")):
            P.dma("sp", S[nm], S5D[dn][:, :], writes=[buf("s5" + nm)])
        X1 = aa([128, 8, 16], F32)
        X2 = aa([128, 8, 16], F32)
        CX1 = aa([128, 8, 16], F32)
        CX2 = aa([128, 8, 16], F32)
        P.dma("sp", X1, S5D["s5_x1"][:, :, :], writes=[buf("s5X1")])
        P.dma("sp", X2, S5D["s5_x2"][:, :, :], writes=[buf("s5X2")])
        P.dma("sp", CX1, S5D["s5_cx1"][:, :, :], writes=[buf("s5CX1")])
        P.dma("sp", CX2, S5D["s5_cx2"][:, :, :], writes=[buf("s5CX2")])
        sg1 = aa([128, 1], F32)
        sg2 = aa([128, 1], F32)
        dt = aa([128, 8], F32)
        lr = aa([128, 8], F32)
        th = aa([128, 8], F32)
        mlr = aa([128, 8, 9], F32)
        mth = aa([128, 8, 9], F32)
        mcol = aa([128, 9], F32)
        mag = aa([128, 8, 9], F32)
        sn = aa([128, 8, 9], F32)
        cs = aa([128, 8, 9], F32)
        AR = aa([128, 8, 9], F32)
        AI = aa([128, 8, 9], F32)
        t8a = aa([128, 8], F32)
        t8b = aa([128, 8], F32)
        t8c = aa([128, 8], F32)
        cr = aa([128, 8], F32)
        ci = aa([128, 8], F32)
        Bst = aa([128, 8, 16], F32)
        Bsw = aa([128, 8, 16], F32)
        tB = aa([128, 8, 16], F32)
        CA = aa([128, 9, 8, 16], F32)
        Bpad = aa([128, 8, 128], F32)
        ABp = aa([128, 128], F32)
        iot = aa([128, 256], F32)
        ang = aa([128, 256], F32)
        wk = [aa([128, 256], F32) for _ in range(4)]
        wki = aa([128, 256], mybir.dt.int32)
        bT = "s5tmp"

        def dv(fn, extra_r=(), extra_w=()):
            P.op("dve", fn, reads=[buf(bT)] + list(extra_r), writes=[buf(bT)] + list(extra_w))

        def sincos(ang_ap, sin_out, cos_out, n):
            y, kf, f, g_ = wk[0][:, 0:n], wk[1][:, 0:n], wk[2][:, 0:n], wk[3][:, 0:n]
            ki = wki[:, 0:n]
            dv(lambda e: e.tensor_scalar(out=y, in0=ang_ap, scalar1=1.0 / (2 * math.pi), scalar2=None, op0=ALU.mult))
            dv(lambda e: e.tensor_copy(out=ki, in_=y))
            dv(lambda e: e.tensor_copy(out=kf, in_=ki))
            dv(lambda e: e.tensor_tensor(out=f, in0=y, in1=kf, op=ALU.subtract))
            for shift, dst in ((0.0, sin_out), (0.25, cos_out)):
                if shift:
                    dv(lambda e: e.tensor_scalar(out=f, in0=f, scalar1=shift, scalar2=None, op0=ALU.add))
                dv(lambda e: e.tensor_scalar(out=g_, in0=f, scalar1=0.5, scalar2=None, op0=ALU.is_gt))
                dv(lambda e: e.tensor_tensor(out=f, in0=f, in1=g_, op=ALU.subtract))
                dv(lambda e: e.tensor_scalar(out=g_, in0=f, scalar1=-0.5, scalar2=None, op0=ALU.is_lt))
                dv(lambda e: e.tensor_tensor(out=f, in0=f, in1=g_, op=ALU.add))
                P.op("act", lambda e, dst=dst: e.activation(out=dst, in_=f, func=AF.Sin, scale=2 * math.pi),
                     reads=[buf(bT)], writes=[buf(bT)])

        rd_in = [buf("s5are"), buf("s5aim"), buf("s5ldt"), buf("s5dvec"), buf("s5X1"), buf("s5X2"),
                 buf("s5CX1"), buf("s5CX2"), buf("identf")]
        P.op("pool", lambda e: e.memset(sg1[0:64, :], 1.0), reads=rd_in, writes=[buf(bT)])
        P.op("pool", lambda e: e.memset(sg1[64:128, :], -1.0), writes=[buf(bT)])
        P.op("pool", lambda e: e.memset(sg2[0:64, :], -1.0), writes=[buf(bT)])
        P.op("pool", lambda e: e.memset(sg2[64:128, :], 1.0), writes=[buf(bT)])
        P.op("pool", lambda e: e.memset(S["PiT"], 0.0), writes=[buf(bT), buf("s5PiT")])
        P.op("pool", lambda e: e.memset(Bpad, 0.0), writes=[buf(bT)])
        P.op("pool", lambda e: e.memset(S["Wint"], 0.0), writes=[buf(bT), buf("s5Wint")])
        P.op("pool", lambda e: e.iota(mcol, pattern=[[1, 9]], base=0, channel_multiplier=0,
                                      allow_small_or_imprecise_dtypes=True), writes=[buf(bT)])
        P.op("pool", lambda e: e.iota(iot, pattern=[[1, 256]], base=1, channel_multiplier=0,
                                      allow_small_or_imprecise_dtypes=True), writes=[buf(bT)])
        dv(lambda e: e.tensor_copy(out=S["PiT"][0:64, 64:128], in_=identf[0:64, 0:64]), extra_w=[buf("s5PiT")])
        dv(lambda e: e.tensor_scalar(out=S["PiT"][64:128, 0:64], in0=identf[64:128, 64:128], scalar1=-1.0,
                                     scalar2=None, op0=ALU.mult), extra_w=[buf("s5PiT")])
        P.op("act", lambda e: e.activation(out=dt, in_=S["ldt"], func=AF.Exp), reads=[buf(bT)], writes=[buf(bT)])
        dv(lambda e: e.tensor_tensor(out=lr, in0=S["are"], in1=dt, op=ALU.mult))
        dv(lambda e: e.tensor_tensor(out=th, in0=S["aim"], in1=dt, op=ALU.mult))
        for g in range(8):
            dv(lambda e, g=g: e.tensor_scalar(out=mlr[:, g, :], in0=mcol, scalar1=lr[:, g:g + 1], scalar2=None,
                                              op0=ALU.mult))
            dv(lambda e, g=g: e.tensor_scalar(out=mth[:, g, :], in0=mcol, scalar1=th[:, g:g + 1], scalar2=None,
                                              op0=ALU.mult))
        fl = "p g m -> p (g m)"
        P.op("act", lambda e: e.activation(out=mag.rearrange(fl), in_=mlr.rearrange(fl), func=AF.Exp),
             reads=[buf(bT)], writes=[buf(bT)])
        sincos(mth.rearrange(fl), sn.rearrange(fl), cs.rearrange(fl), 72)
        dv(lambda e: e.tensor_tensor(out=AR.rearrange(fl), in0=mag.rearrange(fl), in1=cs.rearrange(fl), op=ALU.mult))
        dv(lambda e: e.tensor_tensor(out=AI.rearrange(fl), in0=mag.rearrange(fl), in1=sn.rearrange(fl), op=ALU.mult))
        dv(lambda e: e.tensor_copy(out=S["rho8"], in_=mag[:, :, 8]), extra_w=[buf("s5rho8")])
        dv(lambda e: e.tensor_copy(out=S["ar8"], in_=AR[:, :, 8]), extra_w=[buf("s5ar8")])
        dv(lambda e: e.tensor_copy(out=S["ai8"], in_=AI[:, :, 8]), extra_w=[buf("s5ai8")])
        dv(lambda e: e.tensor_scalar(out=t8a, in0=AR[:, :, 1], scalar1=-1.0, scalar2=None, op0=ALU.add))
        dv(lambda e: e.tensor_tensor(out=t8b, in0=S["are"], in1=S["are"], op=ALU.mult))
        dv(lambda e: e.tensor_tensor(out=t8c, in0=S["aim"], in1=S["aim"], op=ALU.mult))
        dv(lambda e: e.tensor_tensor(out=t8b, in0=t8b, in1=t8c, op=ALU.add))
        dv(lambda e: e.reciprocal(out=t8b, in_=t8b))
        dv(lambda e: e.tensor_tensor(out=cr, in0=t8a, in1=S["are"], op=ALU.mult))
        dv(lambda e: e.tensor_tensor(out=t8c, in0=AI[:, :, 1], in1=S["aim"], op=ALU.mult))
        dv(lambda e: e.tensor_tensor(out=cr, in0=cr, in1=t8c, op=ALU.add))
        dv(lambda e: e.tensor_tensor(out=cr, in0=cr, in1=t8b, op=ALU.mult))
        dv(lambda e: e.tensor_tensor(out=ci, in0=AI[:, :, 1], in1=S["are"], op=ALU.mult))
        dv(lambda e: e.tensor_tensor(out=t8c, in0=t8a, in1=S["aim"], op=ALU.mult))
        dv(lambda e: e.tensor_tensor(out=ci, in0=ci, in1=t8c, op=ALU.subtract))
        dv(lambda e: e.tensor_tensor(out=ci, in0=ci, in1=t8b, op=ALU.mult))
        dv(lambda e: e.tensor_scalar(out=X2.rearrange("p g h -> p (g h)"), in0=X2.rearrange("p g h -> p (g h)"),
                                     scalar1=sg2[:, 0:1], scalar2=None, op0=ALU.mult))
        dv(lambda e: e.tensor_scalar(out=CX1.rearrange("p g h -> p (g h)"), in0=CX1.rearrange("p g h -> p (g h)"),
                                     scalar1=sg1[:, 0:1], scalar2=None, op0=ALU.mult))
        for g in range(8):
            dv(lambda e, g=g: e.tensor_scalar(out=tB[:, g, :], in0=X2[:, g, :], scalar1=ci[:, g:g + 1], scalar2=None,
                                              op0=ALU.mult))
            dv(lambda e, g=g: e.scalar_tensor_tensor(out=Bst[:, g, :], in0=X1[:, g, :], scalar=cr[:, g:g + 1],
                                                     in1=tB[:, g, :], op0=ALU.mult, op1=ALU.add))
            dv(lambda e, g=g: e.tensor_scalar(out=tB[:, g, :], in0=X1[:, g, :], scalar1=ci[:, g:g + 1], scalar2=None,
                                              op0=ALU.mult))
            dv(lambda e, g=g: e.scalar_tensor_tensor(out=Bsw[:, g, :], in0=X2[:, g, :], scalar=cr[:, g:g + 1],
                                                     in1=tB[:, g, :], op0=ALU.mult, op1=ALU.subtract))
            dv(lambda e, g=g: e.tensor_copy(out=Bpad[:, g, 16 * g:16 * g + 16], in_=Bst[:, g, :]))
            for m in range(9):
                dv(lambda e, g=g, m=m: e.tensor_scalar(out=tB[:, g, :], in0=CX2[:, g, :], scalar1=AI[:, g, m:m + 1],
                                                       scalar2=None, op0=ALU.mult))
                dv(lambda e, g=g, m=m: e.scalar_tensor_tensor(out=CA[:, m, g, :], in0=CX1[:, g, :],
                                                              scalar=AR[:, g, m:m + 1], in1=tB[:, g, :],
                                                              op0=ALU.mult, op1=ALU.subtract))
        for j in range(8):
            for g in range(8):
                dv(lambda e, j=j, g=g: e.tensor_copy(out=S["Wint"][:, j, g, 16 * g:16 * g + 16], in_=CA[:, j + 1, g, :]),
                   extra_w=[buf("s5Wint")])
        for tau in range(8):
            for g in range(8):
                P.op("pe", lambda e, tau=tau, g=g: e.matmul(pM[:, 16 * g:16 * g + 16], lhsT=Bpad[:, g, :],
                                                            rhs=CA[:, tau, g, :], start=True, stop=True),
                     reads=[buf(bT)], writes=[buf("pM")])
            if tau == 0:
                P.op("dve", lambda e: e.scalar_tensor_tensor(out=S["Wfir"][:, 0, :], in0=identf, scalar=S["dvec"][:, 0:1],
                                                             in1=pM[:, 0:128], op0=ALU.mult, op1=ALU.add),
                     reads=[buf("pM"), buf(bT)], writes=[buf("s5Wfir")])
            else:
                P.op("dve", lambda e, tau=tau: e.tensor_copy(out=S["Wfir"][:, tau, :], in_=pM[:, 0:128]),
                     reads=[buf("pM")], writes=[buf("s5Wfir")])
        for s in range(8):
            m = 7 - s
            for g in range(8):
                dv(lambda e, g=g, m=m: e.tensor_scalar(out=tB[:, g, :], in0=Bsw[:, g, :], scalar1=AI[:, g, m:m + 1],
                                                       scalar2=None, op0=ALU.mult))
                dv(lambda e, g=g: e.memset(ABp, 0.0))
                dv(lambda e, g=g, m=m: e.scalar_tensor_tensor(out=ABp[:, 16 * g:16 * g + 16], in0=Bst[:, g, :],
                                                              scalar=AR[:, g, m:m + 1], in1=tB[:, g, :],
                                                              op0=ALU.mult, op1=ALU.add))
                P.op("pe", lambda e: e.transpose(out=pM[:, 0:128], in_=ABp, identity=identf),
                     reads=[buf(bT), buf("identf")], writes=[buf("pM")])
                P.op("dve", lambda e, s=s, g=g: e.tensor_copy(out=S["Wst"][:, s, g, :], in_=pM[:, 0:128]),
                     reads=[buf("pM")], writes=[buf("s5Wst"), buf(bT)])
        for g in range(8):
            dv(lambda e, g=g: e.tensor_scalar(out=ang, in0=iot, scalar1=mth[:, g, 8:9], scalar2=None, op0=ALU.mult))
            sincos(ang, S["SIN"][:, g, :], S["COS"][:, g, :], 256)
        P.op("dve", lambda e: e.tensor_copy(out=wk[0], in_=wk[0]), reads=[buf(bT)],
             writes=[buf(bT), buf("s5SIN"), buf("s5COS")])
        P.op("dve", lambda e: e.memset(S["X0"][0], 0.0), writes=[buf("s5X00")])
        P.cur_prio = 0
        return S

    def s5_alloc_work():
        Wk = {}
        Wk["Ssb"] = aa([128, 256], F32)
        Wk["t1"] = aa([128, 256], F32)
        Wk["t2"] = aa([128, 256], F32)
        Wk["Wsc"] = aa([128, 256], F32)
        Wk["Xn"] = aa([128, 256], F32)
        Wk["Xin"] = aa([128, 8, 256], BF16)
        for nm in ("S4", "T1", "T2"):
            Wk[nm] = aa([128, 4, 256], F32)
        Wk["W4"] = Wk["S4"]
        Wk["X4"] = Wk["T2"]
        Wk["ysb"] = aa([128, 2, 256], F32)
        Wk["g1"] = aa([128, 2, 256], F32)
        Wk["g2"] = aa([128, 2, 256], F32)
        Wk["zfull"] = aa([128, 2048], BF16)
        assert ar_off[0] >= setup_extent[0], (ar_off[0], setup_extent[0])
        Wk["ub"] = aa([128, 8, 256], BF16)
        return Wk

    GC = math.sqrt(2.0 / math.pi)
    YBANK = [zb(2, 0), zb(2, 1), zb(3, 0), zb(3, 1)]
    YBUF = ["pZ20", "pZ21", "pZ30", "pZ31"]

    def s5_fir_inter(S, Wk, nc_, dst_ap, dst_buf):
        ub, Xin, zfull = Wk["ub"], Wk["Xin"], Wk["zfull"]
        zv = zfull.rearrange("p (c j) -> p j c", j=8)
        for j in range(8):
            bk = j // 2
            Yj = YBANK[bk][:, (j % 2) * 256:(j % 2) * 256 + nc_]
            for tau in range(j + 1):
                P.op("pe", lambda e, Yj=Yj, tau=tau, j=j: e.matmul(Yj, lhsT=S["Wfir"][:, tau, :], rhs=ub[:, j - tau, 0:nc_],
                                                                 start=(tau == 0), stop=False),
                     reads=[buf("s5ub"), buf("s5Wfir")], writes=[buf(YBUF[bk])])
            for g in range(8):
                P.op("pe", lambda e, Yj=Yj, j=j, g=g: e.matmul(Yj, lhsT=S["Wint"][:, j, g, :], rhs=Xin[:, g, 0:nc_],
                                                             start=False, stop=(g == 7)),
                     reads=[buf("s5Xin"), buf("s5Wint")], writes=[buf(YBUF[bk])])
            if j % 2 == 1:
                Yb = YBANK[bk].rearrange("p (j c) -> p j c", j=2)[:, :, 0:nc_]
                ysb, g1, g2 = Wk["ysb"][:, :, 0:nc_], Wk["g1"][:, :, 0:nc_], Wk["g2"][:, :, 0:nc_]
                P.op("act", lambda e, Yb=Yb: e.activation(out=ysb, in_=Yb, func=AF.Copy),
                     reads=[buf(YBUF[bk])], writes=[buf("s5ysb")])
                P.op("dve", lambda e: e.tensor_tensor(out=g1, in0=ysb, in1=ysb, op=ALU.mult),
                     reads=[buf("s5ysb")], writes=[buf("s5g1")])
                P.op("dve", lambda e: e.tensor_scalar(out=g1, in0=g1, scalar1=0.044715, scalar2=1.0, op0=ALU.mult,
                                                      op1=ALU.add), reads=[buf("s5g1")], writes=[buf("s5g1")])
                P.op("dve", lambda e: e.tensor_tensor(out=g1, in0=g1, in1=ysb, op=ALU.mult),
                     reads=[buf("s5g1"), buf("s5ysb")], writes=[buf("s5g1")])
                P.op("dve", lambda e: e.tensor_scalar(out=g1, in0=g1, scalar1=-18.0, scalar2=None, op0=ALU.max),
                     reads=[buf("s5g1")], writes=[buf("s5g1")])
                P.op("act", lambda e: e.activation(out=g2, in_=g1, func=AF.Exp, scale=-2.0 * GC),
                     reads=[buf("s5g1")], writes=[buf("s5g2")])
                P.op("act", lambda e: e.activation(out=g2, in_=g2, func=AF.Ln, bias=1.0, scale=1.0),
                     reads=[buf("s5g2")], writes=[buf("s5g2")])
                P.op("act", lambda e: e.activation(out=g2, in_=g2, func=AF.Exp, scale=-1.0),
                     reads=[buf("s5g2")], writes=[buf("s5g2")])
                P.op("dve", lambda e, j=j: e.tensor_tensor(out=zv[:, j - 1:j + 1, 0:nc_], in0=ysb, in1=g2, op=ALU.mult),
                     reads=[buf("s5ysb"), buf("s5g2")], writes=[buf("s5zfull")])
        P.dma("pool", dst_ap, zfull[:, 0:nc_ * 8], reads=[buf("s5zfull")], writes=[dst_buf])

    def s5_states(S, Wk, nc_, g, dstps, dstbuf):
        for s in range(8):
            P.op("pe", lambda e, s=s: e.matmul(dstps[:, 0:nc_], lhsT=S["Wst"][:, s, g, :], rhs=Wk["ub"][:, s, 0:nc_],
                                                 start=(s == 0), stop=(s == 7)),
                 reads=[buf("s5ub"), buf("s5Wst")], writes=[buf(dstbuf)])

    def s5_supertile(S, Wk, sti, ntok, tok0, final_k=None):
        nc_ = ntok // 8
        X0 = S["X0"][sti % 2]
        X0n = S["X0"][(sti + 1) % 2]
        bX0, bX0n = buf(f"s5X0{sti % 2}"), buf(f"s5X0{(sti + 1) % 2}")
        Xin = Wk["Xin"]
        S4, T1, T2, W4, X4 = (Wk[k] for k in ("S4", "T1", "T2", "W4", "X4"))
        ZA = Z[0].rearrange("p (g c) -> p g c", g=4)
        ZB = Z[1].rearrange("p (g c) -> p g c", g=4)
        bZA = [buf("pZ00"), buf("pZ01")]
        bZB = [buf("pZ10"), buf("pZ11")]
        for hf in range(2):
            gs = slice(4 * hf, 4 * hf + 4)
            for gl in range(4):
                g = 4 * hf + gl
                for s in range(8):
                    P.op("pe", lambda e, s=s, g=g, gl=gl: e.matmul(ZA[:, gl, 0:nc_], lhsT=S["Wst"][:, s, g, :],
                                                                   rhs=Wk["ub"][:, s, 0:nc_], start=(s == 0), stop=(s == 7)),
                         reads=[buf("s5ub"), buf("s5Wst")], writes=[bZA[gl // 2]])
            P.op("dve", lambda e: e.tensor_copy(out=S4[:, :, 0:nc_], in_=ZA[:, :, 0:nc_]), reads=bZA, writes=[buf("s5S4")],
                 deps=(setup_done if (sti == 0 and hf == 0) else ()))
            for gl in range(4):
                P.op("pe", lambda e, gl=gl: e.matmul(ZB[:, gl, 0:nc_], lhsT=S["PiT"], rhs=S4[:, gl, 0:nc_], start=True, stop=True),
                     reads=[buf("s5S4"), buf("s5PiT")], writes=[bZB[gl // 2]])
            P.op("dve", lambda e, gs=gs: e.tensor_tensor(out=T1[:, :, 0:nc_], in0=S["SIN"][:, gs, 0:nc_], in1=ZB[:, :, 0:nc_],
                                                         op=ALU.mult), reads=bZB + [buf("s5SIN")], writes=[buf("s5T1")])
            P.op("dve", lambda e, gs=gs: e.tensor_tensor(out=T2[:, :, 0:nc_], in0=S["COS"][:, gs, 0:nc_], in1=S4[:, :, 0:nc_],
                                                         op=ALU.mult), reads=[buf("s5S4"), buf("s5COS")], writes=[buf("s5T2")])
            P.op("dve", lambda e: e.tensor_tensor(out=T2[:, :, 0:nc_], in0=T2[:, :, 0:nc_], in1=T1[:, :, 0:nc_], op=ALU.subtract),
                 reads=[buf("s5T1"), buf("s5T2")], writes=[buf("s5T2")])
            for gl in range(4):
                g = 4 * hf + gl
                P.op("dve", lambda e, g=g, gl=gl: e.tensor_tensor_scan(
                    out=W4[:, gl, 0:nc_], data0=S["rho8"][:, g:g + 1].to_broadcast([128, nc_]), data1=T2[:, gl, 0:nc_],
                    initial=X0[:, g:g + 1], op0=ALU.mult, op1=ALU.add),
                    reads=[buf("s5T2"), buf("s5rho8"), bX0], writes=[buf("s5S4")])
            for gl in range(4):
                P.op("pe", lambda e, gl=gl: e.matmul(ZB[:, gl, 0:nc_], lhsT=S["PiT"], rhs=W4[:, gl, 0:nc_], start=True, stop=True),
                     reads=[buf("s5S4"), buf("s5PiT")], writes=[bZB[gl // 2]])
            P.op("dve", lambda e, gs=gs: e.tensor_tensor(out=T1[:, :, 0:nc_], in0=S["SIN"][:, gs, 0:nc_], in1=ZB[:, :, 0:nc_],
                                                         op=ALU.mult), reads=bZB + [buf("s5SIN")], writes=[buf("s5T1")])
            P.op("dve", lambda e, gs=gs: e.tensor_tensor(out=X4[:, :, 0:nc_], in0=S["COS"][:, gs, 0:nc_], in1=W4[:, :, 0:nc_],
                                                         op=ALU.mult), reads=[buf("s5S4"), buf("s5COS")], writes=[buf("s5T2")])
            P.op("dve", lambda e: e.tensor_tensor(out=X4[:, :, 0:nc_], in0=X4[:, :, 0:nc_], in1=T1[:, :, 0:nc_], op=ALU.add),
                 reads=[buf("s5T1"), buf("s5T2")], writes=[buf("s5T2")])
            P.op("dve", lambda e, gs=gs: e.tensor_copy(out=Xin[:, gs, 0:1], in_=X0[:, gs].rearrange("p (g o) -> p g o", o=1)),
                 reads=[bX0], writes=[buf("s5Xin")])
            if nc_ > 1:
                P.op("dve", lambda e, gs=gs: e.tensor_copy(out=Xin[:, gs, 1:nc_], in_=X4[:, :, 0:nc_ - 1]),
                     reads=[buf("s5T2")], writes=[buf("s5Xin")])
            P.op("dve", lambda e, gs=gs: e.tensor_copy(out=X0n[:, gs].rearrange("p (g o) -> p g o", o=1), in_=X4[:, :, nc_ - 1:nc_]),
                 reads=[buf("s5T2")], writes=[bX0n])
            if final_k is not None:
                P.op("dve", lambda e, gs=gs: e.tensor_copy(out=S5FIN[:, gs].rearrange("p (g o) -> p g o", o=1),
                                                           in_=X4[:, :, final_k - 1:final_k]),
                     reads=[buf("s5T2")], writes=[buf("s5fin")])
        s5_fir_inter(S, Wk, nc_, zin_ap(tok0, ntok), buf(f"zinP{tok0 // PZ}"))

    def s5_sample(S, Wk):
        nc_ = 32
        Ssb, t1, t2, Xn, Xin = (Wk[k] for k in ("Ssb", "t1", "t2", "Xn", "Xin"))
        pA, pB, bA, bB = zb(1, 0), zb(1, 1), "pZ10", "pZ11"
        P.dma("sp", S5X0, s5x0_d[:, :, :], writes=[buf("s5x0s")])
        Sv = Ssb[:, 0:32].rearrange("p (q c) -> p q c", c=2)
        Xv = Xin[:, :, 0:32].rearrange("p g (q c) -> p g q c", c=2)
        for g in range(8):
            s5_states(S, Wk, nc_, g, pA, bA)
            P.op("dve", lambda e: e.tensor_copy(out=Ssb[:, 0:nc_], in_=pA[:, 0:nc_]), reads=[buf(bA)], writes=[buf("s5Ssb")])
            Z0 = S5X0[:, g, :]
            X1 = Xn[:, 0:16]
            Fn = Xn[:, 16:32]
            P.op("pe", lambda e, Z0=Z0: e.matmul(pB[:, 0:16], lhsT=S["PiT"], rhs=Z0, start=True, stop=True),
                 reads=[buf("s5x0s"), buf("s5PiT")], writes=[buf(bB)])
            P.op("dve", lambda e, g=g: e.tensor_scalar(out=t1[:, 0:16], in0=pB[:, 0:16], scalar1=S["ai8"][:, g:g + 1], scalar2=None,
                                                       op0=ALU.mult), reads=[buf(bB), buf("s5ai8")], writes=[buf("s5t1")])
            P.op("dve", lambda e, g=g, Z0=Z0: e.scalar_tensor_tensor(out=X1, in0=Z0, scalar=S["ar8"][:, g:g + 1], in1=t1[:, 0:16],
                                                                      op0=ALU.mult, op1=ALU.add),
                 reads=[buf("s5x0s"), buf("s5t1"), buf("s5ar8")], writes=[buf("s5Xn")])
            P.op("dve", lambda e: e.tensor_tensor(out=X1, in0=X1, in1=Sv[:, :, 0], op=ALU.add),
                 reads=[buf("s5Xn"), buf("s5Ssb")], writes=[buf("s5Xn")])
            P.op("pe", lambda e: e.matmul(pB[:, 0:16], lhsT=S["PiT"], rhs=X1, start=True, stop=True),
                 reads=[buf("s5Xn"), buf("s5PiT")], writes=[buf(bB)])
            P.op("dve", lambda e, g=g: e.tensor_scalar(out=t1[:, 0:16], in0=pB[:, 0:16], scalar1=S["ai8"][:, g:g + 1], scalar2=None,
                                                       op0=ALU.mult), reads=[buf(bB), buf("s5ai8")], writes=[buf("s5t1")])
            P.op("dve", lambda e, g=g: e.scalar_tensor_tensor(out=Fn, in0=X1, scalar=S["ar8"][:, g:g + 1], in1=t1[:, 0:16],
                                                              op0=ALU.mult, op1=ALU.add),
                 reads=[buf("s5Xn"), buf("s5t1"), buf("s5ar8")], writes=[buf("s5Xn")])
            P.op("dve", lambda e, g=g: e.tensor_tensor(out=S5FINS[:, g, :], in0=Fn, in1=Sv[:, :, 1], op=ALU.add),
                 reads=[buf("s5Xn"), buf("s5Ssb")], writes=[buf("s5fins")])
            P.op("dve", lambda e, g=g, Z0=Z0: e.tensor_copy(out=Xv[:, g, :, 0], in_=Z0), reads=[buf("s5x0s")], writes=[buf("s5Xin")])
            P.op("dve", lambda e, g=g: e.tensor_copy(out=Xv[:, g, :, 1], in_=X1), reads=[buf("s5Xn")], writes=[buf("s5Xin")])
        s5_fir_inter(S, Wk, nc_, zin_ap(TP, 256), buf(f"zinP{TP // PZ}"))

    GROUPS = [[0, 1, 2, 3], [4, 5, 6, 7]]

    def ag1(pi):
        P.coll(lambda e: e.collective_compute("AllGather", ALU.bypass, replica_groups=GROUPS,
                                              ins=[zin_p[pi]], outs=[zall_p[pi]]),
               reads=[buf(f"zinP{pi}")], writes=[buf(f"zallP{pi}")])

    def phase_a_tile(ti, sample=False):
        NS = 2 if sample else 4
        N = NS * 128
        t0 = TP if sample else ti * TW
        j0 = NBLK if sample else ti * 4
        par = ti % 2
        xsrc = xs_d if sample else x_d
        xrow0 = 0 if sample else t0
        ko, vo, lo = (ks_o, vs_o, lfs_o) if sample else (k_o, v_o, lf_o)
        orow0 = 0 if sample else t0
        XT = xT[par]
        bXT = buf(f"xT{par}")
        for s in range(NS):
            slot = (ti * 4 + s) % XR
            bx = buf(f"xr{slot}")
            P.dma("sp", xr[slot], xsrc[xrow0 + s * 128:xrow0 + (s + 1) * 128, :], writes=[bx])
            P.op("act", lambda e, slot=slot, s=s: e.activation(
                out=junk, in_=xr[slot], func=AF.Square, accum_out=ss[:, s:s + 1]),
                reads=[bx], writes=[buf("junk"), buf("ss")], fuse=False, cost=1.2)
        P.op("act", lambda e: e.activation(out=lnt[:, 0:NS], in_=ss[:, 0:NS], func=AF.Ln, bias=EPS, scale=1.0 / D_MODEL),
             reads=[buf("ss")], writes=[buf("lnt")])
        RS = rstd[par]
        bRS = buf(f"rstd{par}")
        P.op("act", lambda e: e.activation(out=RS[:, 0:NS], in_=lnt[:, 0:NS], func=AF.Exp, scale=-0.5),
             reads=[buf("lnt")], writes=[bRS])
        for s in range(NS):
            slot = (ti * 4 + s) % XR
            bx = buf(f"xr{slot}")
            xp = s % 2
            P.op("dve", lambda e, slot=slot, s=s, xp=xp: e.tensor_scalar(
                out=xs[xp], in0=xr[slot], scalar1=RS[:, s:s + 1], scalar2=None, op0=ALU.mult),
                reads=[bx, bRS], writes=[buf(f"xs{xp}")])
            for kc in range(8):
                P.op("pe", lambda e, xp=xp, kc=kc: e.transpose(
                    out=pT[xp][:, kc, :], in_=xs[xp][:, kc * 128:(kc + 1) * 128], identity=identb),
                    reads=[buf(f"xs{xp}"), buf("identb")], writes=[buf(f"pT{xp}")])
            if s % 2 == 0:
                P.op("dve", lambda e, xp=xp, s=s: e.tensor_copy(out=XT[:, :, s * 128:(s + 1) * 128], in_=pT[xp]),
                     reads=[buf(f"pT{xp}")], writes=[bXT])
            else:
                P.op("act", lambda e, xp=xp, s=s: e.activation(
                    out=XT[:, :, s * 128:(s + 1) * 128], in_=pT[xp], func=AF.Copy),
                    reads=[buf(f"pT{xp}")], writes=[bXT])
            yield "front"
        yield "FRONT_DONE"

        pp_i = [0]

        def proj(col0):
            i = pp_i[0] % 2
            pp_i[0] += 1
            for kc in range(8):
                P.op("pe", lambda e, kc=kc, i=i: e.matmul(
                    pP[i][:, 0:N], lhsT=Wb[:, kc, col0:col0 + 128], rhs=XT[:, kc, 0:N], start=(kc == 0), stop=(kc == 7)),
                    reads=[bXT, buf("Wb")], writes=[buf(f"pP{i}")])
            return i

        def headnorm2(items):
            for k_, (i, gain, gain_buf, out_ap, out_buf) in enumerate(items):
                P.op("act", lambda e, i=i, k_=k_: e.activation(out=sq2[k_][:, 0:N], in_=pP[i][:, 0:N], func=AF.Square),
                     reads=[buf(f"pP{i}")], writes=[buf(SQN[k_])])
            for k_, (i, gain, gain_buf, out_ap, out_buf) in enumerate(items):
                P.op("pe", lambda e, k_=k_: e.matmul(pMs[k_][:, 0:N], lhsT=BO, rhs=sq2[k_][:, 0:N], start=True, stop=True),
                     reads=[buf(SQN[k_]), buf("BO")], writes=[buf(pMn[k_])])
            for k_, (i, gain, gain_buf, out_ap, out_buf) in enumerate(items):
                P.op("act", lambda e, k_=k_: e.activation(out=ln2[k_][:, 0:N], in_=pMs[k_][:, 0:N], func=AF.Ln, bias=EPS, scale=1.0),
                     reads=[buf(pMn[k_])], writes=[buf(LNN[k_])])
            for k_, (i, gain, gain_buf, out_ap, out_buf) in enumerate(items):
                P.op("act", lambda e, k_=k_: e.activation(out=rr2[k_][:, 0:N], in_=ln2[k_][:, 0:N], func=AF.Exp, scale=-0.5),
                     reads=[buf(LNN[k_])], writes=[buf(RRN[k_])])
            for k_, (i, gain, gain_buf, out_ap, out_buf) in enumerate(items):
                P.op("dve", lambda e, i=i, k_=k_, gain=gain, out_ap=out_ap: e.scalar_tensor_tensor(
                    out=out_ap, in0=pP[i][:, 0:N], scalar=gain, in1=rr2[k_][:, 0:N], op0=ALU.mult, op1=ALU.mult),
                    reads=[buf(f"pP{i}"), buf(RRN[k_]), gain_buf], writes=[out_buf])

        sq2, ln2, rr2 = [sq, sqB], [lnb, lnbB], [rr, rrB]
        SQN, LNN, RRN = ["sq0", "sgb0"], ["lnb0", "kst"], ["rr0", "vst"]
        pMs, pMn = [pM, pK.rearrange("p s f -> p (s f)")], ["pM", "pK"]
        iq = proj(0)
        ik = proj(128)
        headnorm2([(iq, qg8[:, 0:1], buf("qg8"), qnb[:, 0:N], buf("qnb")),
                   (ik, kg[:, 0:1], buf("kg"), knf[:, 0:N], buf("knf"))])
        for h in range(2):
            P.dma("pool", qt_d[h, 0:64, t0:t0 + N], qnb[h * 64:(h + 1) * 64, 0:N],
                  reads=[buf("qnb")], writes=[buf(f"qt_d{ti}")])
        yield "back"
        P.op("act", lambda e: e.activation(out=knb[:, 0:N], in_=knf[:, 0:N], func=AF.Copy),
             reads=[buf("knf")], writes=[buf("knb")])
        for h in range(2):
            P.dma("pool", kt_d[h, :, t0:t0 + N], knb[h * 64:(h + 1) * 64, 0:N],
                  reads=[buf("knb")], writes=[buf(f"kt_d{ti}")])
        for s in range(NS):
            P.op("pe", lambda e, s=s: e.transpose(out=pK[:, s, :], in_=knf[:, s * 128:(s + 1) * 128], identity=identf),
                 reads=[buf("knf"), buf("identf")], writes=[buf("pK")])
        P.op("dve", lambda e: e.tensor_copy(out=kst[:, 0:NS, :], in_=pK[:, 0:NS, :]), reads=[buf("pK")], writes=[buf("kst")])
        out_dmas.append(P.dma("pool", ko[orow0:orow0 + N, :].rearrange("(s p) f -> p s f", p=128), kst[:, 0:NS, :],
                              reads=[buf("kst")]))
        yield "back"
        for s in range(NS):
            for kc in range(8):
                P.op("pe", lambda e, s=s, kc=kc: e.matmul(
                    pV[:, s, :], lhsT=XT[:, kc, s * 128:(s + 1) * 128], rhs=Wb[:, kc, 256:384],
                    start=(kc == 0), stop=(kc == 7)),
                    reads=[bXT, buf("Wb")], writes=[buf("pV")])
        for s in range(NS):
            for kc in range(8):
                P.op("pe", lambda e, s=s, kc=kc: e.matmul(
                    pS[:, 2 * s:2 * s + 2], lhsT=XT[:, kc, s * 128:(s + 1) * 128], rhs=Wb[:, kc, 768:770],
                    start=(kc == 0), stop=(kc == 7)),
                    reads=[bXT, buf("Wb")], writes=[buf("pS")])
        P.op("act", lambda e: e.activation(out=vst[:, 0:NS, :], in_=pV[:, 0:NS, :], func=AF.Copy),
             reads=[buf("pV")], writes=[buf("vst")])
        out_dmas.append(P.dma("pool", vo[orow0:orow0 + N, :].rearrange("(s p) f -> p s f", p=128), vst[:, 0:NS, :],
                              reads=[buf("vst")]))
        P.op("dve", lambda e: e.tensor_copy(
            out=VP[:, j0:j0 + NS, :, 0:64], in_=pV[:, 0:NS, :].rearrange("p s (h d) -> p s h d", h=2)),
            reads=[buf("pV")], writes=[buf("VP")])
        yield "back"
        pSv = pS[:, 0:2 * NS].rearrange("p (s h) -> p s h", h=2)
        for h in range(2):
            P.op("act", lambda e, h=h: e.activation(
                out=ef[:, 0:NS, h], in_=pSv[:, :, h], func=AF.Exp, bias=nbf[:, h:h + 1], scale=-1.0),
                reads=[buf("pS"), buf("nbf")], writes=[buf("ef")])
        P.op("act", lambda e: e.activation(out=lf[:, 0:NS, :], in_=ef[:, 0:NS, :], func=AF.Ln, bias=1.0, scale=1.0),
             reads=[buf("ef")], writes=[buf("lf")])
        P.op("dve", lambda e: e.tensor_scalar(out=lf[:, 0:NS, :], in0=lf[:, 0:NS, :], scalar1=-1.0, scalar2=None, op0=ALU.mult),
             reads=[buf("lf")], writes=[buf("lf")])
        out_dmas.append(P.dma("pool", lo[orow0:orow0 + N, :].rearrange("(s p) h -> p s h", p=128), lf[:, 0:NS, :],
                              reads=[buf("lf")]))
        yield "back"
        for s in range(NS):
            P.op("pe", lambda e, s=s: e.transpose(out=pM[0:2, s * 128:(s + 1) * 128], in_=lf[:, s, :],
                                                   identity=identf),
                 reads=[buf("lf"), buf("identf")], writes=[buf("pM")])
        CR = cumrow[par]
        CRp = cumrow[1 - par]
        init = 0.0 if (ti == 0 or sample) else CRp[:, TW - 1:TW]
        d0 = segm if sample else ones2
        P.op("dve", lambda e: e.tensor_tensor_scan(
            out=CR[:, 0:N], data0=d0[:, 0:N], data1=pM[0:2, 0:N], initial=init, op0=ALU.mult, op1=ALU.add),
            reads=[buf("pM"), buf("ones2"), buf("segm"), buf(f"cumrow{1 - par}")], writes=[buf(f"cumrow{par}")])
        P.op("dve", lambda e: e.tensor_copy(out=cumb[:, 0:N], in_=CR[:, 0:N]),
             reads=[buf(f"cumrow{par}")], writes=[buf("cumb")])
        for h in range(2):
            P.dma("pool", qt_d[h, 64:65, t0:t0 + N], cumb[h:h + 1, 0:N],
                  reads=[buf("cumb")], writes=[buf(f"qt_d{ti}")])
        for s in range(NS):
            P.op("pe", lambda e, s=s: e.transpose(out=pS[:, 16 + 2 * s:16 + 2 * s + 2],
                                                   in_=CR[:, s * 128:(s + 1) * 128], identity=identf[0:2, 0:2]),
                 reads=[buf(f"cumrow{par}"), buf("identf")], writes=[buf("pS")])
        P.op("dve", lambda e: e.tensor_scalar(
            out=NCK[:, j0:j0 + NS, :], in0=pS[:, 16:16 + 2 * NS].rearrange("p (s h) -> p s h", h=2),
            scalar1=-1.0, scalar2=None, op0=ALU.mult),
            reads=[buf("pS")], writes=[buf("NCK")])
        yield "back"
        iga = proj(384)
        igs = proj(640)
        silu2_from_psum([(iga, sgb[0], buf("sgb0")), (igs, sgb[1], buf("sgb1"))], N,
                        [(lnb, "lnb0", rr, "rr0"), (lnbB, "kst", rrB, "vst")])
        P.dma("pool", sgs_d[:, t0:t0 + N], sgb[1][:, 0:N], reads=[buf("sgb1")], writes=[buf(f"sgs_d{ti}")])
        for h in range(2):
            P.dma("pool", sga_d[h, :, t0:t0 + N], sgb[0][h * 64:(h + 1) * 64, 0:N],
                  reads=[buf("sgb0")], writes=[buf(f"sga_d{ti}")])
        yield "back"
        i = proj(512)
        uo = 0 if sample else (ti % 4) * 64
        P.op("act", lambda e, i=i: e.activation(out=WK["ub"][:, :, uo:uo + N // 8],
                                                in_=pP[i][:, 0:N].rearrange("p (c j) -> p j c", j=8), func=AF.Copy),
             reads=[buf(f"pP{i}")], writes=[buf("s5ub")])
        yield "back"
        if sample:
            P.cur_deps = tuple(setup_done)
            s5_sample(S5S, WK)
            P.cur_deps = ()
            if "X" in stages:
                ag1(TP // PZ)
        elif ti % 4 == 3 or ti == NTILE - 1:
            sti = ti // 4
            tok0 = sti * 2048
            ntok = t0 + TW - tok0
            tf = min(L_REAL, TP)
            fk = None
            if (tf - 1) // 2048 == sti:
                fk = (tf - tok0) // 8
            if sti == 0:
                P.cur_deps = tuple(setup_done)
            s5_supertile(S5S, WK, sti, ntok, tok0, fk)
            P.cur_deps = ()
            if "X" in stages and (tok0 + ntok) % PZ == 0:
                ag1(tok0 // PZ)

    if "A" in stages:
        S5S = s5_alloc_weights()
        XR = 4
        xr = [aa([128, D_MODEL], F32) for i in range(XR)]
        junk = aa([128, D_MODEL], BF16)
        xs = [aa([128, D_MODEL], BF16) for i in range(2)]
        xT = [aa([128, 8, TW], BF16) for i in range(2)]
        ss = aa([128, 4], F32)
        lnt = aa([128, 4], F32)
        rstd = [aa([128, 4], F32) for i in range(2)]
        sq = aa([128, TW], BF16)
        lnb = aa([128, TW], F32)
        rr = aa([128, TW], F32)
        qnb = aa([128, TW], BF16)
        knf = aa([128, TW], F32)
        knb = aa([128, TW], BF16)
        kst = aa([128, 4, 128], F32)
        vst = aa([128, 4, 128], F32)
        ef = aa([128, 4, 2], F32)
        lf = aa([128, 4, 2], F32)
        cumb = aa([2, TW], BF16)
        sgb = [aa([128, TW], BF16) for i in range(2)]
        sqB = sgb[0]
        lnbB = kst.rearrange("p s f -> p (s f)")
        rrB = vst.rearrange("p s f -> p (s f)")

        mark2 = ar_off[0]
        s5_setup(S5S)
        setup_extent[0] = ar_off[0]
        setup_done = [buf("s5tmp").w, buf("Wb").w]
        ar_off[0] = mark2
        WK = s5_alloc_work()
        gens = [phase_a_tile(ti) for ti in range(NTILE)] + [phase_a_tile(NTILE, sample=True)]
        front_done = [False] * len(gens)

        def step(gi):
            try:
                r = next(gens[gi])
            except StopIteration:
                return False
            if r == "FRONT_DONE":
                front_done[gi] = True
            return True

        while not front_done[0]:
            step(0)
        for gi in range(len(gens)):
            alive = True
            while alive:
                alive = step(gi)
                if gi + 1 < len(gens) and not front_done[gi + 1]:
                    step(gi + 1)
            if gi + 1 < len(gens):
                while not front_done[gi + 1]:
                    step(gi + 1)
        out_dmas.append(P.dma("sp", s5fin_o[:, :], S5FIN, reads=[buf("s5fin")]))
        out_dmas.append(P.dma("sp", s5fins_o[:, :, :], S5FINS, reads=[buf("s5fins")]))

    n_real = min(L_REAL, TP)
    qbs = []
    q0 = 0
    while q0 < n_real:
        ql = min(QB, n_real - q0)
        qbs.append((q0, ql))
        q0 += ql

    def tiles_of(a, b):
        return range(a // TW, (b + TW - 1) // TW)

    step = [0]

    def attend(qi, h, q0, qlen):
        par = qi % 2
        nkb = (q0 + qlen + 127) // 128
        halves = [(a, min(a + 512, qlen)) for a in range(0, qlen, 512)]
        pO = Z[2]
        bQ = buf(f"QA{par}{h}")
        bKT = buf(f"KT{h}")
        base = step[0]
        step[0] += nkb

        def clo(j):
            return max(0, 128 * j - q0)

        def zbufs(zi, lo, hi):
            return [buf(f"pZ{zi}{hf}") for hf in range(2) if lo < (hf + 1) * 512 and hi > hf * 512]

        def qk(j):
            zi = (base + j) % 2
            c_lo = clo(j)
            for (a, b) in halves:
                lo = max(a, c_lo)
                if lo >= b:
                    continue
                diag = (128 * j >= q0) and (a <= c_lo < b)
                P.op("pe", lambda e, zi=zi, lo=lo, b=b, diag=diag: e.matmul(
                    Z[zi][:, lo:b], lhsT=KT[h][:, 128 * j:128 * j + 128], rhs=QA[par][h][:, lo:b],
                    start=True, stop=not diag),
                    reads=[bKT, bQ], writes=zbufs(zi, lo, b))
                if diag:
                    w = min(128, qlen - c_lo)
                    P.op("pe", lambda e, zi=zi, c_lo=c_lo, w=w: e.matmul(
                        Z[zi][:, c_lo:c_lo + w], lhsT=identb, rhs=MN[:, 0:w], start=False, stop=True),
                        reads=[buf("identb"), buf("MN")], writes=zbufs(zi, c_lo, c_lo + w))

        def ex(j):
            zi = (base + j) % 2
            pi = (base + j) % 3
            c_lo = clo(j)
            P.op("act", lambda e, zi=zi, pi=pi, c_lo=c_lo: e.activation(
                out=PT[pi][:, c_lo:qlen], in_=Z[zi][:, c_lo:qlen], func=AF.Exp,
                bias=NCK[:, j, h:h + 1], scale=1.0),
                reads=zbufs(zi, c_lo, qlen) + [buf("NCK")], writes=[buf(f"PT{pi}")], cost=0.25 + (qlen - c_lo) / 1200.0)

        def pv(j):
            pi = (base + j) % 3
            c_lo = clo(j)
            for (a, b) in halves:
                lo = max(a, c_lo)
                if lo >= b:
                    continue
                j_last = min(nkb - 1, (q0 + b - 1) // 128)
                P.op("pe", lambda e, pi=pi, lo=lo, b=b, j_last=j_last: e.matmul(
                    pO[0:65, lo:b], lhsT=VP[:, j, h, :], rhs=PT[pi][:, lo:b],
                    start=(j == 0), stop=(j == j_last)),
                    reads=[buf("VP"), buf(f"PT{pi}")], writes=zbufs(2, lo, b))

        qk(0)
        for j in range(nkb):
            if j + 1 < nkb:
                qk(j + 1)
            ex(j)
            pv(j)
        ob = osb[h]
        bob = buf(f"osb{h}")
        P.op("dve", lambda e: e.tensor_copy(out=ob[:, 0:qlen], in_=pO[0:65, 0:qlen]),
             reads=zbufs(2, 0, qlen), writes=[bob])
        P.op("dve", lambda e: e.reciprocal(out=ob[64:65, 0:qlen], in_=ob[64:65, 0:qlen]),
             reads=[bob], writes=[bob])
        for (a, b) in halves:
            P.op("pe", lambda e, a=a, b=b: e.matmul(pO[0:64, a:b], lhsT=onesP[64:65, 0:64], rhs=ob[64:65, a:b],
                                                    start=True, stop=True),
                 reads=[bob, buf("onesP")], writes=zbufs(2, a, b))
        P.op("dve", lambda e: e.tensor_tensor(out=ob[0:64, 0:qlen], in0=ob[0:64, 0:qlen], in1=pO[0:64, 0:qlen],
                                              op=ALU.mult),
             reads=[bob] + zbufs(2, 0, qlen), writes=[bob])
        P.op("dve", lambda e: e.tensor_tensor(out=attg[h][:, 0:qlen], in0=ob[0:64, 0:qlen],
                                              in1=SG[par][h][:, 0:qlen], op=ALU.mult),
             reads=[bob, buf(f"SG{par}{h}")], writes=[buf(f"attg{h}")])
        P.dma("pool", mixin_ap(h * 64, (h + 1) * 64, q0, qlen), attg[h][:, 0:qlen],
              reads=[buf(f"attg{h}")], writes=[buf(f"mixinP{q0 // PM_}")])

    def sample_attention():
        ar_off[0] = 0
        clf = aa([32, 1024], F32)
        ccum = aa([32, 1024], F32)
        NCKc = aa([128, 8, 32], F32)
        MSK = aa([128, 8, 16], BF16)
        negt = aa([128, 16], F32)
        KTn = [aa([65, 256], BF16) for h in range(2)]
        QAs = [aa([65, 256], BF16) for h in range(2)]
        SGs = [aa([64, 256], BF16) for h in range(2)]
        kst_ = [aa([64, 1024], F32) for i in range(4)]
        vst_ = [aa([128, 8, 64], F32) for i in range(4)]
        KTc = [aa([65, 1024], BF16) for i in range(4)]
        VSc = [aa([128, 8, 65], BF16) for i in range(4)]
        PTs = [aa([128, 9, 16], BF16) for i in range(4)]
        obs_l = [aa([65, 16], F32) for i in range(4)]
        attS = [aa([64, 256], BF16) for h in range(2)]
        P.dma("sp", clf, clf_d[:, :], writes=[buf("clf")])
        P.op("dve", lambda e: e.tensor_tensor_scan(out=ccum, data0=onesP[0:32, 0:1].to_broadcast([32, 1024]), data1=clf,
                                                   initial=0.0, op0=ALU.mult, op1=ALU.add),
             reads=[buf("clf"), buf("onesP")], writes=[buf("ccum")])
        P.op("dve", lambda e: e.tensor_scalar(out=clf, in0=ccum, scalar1=ccum[:, 1023:1024], scalar2=-1.0,
                                              op0=ALU.subtract, op1=ALU.mult),
             reads=[buf("ccum")], writes=[buf("clf")])
        for blk in range(8):
            P.op("pe", lambda e, blk=blk: e.transpose(out=pM[:, blk * 32:(blk + 1) * 32], in_=clf[:, blk * 128:(blk + 1) * 128],
                                                       identity=identf[0:32, 0:32]),
                 reads=[buf("clf"), buf("identf")], writes=[buf("pM")])
        P.op("dve", lambda e: e.tensor_copy(out=NCKc.rearrange("p b r -> p (b r)"), in_=pM[:, 0:256]),
             reads=[buf("pM")], writes=[buf("NCKc")])
        for qq in range(8):
            P.op("pool", lambda e: e.memset(negt, NEG), writes=[buf("negt")])
            P.op("pool", lambda e, qq=qq: e.affine_select(out=negt, in_=negt, pattern=[[0, 16]], compare_op=ALU.is_ge,
                                                          fill=0.0, base=16 * qq - 1, channel_multiplier=-1),
                 reads=[buf("negt")], writes=[buf("negt")])
            P.op("pool", lambda e, qq=qq: e.tensor_copy(out=MSK[:, qq, :], in_=negt), reads=[buf("negt")], writes=[buf("MSK")])
            P.op("pool", lambda e: e.memset(negt, NEG), writes=[buf("negt")])
            P.op("pool", lambda e, qq=qq: e.affine_select(out=negt, in_=negt, pattern=[[-1, 16]], compare_op=ALU.is_gt,
                                                          fill=0.0, base=-16 * qq, channel_multiplier=1),
                 reads=[buf("negt")], writes=[buf("negt")])
            P.op("pool", lambda e, qq=qq: e.tensor_tensor(out=negt, in0=negt, in1=MSK[:, qq, :], op=ALU.add),
                 reads=[buf("negt"), buf("MSK")], writes=[buf("negt")])
            P.op("pool", lambda e, qq=qq: e.tensor_copy(out=MSK[:, qq, :], in_=negt), reads=[buf("negt")], writes=[buf("MSK")])
        for h in range(2):
            P.dma("sp", KTn[h][0:64, :], kt_d[h, :, TP:TP + 256], reads=[buf(f"kt_d{NTILE}")], writes=[buf(f"KTn{h}")])
            P.op("pool", lambda e, h=h: e.memset(KTn[h][64:65, :], 1.0), writes=[buf(f"KTn{h}")])
            P.dma("sp", QAs[h], qt_d[h, :, TP:TP + 256], reads=[buf(f"qt_d{NTILE}")], writes=[buf(f"QAs{h}")])
            P.dma("sp", SGs[h], sga_d[h, :, TP:TP + 256], reads=[buf(f"sga_d{NTILE}")], writes=[buf(f"SGs{h}")])
            for i in range(2):
                pass
        for i in range(4):
            P.op("pool", lambda e, i=i: e.memset(KTc[i][64:65, :], 1.0), writes=[buf(f"KTc{i}")])
            P.op("pool", lambda e, i=i: e.memset(VSc[i][:, :, 64:65], 1.0), writes=[buf(f"VSc{i}")])
        def one_qh(q, h, i):
            if True:
                r = q * 2 + h
                obs = obs_l[i]
                bobs = buf(f"obs{i}")
                P.dma("sp", kst_[i], kc_d[q, h, :, :], writes=[buf(f"kst_{i}")])
                P.dma("sp", vst_[i], vc_d[q, h, :, :].rearrange("(b p) d -> p b d", p=128), writes=[buf(f"vst_{i}")])
                P.op("dve", lambda e, i=i: e.tensor_copy(out=KTc[i][0:64, :], in_=kst_[i]),
                     reads=[buf(f"kst_{i}")], writes=[buf(f"KTc{i}")])
                P.op("pool", lambda e, i=i: e.tensor_copy(out=VSc[i][:, :, 0:64], in_=vst_[i]),
                     reads=[buf(f"vst_{i}")], writes=[buf(f"VSc{i}")])
                zi = i
                Sps = Z[zi][:, 0:144].rearrange("p (b t) -> p b t", t=16)
                qs = slice(q * 16, (q + 1) * 16)
                sb_, qq = q // 8, q % 8
                for blk in range(8):
                    P.op("pe", lambda e, blk=blk, i=i, Sps=Sps, qs=qs: e.matmul(
                        Sps[:, blk, :], lhsT=KTc[i][:, blk * 128:(blk + 1) * 128], rhs=QAs[h][:, qs], start=True, stop=True),
                        reads=[buf(f"KTc{i}"), buf(f"QAs{h}")], writes=[buf(f"pZ{zi}0")])
                P.op("pe", lambda e, Sps=Sps, qs=qs, sb_=sb_: e.matmul(
                    Sps[:, 8, :], lhsT=KTn[h][:, sb_ * 128:(sb_ + 1) * 128], rhs=QAs[h][:, qs], start=True, stop=False),
                    reads=[buf(f"KTn{h}"), buf(f"QAs{h}")], writes=[buf(f"pZ{zi}0")])
                P.op("pe", lambda e, Sps=Sps, qq=qq: e.matmul(Sps[:, 8, :], lhsT=identb, rhs=MSK[:, qq, :], start=False, stop=True),
                     reads=[buf("identb"), buf("MSK")], writes=[buf(f"pZ{zi}0")])
                for blk in range(9):
                    bias = NCKc[:, blk, r:r + 1] if blk < 8 else NCK[:, NBLK + sb_, h:h + 1]
                    P.op("act", lambda e, blk=blk, i=i, Sps=Sps, bias=bias: e.activation(
                        out=PTs[i][:, blk, :], in_=Sps[:, blk, :], func=AF.Exp, bias=bias, scale=1.0),
                        reads=[buf(f"pZ{zi}0"), buf("NCKc"), buf("NCK")], writes=[buf(f"PTs{i}")])
                pO = Z[i][0:65, 512:528]
                for blk in range(9):
                    lhs = VSc[i][:, blk, :] if blk < 8 else VP[:, NBLK + sb_, h, :]
                    P.op("pe", lambda e, blk=blk, i=i, lhs=lhs, pO=pO: e.matmul(pO, lhsT=lhs, rhs=PTs[i][:, blk, :],
                                                                            start=(blk == 0), stop=(blk == 8)),
                         reads=[buf(f"VSc{i}"), buf("VP"), buf(f"PTs{i}")], writes=[buf(f"pZ{i}1")])
                P.op("dve", lambda e, pO=pO: e.tensor_copy(out=obs, in_=pO), reads=[buf(f"pZ{i}1")], writes=[bobs])
                P.op("dve", lambda e: e.reciprocal(out=obs[64:65, :], in_=obs[64:65, :]), reads=[bobs], writes=[bobs])
                P.op("pe", lambda e, i=i: e.matmul(Z[i][0:64, 512:528], lhsT=onesP[64:65, 0:64], rhs=obs[64:65, :],
                                                   start=True, stop=True),
                     reads=[bobs, buf("onesP")], writes=[buf(f"pZ{i}1")])
                P.op("dve", lambda e, i=i: e.tensor_tensor(out=obs[0:64, :], in0=obs[0:64, :], in1=Z[i][0:64, 512:528], op=ALU.mult),
                     reads=[bobs, buf(f"pZ{i}1")], writes=[bobs])
                P.op("dve", lambda e, qs=qs: e.tensor_tensor(out=attS[h][:, qs], in0=obs[0:64, :], in1=SGs[h][:, qs], op=ALU.mult),
                     reads=[bobs, buf(f"SGs{h}")], writes=[buf(f"attS{h}")])
        it = 0
        for q in range(16):
            for h in range(2):
                one_qh(q, h, it % 4)
                it += 1
        for h in range(2):
            P.dma("pool", mixin_ap(h * 64, (h + 1) * 64, TP, 256), attS[h], reads=[buf(f"attS{h}")],
                  writes=[buf(f"mixinP{TP // PM_}")])

    tiles_x = [(ti * TW, TW) for ti in range(NTILE)] + [(TP, 256)]

    def g_alloc():
        G_ = {}
        G_['wgs'] = aa([128, 4, 128], F32)
        G_['wg'] = aa([128, 4, 128], BF16)
        G_['bg'] = aa([128, 1], F32)
        G_['nbg'] = aa([128, 1], F32)
        G_['zl'] = [aa([128, 4, TW], BF16) for i in range(2)]
        G_['zm'] = [aa([128, TW], BF16) for i in range(2)]
        G_['sgm'] = [aa([128, TW], BF16) for i in range(2)]
        G_['tg'] = aa([128, TW], F32)
        G_['tg2'] = aa([128, TW], F32)
        G_['s5o'] = [aa([128, TW], BF16) for i in range(2)]
        return G_

    def g_setup(G_):
        wgs, wg, bg, nbg = G_["wgs"], G_["wg"], G_["bg"], G_["nbg"]
        P.dma("sp", wgs, wglu_d.rearrange("(kc p) f -> p kc f", p=128), writes=[buf("wgs")])
        P.dma("sp", bg, bglu_d[:, :], writes=[buf("bg")])
        P.op("dve", lambda e: e.tensor_copy(out=wg, in_=wgs), reads=[buf("wgs")], writes=[buf("wg")])
        P.op("dve", lambda e: e.tensor_scalar(out=nbg, in0=bg, scalar1=-1.0, scalar2=None, op0=ALU.mult),
             reads=[buf("bg")], writes=[buf("nbg")])

    def g_tile(G_, k):
        wg, nbg, zl, zm, sgm, tg, tg2, s5o = (G_[x] for x in ("wg", "nbg", "zl", "zm", "sgm", "tg", "tg2", "s5o"))
        pG = zb(3, 0)
        c0, n = tiles_x[k]
        if True:
            i = k % 2
            P.dma("sp", zl[i][:, :, 0:n], zall_ap(c0, n).rearrange("(kc p) t -> p kc t", p=128),
                  reads=[buf(f"zallP{c0 // PZ}")], writes=[buf(f"zl{i}")])
            P.dma("sp", zm[i][:, 0:n], zin_ap(c0, n), reads=[buf(f"zinP{c0 // PZ}")], writes=[buf(f"zm{i}")])
            P.dma("sp", sgm[i][:, 0:n], sgs_d[:, c0:c0 + n], reads=[buf(f"sgs_d{k}")], writes=[buf(f"sgm{i}")])
            for kc in range(4):
                P.op("pe", lambda e, i=i, kc=kc, n=n: e.matmul(pG[:, 0:n], lhsT=wg[:, kc, :], rhs=zl[i][:, kc, 0:n],
                                                               start=(kc == 0), stop=(kc == 3)),
                     reads=[buf(f"zl{i}"), buf("wg")], writes=[buf("pZ30")])
            P.op("act", lambda e, i=i, n=n: e.activation(out=tg[:, 0:n], in_=pG[:, 0:n], func=AF.Exp, bias=nbg[:, 0:1],
                                                         scale=-1.0), reads=[buf("pZ30"), buf("nbg")], writes=[buf("tg")])
            P.op("act", lambda e, n=n: e.activation(out=tg[:, 0:n], in_=tg[:, 0:n], func=AF.Ln, bias=1.0, scale=1.0),
                 reads=[buf("tg")], writes=[buf("tg")])
            P.op("act", lambda e, n=n: e.activation(out=tg[:, 0:n], in_=tg[:, 0:n], func=AF.Exp, scale=-1.0),
                 reads=[buf("tg")], writes=[buf("tg")])
            P.op("dve", lambda e, i=i, n=n: e.tensor_tensor(out=tg2[:, 0:n], in0=zm[i][:, 0:n], in1=tg[:, 0:n], op=ALU.mult),
                 reads=[buf("tg"), buf(f"zm{i}")], writes=[buf("tg2")])
            P.op("dve", lambda e, i=i, n=n: e.tensor_tensor(out=s5o[i][:, 0:n], in0=tg2[:, 0:n], in1=sgm[i][:, 0:n], op=ALU.mult),
                 reads=[buf("tg2"), buf(f"sgm{i}")], writes=[buf(f"s5o{i}")])
            P.dma("pool", mixin_ap(128, 256, c0, n), s5o[i][:, 0:n], reads=[buf(f"s5o{i}")], writes=[buf(f"mixinP{c0 // PM_}")])

    def c_alloc():
        C_ = {}
        C_['wos'] = [aa([128, 256], F32) for i in range(2)]
        C_['wo'] = aa([128, 8, 256], BF16)
        C_['ml'] = [aa([128, 8, TW], BF16) for i in range(2)]
        C_['xc'] = [aa([128, 4, 256], F32) for i in range(1)]
        C_['yst'] = [aa([128, 4, 256], F32) for i in range(1)]
        return C_

    def c_setup(C_):
        wos, wo = C_['wos'], C_['wo']
        for kc in range(8):
            s = kc % 2
            P.dma("sp", wos[s], wout_d[kc * 128:(kc + 1) * 128, :], writes=[buf(f"wos{s}")])
            P.op("dve", lambda e, kc=kc, s=s: e.tensor_copy(out=wo[:, kc, :], in_=wos[s]),
                 reads=[buf(f"wos{s}")], writes=[buf("wo")])

    def c_tile(C_, k):
        wo, ml, xc, yst = C_['wo'], C_['ml'], C_['xc'], C_['yst']
        pC = zb(3, 1)
        c0, n = tiles_x[k]
        if True:
            i = k % 2
            ns = n // 128
            P.dma("sp", ml[i][:, :, 0:n], mixall_ap(c0, n).rearrange("(kc p) t -> p kc t", p=128),
                  reads=[buf(f"mixallP{c0 // PM_}")], writes=[buf(f"ml{i}")])
            P.dma("sp", xc[0][:, 0:ns, :], xc_d[c0:c0 + n, :].rearrange("(s p) c -> p s c", p=128), writes=[buf("xc0")])
            for s2 in range(0, ns, 2):
                for s in range(s2, min(s2 + 2, ns)):
                    for kc in range(8):
                        P.op("pe", lambda e, i=i, s=s, kc=kc: e.matmul(
                            pC[:, (s % 2) * 256:(s % 2) * 256 + 256], lhsT=ml[i][:, kc, s * 128:(s + 1) * 128], rhs=wo[:, kc, :],
                            start=(kc == 0), stop=(kc == 7)),
                            reads=[buf(f"ml{i}"), buf("wo")], writes=[buf("pZ31")])
                w2 = min(2, ns - s2)
                P.op("dve", lambda e, s2=s2, w2=w2: e.tensor_tensor(
                    out=yst[0][:, s2:s2 + w2, :], in0=pC.rearrange("p (s c) -> p s c", s=2)[:, 0:w2, :],
                    in1=xc[0][:, s2:s2 + w2, :], op=ALU.add),
                    reads=[buf("pZ31"), buf("xc0")], writes=[buf("yst0")])
            out_dmas.append(P.dma("pool", y_o[c0:c0 + n, :].rearrange("(s p) c -> p s c", p=128), yst[0][:, 0:ns, :],
                                  reads=[buf("yst0")]))

    def ag2(pi):
        P.coll(lambda e: e.collective_compute("AllGather", ALU.bypass, replica_groups=GROUPS,
                                              ins=[mixin_p[pi]], outs=[mixall_p[pi]]),
               reads=[buf(f"mixinP{pi}")], writes=[buf(f"mixallP{pi}")])

    if "SAMP" in stages:
        P.barrier()
        sample_attention()
    P.barrier()
    ar_off[0] = 0
    KT = [aa([65, TP], BF16) for h in range(2)]
    QA = [[aa([65, QB], BF16) for h in range(2)] for p in range(2)]
    SG = [[aa([64, QB], BF16) for h in range(2)] for p in range(2)]
    PT = [aa([128, QB], BF16) for i in range(3)]
    osb = [aa([65, QB], F32) for h in range(2)]
    attg = [aa([64, QB], BF16) for h in range(2)]
    if "ATT" in stages:
        do_x = "X" in stages
        if do_x:
            G_ = g_alloc()
            C_ = c_alloc()
            g_setup(G_)
            c_setup(C_)
        ntl = (n_real + TW - 1) // TW
        for h in range(2):
            P.dma("sp", KT[h][0:64, 0:ntl * TW], kt_d[h, :, 0:ntl * TW],
                  reads=[buf(f"kt_d{ti}") for ti in range(ntl)], writes=[buf(f"KT{h}")])
            P.op("pool", lambda e, h=h: e.memset(KT[h][64:65, :], 1.0), writes=[buf(f"KT{h}")])
        ntile_x = len(tiles_x)
        g_next = [0]
        c_queue = []
        ag2_done = [0]

        def after_unit(u, last):
            if not do_x:
                return
            while g_next[0] < ntile_x and (g_next[0] <= u or last):
                g_tile(G_, g_next[0])
                g_next[0] += 1
            while ag2_done[0] < npm:
                pi = ag2_done[0]
                tok_end = min((pi + 1) * PM_, TX)
                need_tiles = [k for k, (c0, n) in enumerate(tiles_x) if c0 < tok_end]
                need_q = [qi for qi, (q0, ql) in enumerate(qbs) if q0 < tok_end]
                if (max(need_tiles) < g_next[0]) and (max(need_q) * 2 + 1 <= u or last):
                    ag2(pi)
                    ag2_done[0] += 1
                    c_queue.extend([(k, u + 4) for k, (c0, n) in enumerate(tiles_x) if pi * PM_ <= c0 < tok_end])
                else:
                    break
            if c_queue and (c_queue[0][1] <= u or last):
                n_emit = len(c_queue) if last else 1
                for _ in range(n_emit):
                    k, _u = c_queue.pop(0)
                    c_tile(C_, k)

        u = 0
        nunits = 2 * len(qbs)
        for qi, (q0, qlen) in enumerate(qbs):
            par = qi % 2
            for h in range(2):
                rd = [buf(f"qt_d{ti}") for ti in tiles_of(q0, q0 + qlen)]
                P.dma("sp", QA[par][h][:, 0:qlen], qt_d[h, :, q0:q0 + qlen], reads=rd, writes=[buf(f"QA{par}{h}")])
                rd = [buf(f"sga_d{ti}") for ti in tiles_of(q0, q0 + qlen)]
                P.dma("sp", SG[par][h][:, 0:qlen], sga_d[h, :, q0:q0 + qlen], reads=rd, writes=[buf(f"SG{par}{h}")])
            for h in range(2):
                attend(qi, h, q0, qlen)
                after_unit(u, u == nunits - 1)
                u += 1

    P.barrier()
    P.wait("sp", out_dmas)
    if os.environ.get("MK_RESCHED", "1") == "1":
        est = P.reschedule(window=int(os.environ.get('MK_WIN', '128')), hop=float(os.environ.get('MK_HOP', '1.5')))
    stats = P.emit(stack)
    stack.close()
    return nc, stats


_CACHE = {}


def _prep_core(c, I):
    b, hp = c // 4, c % 4
    f32 = np.float32
    x = np.zeros((TP, D_MODEL), f32)
    x[:N_META] = I["meta_tokens"]
    nreal = min(L_REAL, TP)
    x[N_META:nreal] = I["x_prompt"][b][:nreal - N_META]
    w = I["w_in"][0]
    cols = np.concatenate([
        np.arange(128 * hp, 128 * hp + 128), 512 + np.arange(128 * hp, 128 * hp + 128),
        1024 + np.arange(128 * hp, 128 * hp + 128), 1544 + np.arange(128 * hp, 128 * hp + 128),
        2056 + np.arange(128 * hp, 128 * hp + 128), 2568 + np.arange(128 * hp, 128 * hp + 128),
        1536 + np.arange(2 * hp, 2 * hp + 2)])
    m = {
        "x": x,
        "w_in_c": np.ascontiguousarray(w[:, cols]),
        "norm_g": np.ascontiguousarray(I["norm_g"][0].reshape(8, 128).T),
        "b_f": np.ascontiguousarray(np.broadcast_to(I["b_f"][0, 2 * hp:2 * hp + 2][None, :], (128, 2))),
        "qg": np.ascontiguousarray(np.tile(I["q_norm_g"][0], 2)[:, None]),
        "kg": np.ascontiguousarray(np.tile(I["k_norm_g"][0], 2)[:, None]),
    }
    G = slice(8 * hp, 8 * hp + 8)
    are, aim = I["s5_a_re"][0, G], I["s5_a_im"][0, G]
    bre = I["s5_b_re"][0, G].transpose(1, 0, 2)
    bim = I["s5_b_im"][0, G].transpose(1, 0, 2)
    cre = I["s5_c_re"][0, G].transpose(2, 0, 1)
    cim = I["s5_c_im"][0, G].transpose(2, 0, 1)
    m.update({
        "s5_are": np.tile(are.T, (2, 1)),
        "s5_aim": np.tile(aim.T, (2, 1)),
        "s5_ldt": np.broadcast_to(I["s5_log_dt"][0, G][None, :], (128, 8)),
        "s5_d": I["s5_d"][0, G].reshape(128, 1),
        "s5_x1": np.concatenate([bre, bim], 0),
        "s5_x2": np.concatenate([bim, bre], 0),
        "s5_cx1": np.concatenate([cre, cim], 0),
        "s5_cx2": np.concatenate([cim, cre], 0),
    })
    Q = slice(16 * b, 16 * b + 16)
    H2 = slice(2 * hp, 2 * hp + 2)
    xsmp = I["x_sample"][Q].reshape(256, D_MODEL)
    ocols = slice(256 * hp, 256 * hp + 256)
    wo = I["w_out"][0]
    rows = np.concatenate([np.concatenate([np.arange(128 * r, 128 * r + 128), 512 + np.arange(128 * r, 128 * r + 128)])
                           for r in range(4)])
    sre = I["state_s5_re"][0, Q, G, :].transpose(2, 1, 0)
    sim_ = I["state_s5_im"][0, Q, G, :].transpose(2, 1, 0)
    m.update({
        "xsmp": xsmp,
        "clf": I["cache_logf"][0, Q, :, H2].transpose(0, 2, 1).reshape(32, 1024),
        "kcT": I["cache_k"][0, Q, :, H2, :].transpose(0, 2, 3, 1),
        "vc": I["cache_v"][0, Q, :, H2, :].transpose(0, 2, 1, 3),
        "wglu_c": I["w_glu"][0][:, 128 * hp:128 * hp + 128],
        "bglu_c": I["b_glu"][0, 128 * hp:128 * hp + 128][:, None],
        "wout_c": wo[rows][:, ocols],
        "x_c": np.concatenate([x[:, ocols], xsmp[:, ocols]], 0),
        "s5x0": np.concatenate([sre, sim_], 0),
    })
    return {k: np.ascontiguousarray(v, dtype=f32) for k, v in m.items()}


def kernel(**inputs):
    I = {k: np.asarray(v) for k, v in inputs.items()}
    if "nc" not in _CACHE:
        _CACHE["nc"] = build_program()[0]
    nc = _CACHE["nc"]
    in_maps = [_prep_core(c, I) for c in range(8)]
    res = run_bass_kernel_spmd(nc, in_maps, core_ids=list(range(8)))
    R = res.results
    f32 = np.float32
    y_p = np.zeros((2, SEQ, D_MODEL), f32)
    y_s = np.zeros((32, 16, D_MODEL), f32)
    k_p = np.zeros((1, 2, L_REAL, 8, 64), f32)
    v_p = np.zeros((1, 2, L_REAL, 8, 64), f32)
    lf_p = np.zeros((1, 2, L_REAL, 8), f32)
    sr_p = np.zeros((1, 2, 32, 64), f32)
    si_p = np.zeros((1, 2, 32, 64), f32)
    k_s = np.zeros((1, 32, 16, 8, 64), f32)
    v_s = np.zeros((1, 32, 16, 8, 64), f32)
    lf_s = np.zeros((1, 32, 16, 8), f32)
    sr_s = np.zeros((1, 32, 32, 64), f32)
    si_s = np.zeros((1, 32, 32, 64), f32)
    for c in range(8):
        b, hp = c // 4, c % 4
        r = R[c]
        nr = min(L_REAL, TP)
        k_p[0, b, :nr, 2 * hp:2 * hp + 2, :] = r["k_out"][:nr].reshape(nr, 2, 64)
        v_p[0, b, :nr, 2 * hp:2 * hp + 2, :] = r["v_out"][:nr].reshape(nr, 2, 64)
        lf_p[0, b, :nr, 2 * hp:2 * hp + 2] = r["logf_out"][:nr]
        if "y_out" in r:
            cols = slice(256 * hp, 256 * hp + 256)
            G = slice(8 * hp, 8 * hp + 8)
            Q = slice(16 * b, 16 * b + 16)
            yo = r["y_out"]
            y_p[b, :nr - N_META, cols] = yo[N_META:nr]
            y_s[Q, :, cols] = yo[TP:TP + 256].reshape(16, 16, 256)
            k_s[0, Q, :, 2 * hp:2 * hp + 2, :] = r["ks_out"].reshape(16, 16, 2, 64)
            v_s[0, Q, :, 2 * hp:2 * hp + 2, :] = r["vs_out"].reshape(16, 16, 2, 64)
            lf_s[0, Q, :, 2 * hp:2 * hp + 2] = r["lfs_out"].reshape(16, 16, 2)
            fin = r["s5fin_out"]
            sr_p[0, b, G, :] = fin[0:64, :].T
            si_p[0, b, G, :] = fin[64:128, :].T
            fs = r["s5fins_out"]
            sr_s[0, Q, G, :] = fs[0:64].transpose(2, 1, 0)
            si_s[0, Q, G, :] = fs[64:128].transpose(2, 1, 0)
    _CACHE["last"] = R
    return (y_p, y_s, k_p, v_p, lf_p, sr_p, si_p, k_s, v_s, lf_s, sr_s, si_s)
```
